# Optimizing a Trainium2 kernel written in Bass

```python
import math
import jax, jax.numpy as jnp
from jax import lax
import numpy as np

D_MODEL = 1024
BATCH = 8
SEQ = 2048
DEPTH = 4

MEM_LEN = 256
N_BRANCH = 4
BRANCH_WIDTH = D_MODEL // 4
RW_HEADS = 4
RW_HEAD_DIM = BRANCH_WIDTH // RW_HEADS
RW_DECAY_RANK = 64
RW_AAA_RANK = 64
RW_GATE_RANK = 160
RW_LN_EPS = 64e-5
RW_SHIFT_COLS = 3 * BRANCH_WIDTH + RW_DECAY_RANK + RW_AAA_RANK + RW_GATE_RANK
CONV_CH = BRANCH_WIDTH
CONV_WIDTH = 31
GLA_HEADS = 4
GLA_DK = BRANCH_WIDTH // 2 // GLA_HEADS
GLA_DV = BRANCH_WIDTH // GLA_HEADS
GLA_GATE_RANK = 16
GLA_TAU = 16.0
GLA_CHUNK = 64
GLA_EPS = 1e-5
FOX_HEADS = 4
FOX_HEAD_DIM = BRANCH_WIDTH // FOX_HEADS
FOX_BLOCK = 128
XA_HEADS = 4
XA_HEAD_DIM = D_MODEL // XA_HEADS
D_FF = ((8 * D_MODEL + 3 * 256 - 1) // (3 * 256)) * 256
ALPHA = (2.0 * DEPTH) ** 0.25
BETA = (8.0 * DEPTH) ** -0.25
LN_EPS = 1e-5

COL_SIZES = (
    RW_SHIFT_COLS,
    2 * CONV_CH,
    GLA_HEADS * GLA_DK,
    GLA_HEADS * GLA_DK,
    GLA_HEADS * GLA_DV,
    GLA_HEADS * GLA_DV,
    GLA_GATE_RANK,
    FOX_HEADS * FOX_HEAD_DIM,
    FOX_HEADS * FOX_HEAD_DIM,
    FOX_HEADS * FOX_HEAD_DIM,
    FOX_HEADS,
    N_BRANCH * D_MODEL,
)
D_IN = sum(COL_SIZES)

kernel_name = "hybrid_rwkv7_conv_gla_fox_deepnorm"


def _split(a, sizes):
    out, start = [], 0
    for s in sizes:
        out.append(a[..., start:start + s])
        start += s
    return out


def _layer_norm(x, g, b, eps=LN_EPS):
    xf = x.astype(jnp.float32)
    mu = jnp.mean(xf, axis=-1, keepdims=True)
    var = jnp.mean(jnp.square(xf - mu), axis=-1, keepdims=True)
    return ((xf - mu) * lax.rsqrt(var + eps)).astype(x.dtype) * g + b


def _token_shift(p, mu):
    prev = jnp.pad(p, ((0, 0), (1, 0), (0, 0)))[:, :-1]
    return p + (prev - p) * mu


def _rwkv7_scan(r, decay, k, v, a, b):
    Bsz, T, H, N = r.shape

    def step(S, inp):
        r_t, w_t, k_t, v_t, a_t, b_t = inp
        sa = jnp.einsum('bhij,bhj->bhi', S, a_t)
        S = (S * w_t[:, :, None, :] + sa[..., None] * b_t[:, :, None, :]
             + v_t[..., None] * k_t[:, :, None, :])
        return S, jnp.einsum('bhij,bhj->bhi', S, r_t)

    xs = tuple(jnp.moveaxis(t, 1, 0) for t in (r, decay, k, v, a, b))
    S0 = jnp.zeros((Bsz, H, N, N), jnp.float32)
    _, y = lax.scan(step, S0, xs)
    return jnp.moveaxis(y, 0, 1)


def _rwkv7_branch(p_rw, mu, w0, w2, a0, a2, g2, k_k, k_a, r_k, ln_g, ln_b):
    Bsz, T, _ = p_rw.shape
    dt = p_rw.dtype
    p = _token_shift(p_rw, mu)
    r, k, v, xw, xa, xg = _split(p, (BRANCH_WIDTH, BRANCH_WIDTH, BRANCH_WIDTH,
                                     RW_DECAY_RANK, RW_AAA_RANK, RW_GATE_RANK))
    w_log = -jax.nn.softplus(-(w0 + jnp.tanh(xw) @ w2).astype(jnp.float32)) - 0.5
    decay = jnp.exp(-jnp.exp(w_log))
    a = jax.nn.sigmoid((a0 + xa @ a2).astype(jnp.float32))
    g = jax.nn.sigmoid(xg) @ g2
    hs = lambda t: t.astype(jnp.float32).reshape(Bsz, T, RW_HEADS, RW_HEAD_DIM)
    kk = hs(k * k_k)
    kk = kk / jnp.maximum(jnp.sqrt(jnp.sum(kk * kk, axis=-1, keepdims=True)), 1e-12)
    k_mod = k.astype(jnp.float32) * (1.0 + (a - 1.0) * k_a.astype(jnp.float32))
    rh, kh, vh, ah, dh = hs(r), hs(k_mod), hs(v), hs(a), hs(decay)
    y = _rwkv7_scan(rh, dh, kh, vh, -kk, kk * ah)
    mu_y = jnp.mean(y, axis=-1, keepdims=True)
    var_y = jnp.mean(jnp.square(y - mu_y), axis=-1, keepdims=True)
    y = (y - mu_y) * lax.rsqrt(var_y + RW_LN_EPS)
    y = y.reshape(Bsz, T, BRANCH_WIDTH) * ln_g.astype(jnp.float32) + ln_b.astype(jnp.float32)
    bonus = jnp.sum(rh * kh * r_k.astype(jnp.float32), axis=-1, keepdims=True) * vh
    y = y + bonus.reshape(Bsz, T, BRANCH_WIDTH)
    return y.astype(dt) * g


def _conv_branch(p_cv, cv_w, cv_b, ln_g, ln_b):
    a, b = _split(p_cv, (CONV_CH, CONV_CH))
    u = a * jax.nn.sigmoid(b)
    u = lax.conv_general_dilated(
        u, cv_w[:, None, :], window_strides=(1,), padding=((CONV_WIDTH - 1, 0),),
        dimension_numbers=('NWC', 'WIO', 'NWC'), feature_group_count=CONV_CH) + cv_b
    u = _layer_norm(u, ln_g, ln_b)
    return jax.nn.silu(u)


def _gla_chunked(q, k, v, log_a):
    Bsz, T, H, dk = q.shape
    dv = v.shape[-1]
    L = GLA_CHUNK
    n = T // L
    to_chunks = lambda t: t.reshape(Bsz, n, L, H, t.shape[-1]).transpose(1, 0, 3, 2, 4)
    q, k, v, log_a = (to_chunks(t) for t in (q, k, v, log_a))
    b = jnp.cumsum(log_a, axis=-2)
    b_last = b[..., -1:, :]
    q_dec = q * jnp.exp(b)
    k_inv = k * jnp.exp(-b)
    k_end = k * jnp.exp(b_last - b)
    mask = jnp.tril(jnp.ones((L, L), dtype=bool))
    scores = jnp.where(mask, jnp.einsum('nbhtd,nbhsd->nbhts', q_dec, k_inv), 0.0)
    o_intra = jnp.einsum('nbhts,nbhsv->nbhtv', scores, v)

    def step(S, inp):
        qd, ke, vc, bl = inp
        o = jnp.einsum('bhtd,bhdv->bhtv', qd, S)
        S = S * jnp.exp(bl[:, :, 0, :])[..., None] + jnp.einsum('bhsd,bhsv->bhdv', ke, vc)
        return S, o

    S0 = jnp.zeros((Bsz, H, dk, dv), jnp.float32)
    _, o_inter = lax.scan(step, S0, (q_dec, k_end, v, b_last))
    o = o_intra + o_inter
    return o.transpose(1, 0, 3, 2, 4).reshape(Bsz, T, H, dv)


def _gla_branch(g_q, g_k, g_v, g_r, g_z, a2, ab, ln_g):
    Bsz, T, _ = g_q.shape
    dt = g_q.dtype
    log_a = jax.nn.log_sigmoid((g_z @ a2 + ab).astype(jnp.float32)) / GLA_TAU
    q = g_q.astype(jnp.float32).reshape(Bsz, T, GLA_HEADS, GLA_DK) * (GLA_DK ** -0.5)
    k = g_k.astype(jnp.float32).reshape(Bsz, T, GLA_HEADS, GLA_DK)
    v = g_v.astype(jnp.float32).reshape(Bsz, T, GLA_HEADS, GLA_DV)
    o = _gla_chunked(q, k, v, log_a.reshape(Bsz, T, GLA_HEADS, GLA_DK))
    o = o * lax.rsqrt(jnp.mean(o * o, axis=-1, keepdims=True) + GLA_EPS)
    o = o.reshape(Bsz, T, GLA_HEADS * GLA_DV).astype(dt) * ln_g
    return o * jax.nn.silu(g_r)


def _fox_branch(f_q, f_k, f_v, f_z, bf):
    Bsz, T, _ = f_q.shape
    heads = lambda t: t.reshape(Bsz, T, FOX_HEADS, FOX_HEAD_DIM).transpose(0, 2, 1, 3)
    qh = heads(f_q) * (FOX_HEAD_DIM ** -0.5)
    kh, vh = heads(f_k), heads(f_v)
    log_f = jax.nn.log_sigmoid((f_z + bf).astype(jnp.float32))
    c = jnp.cumsum(log_f, axis=1).transpose(0, 2, 1)
    diag = jnp.tril(jnp.ones((FOX_BLOCK, FOX_BLOCK), dtype=bool))
    outs = []
    for i in range(T // FOX_BLOCK):
        q0, q1 = i * FOX_BLOCK, (i + 1) * FOX_BLOCK
        logits = jnp.einsum('bhtd,bhsd->bhts', qh[:, :, q0:q1], kh[:, :, :q1]).astype(jnp.float32)
        logits = logits + c[:, :, q0:q1, None] - c[:, :, None, :q1]
        mask = jnp.concatenate([jnp.ones((FOX_BLOCK, q0), dtype=bool), diag], axis=1)
        probs = jax.nn.softmax(jnp.where(mask, logits, -jnp.inf), axis=-1).astype(vh.dtype)
        outs.append(jnp.einsum('bhts,bhsd->bhtd', probs, vh[:, :, :q1]))
    o = jnp.concatenate(outs, axis=2)
    return o.transpose(0, 2, 1, 3).reshape(Bsz, T, FOX_HEADS * FOX_HEAD_DIM)


def _cross_attention(x, mem, wq, wk, wv, wo):
    Bsz, T, D = x.shape
    M = mem.shape[1]
    q = (x @ wq).reshape(Bsz, T, XA_HEADS, XA_HEAD_DIM)
    k = (mem @ wk).reshape(Bsz, M, XA_HEADS, XA_HEAD_DIM)
    v = (mem @ wv).reshape(Bsz, M, XA_HEADS, XA_HEAD_DIM)
    s = jnp.einsum('bthd,bmhd->bhtm', q, k).astype(jnp.float32) * (XA_HEAD_DIM ** -0.5)
    p = jax.nn.softmax(s, axis=-1).astype(x.dtype)
    o = jnp.einsum('bhtm,bmhd->bthd', p, v).reshape(Bsz, T, D)
    return o @ wo


def _swiglu(x, w1, w3, w2):
    return (jax.nn.silu(x @ w1) * (x @ w3)) @ w2


def setup_inputs(seed: int = 0) -> dict:
    key = jax.random.key(seed)
    ks = iter(jax.random.split(key, 48))
    L, D = DEPTH, D_MODEL
    nrm = lambda shape, scale: jax.random.normal(next(ks), shape, jnp.float32) * scale
    return {
        "x": nrm((BATCH, SEQ, D), 1.0),
        "mem": nrm((BATCH, MEM_LEN, D), 1.0),
        "w_in": nrm((L, D, D_IN), D ** -0.5),
        "rw_mu": jax.random.uniform(next(ks), (L, RW_SHIFT_COLS), jnp.float32),
        "rw_w0": -1.0 + nrm((L, BRANCH_WIDTH), 0.5),
        "rw_w2": nrm((L, RW_DECAY_RANK, BRANCH_WIDTH), RW_DECAY_RANK ** -0.5),
        "rw_a0": nrm((L, BRANCH_WIDTH), 0.1),
        "rw_a2": nrm((L, RW_AAA_RANK, BRANCH_WIDTH), RW_AAA_RANK ** -0.5),
        "rw_g2": nrm((L, RW_GATE_RANK, BRANCH_WIDTH), RW_GATE_RANK ** -0.5),
        "rw_kk": 1.0 + nrm((L, BRANCH_WIDTH), 0.1),
        "rw_ka": 1.0 + nrm((L, BRANCH_WIDTH), 0.1),
        "rw_rk": nrm((L, RW_HEADS, RW_HEAD_DIM), 0.1),
        "rw_ln_g": 1.0 + nrm((L, BRANCH_WIDTH), 0.02),
        "rw_ln_b": nrm((L, BRANCH_WIDTH), 0.02),
        "rw_up": nrm((L, BRANCH_WIDTH, D), BRANCH_WIDTH ** -0.5),
        "cv_w": nrm((L, CONV_WIDTH, CONV_CH), CONV_WIDTH ** -0.5),
        "cv_b": nrm((L, CONV_CH), 0.02),
        "cv_ln_g": 1.0 + nrm((L, CONV_CH), 0.02),
        "cv_ln_b": nrm((L, CONV_CH), 0.02),
        "cv_up": nrm((L, CONV_CH, D), CONV_CH ** -0.5),
        "gla_a2": nrm((L, GLA_GATE_RANK, GLA_HEADS * GLA_DK), GLA_GATE_RANK ** -0.5),
        "gla_ab": nrm((L, GLA_HEADS * GLA_DK), 0.1),
        "gla_ln_g": 1.0 + nrm((L, GLA_HEADS * GLA_DV), 0.02),
        "gla_up": nrm((L, GLA_HEADS * GLA_DV, D), (GLA_HEADS * GLA_DV) ** -0.5),
        "fox_bf": 3.0 + nrm((L, FOX_HEADS), 0.5),
        "fox_up": nrm((L, FOX_HEADS * FOX_HEAD_DIM, D), (FOX_HEADS * FOX_HEAD_DIM) ** -0.5),
        "gate_b": nrm((L, N_BRANCH, D), 0.1),
        "w_out": nrm((L, D, D), BETA * D ** -0.5),
        "xa_wq": nrm((L, D, D), D ** -0.5),
        "xa_wk": nrm((L, D, D), D ** -0.5),
        "xa_wv": nrm((L, D, D), D ** -0.5),
        "xa_wo": nrm((L, D, D), BETA * D ** -0.5),
        "ffn_w1": nrm((L, D, D_FF), D ** -0.5),
        "ffn_w3": nrm((L, D, D_FF), D ** -0.5),
        "ffn_w2": nrm((L, D_FF, D), BETA * D_FF ** -0.5),
        "ln_g": 1.0 + nrm((L, 3, D), 0.02),
        "ln_b": nrm((L, 3, D), 0.02),
    }


def reference(x, mem, w_in, rw_mu, rw_w0, rw_w2, rw_a0, rw_a2, rw_g2, rw_kk, rw_ka, rw_rk,
              rw_ln_g, rw_ln_b, rw_up, cv_w, cv_b, cv_ln_g, cv_ln_b, cv_up, gla_a2, gla_ab,
              gla_ln_g, gla_up, fox_bf, fox_up, gate_b, w_out, xa_wq, xa_wk, xa_wv, xa_wo,
              ffn_w1, ffn_w3, ffn_w2, ln_g, ln_b):
    Bsz, T, D = x.shape
    for l in range(DEPTH):
        p = x @ w_in[l]
        (p_rw, p_cv, g_q, g_k, g_v, g_r, g_z,
         f_q, f_k, f_v, f_z, p_gate) = _split(p, COL_SIZES)
        u_rw = _rwkv7_branch(p_rw, rw_mu[l], rw_w0[l], rw_w2[l], rw_a0[l], rw_a2[l], rw_g2[l],
                             rw_kk[l], rw_ka[l], rw_rk[l], rw_ln_g[l], rw_ln_b[l]) @ rw_up[l]
        u_cv = _conv_branch(p_cv, cv_w[l], cv_b[l], cv_ln_g[l], cv_ln_b[l]) @ cv_up[l]
        u_gla = _gla_branch(g_q, g_k, g_v, g_r, g_z, gla_a2[l], gla_ab[l], gla_ln_g[l]) @ gla_up[l]
        u_fox = _fox_branch(f_q, f_k, f_v, f_z, fox_bf[l]) @ fox_up[l]
        gates = jax.nn.sigmoid(p_gate.reshape(Bsz, T, N_BRANCH, D) + gate_b[l])
        merged = (gates[:, :, 0] * u_rw + gates[:, :, 1] * u_cv
                  + gates[:, :, 2] * u_gla + gates[:, :, 3] * u_fox)
        x = _layer_norm(ALPHA * x + merged @ w_out[l], ln_g[l, 0], ln_b[l, 0])
        y = _cross_attention(x, mem, xa_wq[l], xa_wk[l], xa_wv[l], xa_wo[l])
        x = _layer_norm(ALPHA * x + y, ln_g[l, 1], ln_b[l, 1])
        y = _swiglu(x, ffn_w1[l], ffn_w3[l], ffn_w2[l])
        x = _layer_norm(ALPHA * x + y, ln_g[l, 2], ln_b[l, 2])
    return x
```

```python
import numpy as np
from contextlib import ExitStack
import concourse.bass as bass
import concourse.mybir as mybir
from concourse.bass_utils import run_bass_kernel_spmd

F32 = mybir.dt.float32
BF16 = mybir.dt.bfloat16
AF = mybir.ActivationFunctionType
ALU = mybir.AluOpType

D = 1024
KC = 8
DEPTH = 4
SEQ = 2048
MEM = 256
DFF = 2816
DIN = 7220
ALPHA = (2.0 * DEPTH) ** 0.25
LN_EPS = 1e-5
COL_GATE = 3124

ENGS = ("pe", "act", "dve", "pool", "sp")
EPOCH = 30000
NDMA_SEM = {"sp": 16, "pool": 12, "act": 4}


class Buf:
    __slots__ = ("name", "w", "rs", "excl")

    def __init__(self, name="", excl=False):
        self.name = name
        self.w = None
        self.rs = []
        self.excl = excl


class Op:
    __slots__ = ("eng", "fn", "pos", "needs_inc", "inc", "isdma", "dsem", "dval", "waits", "vc")


class Prog:
    def __init__(self, nc):
        self.nc = nc
        self.ops = {e: [] for e in ENGS}
        self.clock = {e: {} for e in ENGS}
        self.dma_uses = {}
        self.dma_rr = {e: 0 for e in NDMA_SEM}
        self.dma_last = {}
        self.out_dmas = []
        self.rd_dmas = []

    def op(self, eng, fn, reads=(), writes=(), extra=(), rt=None):
        o = Op()
        o.eng = eng; o.fn = fn
        o.isdma = False; o.needs_inc = False; o.inc = None
        ex = list(extra)
        force = None
        if eng == "pe":
            lr = getattr(self, "last_rt", None)
            cur = rt if rt is not None else "full"
            if lr is not None and lr[1] != cur and (lr[1] != "full" and cur != "full"):
                force = lr[0]
            self._force = force
        self._record(o, reads, writes, ex)
        if eng == "pe":
            self.last_rt = (o, rt if rt is not None else "full")
        return o

    def dma(self, queue, fn, reads=(), writes=(), is_output=False):
        o = Op()
        o.eng = queue; o.fn = fn
        o.isdma = True; o.needs_inc = False; o.inc = None
        k = self.dma_rr[queue]
        self.dma_rr[queue] = (k + 1) % NDMA_SEM[queue]
        key = (queue, k)
        uses = self.dma_uses.get(key, 0)
        o.dsem = key
        o.dval = 16 * (uses + 1)
        self.dma_uses[key] = uses + 1
        prev = self.dma_last.get(key)
        self.dma_last[key] = o
        self._record(o, reads, writes, [prev] if prev is not None else [])
        if is_output:
            self.out_dmas.append(o)
        if len(reads) > 0:
            self.rd_dmas.append(o)
        return o

    def barrier(self):
        last = {e: (self.ops[e][-1] if self.ops[e] else None) for e in ("pe", "act", "dve", "pool")}
        for e in ("pe", "act", "dve", "pool", "sp"):
            ex = []
            for f, o in last.items():
                if f == e or o is None:
                    continue
                j = len(self.ops[f]) - 1
                while j >= 0 and (self.ops[f][j].isdma or self.ops[f][j].fn is None):
                    j -= 1
                if j >= 0:
                    ex.append(self.ops[f][j])
            self.op(e, None, extra=ex + list(self.rd_dmas))
        self.rd_dmas = []

    def _record(self, o, reads, writes, extra=()):
        if any(b.excl for b in reads):
            writes = list(writes) + [b for b in reads if b.excl and b not in writes]
            reads = [b for b in reads if not b.excl]
        e = o.eng
        lst = self.ops[e]
        o.pos = len(lst) + 1
        deps = []
        for b in reads:
            if b.w is not None:
                deps.append((b.w, "raw"))
        for b in writes:
            if b.w is not None:
                deps.append((b.w, "waw"))
            for r in b.rs:
                deps.append((r, "war"))
        for d in extra:
            deps.append((d, "raw"))
        clk = self.clock[e]
        waits = []
        force = getattr(self, "_force", None)
        self._force = None
        if force is not None and e == "pe" and clk.get(("self", e), 0) < force.pos:
            waits.append(force)
            force.needs_inc = True
            clk[("self", e)] = force.pos
        best = {}
        d2 = []
        for (y, kind) in deps:
            if y is o:
                continue
            if (not y.isdma) and y.eng != e:
                if y.eng not in best or best[y.eng].pos < y.pos:
                    best[y.eng] = y
            else:
                d2.append((y, kind))
        deps = d2 + [(y, "raw") for y in best.values()]
        for (y, kind) in deps:
            if y is o:
                continue
            if y.isdma:
                if clk.get(y.dsem, 0) >= y.dval:
                    continue
                waits.append(y)
                self._merge(clk, y.vc)
            elif y.eng == e:
                if e != "pe" and clk.get(("self", e), 0) < y.pos and y.fn is not None:
                    waits.append(y)
                    y.needs_inc = True
                    clk[("self", e)] = y.pos
            else:
                if clk.get(y.eng, 0) >= y.pos:
                    continue
                waits.append(y)
                y.needs_inc = True
                self._merge(clk, y.vc)
        o.waits = waits
        if o.isdma:
            vc = dict(clk)
            vc[o.dsem] = o.dval
            o.vc = vc
            clk[e] = o.pos
        else:
            clk[e] = o.pos
            o.vc = dict(clk)
        lst.append(o)
        for b in reads:
            b.rs.append(o)
        for b in writes:
            b.w = o
            b.rs = []

    @staticmethod
    def _merge(clk, vc):
        for k, v in vc.items():
            if isinstance(k, tuple) and k and k[0] == "self":
                continue
            if clk.get(k, 0) < v:
                clk[k] = v

    def finalize(self, block, stack):
        nc = self.nc
        fin = self.op("sp", None)
        clk = self.clock["sp"]
        for d in self.out_dmas:
            if clk.get(d.dsem, 0) < d.dval:
                fin.waits.append(d)
                clk[d.dsem] = d.dval
        esems = {}
        for e in ENGS:
            c = 0
            for o in self.ops[e]:
                if o.needs_inc and not o.isdma:
                    assert o.fn is not None
                    c += 1
                    o.inc = c
            nep = c // EPOCH + 1
            esems[e] = [stack.enter_context(nc.semaphore(f"s_{e}_{i}")) for i in range(nep)]
        dsems = {}
        for (q, k) in self.dma_uses:
            dsems[(q, k)] = stack.enter_context(nc.semaphore(f"d_{q}_{k}"))
        stats = {e: [len(self.ops[e]), 0] for e in ENGS}

        def emit(e, eng):
            for o in self.ops[e]:
                for y in o.waits:
                    if y.isdma:
                        eng.wait_ge(dsems[y.dsem], y.dval)
                    else:
                        ep = (y.inc - 1) // EPOCH
                        eng.wait_ge(esems[y.eng][ep], y.inc - ep * EPOCH)
                    stats[e][1] += 1
                if o.fn is None:
                    continue
                ins = o.fn(eng)
                if o.isdma:
                    ins.then_inc(dsems[o.dsem], 16)
                elif o.needs_inc:
                    ep = (o.inc - 1) // EPOCH
                    ins.then_inc(esems[e][ep], 1)

        @block.tensor
        def _(eng):
            emit("pe", eng)

        @block.scalar
        def _(eng):
            emit("act", eng)

        @block.vector
        def _(eng):
            emit("dve", eng)

        @block.gpsimd
        def _(eng):
            emit("pool", eng)

        @block.sync
        def _(eng):
            emit("sp", eng)
        return stats


def make_consts():
    c = {}
    c["ident"] = np.eye(128, dtype=np.float32)
    c["ones"] = np.ones((128, 128), np.float32)
    b64 = np.zeros((128, 128), np.float32)
    b64[:64, :64] = 1; b64[64:, 64:] = 1
    c["bones64"] = b64
    s = np.arange(128)[:, None]
    t = np.arange(128)[None, :]
    c["iu128"] = (s <= t).astype(np.float32)
    p = np.arange(128)[:, None]
    f = np.arange(128)[None, :]
    same = (p // 32) == (f // 32)
    c["bd32_iu"] = (same & ((p % 32) <= (f % 32))).astype(np.float32)
    su = (same & ((p % 32) < (f % 32))).astype(np.float32)
    sl_ = (same & ((p % 32) > (f % 32))).astype(np.float32)
    c["rwmask4"] = np.concatenate([su, sl_, su, c["bd32_iu"]], axis=1)
    names = list(c.keys())
    offs = {}
    o = 0
    for n in names:
        offs[n] = (o, c[n].shape[1])
        o += c[n].shape[1]
    arr = np.concatenate([c[n] for n in names], axis=1)
    return arr, offs


CONSTS, COFF = make_consts()
NCONST = CONSTS.shape[1]

PV = {}


def _pv_layout():
    o = 0
    for n, k in (("rw_mu", 9), ("rw_w0", 2), ("rw_a0", 2), ("rw_kk", 2), ("rw_ka", 2), ("rw_rk", 2),
                 ("rw_ln_g", 2), ("rw_ln_b", 2), ("cv_w", 62), ("cv_b", 2), ("cv_ln_g", 2),
                 ("cv_ln_b", 2), ("gla_ab", 1), ("gla_ln_g", 2), ("fox_bf", 1), ("gate_b", 32),
                 ("ln_g", 24), ("ln_b", 24)):
        PV[n] = (o, k)
        o += k
    return o


NPV = _pv_layout()


def _cols(v):
    v = np.asarray(v, np.float32).reshape(-1)
    n = v.shape[0]
    k = (n + 127) // 128
    buf = np.zeros((k * 128,), np.float32)
    buf[:n] = v
    return buf.reshape(k, 128).T


def pack_pvec(inp, l):
    out = np.zeros((128, NPV), np.float32)

    def put(name, arr):
        o, k = PV[name]
        assert arr.shape == (128, k), (name, arr.shape, k)
        out[:, o:o + k] = arr

    put("rw_mu", _cols(inp["rw_mu"][l]))
    for n in ("rw_w0", "rw_a0", "rw_kk", "rw_ka", "rw_ln_g", "rw_ln_b", "cv_b", "cv_ln_g", "cv_ln_b",
              "gla_ab", "gla_ln_g"):
        put(n, _cols(inp[n][l]))
    put("rw_rk", _cols(inp["rw_rk"][l].reshape(-1)))
    cw = np.asarray(inp["cv_w"][l], np.float32)
    cwp = cw.T.reshape(2, 128, 31).transpose(1, 0, 2).reshape(128, 62)
    put("cv_w", cwp)
    put("fox_bf", _cols(inp["fox_bf"][l]))
    put("gate_b", _cols(inp["gate_b"][l].reshape(-1)))
    put("ln_g", _cols(inp["ln_g"][l].reshape(-1)))
    put("ln_b", _cols(inp["ln_b"][l].reshape(-1)))
    return out


class K:
    pass


def build(T=SEQ, L=DEPTH, stages=("mix", "xa", "ffn"), dbg=(), mixers=("rw", "cv", "gla", "fox")):
    nc = bass.Bass("TRN2", target_bir_lowering=False)
    NTB = T // 512
    k = K()
    k.nc = nc; k.T = T; k.L = L; k.NTB = NTB; k.mixers = mixers
    dr = lambda n, s, kind="ExternalInput", dt=F32: nc.dram_tensor(n, s, dt, kind=kind).ap()
    k.x_d = dr("x", [T, D])
    if "xa" in stages:
        k.mem_d = dr("mem", [MEM, D])
    k.consts_d = dr("consts", [128, NCONST])
    k.pvec_d = dr("pvec", [L, 128, NPV])
    if "mix" in stages:
        k.w_in_d = dr("w_in", [L, D, DIN])
        k.rw_wa_d = dr("rw_wa", [L, 128, 256])
        k.rw_g2_d = dr("rw_g2", [L, 160, 256])
        k.gla_a2_d = dr("gla_a2", [L, 16, 128])
        k.ups_d = [dr(n, [L, 256, D]) for n in ("rw_up", "cv_up", "gla_up", "fox_up")]
        k.w_out_d = dr("w_out", [L, D, D])
    if "xa" in stages:
        k.xa_wq_d = dr("xa_wq", [L, D, D]); k.xa_wk_d = dr("xa_wk", [L, D, D])
        k.xa_wv_d = dr("xa_wv", [L, D, D]); k.xa_wo_d = dr("xa_wo", [L, D, D])
    if "ffn" in stages:
        k.w1_d = dr("ffn_w1", [L, D, DFF]); k.w3_d = dr("ffn_w3", [L, D, DFF]); k.w2_d = dr("ffn_w2", [L, DFF, D])
    k.out_d = dr("out", [T, D], kind="ExternalOutput")
    k.dbg_d = {}
    for (name, shape) in dbg:
        k.dbg_d[name] = dr("dbg_" + name, list(shape), kind="ExternalOutput", dt=BF16 if name in ("brT", "mg") else F32)

    with ExitStack() as st:
        k.st = st
        sb = lambda n, s, d=F32: st.enter_context(nc.sbuf_tensor(n, s, d))
        k.xres = sb("xres", [128, KC, T]); k.b_xres = [[Buf() for _ in range(NTB)] for _ in range(KC)]
        k.xT = sb("xT", [128, KC, T], BF16); k.b_xT = [[Buf() for _ in range(NTB)] for _ in range(KC)]
        k.cst = sb("cst", [128, NCONST]); k.b_cst = Buf()
        k.cstb = sb("cstb", [128, NCONST], BF16); k.b_cstb = Buf()
        k.pv = sb("pv", [128, NPV]); k.b_pv = Buf()
        k.npv = sb("npv", [128, NPV]); k.b_npv = Buf()
        NW = 3
        k.NW = NW
        k.wr = [sb(f"wr{i}", [128, KC, 512], BF16) for i in range(NW)]
        k.b_wr = [Buf() for _ in range(NW)]
        k.wr_i = 0
        k.ps = [st.enter_context(nc.psum_tensor(f"ps{i}", [128, 512], F32)) for i in range(8)]
        k.b_ps = [Buf(excl=True) for _ in range(8)]
        k.ps_i = 0
        k.held = set()
        k.psb = [p.bitcast(BF16) for p in k.ps]
        block = st.enter_context(nc.Block())
        P = Prog(nc)
        k.P = P

        prologue(k)
        for l in range(L):
            layer_params(k, l)
            if "mix" in stages:
                stage_mix(k, l)
            if "xa" in stages:
                stage_xa(k, l)
            if "ffn" in stages:
                stage_ffn(k, l)
        epilogue(k)
        k.stats = P.finalize(block, st)
    return nc, k


_UID = [0]


def sbt(k, stack, name, shape, dt=F32):
    _UID[0] += 1
    return stack.enter_context(k.nc.sbuf_tensor(f"{name}_{_UID[0]}", list(shape), dt))


def cs(k, name, bf=False):
    o, n = COFF[name]
    return (k.cstb if bf else k.cst)[:, o:o + n]


def pcol(k, name, j=0, neg=False, rows=128, r0=0):
    o, n = PV[name]
    t = k.npv if neg else k.pv
    return t[r0:r0 + rows, o + j:o + j + 1]


def mm(e, out, lhsT, rhs, pos, start=True, stop=True):
    return e.matmul(out, lhsT=lhsT, rhs=rhs, start=start, stop=stop, tile_position=pos)


def tr(e, out, in_, identity, pos):
    return e.transpose(out=out, in_=in_, identity=identity, tile_position=pos)


def nbank(k, hold=False):
    i = k.ps_i
    while i in k.held:
        i = (i + 1) % 8
    k.ps_i = (i + 1) % 8
    if hold:
        k.held.add(i)
    return k.ps[i], k.b_ps[i]


def release(k, pt):
    for i in range(8):
        if k.ps[i] is pt:
            k.held.discard(i)
            return
    raise AssertionError


def load_w(k, src, ncols, nk=KC):
    i = k.wr_i
    k.wr_i = (i + 1) % k.NW
    slot, b = k.wr[i], k.b_wr[i]
    v = src.rearrange("(c p) n -> p c n", p=128)
    k.P.dma("pool", lambda e: e.dma_start(out=slot[:, 0:nk, 0:ncols], in_=v), writes=[b])
    return slot, b


def prologue(k):
    P = k.P; T = k.T
    P.dma("sp", lambda e: e.dma_start(out=k.cst[:], in_=k.consts_d), writes=[k.b_cst])
    P.op("dve", lambda e: e.tensor_copy(out=k.cstb[:], in_=k.cst[:]), reads=[k.b_cst], writes=[k.b_cstb])
    for i in range(8):
        P.op("dve", lambda e, i=i: e.memset(k.ps[i][:], 0.0), writes=[k.b_ps[i]])
    with ExitStack() as s2:
        xin = [sbt(k, s2, f"xin{i}", [128, D], F32) for i in range(2)]
        b_xin = [Buf(), Buf()]
        ident = cs(k, "ident")
        for tt in range(T // 128):
            j = tt % 2
            P.dma("sp", lambda e, tt=tt, j=j: e.dma_start(out=xin[j][:], in_=k.x_d[tt * 128:(tt + 1) * 128, :]),
                  writes=[b_xin[j]])
            tb = tt // 4
            for g in range(2):
                pt, bp = nbank(k)
                for q in range(4):
                    c = g * 4 + q
                    P.op("pe", lambda e, pt=pt, j=j, c=c, q=q: e.transpose(
                        out=pt[:, q * 128:(q + 1) * 128], in_=xin[j][:, c * 128:(c + 1) * 128], identity=ident),
                        reads=[b_xin[j], k.b_cst], writes=[bp])
                for q in range(4):
                    c = g * 4 + q
                    dst = slice(tt * 128, (tt + 1) * 128)
                    P.op("act", lambda e, pt=pt, c=c, q=q, dst=dst: e.activation(
                        out=k.xres[:, c, dst], in_=pt[:, q * 128:(q + 1) * 128], func=AF.Copy),
                        reads=[bp], writes=[k.b_xres[c][tb]])
                    P.op("dve", lambda e, pt=pt, c=c, q=q, dst=dst: e.tensor_copy(
                        out=k.xT[:, c, dst], in_=pt[:, q * 128:(q + 1) * 128]),
                        reads=[bp], writes=[k.b_xT[c][tb]])
        P.barrier()


def layer_params(k, l):
    P = k.P
    P.dma("sp", lambda e: e.dma_start(out=k.pv[:], in_=k.pvec_d[l]), writes=[k.b_pv])
    P.op("dve", lambda e: e.tensor_scalar(out=k.npv[:], in0=k.pv[:], scalar1=-1.0, scalar2=None, op0=ALU.mult),
         reads=[k.b_pv], writes=[k.b_npv])


def epilogue(k):
    P = k.P; T = k.T
    with ExitStack() as s2:
        xo = [sbt(k, s2, f"xo{i}", [128, D], F32) for i in range(2)]
        b_xo = [Buf(), Buf()]
        ident = cs(k, "ident")
        for tt in range(T // 128):
            j = tt % 2
            tb = tt // 4
            for g in range(2):
                pt, bp = nbank(k)
                for q in range(4):
                    c = g * 4 + q
                    P.op("pe", lambda e, pt=pt, c=c, q=q, tt=tt: e.transpose(
                        out=pt[:, q * 128:(q + 1) * 128], in_=k.xres[:, c, tt * 128:(tt + 1) * 128], identity=ident),
                        reads=[k.b_xres[c][tb], k.b_cst], writes=[bp])
                eng = "act" if g == 0 else "dve"
                if g == 0:
                    P.op("act", lambda e, pt=pt, j=j: e.activation(out=xo[j][:, 0:512], in_=pt[:], func=AF.Copy),
                         reads=[bp], writes=[b_xo[j]])
                else:
                    P.op("dve", lambda e, pt=pt, j=j: e.tensor_copy(out=xo[j][:, 512:1024], in_=pt[:]),
                         reads=[bp], writes=[b_xo[j]])
            P.dma("sp", lambda e, tt=tt, j=j: e.dma_start(out=k.out_d[tt * 128:(tt + 1) * 128, :], in_=xo[j][:]),
                  reads=[b_xo[j]], is_output=True)
        P.barrier()


def ln_block(k, l, s, tb, zsq, b_zsq, st_t, b_st):
    P = k.P
    sl = slice(tb * 512, (tb + 1) * 512)
    rstd, nmr, b_r, b_n = ln_stats(k, [(k.xres[:, c, sl], k.b_xres[c][tb]) for c in range(KC)], D, LN_EPS, zsq, b_zsq, st_t, b_st)
    og, _ = PV["ln_g"]
    ob, _ = PV["ln_b"]
    for c in range(KC):
        xs = k.xres[:, c, sl]
        P.op("dve", lambda e, xs=xs: e.tensor_tensor(out=xs, in0=xs, in1=rstd, op=ALU.mult),
             reads=[k.b_xres[c][tb], b_r], writes=[k.b_xres[c][tb]])
        P.op("pool", lambda e, xs=xs: e.tensor_tensor(out=xs, in0=xs, in1=nmr, op=ALU.add),
             reads=[k.b_xres[c][tb], b_n], writes=[k.b_xres[c][tb]])
        gcol = k.pv[:, og + s * 8 + c: og + s * 8 + c + 1]
        bcol = k.pv[:, ob + s * 8 + c: ob + s * 8 + c + 1]
        P.op("act", lambda e, xs=xs, gcol=gcol, bcol=bcol: e.activation(out=xs, in_=xs, func=AF.Identity, scale=gcol, bias=bcol),
             reads=[k.b_xres[c][tb], k.b_pv], writes=[k.b_xres[c][tb]])
        P.op("pool", lambda e, xs=xs, c=c: e.tensor_copy(out=k.xT[:, c, sl], in_=xs),
             reads=[k.b_xres[c][tb]], writes=[k.b_xT[c][tb]])


def out_proj_ln(k, l, s, src, b_src, nkc, w_d, first=True, last=True, alpha_first=True, row0=0,
                lnbufs=None):
    P = k.P
    NTB = k.NTB
    halves = []
    for h in range(2):
        slot, b = load_w(k, w_d[row0:row0 + nkc * 128, h * 512:(h + 1) * 512], 512, nk=nkc)
        halves.append((slot, b))
    for tb in range(NTB):
        sl = slice(tb * 512, (tb + 1) * 512)
        for fc in range(KC):
            slot, bw = halves[fc // 4]
            co = (fc % 4) * 128
            pt, bp = nbank(k)
            for c in range(nkc):
                P.op("pe", lambda e, pt=pt, slot=slot, c=c, co=co, sl=sl: e.matmul(
                    pt[:], lhsT=slot[:, c, co:co + 128], rhs=src[:, c, sl], start=(c == 0), stop=(c == nkc - 1)),
                    reads=[bw, b_src[c][tb]], writes=[bp])
            xs = k.xres[:, fc, sl]
            if first:
                P.op("dve", lambda e, pt=pt, xs=xs: e.scalar_tensor_tensor(
                    out=xs, in0=xs, scalar=ALPHA, in1=pt[:], op0=ALU.mult, op1=ALU.add),
                    reads=[bp, k.b_xres[fc][tb]], writes=[k.b_xres[fc][tb]])
            else:
                P.op("dve", lambda e, pt=pt, xs=xs: e.tensor_tensor(out=xs, in0=xs, in1=pt[:], op=ALU.add),
                     reads=[bp, k.b_xres[fc][tb]], writes=[k.b_xres[fc][tb]])
        if last:
            ln_block(k, l, s, tb, *lnbufs)


def alloc_ln(k, s2):
    zsq = sbt(k, s2, "zsq", [128, 2, 512], F32)
    st_t = sbt(k, s2, "lnst", [128, 3, 512], F32)
    return (zsq, [Buf(), Buf()], st_t, [Buf() for _ in range(3)])


def stage_ffn(k, l):
    P = k.P; T = k.T; NTB = k.NTB
    parts = [(0, 8), (8, 16), (16, 22)]
    with ExitStack() as s2:
        g = sbt(k, s2, "ffg", [128, 8, T], BF16)
        b_g = [[Buf() for _ in range(NTB)] for _ in range(8)]
        sg = [sbt(k, s2, f"ffs{i}", [128, 512], F32) for i in range(2)]
        b_sg = [Buf(), Buf()]
        lnb = alloc_ln(k, s2)
        si = 0
        for pi, (c0, c1) in enumerate(parts):
            n = c1 - c0
            for q0 in range(c0, c1, 4):
                nq = min(4, c1 - q0)
                s1, bw1 = load_w(k, k.w1_d[l][:, q0 * 128:(q0 + nq) * 128], nq * 128)
                s3, bw3 = load_w(k, k.w3_d[l][:, q0 * 128:(q0 + nq) * 128], nq * 128)
                for q in range(nq):
                    cg = q0 + q - c0
                    for tb in range(NTB):
                        sl = slice(tb * 512, (tb + 1) * 512)
                        p1, bp1 = nbank(k)
                        p3, bp3 = nbank(k)
                        for c in range(KC):
                            P.op("pe", lambda e, p1=p1, s1=s1, c=c, q=q, sl=sl: e.matmul(
                                p1[:], lhsT=s1[:, c, q * 128:(q + 1) * 128], rhs=k.xT[:, c, sl], start=(c == 0), stop=(c == KC - 1)),
                                reads=[bw1, k.b_xT[c][tb]], writes=[bp1])
                        for c in range(KC):
                            P.op("pe", lambda e, p3=p3, s3=s3, c=c, q=q, sl=sl: e.matmul(
                                p3[:], lhsT=s3[:, c, q * 128:(q + 1) * 128], rhs=k.xT[:, c, sl], start=(c == 0), stop=(c == KC - 1)),
                                reads=[bw3, k.b_xT[c][tb]], writes=[bp3])
                        j = si % 2
                        si += 1
                        P.op("act", lambda e, p1=p1, j=j: e.activation(out=sg[j][:], in_=p1[:], func=AF.Silu),
                             reads=[bp1], writes=[b_sg[j]])
                        P.op("dve", lambda e, p3=p3, j=j, cg=cg, sl=sl: e.tensor_tensor(
                            out=g[:, cg, sl], in0=sg[j][:], in1=p3[:], op=ALU.mult),
                            reads=[bp3, b_sg[j]], writes=[b_g[cg][tb]])
            out_proj_ln(k, l, 2, g, b_g, n, k.w2_d[l], first=(pi == 0), last=(pi == len(parts) - 1),
                        row0=c0 * 128, lnbufs=lnb)
        P.barrier()


def stage_xa(k, l):
    P = k.P; T = k.T; NTB = k.NTB
    ident = cs(k, "ident")
    with ExitStack() as s2:
        sbt_ = lambda n, s, d=F32: sbt(k, s2, n, s, d)
        memT = sbt_("memT", [128, KC, MEM], BF16)
        b_memT = Buf()
        k.xaK = sbt_("xaK", [128, KC, MEM], BF16); k.b_xaK = Buf()
        k.xaV = sbt_("xaV", [128, 2, D], BF16); k.b_xaV = Buf()
        with ExitStack() as s3:
            mt = [sbt(k, s3, f"memin{i}", [128, D], F32) for i in range(2)]
            b_mt = [Buf(), Buf()]
            for m in range(2):
                P.dma("sp", lambda e, m=m: e.dma_start(out=mt[m][:], in_=k.mem_d[m * 128:(m + 1) * 128, :]), writes=[b_mt[m]])
                for g in range(2):
                    pt, bp = nbank(k)
                    for q in range(4):
                        c = g * 4 + q
                        P.op("pe", lambda e, pt=pt, m=m, c=c, q=q: e.transpose(
                            out=pt[:, q * 128:(q + 1) * 128], in_=mt[m][:, c * 128:(c + 1) * 128], identity=ident),
                            reads=[b_mt[m], k.b_cst], writes=[bp])
                    for q in range(4):
                        c = g * 4 + q
                        P.op("dve", lambda e, pt=pt, m=m, c=c, q=q: e.tensor_copy(
                            out=memT[:, c, m * 128:(m + 1) * 128], in_=pt[:, q * 128:(q + 1) * 128]),
                            reads=[bp], writes=[b_memT])
            P.barrier()
        for h in range(2):
            slot, bw = load_w(k, k.xa_wk_d[l][:, h * 512:(h + 1) * 512], 512)
            for q in range(4):
                fc = h * 4 + q
                pt, bp = nbank(k)
                for c in range(KC):
                    P.op("pe", lambda e, pt=pt, slot=slot, c=c, q=q: e.matmul(
                        pt[:, 0:MEM], lhsT=slot[:, c, q * 128:(q + 1) * 128], rhs=memT[:, c, :], start=(c == 0), stop=(c == KC - 1)),
                        reads=[bw, b_memT], writes=[bp])
                P.op("act", lambda e, pt=pt, fc=fc: e.activation(out=k.xaK[:, fc, :], in_=pt[:, 0:MEM], func=AF.Copy),
                     reads=[bp], writes=[k.b_xaK])
        for h in range(2):
            slot, bw = load_w(k, k.xa_wv_d[l][:, h * 512:(h + 1) * 512], 512)
            for m in range(2):
                pt, bp = nbank(k)
                for c in range(KC):
                    P.op("pe", lambda e, pt=pt, slot=slot, c=c, m=m: e.matmul(
                        pt[:], lhsT=memT[:, c, m * 128:(m + 1) * 128], rhs=slot[:, c, :], start=(c == 0), stop=(c == KC - 1)),
                        reads=[bw, b_memT], writes=[bp])
                P.op("act", lambda e, pt=pt, m=m, h=h: e.activation(out=k.xaV[:, m, h * 512:(h + 1) * 512], in_=pt[:], func=AF.Copy),
                     reads=[bp], writes=[k.b_xaV])
        oT = sbt_("xa_oT", [128, KC, T], BF16)
        b_oT = [[Buf() for _ in range(NTB)] for _ in range(KC)]
        qT = sbt_("xa_qT", [128, 2, T], BF16)
        b_qT = [[Buf() for _ in range(NTB)] for _ in range(2)]
        PT = [sbt_(f"xa_PT{i}", [128, 2, 512], BF16) for i in range(2)]
        b_PT = [[Buf(), Buf()], [Buf(), Buf()]]
        rden = [sbt_(f"xa_rd{i}", [128, 512]) for i in range(2)]
        b_rden = [Buf(), Buf()]
        onesb = cs(k, "ones", bf=True)
        scale = 1.0 / 16.0
        it = 0
        for hh in range(2):
            slot, bw = load_w(k, k.xa_wq_d[l][:, hh * 512:(hh + 1) * 512], 512)
            for h2 in range(2):
                h = hh * 2 + h2
                for tb in range(NTB):
                    sl = slice(tb * 512, (tb + 1) * 512)
                    for dc in range(2):
                        pt, bp = nbank(k)
                        co = (h2 * 2 + dc) * 128
                        for c in range(KC):
                            P.op("pe", lambda e, pt=pt, slot=slot, c=c, co=co, sl=sl: e.matmul(
                                pt[:], lhsT=slot[:, c, co:co + 128], rhs=k.xT[:, c, sl], start=(c == 0), stop=(c == KC - 1)),
                                reads=[bw, k.b_xT[c][tb]], writes=[bp])
                        P.op("act", lambda e, pt=pt, dc=dc, sl=sl: e.activation(out=qT[:, dc, sl], in_=pt[:], func=AF.Copy),
                             reads=[bp], writes=[b_qT[dc][tb]])
                    j = it % 2
                    it += 1
                    for m in range(2):
                        pt, bp = nbank(k)
                        for dc in range(2):
                            P.op("pe", lambda e, pt=pt, h=h, dc=dc, m=m, sl=sl: e.matmul(
                                pt[:], lhsT=k.xaK[:, h * 2 + dc, m * 128:(m + 1) * 128], rhs=qT[:, dc, sl],
                                start=(dc == 0), stop=(dc == 1)),
                                reads=[k.b_xaK, b_qT[dc][tb]], writes=[bp])
                        P.op("act", lambda e, pt=pt, j=j, m=m: e.activation(out=PT[j][:, m, :], in_=pt[:], func=AF.Exp, scale=scale),
                             reads=[bp], writes=[b_PT[j][m]])
                    pd, bpd = nbank(k)
                    for m in range(2):
                        P.op("pe", lambda e, pd=pd, j=j, m=m: e.matmul(pd[:], lhsT=onesb, rhs=PT[j][:, m, :], start=(m == 0), stop=(m == 1)),
                             reads=[k.b_cstb, b_PT[j][m]], writes=[bpd])
                    P.op("dve", lambda e, pd=pd, j=j: e.reciprocal(out=rden[j][:], in_=pd[:]), reads=[bpd], writes=[b_rden[j]])
                    for dc in range(2):
                        po, bpo = nbank(k)
                        for m in range(2):
                            P.op("pe", lambda e, po=po, j=j, m=m, h=h, dc=dc: e.matmul(
                                po[:], lhsT=k.xaV[:, m, h * 256 + dc * 128: h * 256 + (dc + 1) * 128], rhs=PT[j][:, m, :],
                                start=(m == 0), stop=(m == 1)),
                                reads=[k.b_xaV, b_PT[j][m]], writes=[bpo])
                        P.op("dve", lambda e, po=po, j=j, h=h, dc=dc, sl=sl: e.tensor_tensor(
                            out=oT[:, h * 2 + dc, sl], in0=po[:], in1=rden[j][:], op=ALU.mult),
                            reads=[bpo, b_rden[j]], writes=[b_oT[h * 2 + dc][tb]])
        lnb = alloc_ln(k, s2)
        out_proj_ln(k, l, 1, oT, b_oT, KC, k.xa_wo_d[l], lnbufs=lnb)
        P.barrier()


def proj_fm(k, slot, bw, col0, ncols, tb):
    P = k.P
    sl = slice(tb * 512, (tb + 1) * 512)
    pt, bp = nbank(k)
    for c in range(KC):
        P.op("pe", lambda e, c=c: e.matmul(pt[0:ncols, :], lhsT=slot[:, c, col0:col0 + ncols], rhs=k.xT[:, c, sl],
                                           start=(c == 0), stop=(c == KC - 1)),
             reads=[bw, k.b_xT[c][tb]], writes=[bp])
    return pt, bp


def ln_stats(k, srcs, nfeat, eps, zsq, b_zsq, st_t, b_st):
    P = k.P
    ones = cs(k, "ones")
    S1, b1 = nbank(k)
    S2, b2 = nbank(k)
    n = len(srcs)
    for i, (ap, b) in enumerate(srcs):
        j = i % 2
        P.op("act", lambda e, j=j, ap=ap: e.activation(out=zsq[:, j, :], in_=ap, func=AF.Square), reads=[b], writes=[b_zsq[j]])
        P.op("pe", lambda e, i=i, ap=ap: e.matmul(S1[:], lhsT=ones, rhs=ap, start=(i == 0), stop=(i == n - 1)),
             reads=[b, k.b_cst], writes=[b1])
        P.op("pe", lambda e, i=i, j=j: e.matmul(S2[:], lhsT=ones, rhs=zsq[:, j, :], start=(i == 0), stop=(i == n - 1)),
             reads=[b_zsq[j], k.b_cst], writes=[b2])
    mean, var, rstd = (st_t[:, i, :] for i in range(3))
    P.op("act", lambda e: e.activation(out=mean, in_=S1[:], func=AF.Copy, scale=1.0 / nfeat), reads=[b1], writes=[b_st[0]])
    P.op("dve", lambda e: e.tensor_tensor(out=var, in0=mean, in1=mean, op=ALU.mult), reads=[b_st[0]], writes=[b_st[1]])
    P.op("dve", lambda e: e.scalar_tensor_tensor(out=var, in0=S2[:], scalar=1.0 / nfeat, in1=var, op0=ALU.mult, op1=ALU.subtract),
         reads=[b2, b_st[1]], writes=[b_st[1]])
    P.op("dve", lambda e: e.tensor_scalar(out=var, in0=var, scalar1=eps, scalar2=None, op0=ALU.add),
         reads=[b_st[1]], writes=[b_st[1]])
    P.op("act", lambda e: e.activation(out=rstd, in_=var, func=AF.Ln), reads=[b_st[1]], writes=[b_st[2]])
    P.op("act", lambda e: e.activation(out=rstd, in_=rstd, func=AF.Exp, scale=-0.5), reads=[b_st[2]], writes=[b_st[2]])
    P.op("dve", lambda e: e.scalar_tensor_tensor(out=mean, in0=mean, scalar=-1.0, in1=rstd, op0=ALU.mult, op1=ALU.mult),
         reads=[b_st[0], b_st[2]], writes=[b_st[0]])
    return rstd, mean, b_st[2], b_st[0]


def mixer_conv(k, l, brT, b_brT):
    P = k.P; T = k.T; NTB = k.NTB
    with ExitStack() as s2:
        ub = sbt(k, s2, "cv_u", [128, 2, 30 + T], F32)
        b_ub = [Buf(), Buf()]
        acc = sbt(k, s2, "cv_acc", [128, 2, T], F32)
        b_acc = [Buf(), Buf()]
        sg = [sbt(k, s2, f"cv_sg{i}", [128, 512], F32) for i in range(2)]
        b_sg = [Buf(), Buf()]
        lnb = alloc_ln(k, s2)
        slot, bw = load_w(k, k.w_in_d[l][:, 1056:1568], 512)
        for ch in range(2):
            P.op("pool", lambda e, ch=ch: e.memset(ub[:, ch, 0:30], 0.0), writes=[b_ub[ch]])
        i = 0
        for tb in range(NTB):
            for ch in range(2):
                pa, bpa = proj_fm(k, slot, bw, ch * 128, 128, tb)
                pb, bpb = proj_fm(k, slot, bw, 256 + ch * 128, 128, tb)
                j = i % 2
                i += 1
                P.op("act", lambda e, pb=pb, j=j: e.activation(out=sg[j][:], in_=pb[:], func=AF.Sigmoid), reads=[bpb], writes=[b_sg[j]])
                P.op("dve", lambda e, pa=pa, j=j, ch=ch, tb=tb: e.tensor_tensor(
                    out=ub[:, ch, 30 + tb * 512: 30 + (tb + 1) * 512], in0=pa[:], in1=sg[j][:], op=ALU.mult),
                    reads=[bpa, b_sg[j]], writes=[b_ub[ch]])
        ow, _ = PV["cv_w"]
        for ch in range(2):
            eng = "dve"
            for kk in range(31):
                wcol = k.pv[:, ow + ch * 31 + kk: ow + ch * 31 + kk + 1]
                if kk == 0:
                    bcol = pcol(k, "cv_b", ch)
                    P.op(eng, lambda e, ch=ch, wcol=wcol, bcol=bcol: e.tensor_scalar(
                        out=acc[:, ch, :], in0=ub[:, ch, 0:T], scalar1=wcol, scalar2=bcol, op0=ALU.mult, op1=ALU.add),
                        reads=[b_ub[ch], k.b_pv], writes=[b_acc[ch]])
                else:
                    P.op(eng, lambda e, ch=ch, wcol=wcol, kk=kk: e.scalar_tensor_tensor(
                        out=acc[:, ch, :], in0=ub[:, ch, kk:kk + T], scalar=wcol, in1=acc[:, ch, :], op0=ALU.mult, op1=ALU.add),
                        reads=[b_ub[ch], k.b_pv, b_acc[ch]], writes=[b_acc[ch]])
        for tb in range(NTB):
            sl = slice(tb * 512, (tb + 1) * 512)
            rstd, nmr, b_r, b_n = ln_stats(k, [(acc[:, ch, sl], b_acc[ch]) for ch in range(2)], 256, LN_EPS, *lnb)
            for ch in range(2):
                a = acc[:, ch, sl]
                P.op("dve", lambda e, a=a, rstd=rstd: e.tensor_tensor(out=a, in0=a, in1=rstd, op=ALU.mult),
                     reads=[b_acc[ch], b_r], writes=[b_acc[ch]])
                P.op("dve", lambda e, a=a, nmr=nmr: e.tensor_tensor(out=a, in0=a, in1=nmr, op=ALU.add),
                     reads=[b_acc[ch], b_n], writes=[b_acc[ch]])
                P.op("act", lambda e, a=a, ch=ch, sl=sl: e.activation(
                    out=brT[:, 2 + ch, sl], in_=a, func=AF.Silu, scale=pcol(k, "cv_ln_g", ch), bias=pcol(k, "cv_ln_b", ch)),
                    reads=[b_acc[ch], k.b_pv], writes=[b_brT[2 + ch][tb]])
        P.barrier()


def mixer_fox(k, l, brT, b_brT):
    P = k.P; T = k.T; NTB = k.NTB
    NT = T // 128
    with ExitStack() as s2:
        fq = sbt(k, s2, "fx_q", [128, 2, T], BF16); b_fq = [[Buf() for _ in range(NTB)] for _ in range(2)]
        fk = sbt(k, s2, "fx_k", [128, 2, T], BF16); b_fk = [[Buf() for _ in range(NTB)] for _ in range(2)]
        fv = sbt(k, s2, "fx_v", [128, NT, 256], BF16); b_fv = [Buf() for _ in range(NT)]
        spl = sbt(k, s2, "fx_spl", [4, T], F32); b_spl = Buf()
        sig = sbt(k, s2, "fx_sig", [4, T], F32); b_sig = Buf()
        rsel = sbt(k, s2, "fx_rsel", [4, NT * 4], F32); b_rsel = Buf()
        stok = sbt(k, s2, "fx_stok", [128, NT * 4], F32); b_stok = Buf()
        sref = sbt(k, s2, "fx_sref", [128, NT * 4], F32); b_sref = Buf()
        bias = sbt(k, s2, "fx_bias", [128, NT, NT], F32); b_bias = Buf()
        PT = [sbt(k, s2, f"fx_PT{i}", [128, 512], BF16) for i in range(2)]
        b_PT = [Buf(), Buf()]
        rd = sbt(k, s2, "fx_rd", [128, 512], F32); b_rd = Buf()
        slA, bwA = load_w(k, k.w_in_d[l][:, 2352:2864], 512)
        slB, bwB = load_w(k, k.w_in_d[l][:, 2864:3124], 260)
        for tb in range(NTB):
            sl = slice(tb * 512, (tb + 1) * 512)
            for ch in range(2):
                pq, bpq = proj_fm(k, slA, bwA, ch * 128, 128, tb)
                P.op("act", lambda e, pq=pq, ch=ch, sl=sl: e.activation(out=fq[:, ch, sl], in_=pq[:], func=AF.Copy, scale=0.125),
                     reads=[bpq], writes=[b_fq[ch][tb]])
                pk, bpk = proj_fm(k, slA, bwA, 256 + ch * 128, 128, tb)
                P.op("dve", lambda e, pk=pk, ch=ch, sl=sl: e.tensor_copy(out=fk[:, ch, sl], in_=pk[:]),
                     reads=[bpk], writes=[b_fk[ch][tb]])
            pz, bpz = proj_fm(k, slB, bwB, 256, 4, tb)
            P.op("act", lambda e, pz=pz, sl=sl: e.activation(out=spl[:, sl], in_=pz[0:4, :], func=AF.Exp, scale=-1.0,
                                                             bias=pcol(k, "fox_bf", 0, neg=True, rows=4)),
                 reads=[bpz, k.b_npv], writes=[b_spl])
            P.op("act", lambda e, sl=sl: e.activation(out=spl[:, sl], in_=spl[:, sl], func=AF.Ln, bias=1.0),
                 reads=[b_spl], writes=[b_spl])
        for tt in range(NT):
            tb = tt // 4
            pt, bp = nbank(k)
            for c in range(KC):
                P.op("pe", lambda e, c=c, tt=tt, pt=pt: e.matmul(pt[:, 0:256], lhsT=k.xT[:, c, tt * 128:(tt + 1) * 128], rhs=slB[:, c, 0:256],
                                                             start=(c == 0), stop=(c == KC - 1)),
                     reads=[bwB, k.b_xT[c][tb]], writes=[bp])
            P.op("act", lambda e, pt=pt, tt=tt: e.activation(out=fv[:, tt, :], in_=pt[:, 0:256], func=AF.Copy), reads=[bp], writes=[b_fv[tt]])
        P.op("dve", lambda e: e.tensor_tensor_scan(out=sig[:], data0=spl[:], data1=spl[:], initial=0.0, op0=ALU.add, op1=ALU.max),
             reads=[b_spl], writes=[b_sig])
        ident = cs(k, "ident")
        pt, bp = nbank(k)
        for tt in range(NT):
            P.op("pe", lambda e, tt=tt: e.transpose(out=pt[:, tt * 4:(tt + 1) * 4], in_=sig[0:4, tt * 128:(tt + 1) * 128], identity=ident[0:4, 0:4]),
                 reads=[b_sig, k.b_cst], writes=[bp])
        P.op("dve", lambda e: e.tensor_copy(out=stok[:], in_=pt[:, 0:NT * 4]), reads=[bp], writes=[b_stok])
        for qs in range(NT):
            P.op("dve", lambda e, qs=qs: e.tensor_scalar(out=rsel[:, qs * 4:(qs + 1) * 4], in0=ident[0:4, 0:4], scalar1=sig[:, qs * 128:qs * 128 + 1],
                                                     scalar2=None, op0=ALU.mult),
                 reads=[b_sig, k.b_cst], writes=[b_rsel])
        pr, bpr = nbank(k)
        ones = cs(k, "ones")
        P.op("pe", lambda e: e.matmul(pr[:, 0:NT * 4], lhsT=ones[0:4, :], rhs=rsel[:], start=True, stop=True),
             reads=[b_rsel, k.b_cst], writes=[bpr])
        P.op("dve", lambda e: e.tensor_copy(out=sref[:], in_=pr[:, 0:NT * 4]), reads=[bpr], writes=[b_sref])
        stok3 = stok[:].rearrange("p (n h) -> p n h", h=4)
        onesb = cs(k, "ones", bf=True)
        iu = cs(k, "iu128", bf=True)
        it = 0
        for h in range(4):
            ch = h // 2
            pb = (h % 2) * 64
            for qs in range(NT):
                P.op("dve", lambda e, h=h, qs=qs: e.tensor_scalar(out=bias[:, qs, :], in0=stok3[:, :, h], scalar1=sref[:, qs * 4 + h:qs * 4 + h + 1],
                                                            scalar2=None, op0=ALU.subtract),
                     reads=[b_stok, b_sref], writes=[b_bias])
            for Q in range(NTB):
                nkt = 4 * (Q + 1)
                po, bpo = nbank(k, hold=True)
                pd, bpd = nbank(k, hold=True)
                for kt in range(nkt):
                    d = kt - 4 * Q
                    q0 = d * 128 if d > 0 else 0
                    j = it % 2
                    it += 1
                    ps_, bps = nbank(k)
                    P.op("pe", lambda e, ps_=ps_, pb=pb, ch=ch, kt=kt, Q=Q, q0=q0: e.matmul(
                        ps_[:, q0:512], lhsT=fk[pb:pb + 64, ch, kt * 128:(kt + 1) * 128], rhs=fq[pb:pb + 64, ch, Q * 512 + q0:(Q + 1) * 512],
                        start=True, stop=True),
                        reads=[b_fk[ch][kt // 4], b_fq[ch][Q]], writes=[bps])
                    for qi in range(q0 // 128, 4):
                        qs = Q * 4 + qi
                        P.op("act", lambda e, ps_=ps_, j=j, qi=qi, qs=qs, h=h, kt=kt: e.activation(
                            out=PT[j][:, qi * 128:(qi + 1) * 128], in_=ps_[:, qi * 128:(qi + 1) * 128], func=AF.Exp,
                            bias=bias[:, qs, kt:kt + 1]),
                            reads=[bps, b_bias], writes=[b_PT[j]])
                    if d >= 0:
                        P.op("pool", lambda e, j=j, q0=q0: e.tensor_tensor(out=PT[j][:, q0:q0 + 128], in0=PT[j][:, q0:q0 + 128], in1=iu, op=ALU.mult),
                             reads=[b_PT[j], k.b_cstb], writes=[b_PT[j]])
                    P.op("pe", lambda e, po=po, pb=pb, j=j, q0=q0, kt=kt, h=h, nkt=nkt: e.matmul(
                        po[pb:pb + 64, q0:512], lhsT=fv[:, kt, h * 64:(h + 1) * 64], rhs=PT[j][:, q0:512], start=(kt == 0), stop=(kt == nkt - 1)),
                        reads=[b_fv[kt], b_PT[j]], writes=[bpo])
                    P.op("pe", lambda e, pd=pd, pb=pb, j=j, q0=q0, kt=kt, nkt=nkt: e.matmul(
                        pd[pb:pb + 64, q0:512], lhsT=onesb[:, 0:64], rhs=PT[j][:, q0:512], start=(kt == 0), stop=(kt == nkt - 1)),
                        reads=[k.b_cstb, b_PT[j]], writes=[bpd])
                P.op("dve", lambda e, pd=pd, pb=pb: e.reciprocal(out=rd[pb:pb + 64, :], in_=pd[pb:pb + 64, :]), reads=[bpd], writes=[b_rd])
                P.op("dve", lambda e, po=po, pb=pb, ch=ch, Q=Q: e.tensor_tensor(
                    out=brT[pb:pb + 64, 6 + ch, Q * 512:(Q + 1) * 512], in0=po[pb:pb + 64, :], in1=rd[pb:pb + 64, :], op=ALU.mult),
                    reads=[bpo, b_rd], writes=[b_brT[6 + ch][Q]])
                release(k, po); release(k, pd)
        P.barrier()


def mixer_gla(k, l, brT, b_brT):
    P = k.P; T = k.T; NTB = k.NTB
    identb = cs(k, "ident", bf=True)
    with ExitStack() as s2:
        f32t = lambda n, shp: sbt(k, s2, n, shp, F32)
        bft = lambda n, shp: sbt(k, s2, n, shp, BF16)
        a2 = f32t("gl_a2", [16, 128]); b_a2 = Buf()
        zT = f32t("gl_z", [16, 512]); b_zT = Buf()
        spl = f32t("gl_spl", [128, 512]); b_spl = Buf()
        bcs = f32t("gl_bcs", [128, 512]); b_bcs = Buf()
        Ep = f32t("gl_Ep", [128, 512]); b_Ep = Buf()
        En = f32t("gl_En", [128, 512]); b_En = Buf()
        ones32 = f32t("gl_ones", [128, 32]); b_ones = Buf()
        qd = bft("gl_qd", [128, 512]); b_qd = Buf()
        ki = bft("gl_ki", [128, 512]); b_ki = Buf()
        vb = bft("gl_vb", [128, 2, 512]); b_vb = Buf()
        sr = f32t("gl_sr", [128, 2, 512]); b_sr = Buf()
        S4f = f32t("gl_S4f", [128, 64]); b_S4f = Buf()
        S4b = bft("gl_S4b", [128, 64]); b_S4b = Buf()
        STm = [bft(f"gl_STm{i}", [128, 128]) for i in range(2)]; b_STm = [Buf(), Buf()]
        V4 = [bft(f"gl_V4{i}", [128, 64]) for i in range(2)]; b_V4 = [Buf(), Buf()]
        KT = [bft(f"gl_KT{i}", [128, 128]) for i in range(2)]; b_KT = [Buf(), Buf()]
        OALL = f32t("gl_OALL", [128, 16, 64]); b_OALL = Buf()
        osq = f32t("gl_osq", [128, 16, 64]); b_osq = Buf()
        ss = f32t("gl_ss", [128, 16]); b_ss = Buf()
        ONALL = bft("gl_ON", [128, 16, 64]); b_ON = Buf()
        slA, bwA = load_w(k, k.w_in_d[l][:, 1568:2080], 512)
        slB, bwB = load_w(k, k.w_in_d[l][:, 2080:2352], 272)
        P.dma("sp", lambda e: e.dma_start(out=a2[:], in_=k.gla_a2_d[l]), writes=[b_a2])
        P.op("pool", lambda e: e.memset(ones32[:], 1.0), writes=[b_ones])
        P.op("pool", lambda e: e.memset(S4f[:], 0.0), writes=[b_S4f])
        P.op("pool", lambda e: e.memset(S4b[:], 0.0), writes=[b_S4b])
        bd_iu = cs(k, "bd32_iu")
        it = 0
        for tb in range(NTB):
            sl = slice(tb * 512, (tb + 1) * 512)
            pz, bpz = proj_fm(k, slB, bwB, 256, 16, tb)
            P.op("act", lambda e, pz=pz: e.activation(out=zT[:], in_=pz[0:16, :], func=AF.Copy), reads=[bpz], writes=[b_zT])
            pla, bpla = nbank(k)
            P.op("pe", lambda e, pla=pla: e.matmul(pla[:], lhsT=a2[:], rhs=zT[:], start=True, stop=True), reads=[b_a2, b_zT], writes=[bpla])
            P.op("act", lambda e, pla=pla: e.activation(out=spl[:], in_=pla[:], func=AF.Exp, scale=-1.0, bias=pcol(k, "gla_ab", 0, neg=True)),
                 reads=[bpla, k.b_npv], writes=[b_spl])
            P.op("act", lambda e: e.activation(out=spl[:], in_=spl[:], func=AF.Ln, bias=1.0), reads=[b_spl], writes=[b_spl])
            for c in range(16):
                cc = slice(c * 32, (c + 1) * 32)
                P.op("dve", lambda e, cc=cc: e.tensor_tensor_scan(out=bcs[:, cc], data0=ones32[:], data1=spl[:, cc], initial=0.0,
                                                              op0=ALU.mult, op1=ALU.add),
                     reads=[b_ones, b_spl], writes=[b_bcs])
            P.op("act", lambda e: e.activation(out=Ep[:], in_=bcs[:], func=AF.Exp, scale=1.0 / 16.0), reads=[b_bcs], writes=[b_Ep])
            P.op("act", lambda e: e.activation(out=En[:], in_=bcs[:], func=AF.Exp, scale=-1.0 / 16.0), reads=[b_bcs], writes=[b_En])
            pq, bpq = proj_fm(k, slA, bwA, 0, 128, tb)
            P.op("dve", lambda e, pq=pq: e.scalar_tensor_tensor(out=qd[:], in0=pq[:], scalar=32.0 ** -0.5, in1=En[:], op0=ALU.mult, op1=ALU.mult),
                 reads=[bpq, b_En], writes=[b_qd])
            pk, bpk = proj_fm(k, slA, bwA, 128, 128, tb)
            P.op("dve", lambda e, pk=pk: e.tensor_tensor(out=ki[:], in0=pk[:], in1=Ep[:], op=ALU.mult), reads=[bpk, b_Ep], writes=[b_ki])
            for ch in range(2):
                pv, bpv = proj_fm(k, slA, bwA, 256 + ch * 128, 128, tb)
                P.op("act", lambda e, pv=pv, ch=ch: e.activation(out=vb[:, ch, :], in_=pv[:], func=AF.Copy), reads=[bpv], writes=[b_vb])
                pr, bpr = proj_fm(k, slB, bwB, ch * 128, 128, tb)
                P.op("act", lambda e, pr=pr, ch=ch: e.activation(out=sr[:, ch, :], in_=pr[:], func=AF.Silu), reads=[bpr], writes=[b_sr])
            for c in range(16):
                cc = slice(c * 32, (c + 1) * 32)
                j = it % 2
                it += 1
                pS, bpS = nbank(k)
                P.op("dve", lambda e, pS=pS: e.memset(pS[:, 0:128], 0.0), writes=[bpS])
                for h in range(4):
                    hs = slice(h * 32, (h + 1) * 32)
                    P.op("pe", lambda e, pS=pS, hs=hs, cc=cc: mm(e, pS[hs, hs], ki[hs, cc], qd[hs, cc], (hs.start, hs.start)),
                         reads=[b_ki, b_qd], writes=[bpS], rt=hs.start)
                P.op("dve", lambda e, pS=pS, j=j: e.tensor_tensor(out=STm[j][:], in0=pS[:, 0:128], in1=bd_iu, op=ALU.mult),
                     reads=[bpS, k.b_cst], writes=[b_STm[j]])
                pTr, bpTr = nbank(k)
                pTb = k.psb[k.ps.index(pTr)]
                P.op("dve", lambda e, pTr=pTr: e.memset(pTr[:, 64:128], 0.0), writes=[bpTr])
                for h in range(4):
                    hs = slice(h * 32, (h + 1) * 32)
                    vs = slice((h % 2) * 64, (h % 2) * 64 + 64)
                    P.op("pe", lambda e, pTb=pTb, hs=hs, vs=vs, h=h, cc=cc: tr(e, pTb[hs, 0:64], vb[vs, h // 2, cc], identb[vs, vs], (vs.start, hs.start)),
                         reads=[b_vb, k.b_cstb], writes=[bpTr], rt=vs.start)
                for h in range(4):
                    hs = slice(h * 32, (h + 1) * 32)
                    P.op("pe", lambda e, pTb=pTb, hs=hs, cc=cc, h=h: tr(e, pTb[hs, 128 + h * 32:128 + (h + 1) * 32], ki[hs, cc], identb[hs, hs], (hs.start, hs.start)),
                         reads=[b_ki, k.b_cstb], writes=[bpTr], rt=hs.start)
                P.op("act", lambda e, pTb=pTb, j=j: e.activation(out=V4[j][:], in_=pTb[:, 0:64], func=AF.Copy), reads=[bpTr], writes=[b_V4[j]])
                P.op("dve", lambda e, pTb=pTb, j=j: e.tensor_copy(out=KT[j][:], in_=pTb[:, 128:256]),
                     reads=[bpTr], writes=[b_KT[j]])
                pO, bpO = nbank(k)
                P.op("pe", lambda e, pO=pO, j=j: e.matmul(pO[:, 0:64], lhsT=STm[j][:], rhs=V4[j][:], start=True, stop=False),
                     reads=[b_STm[j], b_V4[j]], writes=[bpO])
                for h in range(4):
                    hs = slice(h * 32, (h + 1) * 32)
                    P.op("pe", lambda e, pO=pO, hs=hs, cc=cc, h=h: mm(e, pO[hs, 0:64], qd[hs, cc], S4b[hs, :], (hs.start, hs.start), start=False, stop=True),
                         reads=[b_qd, b_S4b], writes=[bpO], rt=hs.start)
                P.op("act", lambda e, pO=pO, c=c: e.activation(out=OALL[:, c, :], in_=pO[:, 0:64], func=AF.Copy), reads=[bpO], writes=[b_OALL])
                pSt, bpSt = nbank(k)
                P.op("pe", lambda e, pSt=pSt, j=j: e.matmul(pSt[:, 0:64], lhsT=KT[j][:], rhs=V4[j][:], start=True, stop=True),
                     reads=[b_KT[j], b_V4[j]], writes=[bpSt])
                P.op("dve", lambda e, pSt=pSt: e.tensor_tensor(out=S4f[:], in0=pSt[:, 0:64], in1=S4f[:], op=ALU.add), reads=[bpSt, b_S4f], writes=[b_S4f])
                P.op("dve", lambda e, c=c: e.tensor_scalar(out=S4f[:], in0=S4f[:], scalar1=En[:, c * 32 + 31:c * 32 + 32], scalar2=None, op0=ALU.mult),
                     reads=[b_S4f, b_En], writes=[b_S4f])
                P.op("pool", lambda e: e.tensor_copy(out=S4b[:], in_=S4f[:]), reads=[b_S4f], writes=[b_S4b])
            P.op("dve", lambda e: e.tensor_tensor(out=osq[:], in0=OALL[:], in1=OALL[:], op=ALU.mult), reads=[b_OALL], writes=[b_osq])
            P.op("dve", lambda e: e.tensor_reduce(out=ss[:], in_=osq[:], axis=mybir.AxisListType.X, op=ALU.add), reads=[b_osq], writes=[b_ss])
            P.op("dve", lambda e: e.tensor_scalar(out=ss[:], in0=ss[:], scalar1=1.0 / 64.0, scalar2=1e-5, op0=ALU.mult, op1=ALU.add),
                 reads=[b_ss], writes=[b_ss])
            P.op("act", lambda e: e.activation(out=ss[:], in_=ss[:], func=AF.Ln), reads=[b_ss], writes=[b_ss])
            P.op("act", lambda e: e.activation(out=ss[:], in_=ss[:], func=AF.Exp, scale=-0.5), reads=[b_ss], writes=[b_ss])
            for c in range(16):
                P.op("pool", lambda e, c=c: e.tensor_scalar(out=ONALL[:, c, :], in0=OALL[:, c, :], scalar1=ss[:, c:c + 1], scalar2=None, op0=ALU.mult),
                     reads=[b_OALL, b_ss], writes=[b_ON])
            pF, bpF = nbank(k)
            pFb = k.psb[k.ps.index(pF)]
            for c in range(16):
                for h in range(4):
                    hs = slice(h * 32, (h + 1) * 32)
                    vs = slice((h % 2) * 64, (h % 2) * 64 + 64)
                    o0 = (h // 2) * 512 + c * 32
                    P.op("pe", lambda e, pFb=pFb, hs=hs, vs=vs, o0=o0, c=c: tr(e, pFb[vs, o0:o0 + 32], ONALL[hs, c, :], identb[hs, hs], (hs.start, vs.start)),
                         reads=[b_ON, k.b_cstb], writes=[bpF], rt=hs.start)
            for fc in range(2):
                P.op("dve", lambda e, pFb=pFb, fc=fc, sl=sl: e.scalar_tensor_tensor(
                    out=brT[:, 4 + fc, sl], in0=pFb[:, fc * 512:(fc + 1) * 512], scalar=pcol(k, "gla_ln_g", fc), in1=sr[:, fc, :],
                    op0=ALU.mult, op1=ALU.mult),
                    reads=[bpF, k.b_pv, b_sr], writes=[b_brT[4 + fc][tb]])
        P.barrier()


def mixer_rwkv(k, l, brT, b_brT):
    P = k.P; T = k.T
    NS = 256
    NSB = T // NS
    NCH = NS // 32
    identb = cs(k, "ident", bf=True)
    bones = cs(k, "bones64")
    with ExitStack() as s2:
        f32t = lambda n, shp: sbt(k, s2, n, shp, F32)
        bft = lambda n, shp: sbt(k, s2, n, shp, BF16)
        B = lambda: Buf()
        wa = f32t("rw_wa", [128, 256]); b_wa = B()
        g2a = bft("rw_g2a", [128, 256]); g2b = bft("rw_g2b", [32, 256]); b_g2 = B()
        omk = f32t("rw_omk", [128, 2]); b_omk = B()
        pprev = f32t("rw_pprev", [128, 9]); b_pprev = B()
        praw = [f32t(f"rw_praw{i}", [128, NS + 1]) for i in range(2)]; b_praw = [B(), B()]
        dtmp = f32t("rw_dtmp", [128, NS]); b_dtmp = B()
        R = [f32t(f"rw_R{i}", [128, NS]) for i in range(2)]; b_R = [B(), B()]
        KX = [f32t(f"rw_KX{i}", [128, NS]) for i in range(2)]; b_KX = [B(), B()]
        V = [f32t(f"rw_V{i}", [128, NS]) for i in range(2)]; b_V = [B(), B()]
        XWA = f32t("rw_XWA", [128, NS]); b_XWA = B()
        XG0 = f32t("rw_XG0", [128, NS]); b_XG0 = B()
        XG1 = f32t("rw_XG1", [32, NS]); b_XG1 = B()
        sgx0 = bft("rw_sgx0", [128, NS]); sgx1 = bft("rw_sgx1", [32, NS]); b_sgx = B()
        EW = f32t("rw_EW", [128, NS]); b_EW = B()
        AL = f32t("rw_AL", [128, NS]); b_AL = B()
        GT = [bft(f"rw_GT{i}", [128, NS]) for i in range(2)]; b_GT = [B(), B()]
        KKN = f32t("rw_KKN", [128, NS]); b_KKN = B()
        TMP = f32t("rw_TMP", [128, NS]); b_TMP = B()
        CS = f32t("rw_CS", [128, NS]); b_CS = B()
        E1 = f32t("rw_E1", [128, NS]); b_E1 = B()
        E2 = f32t("rw_E2", [128, NS]); b_E2 = B()
        E3 = f32t("rw_E3", [128, NS]); b_E3 = B()
        WC = [f32t(f"rw_WC{i}", [128, NCH]) for i in range(2)]; b_WC = [B(), B()]
        BON = [f32t(f"rw_BON{i}", [128, NS]) for i in range(2)]; b_BON = [B(), B()]
        ones32 = f32t("rw_ones", [128, 32]); b_ones = B()
        rh = [bft(f"rw_rh{i}", [128, NS]) for i in range(2)]; b_rh = [B(), B()]
        kh = [bft(f"rw_kh{i}", [128, NS]) for i in range(2)]; b_kh = [B(), B()]
        bh = [bft(f"rw_bh{i}", [128, NS]) for i in range(2)]; b_bh = [B(), B()]
        ah = [bft(f"rw_ah{i}", [128, NS]) for i in range(2)]; b_ah = [B(), B()]
        vb = [bft(f"rw_vb{i}", [128, NS]) for i in range(2)]; b_vb = [B(), B()]
        M4 = [bft(f"rw_M4{i}", [128, 4, 128]) for i in range(2)]; b_M4 = [B(), B()]
        RKT = [bft(f"rw_RKT{i}", [128, 128]) for i in range(2)]; b_RKT = [B(), B()]
        Am = [bft(f"rw_A{i}", [128, 128]) for i in range(2)]; b_A = [B(), B()]
        ATm = [bft(f"rw_AT{i}", [128, 128]) for i in range(2)]; b_AT = [B(), B()]
        TT = [bft(f"rw_TT{i}", [128, 128]) for i in range(2)]; b_TT = [B(), B()]
        BK = [bft(f"rw_BK{i}", [128, 4, 128]) for i in range(2)]; b_BK = [B(), B()]
        V4 = [bft(f"rw_V4{i}", [128, 64]) for i in range(2)]; b_V4 = [B(), B()]
        Xb = [bft(f"rw_Xb{i}", [128, 64]) for i in range(2)]; b_Xb = [B(), B()]
        Ub = [bft(f"rw_Ub{i}", [128, 64]) for i in range(2)]; b_Ub = [B(), B()]
        Hf = [f32t(f"rw_Hf{i}", [128, 64]) for i in range(2)]; b_Hf = [B(), B()]
        Hb = [bft(f"rw_Hb{i}", [128, 64]) for i in range(2)]; b_Hb = [B(), B()]
        YALL = f32t("rw_YALL", [128, NCH, 64]); b_YALL = B()
        ysq = f32t("rw_ysq", [128, NCH, 64]); b_ysq = B()
        s1 = f32t("rw_s1", [128, NCH]); b_s1 = B()
        s2_ = f32t("rw_s2", [128, NCH]); b_s2 = B()
        YN = bft("rw_YN", [128, NCH, 64]); b_YN = B()
        y1 = f32t("rw_y1", [128, NS]); b_y1 = B()
        masks = cs(k, "rwmask4")
        bd_iu = cs(k, "bd32_iu")

        slA, bwA = load_w(k, k.w_in_d[l][:, 0:512], 512)
        slB, bwB = load_w(k, k.w_in_d[l][:, 512:1024], 512)
        slC, bwC = load_w(k, k.w_in_d[l][:, 1024:1056], 32)
        P.dma("sp", lambda e: e.dma_start(out=wa[:], in_=k.rw_wa_d[l]), writes=[b_wa])
        P.dma("pool", lambda e: e.dma_start(out=g2a[:], in_=k.rw_g2_d[l][0:128, :]), writes=[b_g2])
        P.dma("pool", lambda e: e.dma_start(out=g2b[:], in_=k.rw_g2_d[l][128:160, :]), writes=[b_g2])
        oka, _ = PV["rw_ka"]
        P.op("dve", lambda e: e.tensor_scalar(out=omk[:], in0=k.pv[:, oka:oka + 2], scalar1=-1.0, scalar2=1.0, op0=ALU.mult, op1=ALU.add),
             reads=[k.b_pv], writes=[b_omk])
        P.op("pool", lambda e: e.memset(pprev[:], 0.0), writes=[b_pprev])
        P.op("pool", lambda e: e.memset(ones32[:], 1.0), writes=[b_ones])
        for hp in range(2):
            P.op("pool", lambda e, hp=hp: e.memset(Hf[hp][:], 0.0), writes=[b_Hf[hp]])
            P.op("pool", lambda e, hp=hp: e.memset(Hb[hp][:], 0.0), writes=[b_Hb[hp]])
        dests = [(R[0], b_R[0], 128), (R[1], b_R[1], 128), (KX[0], b_KX[0], 128), (KX[1], b_KX[1], 128),
                 (V[0], b_V[0], 128), (V[1], b_V[1], 128), (XWA, b_XWA, 128), (XG0, b_XG0, 128), (XG1, b_XG1, 32)]
        ip = 0
        it = 0

        def proj_n(slot, bw, col0, ncols, s0):
            tb = s0 // 512
            pt, bp = nbank(k)
            for c in range(KC):
                P.op("pe", lambda e, c=c: e.matmul(pt[0:ncols, 0:NS], lhsT=slot[:, c, col0:col0 + ncols], rhs=k.xT[:, c, s0:s0 + NS],
                                                   start=(c == 0), stop=(c == KC - 1)),
                     reads=[bw, k.b_xT[c][tb]], writes=[bp])
            return pt, bp

        for sb in range(NSB):
            s0 = sb * NS
            tb = s0 // 512
            for f, (dst, bd, rows) in enumerate(dests):
                if f < 4:
                    pt, bp = proj_n(slA, bwA, f * 128, 128, s0)
                elif f < 8:
                    pt, bp = proj_n(slB, bwB, (f - 4) * 128, 128, s0)
                else:
                    pt, bp = proj_n(slC, bwC, 0, 32, s0)
                j = ip % 2
                ip += 1
                P.op("pool", lambda e, j=j, f=f, rows=rows: e.tensor_copy(out=praw[j][0:rows, 0:1], in_=pprev[0:rows, f:f + 1]),
                     reads=[b_pprev], writes=[b_praw[j]])
                P.op("act", lambda e, j=j, pt=pt, rows=rows: e.activation(out=praw[j][0:rows, 1:NS + 1], in_=pt[0:rows, 0:NS], func=AF.Copy),
                     reads=[bp], writes=[b_praw[j]])
                P.op("pool", lambda e, j=j, f=f, rows=rows: e.tensor_copy(out=pprev[0:rows, f:f + 1], in_=praw[j][0:rows, NS:NS + 1]),
                     reads=[b_praw[j]], writes=[b_pprev])
                P.op("dve", lambda e, j=j, rows=rows: e.tensor_tensor(out=dtmp[0:rows, :], in0=praw[j][0:rows, 0:NS], in1=praw[j][0:rows, 1:NS + 1], op=ALU.subtract),
                     reads=[b_praw[j]], writes=[b_dtmp])
                P.op("dve", lambda e, j=j, rows=rows, f=f, dst=dst: e.scalar_tensor_tensor(
                    out=dst[0:rows, :], in0=dtmp[0:rows, :], scalar=pcol(k, "rw_mu", f, rows=rows), in1=praw[j][0:rows, 1:NS + 1],
                    op0=ALU.mult, op1=ALU.add),
                    reads=[b_dtmp, b_praw[j], k.b_pv], writes=[bd])
            P.op("act", lambda e: e.activation(out=sgx0[:], in_=XG0[:], func=AF.Sigmoid), reads=[b_XG0], writes=[b_sgx])
            P.op("act", lambda e: e.activation(out=sgx1[:], in_=XG1[:], func=AF.Sigmoid), reads=[b_XG1], writes=[b_sgx])
            for hp in range(2):
                pg, bpg = nbank(k)
                P.op("pe", lambda e, pg=pg, hp=hp: e.matmul(pg[:, 0:NS], lhsT=g2a[:, hp * 128:(hp + 1) * 128], rhs=sgx0[:], start=True, stop=False),
                     reads=[b_g2, b_sgx], writes=[bpg])
                P.op("pe", lambda e, pg=pg, hp=hp: e.matmul(pg[:, 0:NS], lhsT=g2b[:, hp * 128:(hp + 1) * 128], rhs=sgx1[:], start=False, stop=True),
                     reads=[b_g2, b_sgx], writes=[bpg])
                P.op("act", lambda e, pg=pg, hp=hp: e.activation(out=GT[hp][:], in_=pg[:, 0:NS], func=AF.Copy), reads=[bpg], writes=[b_GT[hp]])
            P.op("act", lambda e: e.activation(out=XWA[0:64, :], in_=XWA[0:64, :], func=AF.Tanh), reads=[b_XWA], writes=[b_XWA])
            for hp in range(2):
                hs = slice(hp * 128, (hp + 1) * 128)
                pw, bpw = nbank(k)
                P.op("pe", lambda e, pw=pw, hs=hs: mm(e, pw[:, 0:NS], wa[0:64, hs], XWA[0:64, :], (0, 0)), reads=[b_wa, b_XWA], writes=[bpw], rt=0)
                P.op("act", lambda e, pw=pw, hp=hp: e.activation(out=EW[:], in_=pw[:, 0:NS], func=AF.Exp, scale=-1.0, bias=pcol(k, "rw_w0", hp, neg=True)),
                     reads=[bpw, k.b_npv], writes=[b_EW])
                P.op("act", lambda e: e.activation(out=EW[:], in_=EW[:], func=AF.Ln, bias=1.0), reads=[b_EW], writes=[b_EW])
                P.op("act", lambda e: e.activation(out=EW[:], in_=EW[:], func=AF.Exp, scale=-1.0, bias=-0.5), reads=[b_EW], writes=[b_EW])
                pa, bpa = nbank(k)
                P.op("pe", lambda e, pa=pa, hs=hs: mm(e, pa[:, 0:NS], wa[64:128, hs], XWA[64:128, :], (64, 0)), reads=[b_wa, b_XWA], writes=[bpa], rt=64)
                P.op("act", lambda e, pa=pa, hp=hp: e.activation(out=AL[:], in_=pa[:, 0:NS], func=AF.Sigmoid, bias=pcol(k, "rw_a0", hp)),
                     reads=[bpa, k.b_pv], writes=[b_AL])
                P.op("dve", lambda e, hp=hp: e.tensor_scalar(out=KKN[:], in0=KX[hp][:], scalar1=pcol(k, "rw_kk", hp), scalar2=None, op0=ALU.mult),
                     reads=[b_KX[hp], k.b_pv], writes=[b_KKN])
                P.op("pool", lambda e: e.tensor_tensor(out=TMP[:], in0=KKN[:], in1=KKN[:], op=ALU.mult), reads=[b_KKN], writes=[b_TMP])
                pss, bpss = nbank(k)
                P.op("pe", lambda e, pss=pss: e.matmul(pss[:, 0:NS], lhsT=bones, rhs=TMP[:], start=True, stop=True), reads=[k.b_cst, b_TMP], writes=[bpss])
                P.op("act", lambda e, pss=pss: e.activation(out=TMP[:], in_=pss[:, 0:NS], func=AF.Ln), reads=[bpss], writes=[b_TMP])
                P.op("act", lambda e: e.activation(out=TMP[:], in_=TMP[:], func=AF.Exp, scale=-0.5), reads=[b_TMP], writes=[b_TMP])
                P.op("dve", lambda e: e.tensor_tensor(out=KKN[:], in0=KKN[:], in1=TMP[:], op=ALU.mult), reads=[b_KKN, b_TMP], writes=[b_KKN])
                P.op("dve", lambda e, hp=hp: e.tensor_scalar(out=TMP[:], in0=AL[:], scalar1=pcol(k, "rw_ka", hp), scalar2=omk[:, hp:hp + 1], op0=ALU.mult, op1=ALU.add),
                     reads=[b_AL, k.b_pv, b_omk], writes=[b_TMP])
                P.op("dve", lambda e, hp=hp: e.tensor_tensor(out=KX[hp][:], in0=KX[hp][:], in1=TMP[:], op=ALU.mult), reads=[b_KX[hp], b_TMP], writes=[b_KX[hp]])
                P.op("dve", lambda e, hp=hp: e.scalar_tensor_tensor(out=TMP[:], in0=R[hp][:], scalar=pcol(k, "rw_rk", hp), in1=KX[hp][:], op0=ALU.mult, op1=ALU.mult),
                     reads=[b_R[hp], b_KX[hp], k.b_pv], writes=[b_TMP])
                pbo, bpbo = nbank(k)
                P.op("pe", lambda e, pbo=pbo: e.matmul(pbo[:, 0:NS], lhsT=bones, rhs=TMP[:], start=True, stop=True), reads=[k.b_cst, b_TMP], writes=[bpbo])
                P.op("dve", lambda e, pbo=pbo, hp=hp: e.tensor_tensor(out=BON[hp][:], in0=pbo[:, 0:NS], in1=V[hp][:], op=ALU.mult),
                     reads=[bpbo, b_V[hp]], writes=[b_BON[hp]])
                P.op("act", lambda e, hp=hp: e.activation(out=vb[hp][:], in_=V[hp][:], func=AF.Copy), reads=[b_V[hp]], writes=[b_vb[hp]])
                for c in range(NCH):
                    cc = slice(c * 32, (c + 1) * 32)
                    P.op("dve", lambda e, cc=cc: e.tensor_tensor_scan(out=CS[:, cc], data0=ones32[:], data1=EW[:, cc], initial=0.0, op0=ALU.mult, op1=ALU.add),
                         reads=[b_ones, b_EW], writes=[b_CS])
                P.op("act", lambda e: e.activation(out=E1[:], in_=CS[:], func=AF.Exp), reads=[b_CS], writes=[b_E1])
                P.op("act", lambda e: e.activation(out=E2[:], in_=CS[:], func=AF.Exp, scale=-1.0), reads=[b_CS], writes=[b_E2])
                P.op("pool", lambda e: e.tensor_tensor(out=TMP[:], in0=EW[:], in1=CS[:], op=ALU.subtract), reads=[b_EW, b_CS, b_TMP], writes=[b_TMP])
                P.op("act", lambda e: e.activation(out=E3[:], in_=TMP[:], func=AF.Exp), reads=[b_TMP], writes=[b_E3])
                E2v = E2[:].rearrange("p (c t) -> p c t", t=32)
                P.op("pool", lambda e, hp=hp, E2v=E2v: e.tensor_copy(out=WC[hp][:], in_=E2v[:, :, 31]), reads=[b_E2], writes=[b_WC[hp]])
                P.op("dve", lambda e, hp=hp: e.tensor_tensor(out=rh[hp][:], in0=R[hp][:], in1=E2[:], op=ALU.mult), reads=[b_R[hp], b_E2], writes=[b_rh[hp]])
                P.op("dve", lambda e, hp=hp: e.tensor_tensor(out=kh[hp][:], in0=KX[hp][:], in1=E1[:], op=ALU.mult), reads=[b_KX[hp], b_E1], writes=[b_kh[hp]])
                P.op("pool", lambda e: e.tensor_tensor(out=TMP[:], in0=KKN[:], in1=AL[:], op=ALU.mult), reads=[b_KKN, b_AL], writes=[b_TMP])
                P.op("dve", lambda e, hp=hp: e.tensor_tensor(out=bh[hp][:], in0=TMP[:], in1=E1[:], op=ALU.mult), reads=[b_TMP, b_E1], writes=[b_bh[hp]])
                P.op("dve", lambda e, hp=hp: e.scalar_tensor_tensor(out=ah[hp][:], in0=KKN[:], scalar=-1.0, in1=E3[:], op0=ALU.mult, op1=ALU.mult),
                     reads=[b_KKN, b_E3], writes=[b_ah[hp]])
            for c in range(NCH):
                cc = slice(c * 32, (c + 1) * 32)
                j = it % 2
                it += 1
                pX_, bpX_ = nbank(k)
                pY_, bpY_ = nbank(k)
                P.op("dve", lambda e, pX_=pX_: e.memset(pX_[:], 0.0), writes=[bpX_])
                P.op("dve", lambda e, pY_=pY_: e.memset(pY_[:, 0:128], 0.0), writes=[bpY_])
                for h in range(4):
                    hp = h // 2
                    ks = slice((h % 2) * 64, (h % 2) * 64 + 64)
                    hs = slice(h * 32, (h + 1) * 32)
                    pos = (ks.start, hs.start)
                    for mi, (lt, blt, rt, brt) in enumerate(((bh, b_bh, ah, b_ah), (ah, b_ah, bh, b_bh), (kh, b_kh, ah, b_ah), (bh, b_bh, rh, b_rh))):
                        P.op("pe", lambda e, pX_=pX_, hs=hs, ks=ks, hp=hp, cc=cc, mi=mi, lt=lt, rt=rt, pos=pos, h=h: mm(
                            e, pX_[hs, mi * 128 + h * 32: mi * 128 + (h + 1) * 32], lt[hp][ks, cc], rt[hp][ks, cc], pos),
                            reads=[blt[hp], brt[hp]], writes=[bpX_], rt=pos[0])
                    P.op("pe", lambda e, pY_=pY_, hs=hs, ks=ks, hp=hp, cc=cc, pos=pos, h=h: mm(
                        e, pY_[hs, h * 32:(h + 1) * 32], kh[hp][ks, cc], rh[hp][ks, cc], pos),
                        reads=[b_kh[hp], b_rh[hp]], writes=[bpY_], rt=pos[0])
                P.op("dve", lambda e, pX_=pX_, j=j: e.tensor_tensor(out=M4[j][:].rearrange("p a b -> p (a b)"), in0=pX_[:], in1=masks, op=ALU.mult),
                     reads=[bpX_, k.b_cst], writes=[b_M4[j]])
                P.op("dve", lambda e, pY_=pY_, j=j: e.tensor_tensor(out=RKT[j][:], in0=pY_[:, 0:128], in1=bd_iu, op=ALU.mult),
                     reads=[bpY_, k.b_cst], writes=[b_RKT[j]])
                LT = M4[j][:, 0, :]; Lm = M4[j][:, 1, :]; AKT = M4[j][:, 2, :]; RBT = M4[j][:, 3, :]
                P.op("pool", lambda e, LT=LT: e.tensor_tensor(out=TT[0][:], in0=LT, in1=identb, op=ALU.add), reads=[b_M4[j], k.b_cstb], writes=[b_TT[0]])
                A_prev, bA_prev, AT_prev, bAT_prev = Lm, b_M4[j], LT, b_M4[j]
                ti = 0
                for kq in range(1, 5):
                    an = kq % 2
                    pA, bpA = nbank(k)
                    P.op("pe", lambda e, pA=pA, AT_prev=AT_prev, A_prev=A_prev: e.matmul(pA[:, 0:128], lhsT=AT_prev, rhs=A_prev, start=True, stop=True),
                         reads=[bA_prev, bAT_prev], writes=[bpA])
                    P.op("act", lambda e, pA=pA, an=an: e.activation(out=Am[an][:], in_=pA[:, 0:128], func=AF.Copy), reads=[bpA], writes=[b_A[an]])
                    if kq < 4:
                        pAT, bpAT = nbank(k)
                        P.op("pe", lambda e, pAT=pAT, AT_prev=AT_prev, A_prev=A_prev: e.matmul(pAT[:, 0:128], lhsT=A_prev, rhs=AT_prev, start=True, stop=True),
                             reads=[bA_prev, bAT_prev], writes=[bpAT])
                        P.op("act", lambda e, pAT=pAT, an=an: e.activation(out=ATm[an][:], in_=pAT[:, 0:128], func=AF.Copy), reads=[bpAT], writes=[b_AT[an]])
                    pT, bpT = nbank(k)
                    P.op("pe", lambda e, pT=pT, an=an, ti=ti: e.matmul(pT[:, 0:128], lhsT=Am[an][:], rhs=TT[ti][:], start=True, stop=True),
                         reads=[b_A[an], b_TT[ti]], writes=[bpT])
                    P.op("dve", lambda e, pT=pT, ti=ti: e.tensor_tensor(out=TT[1 - ti][:], in0=pT[:, 0:128], in1=TT[ti][:], op=ALU.add),
                         reads=[bpT, b_TT[ti]], writes=[b_TT[1 - ti]])
                    ti = 1 - ti
                    A_prev, bA_prev, AT_prev, bAT_prev = Am[an][:], b_A[an], ATm[an][:], b_AT[an]
                TTf, bTTf = TT[ti], b_TT[ti]
                pTr, bpTr = nbank(k)
                pTb = k.psb[k.ps.index(pTr)]
                P.op("dve", lambda e, pTr=pTr: e.memset(pTr[:, 0:256], 0.0), writes=[bpTr])
                for h in range(4):
                    hp = h // 2
                    ks = slice((h % 2) * 64, (h % 2) * 64 + 64)
                    hs = slice(h * 32, (h + 1) * 32)
                    pos = (ks.start, hs.start)
                    P.op("pe", lambda e, pTb=pTb, hs=hs, ks=ks, hp=hp, cc=cc, pos=pos: tr(e, pTb[hs, hp * 128 + ks.start: hp * 128 + ks.start + 64], bh[hp][ks, cc], identb[ks, ks], pos),
                         reads=[b_bh[hp], k.b_cstb], writes=[bpTr], rt=pos[0])
                    P.op("pe", lambda e, pTb=pTb, hs=hs, ks=ks, hp=hp, cc=cc, pos=pos: tr(e, pTb[hs, 256 + hp * 128 + ks.start: 256 + hp * 128 + ks.start + 64], kh[hp][ks, cc], identb[ks, ks], pos),
                         reads=[b_kh[hp], k.b_cstb], writes=[bpTr], rt=pos[0])
                    P.op("pe", lambda e, pTb=pTb, hs=hs, ks=ks, hp=hp, cc=cc, pos=pos: tr(e, pTb[hs, 512:576], vb[hp][ks, cc], identb[ks, ks], pos),
                         reads=[b_vb[hp], k.b_cstb], writes=[bpTr], rt=pos[0])
                P.op("dve", lambda e, pTb=pTb, j=j: e.tensor_copy(out=BK[j][:].rearrange("p a b -> p (a b)"), in_=pTb[:, 0:512]),
                     reads=[bpTr], writes=[b_BK[j]])
                P.op("act", lambda e, pTb=pTb, j=j: e.activation(out=V4[j][:], in_=pTb[:, 512:576], func=AF.Copy), reads=[bpTr], writes=[b_V4[j]])
                pX, bpX = nbank(k)
                P.op("pe", lambda e, pX=pX, AKT=AKT, j=j: e.matmul(pX[:, 0:64], lhsT=AKT, rhs=V4[j][:], start=True, stop=False),
                     reads=[b_M4[j], b_V4[j]], writes=[bpX])
                for h in range(4):
                    hp = h // 2
                    ks = slice((h % 2) * 64, (h % 2) * 64 + 64)
                    hs = slice(h * 32, (h + 1) * 32)
                    P.op("pe", lambda e, pX=pX, hs=hs, ks=ks, hp=hp, cc=cc: mm(e, pX[hs, 0:64], ah[hp][ks, cc], Hb[hp][ks, :], (ks.start, hs.start), start=False, stop=True),
                         reads=[b_ah[hp], b_Hb[hp]], writes=[bpX], rt=ks.start)
                P.op("act", lambda e, pX=pX, j=j: e.activation(out=Xb[j][:], in_=pX[:, 0:64], func=AF.Copy), reads=[bpX], writes=[b_Xb[j]])
                pU, bpU = nbank(k)
                P.op("pe", lambda e, pU=pU, TTf=TTf, j=j: e.matmul(pU[:, 0:64], lhsT=TTf[:], rhs=Xb[j][:], start=True, stop=True),
                     reads=[bTTf, b_Xb[j]], writes=[bpU])
                P.op("act", lambda e, pU=pU, j=j: e.activation(out=Ub[j][:], in_=pU[:, 0:64], func=AF.Copy), reads=[bpU], writes=[b_Ub[j]])
                pY, bpY = nbank(k)
                P.op("pe", lambda e, pY=pY, RBT=RBT, j=j: e.matmul(pY[:, 0:64], lhsT=RBT, rhs=Ub[j][:], start=True, stop=False),
                     reads=[b_M4[j], b_Ub[j]], writes=[bpY])
                P.op("pe", lambda e, pY=pY, j=j: e.matmul(pY[:, 0:64], lhsT=RKT[j][:], rhs=V4[j][:], start=False, stop=False),
                     reads=[b_RKT[j], b_V4[j]], writes=[bpY])
                for h in range(4):
                    hp = h // 2
                    ks = slice((h % 2) * 64, (h % 2) * 64 + 64)
                    hs = slice(h * 32, (h + 1) * 32)
                    P.op("pe", lambda e, pY=pY, hs=hs, ks=ks, hp=hp, cc=cc: mm(e, pY[hs, 0:64], rh[hp][ks, cc], Hb[hp][ks, :], (ks.start, hs.start), start=False, stop=True),
                         reads=[b_rh[hp], b_Hb[hp]], writes=[bpY], rt=ks.start)
                P.op("act", lambda e, pY=pY, c=c: e.activation(out=YALL[:, c, :], in_=pY[:, 0:64], func=AF.Copy), reads=[bpY], writes=[b_YALL])
                for hp in range(2):
                    pH, bpH = nbank(k)
                    P.op("pe", lambda e, pH=pH, hp=hp, j=j: e.matmul(pH[:, 0:64], lhsT=BK[j][:, hp, :], rhs=Ub[j][:], start=True, stop=False),
                         reads=[b_BK[j], b_Ub[j]], writes=[bpH])
                    P.op("pe", lambda e, pH=pH, hp=hp, j=j: e.matmul(pH[:, 0:64], lhsT=BK[j][:, 2 + hp, :], rhs=V4[j][:], start=False, stop=True),
                         reads=[b_BK[j], b_V4[j]], writes=[bpH])
                    P.op("dve", lambda e, pH=pH, hp=hp: e.tensor_tensor(out=Hf[hp][:], in0=pH[:, 0:64], in1=Hf[hp][:], op=ALU.add),
                         reads=[bpH, b_Hf[hp]], writes=[b_Hf[hp]])
                    P.op("dve", lambda e, hp=hp, c=c: e.tensor_scalar(out=Hf[hp][:], in0=Hf[hp][:], scalar1=WC[hp][:, c:c + 1], scalar2=None, op0=ALU.mult),
                         reads=[b_Hf[hp], b_WC[hp]], writes=[b_Hf[hp]])
                    P.op("pool", lambda e, hp=hp: e.tensor_copy(out=Hb[hp][:], in_=Hf[hp][:]), reads=[b_Hf[hp]], writes=[b_Hb[hp]])
            P.op("dve", lambda e: e.tensor_reduce(out=s1[:], in_=YALL[:], axis=mybir.AxisListType.X, op=ALU.add), reads=[b_YALL], writes=[b_s1])
            P.op("pool", lambda e: e.tensor_tensor(out=ysq[:], in0=YALL[:], in1=YALL[:], op=ALU.mult), reads=[b_YALL], writes=[b_ysq])
            P.op("dve", lambda e: e.tensor_reduce(out=s2_[:], in_=ysq[:], axis=mybir.AxisListType.X, op=ALU.add), reads=[b_ysq], writes=[b_s2])
            P.op("dve", lambda e: e.tensor_scalar(out=s1[:], in0=s1[:], scalar1=1.0 / 64.0, scalar2=None, op0=ALU.mult), reads=[b_s1], writes=[b_s1])
            P.op("dve", lambda e: e.scalar_tensor_tensor(out=s2_[:], in0=s2_[:], scalar=1.0 / 64.0, in1=s2_[:], op0=ALU.mult, op1=ALU.bypass) if False else
                 e.tensor_scalar(out=s2_[:], in0=s2_[:], scalar1=1.0 / 64.0, scalar2=64e-5, op0=ALU.mult, op1=ALU.add), reads=[b_s2], writes=[b_s2])
            P.op("dve", lambda e: e.tensor_tensor(out=ysq[:, :, 0], in0=s1[:], in1=s1[:], op=ALU.mult), reads=[b_s1, b_ysq], writes=[b_ysq])
            P.op("dve", lambda e: e.tensor_tensor(out=s2_[:], in0=s2_[:], in1=ysq[:, :, 0], op=ALU.subtract), reads=[b_s2, b_ysq], writes=[b_s2])
            P.op("act", lambda e: e.activation(out=s2_[:], in_=s2_[:], func=AF.Ln), reads=[b_s2], writes=[b_s2])
            P.op("act", lambda e: e.activation(out=s2_[:], in_=s2_[:], func=AF.Exp, scale=-0.5), reads=[b_s2], writes=[b_s2])
            P.op("dve", lambda e: e.scalar_tensor_tensor(out=s1[:], in0=s1[:], scalar=-1.0, in1=s2_[:], op0=ALU.mult, op1=ALU.mult), reads=[b_s1, b_s2], writes=[b_s1])
            for c in range(NCH):
                P.op("act", lambda e, c=c: e.activation(out=YN[:, c, :], in_=YALL[:, c, :], func=AF.Identity, scale=s2_[:, c:c + 1], bias=s1[:, c:c + 1]),
                     reads=[b_YALL, b_s1, b_s2], writes=[b_YN])
            pF, bpF = nbank(k)
            pFb = k.psb[k.ps.index(pF)]
            for c in range(NCH):
                for h in range(4):
                    hs = slice(h * 32, (h + 1) * 32)
                    vs = slice((h % 2) * 64, (h % 2) * 64 + 64)
                    o0 = (h // 2) * 512 + c * 32
                    P.op("pe", lambda e, pFb=pFb, hs=hs, vs=vs, o0=o0, c=c: tr(e, pFb[vs, o0:o0 + 32], YN[hs, c, :], identb[hs, hs], (hs.start, vs.start)),
                         reads=[b_YN, k.b_cstb], writes=[bpF], rt=hs.start)
            for hp in range(2):
                P.op("act", lambda e, pFb=pFb, hp=hp: e.activation(out=y1[:], in_=pFb[:, hp * 512: hp * 512 + NS], func=AF.Identity,
                                                               scale=pcol(k, "rw_ln_g", hp), bias=pcol(k, "rw_ln_b", hp)),
                     reads=[bpF, k.b_pv], writes=[b_y1])
                P.op("dve", lambda e, hp=hp: e.tensor_tensor(out=y1[:], in0=y1[:], in1=BON[hp][:], op=ALU.add), reads=[b_y1, b_BON[hp]], writes=[b_y1])
                P.op("dve", lambda e, hp=hp, s0=s0: e.tensor_tensor(out=brT[:, hp, s0:s0 + NS], in0=y1[:], in1=GT[hp][:], op=ALU.mult),
                     reads=[b_y1, b_GT[hp]], writes=[b_brT[hp][tb]])
        P.barrier()

def stage_gate(k, l, brT, b_brT):
    P = k.P; T = k.T; NTB = k.NTB
    with ExitStack() as s2:
        mg = sbt(k, s2, "mg", [128, 4, T], BF16)
        b_mg = [[Buf() for _ in range(NTB)] for _ in range(4)]
        acc = sbt(k, s2, "mg_acc", [128, 512], F32); b_acc = Buf()
        sg = [sbt(k, s2, f"mg_sg{i}", [128, 512], F32) for i in range(2)]; b_sg = [Buf(), Buf()]
        pr = [sbt(k, s2, f"mg_pr{i}", [128, 512], F32) for i in range(2)]; b_pr = [Buf(), Buf()]
        ups = [sbt(k, s2, f"mg_ups{i}", [128, 2, 4, 128], BF16) for i in range(2)]
        b_ups = [Buf(), Buf()]
        lnb = alloc_ln(k, s2)
        og, _ = PV["gate_b"]
        i = 0
        for half in range(2):
            for fq in range(4):
                fc = half * 4 + fq
                u = fc % 2
                for b in range(4):
                    v = k.ups_d[b][l][:, fc * 128:(fc + 1) * 128].rearrange("(c p) n -> p c n", p=128)
                    P.dma("pool", lambda e, b=b, v=v, u=u: e.dma_start(out=ups[u][:, :, b, :], in_=v), writes=[b_ups[u]])
                i0 = k.wr_i
                k.wr_i = (i0 + 1) % k.NW
                slot, bw = k.wr[i0], k.b_wr[i0]
                for b in range(4):
                    c0 = COL_GATE + b * D + fc * 128
                    v = k.w_in_d[l][:, c0:c0 + 128].rearrange("(c p) n -> p c n", p=128)
                    P.dma("pool", lambda e, b=b, v=v, slot=slot: e.dma_start(out=slot[:, :, b * 128:(b + 1) * 128], in_=v), writes=[bw])
                for tb in range(NTB):
                    sl = slice(tb * 512, (tb + 1) * 512)
                    for b in range(4):
                        pg, bpg = nbank(k)
                        for c in range(KC):
                            P.op("pe", lambda e, pg=pg, c=c, b=b, sl=sl, slot=slot: e.matmul(
                                pg[:], lhsT=slot[:, c, b * 128:(b + 1) * 128], rhs=k.xT[:, c, sl], start=(c == 0), stop=(c == KC - 1)),
                                reads=[bw, k.b_xT[c][tb]], writes=[bpg])
                        pu, bpu = nbank(k)
                        for c in range(2):
                            P.op("pe", lambda e, pu=pu, c=c, b=b, sl=sl, u=u: e.matmul(
                                pu[:], lhsT=ups[u][:, c, b, :], rhs=brT[:, b * 2 + c, sl], start=(c == 0), stop=(c == 1)),
                                reads=[b_ups[u], b_brT[b * 2 + c][tb]], writes=[bpu])
                        j = i % 2
                        i += 1
                        gcol = k.pv[:, og + b * 8 + fc: og + b * 8 + fc + 1]
                        P.op("act", lambda e, pg=pg, j=j, gcol=gcol: e.activation(out=sg[j][:], in_=pg[:], func=AF.Sigmoid, bias=gcol),
                             reads=[bpg, k.b_pv], writes=[b_sg[j]])
                        if b == 0:
                            P.op("dve", lambda e, pu=pu, j=j: e.tensor_tensor(out=acc[:], in0=pu[:], in1=sg[j][:], op=ALU.mult),
                                 reads=[bpu, b_sg[j]], writes=[b_acc])
                        else:
                            P.op("dve", lambda e, pu=pu, j=j: e.tensor_tensor(out=pr[j][:], in0=pu[:], in1=sg[j][:], op=ALU.mult),
                                 reads=[bpu, b_sg[j]], writes=[b_pr[j]])
                            if b < 3:
                                P.op("pool", lambda e, j=j: e.tensor_tensor(out=acc[:], in0=acc[:], in1=pr[j][:], op=ALU.add),
                                     reads=[b_acc, b_pr[j]], writes=[b_acc])
                            else:
                                P.op("pool", lambda e, j=j, fq=fq, sl=sl: e.tensor_tensor(out=mg[:, fq, sl], in0=acc[:], in1=pr[j][:], op=ALU.add),
                                     reads=[b_acc, b_pr[j]], writes=[b_mg[fq][tb]])
            if "mg" in k.dbg_d:
                for c in range(4):
                    P.dma("sp", lambda e, c=c, half=half: e.dma_start(out=k.dbg_d["mg"][half * 4 + c], in_=mg[:, c, :]),
                          reads=[b_mg[c][tb] for tb in range(NTB)], is_output=True)
            out_proj_ln(k, l, 0, mg, b_mg, 4, k.w_out_d[l], first=(half == 0), last=(half == 1), row0=half * 512, lnbufs=lnb)
        P.barrier()


def stage_mix(k, l):
    P = k.P; T = k.T; NTB = k.NTB
    with ExitStack() as s2:
        brT = sbt(k, s2, "brT", [128, 8, T], BF16)
        b_brT = [[Buf() for _ in range(NTB)] for _ in range(8)]
        todo = k.mixers
        if "rw" in todo:
            mixer_rwkv(k, l, brT, b_brT)
        else:
            for c in (0, 1):
                P.op("pool", lambda e, c=c: e.memset(brT[:, c, :], 0.0), writes=[b_brT[c][tb] for tb in range(NTB)])
        if "cv" in todo:
            mixer_conv(k, l, brT, b_brT)
        else:
            for c in (2, 3):
                P.op("pool", lambda e, c=c: e.memset(brT[:, c, :], 0.0), writes=[b_brT[c][tb] for tb in range(NTB)])
        if "gla" in todo:
            mixer_gla(k, l, brT, b_brT)
        else:
            for c in (4, 5):
                P.op("pool", lambda e, c=c: e.memset(brT[:, c, :], 0.0), writes=[b_brT[c][tb] for tb in range(NTB)])
        if "fox" in todo:
            mixer_fox(k, l, brT, b_brT)
        else:
            for c in (6, 7):
                P.op("pool", lambda e, c=c: e.memset(brT[:, c, :], 0.0), writes=[b_brT[c][tb] for tb in range(NTB)])
        if "brT" in k.dbg_d:
            for c in range(8):
                P.dma("sp", lambda e, c=c: e.dma_start(out=k.dbg_d["brT"][c], in_=brT[:, c, :]),
                      reads=[b_brT[c][tb] for tb in range(NTB)], is_output=True)
        stage_gate(k, l, brT, b_brT)


def prep_inputs(inp, L=DEPTH):
    f = lambda a: np.ascontiguousarray(np.asarray(a, dtype=np.float32))
    shared = {
        "consts": CONSTS,
        "pvec": np.stack([pack_pvec(inp, l) for l in range(L)]),
        "w_in": f(inp["w_in"][:L]),
        "rw_wa": f(np.concatenate([np.asarray(inp["rw_w2"][:L]), np.asarray(inp["rw_a2"][:L])], axis=1)),
        "rw_g2": f(inp["rw_g2"][:L]),
        "gla_a2": f(inp["gla_a2"][:L]),
        "w_out": f(inp["w_out"][:L]),
    }
    for n in ("rw_up", "cv_up", "gla_up", "fox_up", "xa_wq", "xa_wk", "xa_wv", "xa_wo", "ffn_w1", "ffn_w3", "ffn_w2"):
        shared[n] = f(inp[n][:L])
    return shared


_CACHE = {}


def kernel(**inputs):
    x = np.asarray(inputs["x"], np.float32)
    mem = np.asarray(inputs["mem"], np.float32)
    B = x.shape[0]
    if "nc" not in _CACHE:
        _CACHE["nc"] = build()[0]
    nc = _CACHE["nc"]
    shared = prep_inputs(inputs)
    in_maps = []
    for b in range(B):
        m = dict(shared)
        m["x"] = np.ascontiguousarray(x[b])
        m["mem"] = np.ascontiguousarray(mem[b])
        in_maps.append(m)
    res = run_bass_kernel_spmd(nc, in_maps, core_ids=list(range(B)))
    return np.stack([r["out"] for r in res.results], axis=0).astype(np.float32)
```

```python
import numpy as np
from contextlib import ExitStack
import concourse.bass as bass
import concourse.mybir as mybir
from concourse.bass_utils import run_bass_kernel_spmd

F32 = mybir.dt.float32
BF16 = mybir.dt.bfloat16
AF = mybir.ActivationFunctionType
ALU = mybir.AluOpType

D = 1024
KC = 8
DEPTH = 4
SEQ = 2048
MEM = 256
DFF = 2816
DIN = 7220
ALPHA = (2.0 * DEPTH) ** 0.25
LN_EPS = 1e-5
COL_GATE = 3124

ENGS = ("pe", "act", "dve", "pool", "sp")
EPOCH = 30000
NDMA_SEM = {"sp": 16, "pool": 12, "act": 4}


class Buf:
    __slots__ = ("name", "w", "rs", "excl")

    def __init__(self, name="", excl=False):
        self.name = name
        self.w = None
        self.rs = []
        self.excl = excl


class Op:
    __slots__ = ("eng", "fn", "pos", "needs_inc", "inc", "isdma", "dsem", "dval", "waits", "vc")


class Prog:
    def __init__(self, nc):
        self.nc = nc
        self.ops = {e: [] for e in ENGS}
        self.clock = {e: {} for e in ENGS}
        self.dma_uses = {}
        self.dma_rr = {e: 0 for e in NDMA_SEM}
        self.dma_last = {}
        self.out_dmas = []
        self.rd_dmas = []

    def op(self, eng, fn, reads=(), writes=(), extra=(), rt=None):
        o = Op()
        o.eng = eng; o.fn = fn
        o.isdma = False; o.needs_inc = False; o.inc = None
        ex = list(extra)
        force = None
        if eng == "pe":
            lr = getattr(self, "last_rt", None)
            cur = rt if rt is not None else "full"
            if lr is not None and lr[1] != cur and (lr[1] != "full" and cur != "full"):
                force = lr[0]
            self._force = force
        self._record(o, reads, writes, ex)
        if eng == "pe":
            self.last_rt = (o, rt if rt is not None else "full")
        return o

    def dma(self, queue, fn, reads=(), writes=(), is_output=False):
        o = Op()
        o.eng = queue; o.fn = fn
        o.isdma = True; o.needs_inc = False; o.inc = None
        k = self.dma_rr[queue]
        self.dma_rr[queue] = (k + 1) % NDMA_SEM[queue]
        key = (queue, k)
        uses = self.dma_uses.get(key, 0)
        o.dsem = key
        o.dval = 16 * (uses + 1)
        self.dma_uses[key] = uses + 1
        prev = self.dma_last.get(key)
        self.dma_last[key] = o
        self._record(o, reads, writes, [prev] if prev is not None else [])
        if is_output:
            self.out_dmas.append(o)
        if len(reads) > 0:
            self.rd_dmas.append(o)
        return o

    def barrier(self):
        last = {e: (self.ops[e][-1] if self.ops[e] else None) for e in ("pe", "act", "dve")}
        for e in ("pe", "act", "dve", "sp"):
            ex = []
            for f, o in last.items():
                if f == e or o is None:
                    continue
                j = len(self.ops[f]) - 1
                while j >= 0 and (self.ops[f][j].isdma or self.ops[f][j].fn is None):
                    j -= 1
                if j >= 0:
                    ex.append(self.ops[f][j])
            self.op(e, None, extra=ex + list(self.rd_dmas))
        self.rd_dmas = []

    def _record(self, o, reads, writes, extra=()):
        if any(b.excl for b in reads):
            writes = list(writes) + [b for b in reads if b.excl and b not in writes]
            reads = [b for b in reads if not b.excl]
        e = o.eng
        lst = self.ops[e]
        o.pos = len(lst) + 1
        deps = []
        for b in reads:
            if b.w is not None:
                deps.append((b.w, "raw"))
        for b in writes:
            if b.w is not None:
                deps.append((b.w, "waw"))
            for r in b.rs:
                deps.append((r, "war"))
        for d in extra:
            deps.append((d, "raw"))
        clk = self.clock[e]
        waits = []
        force = getattr(self, "_force", None)
        self._force = None
        if force is not None and e == "pe" and clk.get(("self", e), 0) < force.pos:
            waits.append(force)
            force.needs_inc = True
            clk[("self", e)] = force.pos
        best = {}
        d2 = []
        for (y, kind) in deps:
            if y is o:
                continue
            if (not y.isdma) and y.eng != e:
                if y.eng not in best or best[y.eng].pos < y.pos:
                    best[y.eng] = y
            else:
                d2.append((y, kind))
        deps = d2 + [(y, "raw") for y in best.values()]
        for (y, kind) in deps:
            if y is o:
                continue
            if y.isdma:
                if clk.get(y.dsem, 0) >= y.dval:
                    continue
                waits.append(y)
                self._merge(clk, y.vc)
            elif y.eng == e:
                if e != "pe" and clk.get(("self", e), 0) < y.pos and y.fn is not None:
                    waits.append(y)
                    y.needs_inc = True
                    clk[("self", e)] = y.pos
            else:
                if clk.get(y.eng, 0) >= y.pos:
                    continue
                waits.append(y)
                y.needs_inc = True
                self._merge(clk, y.vc)
        o.waits = waits
        if o.isdma:
            vc = dict(clk)
            vc[o.dsem] = o.dval
            o.vc = vc
            clk[e] = o.pos
        else:
            clk[e] = o.pos
            o.vc = dict(clk)
        lst.append(o)
        for b in reads:
            b.rs.append(o)
        for b in writes:
            b.w = o
            b.rs = []

    @staticmethod
    def _merge(clk, vc):
        for k, v in vc.items():
            if isinstance(k, tuple) and k and k[0] == "self":
                continue
            if clk.get(k, 0) < v:
                clk[k] = v

    def finalize(self, block, stack):
        nc = self.nc
        fin = self.op("sp", None)
        clk = self.clock["sp"]
        for d in self.out_dmas:
            if clk.get(d.dsem, 0) < d.dval:
                fin.waits.append(d)
                clk[d.dsem] = d.dval
        esems = {}
        for e in ENGS:
            c = 0
            for o in self.ops[e]:
                if o.needs_inc and not o.isdma:
                    assert o.fn is not None
                    c += 1
                    o.inc = c
            nep = c // EPOCH + 1
            esems[e] = [stack.enter_context(nc.semaphore(f"s_{e}_{i}")) for i in range(nep)]
        dsems = {}
        for (q, k) in self.dma_uses:
            dsems[(q, k)] = stack.enter_context(nc.semaphore(f"d_{q}_{k}"))
        stats = {e: [len(self.ops[e]), 0] for e in ENGS}

        def emit(e, eng):
            for o in self.ops[e]:
                for y in o.waits:
                    if y.isdma:
                        eng.wait_ge(dsems[y.dsem], y.dval)
                    else:
                        ep = (y.inc - 1) // EPOCH
                        eng.wait_ge(esems[y.eng][ep], y.inc - ep * EPOCH)
                    stats[e][1] += 1
                if o.fn is None:
                    continue
                ins = o.fn(eng)
                if o.isdma:
                    ins.then_inc(dsems[o.dsem], 16)
                elif o.needs_inc:
                    ep = (o.inc - 1) // EPOCH
                    ins.then_inc(esems[e][ep], 1)

        @block.tensor
        def _(eng):
            emit("pe", eng)

        @block.scalar
        def _(eng):
            emit("act", eng)

        @block.vector
        def _(eng):
            emit("dve", eng)

        @block.gpsimd
        def _(eng):
            emit("pool", eng)

        @block.sync
        def _(eng):
            emit("sp", eng)
        return stats


def make_consts():
    c = {}
    c["ident"] = np.eye(128, dtype=np.float32)
    c["ones"] = np.ones((128, 128), np.float32)
    b64 = np.zeros((128, 128), np.float32)
    b64[:64, :64] = 1; b64[64:, 64:] = 1
    c["bones64"] = b64
    s = np.arange(128)[:, None]
    t = np.arange(128)[None, :]
    c["iu128"] = (s <= t).astype(np.float32)
    p = np.arange(128)[:, None]
    f = np.arange(128)[None, :]
    same = (p // 32) == (f // 32)
    c["bd32_iu"] = (same & ((p % 32) <= (f % 32))).astype(np.float32)
    su = (same & ((p % 32) < (f % 32))).astype(np.float32)
    sl_ = (same & ((p % 32) > (f % 32))).astype(np.float32)
    c["rwmask4"] = np.concatenate([su, sl_, su, c["bd32_iu"]], axis=1)
    names = list(c.keys())
    offs = {}
    o = 0
    for n in names:
        offs[n] = (o, c[n].shape[1])
        o += c[n].shape[1]
    arr = np.concatenate([c[n] for n in names], axis=1)
    return arr, offs


CONSTS, COFF = make_consts()
NCONST = CONSTS.shape[1]

PV = {}


def _pv_layout():
    o = 0
    for n, k in (("rw_mu", 9), ("rw_w0", 2), ("rw_a0", 2), ("rw_kk", 2), ("rw_ka", 2), ("rw_rk", 2),
                 ("rw_ln_g", 2), ("rw_ln_b", 2), ("cv_w", 62), ("cv_b", 2), ("cv_ln_g", 2),
                 ("cv_ln_b", 2), ("gla_ab", 1), ("gla_ln_g", 2), ("fox_bf", 1), ("gate_b", 32),
                 ("ln_g", 24), ("ln_b", 24)):
        PV[n] = (o, k)
        o += k
    return o


NPV = _pv_layout()


def _cols(v):
    v = np.asarray(v, np.float32).reshape(-1)
    n = v.shape[0]
    k = (n + 127) // 128
    buf = np.zeros((k * 128,), np.float32)
    buf[:n] = v
    return buf.reshape(k, 128).T


def pack_pvec(inp, l):
    out = np.zeros((128, NPV), np.float32)

    def put(name, arr):
        o, k = PV[name]
        assert arr.shape == (128, k), (name, arr.shape, k)
        out[:, o:o + k] = arr

    put("rw_mu", _cols(inp["rw_mu"][l]))
    for n in ("rw_w0", "rw_a0", "rw_kk", "rw_ka", "rw_ln_g", "rw_ln_b", "cv_b", "cv_ln_g", "cv_ln_b",
              "gla_ab", "gla_ln_g"):
        put(n, _cols(inp[n][l]))
    put("rw_rk", _cols(inp["rw_rk"][l].reshape(-1)))
    cw = np.asarray(inp["cv_w"][l], np.float32)
    cwp = cw.T.reshape(2, 128, 31).transpose(1, 0, 2).reshape(128, 62)
    put("cv_w", cwp)
    put("fox_bf", _cols(inp["fox_bf"][l]))
    put("gate_b", _cols(inp["gate_b"][l].reshape(-1)))
    put("ln_g", _cols(inp["ln_g"][l].reshape(-1)))
    put("ln_b", _cols(inp["ln_b"][l].reshape(-1)))
    return out


class K:
    pass


def build(T=SEQ, L=DEPTH, stages=("mix", "xa", "ffn"), dbg=(), mixers=("rw", "cv", "gla", "fox")):
    nc = bass.Bass("TRN2", target_bir_lowering=False)
    NTB = T // 512
    k = K()
    k.nc = nc; k.T = T; k.L = L; k.NTB = NTB; k.mixers = mixers
    dr = lambda n, s, kind="ExternalInput", dt=F32: nc.dram_tensor(n, s, dt, kind=kind).ap()
    k.x_d = dr("x", [T, D])
    if "xa" in stages:
        k.mem_d = dr("mem", [MEM, D])
    k.consts_d = dr("consts", [128, NCONST])
    k.pvec_d = dr("pvec", [L, 128, NPV])
    if "mix" in stages:
        k.w_in_d = dr("w_in", [L, D, DIN])
        k.rw_wa_d = dr("rw_wa", [L, 128, 256])
        k.rw_g2_d = dr("rw_g2", [L, 160, 256])
        k.gla_a2_d = dr("gla_a2", [L, 16, 128])
        k.ups_d = [dr(n, [L, 256, D]) for n in ("rw_up", "cv_up", "gla_up", "fox_up")]
        k.w_out_d = dr("w_out", [L, D, D])
    if "xa" in stages:
        k.xa_wq_d = dr("xa_wq", [L, D, D]); k.xa_wk_d = dr("xa_wk", [L, D, D])
        k.xa_wv_d = dr("xa_wv", [L, D, D]); k.xa_wo_d = dr("xa_wo", [L, D, D])
    if "ffn" in stages:
        k.w1_d = dr("ffn_w1", [L, D, DFF]); k.w3_d = dr("ffn_w3", [L, D, DFF]); k.w2_d = dr("ffn_w2", [L, DFF, D])
    k.out_d = dr("out", [T, D], kind="ExternalOutput")
    k.dbg_d = {}
    for (name, shape) in dbg:
        k.dbg_d[name] = dr("dbg_" + name, list(shape), kind="ExternalOutput", dt=BF16 if name in ("brT", "mg") else F32)

    with ExitStack() as st:
        k.st = st
        sb = lambda n, s, d=F32: st.enter_context(nc.sbuf_tensor(n, s, d))
        k.xres = sb("xres", [128, KC, T]); k.b_xres = [[Buf() for _ in range(NTB)] for _ in range(KC)]
        k.xT = sb("xT", [128, KC, T], BF16); k.b_xT = [[Buf() for _ in range(NTB)] for _ in range(KC)]
        k.cst = sb("cst", [128, NCONST]); k.b_cst = Buf()
        k.cstb = sb("cstb", [128, NCONST], BF16); k.b_cstb = Buf()
        k.pv = sb("pv", [128, NPV]); k.b_pv = Buf()
        k.npv = sb("npv", [128, NPV]); k.b_npv = Buf()
        k.g2a = sb("rw_g2a", [128, 256], BF16); k.g2b = sb("rw_g2b", [32, 256], BF16); k.b_g2 = Buf()
        k.ups = [sb(f"mg_ups{i}", [128, 2, 4, 128], BF16) for i in range(2)]; k.b_ups = [Buf(), Buf()]
        NW = 3
        k.NW = NW
        k.wr = [sb(f"wr{i}", [128, KC, 512], BF16) for i in range(NW)]
        k.b_wr = [Buf() for _ in range(NW)]
        k.wr_i = 0
        k.ps = [st.enter_context(nc.psum_tensor(f"ps{i}", [128, 512], F32)) for i in range(8)]
        k.b_ps = [Buf(excl=True) for _ in range(8)]
        k.ps_i = 0
        k.held = set()
        k.psb = [p.bitcast(BF16) for p in k.ps]
        block = st.enter_context(nc.Block())
        P = Prog(nc)
        k.P = P

        prologue(k)
        for l in range(L):
            layer_params(k, l)
            if "mix" in stages:
                stage_mix(k, l)
            if "xa" in stages:
                stage_xa(k, l)
            if "ffn" in stages:
                stage_ffn(k, l)
        epilogue(k)
        k.stats = P.finalize(block, st)
    return nc, k


_UID = [0]


def sbt(k, stack, name, shape, dt=F32):
    _UID[0] += 1
    return stack.enter_context(k.nc.sbuf_tensor(f"{name}_{_UID[0]}", list(shape), dt))


def cs(k, name, bf=False):
    o, n = COFF[name]
    return (k.cstb if bf else k.cst)[:, o:o + n]


def pcol(k, name, j=0, neg=False, rows=128, r0=0):
    o, n = PV[name]
    t = k.npv if neg else k.pv
    return t[r0:r0 + rows, o + j:o + j + 1]


def mm(e, out, lhsT, rhs, pos, start=True, stop=True):
    return e.matmul(out, lhsT=lhsT, rhs=rhs, start=start, stop=stop, tile_position=pos)


def tr(e, out, in_, identity, pos):
    return e.transpose(out=out, in_=in_, identity=identity, tile_position=pos)


def nbank(k, hold=False):
    i = k.ps_i
    while i in k.held:
        i = (i + 1) % 8
    k.ps_i = (i + 1) % 8
    if hold:
        k.held.add(i)
    return k.ps[i], k.b_ps[i]


def release(k, pt):
    for i in range(8):
        if k.ps[i] is pt:
            k.held.discard(i)
            return
    raise AssertionError


def load_w(k, src, ncols, nk=KC):
    i = k.wr_i
    k.wr_i = (i + 1) % k.NW
    slot, b = k.wr[i], k.b_wr[i]
    v = src.rearrange("(c p) n -> p c n", p=128)
    k.P.dma("pool", lambda e: e.dma_start(out=slot[:, 0:nk, 0:ncols], in_=v), writes=[b])
    return slot, b


def prologue(k):
    P = k.P; T = k.T
    P.dma("sp", lambda e: e.dma_start(out=k.cst[:], in_=k.consts_d), writes=[k.b_cst])
    P.op("dve", lambda e: e.tensor_copy(out=k.cstb[:], in_=k.cst[:]), reads=[k.b_cst], writes=[k.b_cstb])
    for i in range(8):
        P.op("dve", lambda e, i=i: e.memset(k.ps[i][:], 0.0), writes=[k.b_ps[i]])
    with ExitStack() as s2:
        xin = [sbt(k, s2, f"xin{i}", [128, D], F32) for i in range(2)]
        b_xin = [Buf(), Buf()]
        ident = cs(k, "ident")
        for tt in range(T // 128):
            j = tt % 2
            P.dma("sp", lambda e, tt=tt, j=j: e.dma_start(out=xin[j][:], in_=k.x_d[tt * 128:(tt + 1) * 128, :]),
                  writes=[b_xin[j]])
            tb = tt // 4
            for g in range(2):
                pt, bp = nbank(k)
                for q in range(4):
                    c = g * 4 + q
                    P.op("pe", lambda e, pt=pt, j=j, c=c, q=q: e.transpose(
                        out=pt[:, q * 128:(q + 1) * 128], in_=xin[j][:, c * 128:(c + 1) * 128], identity=ident),
                        reads=[b_xin[j], k.b_cst], writes=[bp])
                for q in range(4):
                    c = g * 4 + q
                    dst = slice(tt * 128, (tt + 1) * 128)
                    P.op("act", lambda e, pt=pt, c=c, q=q, dst=dst: e.activation(
                        out=k.xres[:, c, dst], in_=pt[:, q * 128:(q + 1) * 128], func=AF.Copy),
                        reads=[bp], writes=[k.b_xres[c][tb]])
                    P.op("dve", lambda e, pt=pt, c=c, q=q, dst=dst: e.tensor_copy(
                        out=k.xT[:, c, dst], in_=pt[:, q * 128:(q + 1) * 128]),
                        reads=[bp], writes=[k.b_xT[c][tb]])
        P.barrier()


def layer_params(k, l):
    P = k.P
    P.dma("sp", lambda e: e.dma_start(out=k.pv[:], in_=k.pvec_d[l]), writes=[k.b_pv])
    P.op("dve", lambda e: e.tensor_scalar(out=k.npv[:], in0=k.pv[:], scalar1=-1.0, scalar2=None, op0=ALU.mult),
         reads=[k.b_pv], writes=[k.b_npv])


def epilogue(k):
    P = k.P; T = k.T
    with ExitStack() as s2:
        xo = [sbt(k, s2, f"xo{i}", [128, D], F32) for i in range(2)]
        b_xo = [Buf(), Buf()]
        ident = cs(k, "ident")
        for tt in range(T // 128):
            j = tt % 2
            tb = tt // 4
            for g in range(2):
                pt, bp = nbank(k)
                for q in range(4):
                    c = g * 4 + q
                    P.op("pe", lambda e, pt=pt, c=c, q=q, tt=tt: e.transpose(
                        out=pt[:, q * 128:(q + 1) * 128], in_=k.xres[:, c, tt * 128:(tt + 1) * 128], identity=ident),
                        reads=[k.b_xres[c][tb], k.b_cst], writes=[bp])
                eng = "act" if g == 0 else "dve"
                if g == 0:
                    P.op("act", lambda e, pt=pt, j=j: e.activation(out=xo[j][:, 0:512], in_=pt[:], func=AF.Copy),
                         reads=[bp], writes=[b_xo[j]])
                else:
                    P.op("dve", lambda e, pt=pt, j=j: e.tensor_copy(out=xo[j][:, 512:1024], in_=pt[:]),
                         reads=[bp], writes=[b_xo[j]])
            P.dma("sp", lambda e, tt=tt, j=j: e.dma_start(out=k.out_d[tt * 128:(tt + 1) * 128, :], in_=xo[j][:]),
                  reads=[b_xo[j]], is_output=True)
        P.barrier()


def ln_block(k, l, s, tb, zsq, b_zsq, st_t, b_st):
    P = k.P
    sl = slice(tb * 512, (tb + 1) * 512)
    rstd, nmr, b_r, b_n = ln_stats(k, [(k.xres[:, c, sl], k.b_xres[c][tb]) for c in range(KC)], D, LN_EPS, zsq, b_zsq, st_t, b_st)
    og, _ = PV["ln_g"]
    ob, _ = PV["ln_b"]
    for c in range(KC):
        xs = k.xres[:, c, sl]
        P.op("dve", lambda e, xs=xs: e.tensor_tensor(out=xs, in0=xs, in1=rstd, op=ALU.mult),
             reads=[k.b_xres[c][tb], b_r], writes=[k.b_xres[c][tb]])
        P.op("dve", lambda e, xs=xs: e.tensor_tensor(out=xs, in0=xs, in1=nmr, op=ALU.add),
             reads=[k.b_xres[c][tb], b_n], writes=[k.b_xres[c][tb]])
        gcol = k.pv[:, og + s * 8 + c: og + s * 8 + c + 1]
        bcol = k.pv[:, ob + s * 8 + c: ob + s * 8 + c + 1]
        P.op("act", lambda e, xs=xs, c=c, gcol=gcol, bcol=bcol: e.activation(out=k.xT[:, c, sl], in_=xs, func=AF.Identity, scale=gcol, bias=bcol),
             reads=[k.b_xres[c][tb], k.b_pv], writes=[k.b_xT[c][tb]])
        P.op("act", lambda e, xs=xs, gcol=gcol, bcol=bcol: e.activation(out=xs, in_=xs, func=AF.Identity, scale=gcol, bias=bcol),
             reads=[k.b_xres[c][tb], k.b_pv], writes=[k.b_xres[c][tb]])


def out_proj_ln(k, l, s, src, b_src, nkc, w_d, first=True, last=True, alpha_first=True, row0=0,
                lnbufs=None):
    P = k.P
    NTB = k.NTB
    halves = []
    for h in range(2):
        slot, b = load_w(k, w_d[row0:row0 + nkc * 128, h * 512:(h + 1) * 512], 512, nk=nkc)
        halves.append((slot, b))
    for tb in range(NTB):
        sl = slice(tb * 512, (tb + 1) * 512)
        for fc in range(KC):
            slot, bw = halves[fc // 4]
            co = (fc % 4) * 128
            pt, bp = nbank(k)
            for c in range(nkc):
                P.op("pe", lambda e, pt=pt, slot=slot, c=c, co=co, sl=sl: e.matmul(
                    pt[:], lhsT=slot[:, c, co:co + 128], rhs=src[:, c, sl], start=(c == 0), stop=(c == nkc - 1)),
                    reads=[bw, b_src[c][tb]], writes=[bp])
            xs = k.xres[:, fc, sl]
            if first:
                P.op("dve", lambda e, pt=pt, xs=xs: e.scalar_tensor_tensor(
                    out=xs, in0=xs, scalar=ALPHA, in1=pt[:], op0=ALU.mult, op1=ALU.add),
                    reads=[bp, k.b_xres[fc][tb]], writes=[k.b_xres[fc][tb]])
            else:
                P.op("dve", lambda e, pt=pt, xs=xs: e.tensor_tensor(out=xs, in0=xs, in1=pt[:], op=ALU.add),
                     reads=[bp, k.b_xres[fc][tb]], writes=[k.b_xres[fc][tb]])
        if last:
            ln_block(k, l, s, tb, *lnbufs)


def alloc_ln(k, s2):
    zsq = sbt(k, s2, "zsq", [128, 2, 512], F32)
    st_t = sbt(k, s2, "lnst", [128, 3, 512], F32)
    return (zsq, [Buf(), Buf()], st_t, [Buf() for _ in range(3)])


def stage_ffn(k, l):
    P = k.P; T = k.T; NTB = k.NTB
    parts = [(0, 8), (8, 16), (16, 22)]
    with ExitStack() as s2:
        g = sbt(k, s2, "ffg", [128, 8, T], BF16)
        b_g = [[Buf() for _ in range(NTB)] for _ in range(8)]
        sg = [sbt(k, s2, f"ffs{i}", [128, 512], F32) for i in range(2)]
        b_sg = [Buf(), Buf()]
        lnb = alloc_ln(k, s2)
        si = 0
        for pi, (c0, c1) in enumerate(parts):
            n = c1 - c0
            for q0 in range(c0, c1, 4):
                nq = min(4, c1 - q0)
                s1, bw1 = load_w(k, k.w1_d[l][:, q0 * 128:(q0 + nq) * 128], nq * 128)
                s3, bw3 = load_w(k, k.w3_d[l][:, q0 * 128:(q0 + nq) * 128], nq * 128)
                for q in range(nq):
                    cg = q0 + q - c0
                    for tb in range(NTB):
                        sl = slice(tb * 512, (tb + 1) * 512)
                        p1, bp1 = nbank(k)
                        p3, bp3 = nbank(k)
                        for c in range(KC):
                            P.op("pe", lambda e, p1=p1, s1=s1, c=c, q=q, sl=sl: e.matmul(
                                p1[:], lhsT=s1[:, c, q * 128:(q + 1) * 128], rhs=k.xT[:, c, sl], start=(c == 0), stop=(c == KC - 1)),
                                reads=[bw1, k.b_xT[c][tb]], writes=[bp1])
                        for c in range(KC):
                            P.op("pe", lambda e, p3=p3, s3=s3, c=c, q=q, sl=sl: e.matmul(
                                p3[:], lhsT=s3[:, c, q * 128:(q + 1) * 128], rhs=k.xT[:, c, sl], start=(c == 0), stop=(c == KC - 1)),
                                reads=[bw3, k.b_xT[c][tb]], writes=[bp3])
                        j = si % 2
                        si += 1
                        P.op("act", lambda e, p1=p1, j=j: e.activation(out=sg[j][:], in_=p1[:], func=AF.Silu),
                             reads=[bp1], writes=[b_sg[j]])
                        P.op("dve", lambda e, p3=p3, j=j, cg=cg, sl=sl: e.tensor_tensor(
                            out=g[:, cg, sl], in0=sg[j][:], in1=p3[:], op=ALU.mult),
                            reads=[bp3, b_sg[j]], writes=[b_g[cg][tb]])
            out_proj_ln(k, l, 2, g, b_g, n, k.w2_d[l], first=(pi == 0), last=(pi == len(parts) - 1),
                        row0=c0 * 128, lnbufs=lnb)
        P.barrier()


def stage_xa(k, l):
    P = k.P; T = k.T; NTB = k.NTB
    ident = cs(k, "ident")
    with ExitStack() as s2:
        sbt_ = lambda n, s, d=F32: sbt(k, s2, n, s, d)
        memT = sbt_("memT", [128, KC, MEM], BF16)
        b_memT = Buf()
        k.xaK = sbt_("xaK", [128, KC, MEM], BF16); k.b_xaK = Buf()
        k.xaV = sbt_("xaV", [128, 2, D], BF16); k.b_xaV = Buf()
        with ExitStack() as s3:
            mt = [sbt(k, s3, f"memin{i}", [128, D], F32) for i in range(2)]
            b_mt = [Buf(), Buf()]
            for m in range(2):
                P.dma("sp", lambda e, m=m: e.dma_start(out=mt[m][:], in_=k.mem_d[m * 128:(m + 1) * 128, :]), writes=[b_mt[m]])
                for g in range(2):
                    pt, bp = nbank(k)
                    for q in range(4):
                        c = g * 4 + q
                        P.op("pe", lambda e, pt=pt, m=m, c=c, q=q: e.transpose(
                            out=pt[:, q * 128:(q + 1) * 128], in_=mt[m][:, c * 128:(c + 1) * 128], identity=ident),
                            reads=[b_mt[m], k.b_cst], writes=[bp])
                    for q in range(4):
                        c = g * 4 + q
                        P.op("dve", lambda e, pt=pt, m=m, c=c, q=q: e.tensor_copy(
                            out=memT[:, c, m * 128:(m + 1) * 128], in_=pt[:, q * 128:(q + 1) * 128]),
                            reads=[bp], writes=[b_memT])
            P.barrier()
        for h in range(2):
            slot, bw = load_w(k, k.xa_wk_d[l][:, h * 512:(h + 1) * 512], 512)
            for q in range(4):
                fc = h * 4 + q
                pt, bp = nbank(k)
                for c in range(KC):
                    P.op("pe", lambda e, pt=pt, slot=slot, c=c, q=q: e.matmul(
                        pt[:, 0:MEM], lhsT=slot[:, c, q * 128:(q + 1) * 128], rhs=memT[:, c, :], start=(c == 0), stop=(c == KC - 1)),
                        reads=[bw, b_memT], writes=[bp])
                P.op("act", lambda e, pt=pt, fc=fc: e.activation(out=k.xaK[:, fc, :], in_=pt[:, 0:MEM], func=AF.Copy),
                     reads=[bp], writes=[k.b_xaK])
        for h in range(2):
            slot, bw = load_w(k, k.xa_wv_d[l][:, h * 512:(h + 1) * 512], 512)
            for m in range(2):
                pt, bp = nbank(k)
                for c in range(KC):
                    P.op("pe", lambda e, pt=pt, slot=slot, c=c, m=m: e.matmul(
                        pt[:], lhsT=memT[:, c, m * 128:(m + 1) * 128], rhs=slot[:, c, :], start=(c == 0), stop=(c == KC - 1)),
                        reads=[bw, b_memT], writes=[bp])
                P.op("act", lambda e, pt=pt, m=m, h=h: e.activation(out=k.xaV[:, m, h * 512:(h + 1) * 512], in_=pt[:], func=AF.Copy),
                     reads=[bp], writes=[k.b_xaV])
        oT = sbt_("xa_oT", [128, KC, T], BF16)
        b_oT = [[Buf() for _ in range(NTB)] for _ in range(KC)]
        qT = sbt_("xa_qT", [128, 2, T], BF16)
        b_qT = [[Buf() for _ in range(NTB)] for _ in range(2)]
        PT = [sbt_(f"xa_PT{i}", [128, 2, 512], BF16) for i in range(2)]
        b_PT = [[Buf(), Buf()], [Buf(), Buf()]]
        rden = [sbt_(f"xa_rd{i}", [128, 512]) for i in range(2)]
        b_rden = [Buf(), Buf()]
        onesb = cs(k, "ones", bf=True)
        scale = 1.0 / 16.0
        it = 0
        for hh in range(2):
            slot, bw = load_w(k, k.xa_wq_d[l][:, hh * 512:(hh + 1) * 512], 512)
            for h2 in range(2):
                h = hh * 2 + h2
                for tb in range(NTB):
                    sl = slice(tb * 512, (tb + 1) * 512)
                    for dc in range(2):
                        pt, bp = nbank(k)
                        co = (h2 * 2 + dc) * 128
                        for c in range(KC):
                            P.op("pe", lambda e, pt=pt, slot=slot, c=c, co=co, sl=sl: e.matmul(
                                pt[:], lhsT=slot[:, c, co:co + 128], rhs=k.xT[:, c, sl], start=(c == 0), stop=(c == KC - 1)),
                                reads=[bw, k.b_xT[c][tb]], writes=[bp])
                        P.op("act", lambda e, pt=pt, dc=dc, sl=sl: e.activation(out=qT[:, dc, sl], in_=pt[:], func=AF.Copy),
                             reads=[bp], writes=[b_qT[dc][tb]])
                    j = it % 2
                    it += 1
                    for m in range(2):
                        pt, bp = nbank(k)
                        for dc in range(2):
                            P.op("pe", lambda e, pt=pt, h=h, dc=dc, m=m, sl=sl: e.matmul(
                                pt[:], lhsT=k.xaK[:, h * 2 + dc, m * 128:(m + 1) * 128], rhs=qT[:, dc, sl],
                                start=(dc == 0), stop=(dc == 1)),
                                reads=[k.b_xaK, b_qT[dc][tb]], writes=[bp])
                        P.op("act", lambda e, pt=pt, j=j, m=m: e.activation(out=PT[j][:, m, :], in_=pt[:], func=AF.Exp, scale=scale),
                             reads=[bp], writes=[b_PT[j][m]])
                    pd, bpd = nbank(k)
                    for m in range(2):
                        P.op("pe", lambda e, pd=pd, j=j, m=m: e.matmul(pd[:], lhsT=onesb, rhs=PT[j][:, m, :], start=(m == 0), stop=(m == 1)),
                             reads=[k.b_cstb, b_PT[j][m]], writes=[bpd])
                    P.op("dve", lambda e, pd=pd, j=j: e.reciprocal(out=rden[j][:], in_=pd[:]), reads=[bpd], writes=[b_rden[j]])
                    for dc in range(2):
                        po, bpo = nbank(k)
                        for m in range(2):
                            P.op("pe", lambda e, po=po, j=j, m=m, h=h, dc=dc: e.matmul(
                                po[:], lhsT=k.xaV[:, m, h * 256 + dc * 128: h * 256 + (dc + 1) * 128], rhs=PT[j][:, m, :],
                                start=(m == 0), stop=(m == 1)),
                                reads=[k.b_xaV, b_PT[j][m]], writes=[bpo])
                        P.op("dve", lambda e, po=po, j=j, h=h, dc=dc, sl=sl: e.tensor_tensor(
                            out=oT[:, h * 2 + dc, sl], in0=po[:], in1=rden[j][:], op=ALU.mult),
                            reads=[bpo, b_rden[j]], writes=[b_oT[h * 2 + dc][tb]])
        lnb = alloc_ln(k, s2)
        out_proj_ln(k, l, 1, oT, b_oT, KC, k.xa_wo_d[l], lnbufs=lnb)
        P.barrier()


def proj_fm(k, slot, bw, col0, ncols, tb):
    P = k.P
    sl = slice(tb * 512, (tb + 1) * 512)
    pt, bp = nbank(k)
    for c in range(KC):
        P.op("pe", lambda e, c=c: e.matmul(pt[0:ncols, :], lhsT=slot[:, c, col0:col0 + ncols], rhs=k.xT[:, c, sl],
                                           start=(c == 0), stop=(c == KC - 1)),
             reads=[bw, k.b_xT[c][tb]], writes=[bp])
    return pt, bp


def ln_stats(k, srcs, nfeat, eps, zsq, b_zsq, st_t, b_st):
    P = k.P
    ones = cs(k, "ones")
    S1, b1 = nbank(k)
    S2, b2 = nbank(k)
    n = len(srcs)
    for i, (ap, b) in enumerate(srcs):
        j = i % 2
        P.op("act", lambda e, j=j, ap=ap: e.activation(out=zsq[:, j, :], in_=ap, func=AF.Square), reads=[b], writes=[b_zsq[j]])
        P.op("pe", lambda e, i=i, ap=ap: e.matmul(S1[:], lhsT=ones, rhs=ap, start=(i == 0), stop=(i == n - 1)),
             reads=[b, k.b_cst], writes=[b1])
        P.op("pe", lambda e, i=i, j=j: e.matmul(S2[:], lhsT=ones, rhs=zsq[:, j, :], start=(i == 0), stop=(i == n - 1)),
             reads=[b_zsq[j], k.b_cst], writes=[b2])
    mean, var, rstd = (st_t[:, i, :] for i in range(3))
    P.op("act", lambda e: e.activation(out=mean, in_=S1[:], func=AF.Copy, scale=1.0 / nfeat), reads=[b1], writes=[b_st[0]])
    P.op("dve", lambda e: e.tensor_tensor(out=var, in0=mean, in1=mean, op=ALU.mult), reads=[b_st[0]], writes=[b_st[1]])
    P.op("dve", lambda e: e.scalar_tensor_tensor(out=var, in0=S2[:], scalar=1.0 / nfeat, in1=var, op0=ALU.mult, op1=ALU.subtract),
         reads=[b2, b_st[1]], writes=[b_st[1]])
    P.op("dve", lambda e: e.tensor_scalar(out=var, in0=var, scalar1=eps, scalar2=None, op0=ALU.add),
         reads=[b_st[1]], writes=[b_st[1]])
    P.op("act", lambda e: e.activation(out=rstd, in_=var, func=AF.Ln), reads=[b_st[1]], writes=[b_st[2]])
    P.op("act", lambda e: e.activation(out=rstd, in_=rstd, func=AF.Exp, scale=-0.5), reads=[b_st[2]], writes=[b_st[2]])
    P.op("dve", lambda e: e.scalar_tensor_tensor(out=mean, in0=mean, scalar=-1.0, in1=rstd, op0=ALU.mult, op1=ALU.mult),
         reads=[b_st[0], b_st[2]], writes=[b_st[0]])
    return rstd, mean, b_st[2], b_st[0]


def mixer_conv(k, l, brT, b_brT):
    P = k.P; T = k.T; NTB = k.NTB
    with ExitStack() as s2:
        ub = sbt(k, s2, "cv_u", [128, 2, 30 + T], F32)
        b_ub = [Buf(), Buf()]
        acc = sbt(k, s2, "cv_acc", [128, 2, T], F32)
        b_acc = [Buf(), Buf()]
        lnb = alloc_ln(k, s2)
        sg = [lnb[0][:, i, :] for i in range(2)]
        b_sg = lnb[1]
        slot, bw = load_w(k, k.w_in_d[l][:, 1056:1568], 512)
        for ch in range(2):
            P.op("dve", lambda e, ch=ch: e.memset(ub[:, ch, 0:30], 0.0), writes=[b_ub[ch]])
        i = 0
        for tb in range(NTB):
            for ch in range(2):
                pa, bpa = proj_fm(k, slot, bw, ch * 128, 128, tb)
                pb, bpb = proj_fm(k, slot, bw, 256 + ch * 128, 128, tb)
                j = i % 2
                i += 1
                P.op("act", lambda e, pb=pb, j=j: e.activation(out=sg[j], in_=pb[:], func=AF.Sigmoid), reads=[bpb], writes=[b_sg[j]])
                P.op("dve", lambda e, pa=pa, j=j, ch=ch, tb=tb: e.tensor_tensor(
                    out=ub[:, ch, 30 + tb * 512: 30 + (tb + 1) * 512], in0=pa[:], in1=sg[j], op=ALU.mult),
                    reads=[bpa, b_sg[j]], writes=[b_ub[ch]])
        ow, _ = PV["cv_w"]
        for ch in range(2):
            eng = "dve"
            for kk in range(31):
                wcol = k.pv[:, ow + ch * 31 + kk: ow + ch * 31 + kk + 1]
                if kk == 0:
                    bcol = pcol(k, "cv_b", ch)
                    P.op(eng, lambda e, ch=ch, wcol=wcol, bcol=bcol: e.tensor_scalar(
                        out=acc[:, ch, :], in0=ub[:, ch, 0:T], scalar1=wcol, scalar2=bcol, op0=ALU.mult, op1=ALU.add),
                        reads=[b_ub[ch], k.b_pv], writes=[b_acc[ch]])
                else:
                    P.op(eng, lambda e, ch=ch, wcol=wcol, kk=kk: e.scalar_tensor_tensor(
                        out=acc[:, ch, :], in0=ub[:, ch, kk:kk + T], scalar=wcol, in1=acc[:, ch, :], op0=ALU.mult, op1=ALU.add),
                        reads=[b_ub[ch], k.b_pv, b_acc[ch]], writes=[b_acc[ch]])
        for tb in range(NTB):
            sl = slice(tb * 512, (tb + 1) * 512)
            rstd, nmr, b_r, b_n = ln_stats(k, [(acc[:, ch, sl], b_acc[ch]) for ch in range(2)], 256, LN_EPS, *lnb)
            for ch in range(2):
                a = acc[:, ch, sl]
                P.op("dve", lambda e, a=a, rstd=rstd: e.tensor_tensor(out=a, in0=a, in1=rstd, op=ALU.mult),
                     reads=[b_acc[ch], b_r], writes=[b_acc[ch]])
                P.op("dve", lambda e, a=a, nmr=nmr: e.tensor_tensor(out=a, in0=a, in1=nmr, op=ALU.add),
                     reads=[b_acc[ch], b_n], writes=[b_acc[ch]])
                P.op("act", lambda e, a=a, ch=ch, sl=sl: e.activation(
                    out=brT[:, 2 + ch, sl], in_=a, func=AF.Silu, scale=pcol(k, "cv_ln_g", ch), bias=pcol(k, "cv_ln_b", ch)),
                    reads=[b_acc[ch], k.b_pv], writes=[b_brT[2 + ch][tb]])
        P.barrier()


def mixer_fox(k, l, brT, b_brT):
    P = k.P; T = k.T; NTB = k.NTB
    NT = T // 128
    with ExitStack() as s2:
        fq = sbt(k, s2, "fx_q", [128, 2, T], BF16); b_fq = [[Buf() for _ in range(NTB)] for _ in range(2)]
        fk = sbt(k, s2, "fx_k", [128, 2, T], BF16); b_fk = [[Buf() for _ in range(NTB)] for _ in range(2)]
        fv = sbt(k, s2, "fx_v", [128, NT, 256], BF16); b_fv = [Buf() for _ in range(NT)]
        spl = sbt(k, s2, "fx_spl", [4, T], F32); b_spl = Buf()
        sig, b_sig = spl, b_spl
        rsel = sbt(k, s2, "fx_rsel", [4, NT * 4], F32); b_rsel = Buf()
        stok = sbt(k, s2, "fx_stok", [128, NT * 4], F32); b_stok = Buf()
        sref = sbt(k, s2, "fx_sref", [128, NT * 4], F32); b_sref = Buf()
        bias = sbt(k, s2, "fx_bias", [128, NT, NT], F32); b_bias = Buf()
        PT = [sbt(k, s2, f"fx_PT{i}", [128, 512], BF16) for i in range(2)]
        b_PT = [Buf(), Buf()]
        rd = sbt(k, s2, "fx_rd", [128, 512], F32); b_rd = Buf()
        slA, bwA = load_w(k, k.w_in_d[l][:, 2352:2864], 512)
        slB, bwB = load_w(k, k.w_in_d[l][:, 2864:3124], 260)
        for tb in range(NTB):
            sl = slice(tb * 512, (tb + 1) * 512)
            for ch in range(2):
                pq, bpq = proj_fm(k, slA, bwA, ch * 128, 128, tb)
                P.op("act", lambda e, pq=pq, ch=ch, sl=sl: e.activation(out=fq[:, ch, sl], in_=pq[:], func=AF.Copy, scale=0.125),
                     reads=[bpq], writes=[b_fq[ch][tb]])
                pk, bpk = proj_fm(k, slA, bwA, 256 + ch * 128, 128, tb)
                P.op("dve", lambda e, pk=pk, ch=ch, sl=sl: e.tensor_copy(out=fk[:, ch, sl], in_=pk[:]),
                     reads=[bpk], writes=[b_fk[ch][tb]])
            pz, bpz = proj_fm(k, slB, bwB, 256, 4, tb)
            P.op("act", lambda e, pz=pz, sl=sl: e.activation(out=spl[:, sl], in_=pz[0:4, :], func=AF.Exp, scale=-1.0,
                                                             bias=pcol(k, "fox_bf", 0, neg=True, rows=4)),
                 reads=[bpz, k.b_npv], writes=[b_spl])
            P.op("act", lambda e, sl=sl: e.activation(out=spl[:, sl], in_=spl[:, sl], func=AF.Ln, bias=1.0),
                 reads=[b_spl], writes=[b_spl])
        for tt in range(NT):
            tb = tt // 4
            pt, bp = nbank(k)
            for c in range(KC):
                P.op("pe", lambda e, c=c, tt=tt, pt=pt: e.matmul(pt[:, 0:256], lhsT=k.xT[:, c, tt * 128:(tt + 1) * 128], rhs=slB[:, c, 0:256],
                                                             start=(c == 0), stop=(c == KC - 1)),
                     reads=[bwB, k.b_xT[c][tb]], writes=[bp])
            P.op("act", lambda e, pt=pt, tt=tt: e.activation(out=fv[:, tt, :], in_=pt[:, 0:256], func=AF.Copy), reads=[bp], writes=[b_fv[tt]])
        P.op("dve", lambda e: e.tensor_tensor_scan(out=sig[:], data0=spl[:], data1=spl[:], initial=0.0, op0=ALU.add, op1=ALU.max),
             reads=[b_spl], writes=[b_sig])
        ident = cs(k, "ident")
        pt, bp = nbank(k)
        for tt in range(NT):
            P.op("pe", lambda e, tt=tt: e.transpose(out=pt[:, tt * 4:(tt + 1) * 4], in_=sig[0:4, tt * 128:(tt + 1) * 128], identity=ident[0:4, 0:4]),
                 reads=[b_sig, k.b_cst], writes=[bp])
        P.op("dve", lambda e: e.tensor_copy(out=stok[:], in_=pt[:, 0:NT * 4]), reads=[bp], writes=[b_stok])
        for qs in range(NT):
            P.op("dve", lambda e, qs=qs: e.tensor_scalar(out=rsel[:, qs * 4:(qs + 1) * 4], in0=ident[0:4, 0:4], scalar1=sig[:, qs * 128:qs * 128 + 1],
                                                     scalar2=None, op0=ALU.mult),
                 reads=[b_sig, k.b_cst], writes=[b_rsel])
        pr, bpr = nbank(k)
        ones = cs(k, "ones")
        P.op("pe", lambda e: e.matmul(pr[:, 0:NT * 4], lhsT=ones[0:4, :], rhs=rsel[:], start=True, stop=True),
             reads=[b_rsel, k.b_cst], writes=[bpr])
        P.op("dve", lambda e: e.tensor_copy(out=sref[:], in_=pr[:, 0:NT * 4]), reads=[bpr], writes=[b_sref])
        stok3 = stok[:].rearrange("p (n h) -> p n h", h=4)
        onesb = cs(k, "ones", bf=True)
        iu = cs(k, "iu128", bf=True)
        it = 0
        for h in range(4):
            ch = h // 2
            pb = (h % 2) * 64
            for qs in range(NT):
                P.op("dve", lambda e, h=h, qs=qs: e.tensor_scalar(out=bias[:, qs, :], in0=stok3[:, :, h], scalar1=sref[:, qs * 4 + h:qs * 4 + h + 1],
                                                            scalar2=None, op0=ALU.subtract),
                     reads=[b_stok, b_sref], writes=[b_bias])
            for Q in range(NTB):
                nkt = 4 * (Q + 1)
                po, bpo = nbank(k, hold=True)
                pd, bpd = nbank(k, hold=True)
                for kt in range(nkt):
                    d = kt - 4 * Q
                    q0 = d * 128 if d > 0 else 0
                    j = it % 2
                    it += 1
                    ps_, bps = nbank(k)
                    P.op("pe", lambda e, ps_=ps_, pb=pb, ch=ch, kt=kt, Q=Q, q0=q0: e.matmul(
                        ps_[:, q0:512], lhsT=fk[pb:pb + 64, ch, kt * 128:(kt + 1) * 128], rhs=fq[pb:pb + 64, ch, Q * 512 + q0:(Q + 1) * 512],
                        start=True, stop=True),
                        reads=[b_fk[ch][kt // 4], b_fq[ch][Q]], writes=[bps])
                    for qi in range(q0 // 128, 4):
                        qs = Q * 4 + qi
                        P.op("act", lambda e, ps_=ps_, j=j, qi=qi, qs=qs, h=h, kt=kt: e.activation(
                            out=PT[j][:, qi * 128:(qi + 1) * 128], in_=ps_[:, qi * 128:(qi + 1) * 128], func=AF.Exp,
                            bias=bias[:, qs, kt:kt + 1]),
                            reads=[bps, b_bias], writes=[b_PT[j]])
                    if d >= 0:
                        P.op("dve", lambda e, j=j, q0=q0: e.tensor_tensor(out=PT[j][:, q0:q0 + 128], in0=PT[j][:, q0:q0 + 128], in1=iu, op=ALU.mult),
                             reads=[b_PT[j], k.b_cstb], writes=[b_PT[j]])
                    P.op("pe", lambda e, po=po, pb=pb, j=j, q0=q0, kt=kt, h=h, nkt=nkt: e.matmul(
                        po[pb:pb + 64, q0:512], lhsT=fv[:, kt, h * 64:(h + 1) * 64], rhs=PT[j][:, q0:512], start=(kt == 0), stop=(kt == nkt - 1)),
                        reads=[b_fv[kt], b_PT[j]], writes=[bpo])
                    P.op("pe", lambda e, pd=pd, pb=pb, j=j, q0=q0, kt=kt, nkt=nkt: e.matmul(
                        pd[pb:pb + 64, q0:512], lhsT=onesb[:, 0:64], rhs=PT[j][:, q0:512], start=(kt == 0), stop=(kt == nkt - 1)),
                        reads=[k.b_cstb, b_PT[j]], writes=[bpd])
                P.op("dve", lambda e, pd=pd, pb=pb: e.reciprocal(out=rd[pb:pb + 64, :], in_=pd[pb:pb + 64, :]), reads=[bpd], writes=[b_rd])
                P.op("dve", lambda e, po=po, pb=pb, ch=ch, Q=Q: e.tensor_tensor(
                    out=brT[pb:pb + 64, 6 + ch, Q * 512:(Q + 1) * 512], in0=po[pb:pb + 64, :], in1=rd[pb:pb + 64, :], op=ALU.mult),
                    reads=[bpo, b_rd], writes=[b_brT[6 + ch][Q]])
                release(k, po); release(k, pd)
        P.barrier()


def mixer_gla(k, l, brT, b_brT):
    P = k.P; T = k.T; NTB = k.NTB
    identb = cs(k, "ident", bf=True)
    with ExitStack() as s2:
        f32t = lambda n, shp: sbt(k, s2, n, shp, F32)
        bft = lambda n, shp: sbt(k, s2, n, shp, BF16)
        a2 = f32t("gl_a2", [16, 128]); b_a2 = Buf()
        zT = f32t("gl_z", [16, 512]); b_zT = Buf()
        spl = f32t("gl_spl", [128, 512]); b_spl = Buf()
        bcs = f32t("gl_bcs", [128, 512]); b_bcs = Buf()
        Ep = f32t("gl_Ep", [128, 512]); b_Ep = Buf()
        En = f32t("gl_En", [128, 512]); b_En = Buf()
        ones32 = f32t("gl_ones", [128, 32]); b_ones = Buf()
        qd = bft("gl_qd", [128, 512]); b_qd = Buf()
        ki = bft("gl_ki", [128, 512]); b_ki = Buf()
        vb = bft("gl_vb", [128, 2, 512]); b_vb = Buf()
        sr = f32t("gl_sr", [128, 2, 512]); b_sr = Buf()
        S4f = f32t("gl_S4f", [128, 64]); b_S4f = Buf()
        S4b = bft("gl_S4b", [128, 64]); b_S4b = Buf()
        STm = [bft(f"gl_STm{i}", [128, 128]) for i in range(2)]; b_STm = [Buf(), Buf()]
        V4 = [bft(f"gl_V4{i}", [128, 64]) for i in range(2)]; b_V4 = [Buf(), Buf()]
        KT = [bft(f"gl_KT{i}", [128, 128]) for i in range(2)]; b_KT = [Buf(), Buf()]
        OALL = f32t("gl_OALL", [128, 16, 64]); b_OALL = Buf()
        osq = f32t("gl_osq", [128, 16, 64]); b_osq = Buf()
        ss = f32t("gl_ss", [128, 16]); b_ss = Buf()
        ONALL = bft("gl_ON", [128, 16, 64]); b_ON = Buf()
        slA, bwA = load_w(k, k.w_in_d[l][:, 1568:2080], 512)
        slB, bwB = load_w(k, k.w_in_d[l][:, 2080:2352], 272)
        P.dma("sp", lambda e: e.dma_start(out=a2[:], in_=k.gla_a2_d[l]), writes=[b_a2])
        P.op("dve", lambda e: e.memset(ones32[:], 1.0), writes=[b_ones])
        P.op("dve", lambda e: e.memset(S4f[:], 0.0), writes=[b_S4f])
        P.op("dve", lambda e: e.memset(S4b[:], 0.0), writes=[b_S4b])
        bd_iu = cs(k, "bd32_iu")
        it = 0
        for tb in range(NTB):
            sl = slice(tb * 512, (tb + 1) * 512)
            pz, bpz = proj_fm(k, slB, bwB, 256, 16, tb)
            P.op("act", lambda e, pz=pz: e.activation(out=zT[:], in_=pz[0:16, :], func=AF.Copy), reads=[bpz], writes=[b_zT])
            pla, bpla = nbank(k)
            P.op("pe", lambda e, pla=pla: e.matmul(pla[:], lhsT=a2[:], rhs=zT[:], start=True, stop=True), reads=[b_a2, b_zT], writes=[bpla])
            P.op("act", lambda e, pla=pla: e.activation(out=spl[:], in_=pla[:], func=AF.Exp, scale=-1.0, bias=pcol(k, "gla_ab", 0, neg=True)),
                 reads=[bpla, k.b_npv], writes=[b_spl])
            P.op("act", lambda e: e.activation(out=spl[:], in_=spl[:], func=AF.Ln, bias=1.0), reads=[b_spl], writes=[b_spl])
            for c in range(16):
                cc = slice(c * 32, (c + 1) * 32)
                P.op("dve", lambda e, cc=cc: e.tensor_tensor_scan(out=bcs[:, cc], data0=ones32[:], data1=spl[:, cc], initial=0.0,
                                                              op0=ALU.mult, op1=ALU.add),
                     reads=[b_ones, b_spl], writes=[b_bcs])
            P.op("act", lambda e: e.activation(out=Ep[:], in_=bcs[:], func=AF.Exp, scale=1.0 / 16.0), reads=[b_bcs], writes=[b_Ep])
            P.op("act", lambda e: e.activation(out=En[:], in_=bcs[:], func=AF.Exp, scale=-1.0 / 16.0), reads=[b_bcs], writes=[b_En])
            pq, bpq = proj_fm(k, slA, bwA, 0, 128, tb)
            P.op("dve", lambda e, pq=pq: e.scalar_tensor_tensor(out=qd[:], in0=pq[:], scalar=32.0 ** -0.5, in1=En[:], op0=ALU.mult, op1=ALU.mult),
                 reads=[bpq, b_En], writes=[b_qd])
            pk, bpk = proj_fm(k, slA, bwA, 128, 128, tb)
            P.op("dve", lambda e, pk=pk: e.tensor_tensor(out=ki[:], in0=pk[:], in1=Ep[:], op=ALU.mult), reads=[bpk, b_Ep], writes=[b_ki])
            for ch in range(2):
                pv, bpv = proj_fm(k, slA, bwA, 256 + ch * 128, 128, tb)
                P.op("act", lambda e, pv=pv, ch=ch: e.activation(out=vb[:, ch, :], in_=pv[:], func=AF.Copy), reads=[bpv], writes=[b_vb])
                pr, bpr = proj_fm(k, slB, bwB, ch * 128, 128, tb)
                P.op("act", lambda e, pr=pr, ch=ch: e.activation(out=sr[:, ch, :], in_=pr[:], func=AF.Silu), reads=[bpr], writes=[b_sr])
            for c in range(16):
                cc = slice(c * 32, (c + 1) * 32)
                j = it % 2
                it += 1
                pS, bpS = nbank(k)
                P.op("dve", lambda e, pS=pS: e.memset(pS[:, 0:128], 0.0), writes=[bpS])
                for h in range(4):
                    hs = slice(h * 32, (h + 1) * 32)
                    P.op("pe", lambda e, pS=pS, hs=hs, cc=cc: mm(e, pS[hs, hs], ki[hs, cc], qd[hs, cc], (hs.start, hs.start)),
                         reads=[b_ki, b_qd], writes=[bpS], rt=hs.start)
                P.op("dve", lambda e, pS=pS, j=j: e.tensor_tensor(out=STm[j][:], in0=pS[:, 0:128], in1=bd_iu, op=ALU.mult),
                     reads=[bpS, k.b_cst], writes=[b_STm[j]])
                pTr, bpTr = nbank(k)
                pTb = k.psb[k.ps.index(pTr)]
                P.op("dve", lambda e, pTr=pTr: e.memset(pTr[:, 64:128], 0.0), writes=[bpTr])
                for h in range(4):
                    hs = slice(h * 32, (h + 1) * 32)
                    vs = slice((h % 2) * 64, (h % 2) * 64 + 64)
                    P.op("pe", lambda e, pTb=pTb, hs=hs, vs=vs, h=h, cc=cc: tr(e, pTb[hs, 0:64], vb[vs, h // 2, cc], identb[vs, vs], (vs.start, hs.start)),
                         reads=[b_vb, k.b_cstb], writes=[bpTr], rt=vs.start)
                for h in range(4):
                    hs = slice(h * 32, (h + 1) * 32)
                    P.op("pe", lambda e, pTb=pTb, hs=hs, cc=cc, h=h: tr(e, pTb[hs, 128 + h * 32:128 + (h + 1) * 32], ki[hs, cc], identb[hs, hs], (hs.start, hs.start)),
                         reads=[b_ki, k.b_cstb], writes=[bpTr], rt=hs.start)
                P.op("act", lambda e, pTb=pTb, j=j: e.activation(out=V4[j][:], in_=pTb[:, 0:64], func=AF.Copy), reads=[bpTr], writes=[b_V4[j]])
                P.op("dve", lambda e, pTb=pTb, j=j: e.tensor_copy(out=KT[j][:], in_=pTb[:, 128:256]),
                     reads=[bpTr], writes=[b_KT[j]])
                pO, bpO = nbank(k)
                P.op("pe", lambda e, pO=pO, j=j: e.matmul(pO[:, 0:64], lhsT=STm[j][:], rhs=V4[j][:], start=True, stop=False),
                     reads=[b_STm[j], b_V4[j]], writes=[bpO])
                for h in range(4):
                    hs = slice(h * 32, (h + 1) * 32)
                    P.op("pe", lambda e, pO=pO, hs=hs, cc=cc, h=h: mm(e, pO[hs, 0:64], qd[hs, cc], S4b[hs, :], (hs.start, hs.start), start=False, stop=True),
                         reads=[b_qd, b_S4b], writes=[bpO], rt=hs.start)
                P.op("act", lambda e, pO=pO, c=c: e.activation(out=OALL[:, c, :], in_=pO[:, 0:64], func=AF.Copy), reads=[bpO], writes=[b_OALL])
                pSt, bpSt = nbank(k)
                P.op("pe", lambda e, pSt=pSt, j=j: e.matmul(pSt[:, 0:64], lhsT=KT[j][:], rhs=V4[j][:], start=True, stop=True),
                     reads=[b_KT[j], b_V4[j]], writes=[bpSt])
                P.op("dve", lambda e, pSt=pSt: e.tensor_tensor(out=S4f[:], in0=pSt[:, 0:64], in1=S4f[:], op=ALU.add), reads=[bpSt, b_S4f], writes=[b_S4f])
                P.op("dve", lambda e, c=c: e.tensor_scalar(out=S4f[:], in0=S4f[:], scalar1=En[:, c * 32 + 31:c * 32 + 32], scalar2=None, op0=ALU.mult),
                     reads=[b_S4f, b_En], writes=[b_S4f])
                P.op("dve", lambda e: e.tensor_copy(out=S4b[:], in_=S4f[:]), reads=[b_S4f], writes=[b_S4b])
            P.op("dve", lambda e: e.tensor_tensor(out=osq[:], in0=OALL[:], in1=OALL[:], op=ALU.mult), reads=[b_OALL], writes=[b_osq])
            P.op("dve", lambda e: e.tensor_reduce(out=ss[:], in_=osq[:], axis=mybir.AxisListType.X, op=ALU.add), reads=[b_osq], writes=[b_ss])
            P.op("dve", lambda e: e.tensor_scalar(out=ss[:], in0=ss[:], scalar1=1.0 / 64.0, scalar2=1e-5, op0=ALU.mult, op1=ALU.add),
                 reads=[b_ss], writes=[b_ss])
            P.op("act", lambda e: e.activation(out=ss[:], in_=ss[:], func=AF.Ln), reads=[b_ss], writes=[b_ss])
            P.op("act", lambda e: e.activation(out=ss[:], in_=ss[:], func=AF.Exp, scale=-0.5), reads=[b_ss], writes=[b_ss])
            for c in range(16):
                P.op("dve", lambda e, c=c: e.tensor_scalar(out=ONALL[:, c, :], in0=OALL[:, c, :], scalar1=ss[:, c:c + 1], scalar2=None, op0=ALU.mult),
                     reads=[b_OALL, b_ss], writes=[b_ON])
            pF, bpF = nbank(k)
            pFb = k.psb[k.ps.index(pF)]
            for c in range(16):
                for h in range(4):
                    hs = slice(h * 32, (h + 1) * 32)
                    vs = slice((h % 2) * 64, (h % 2) * 64 + 64)
                    o0 = (h // 2) * 512 + c * 32
                    P.op("pe", lambda e, pFb=pFb, hs=hs, vs=vs, o0=o0, c=c: tr(e, pFb[vs, o0:o0 + 32], ONALL[hs, c, :], identb[hs, hs], (hs.start, vs.start)),
                         reads=[b_ON, k.b_cstb], writes=[bpF], rt=hs.start)
            for fc in range(2):
                P.op("dve", lambda e, pFb=pFb, fc=fc, sl=sl: e.scalar_tensor_tensor(
                    out=brT[:, 4 + fc, sl], in0=pFb[:, fc * 512:(fc + 1) * 512], scalar=pcol(k, "gla_ln_g", fc), in1=sr[:, fc, :],
                    op0=ALU.mult, op1=ALU.mult),
                    reads=[bpF, k.b_pv, b_sr], writes=[b_brT[4 + fc][tb]])
        P.barrier()


def mixer_rwkv(k, l, brT, b_brT):
    P = k.P; T = k.T
    NS = 256
    NSB = T // NS
    NCH = NS // 32
    identb = cs(k, "ident", bf=True)
    bones = cs(k, "bones64")
    with ExitStack() as s2:
        f32t = lambda n, shp: sbt(k, s2, n, shp, F32)
        bft = lambda n, shp: sbt(k, s2, n, shp, BF16)
        B = lambda: Buf()
        wa = f32t("rw_wa", [128, 256]); b_wa = B()
        g2a, g2b, b_g2 = k.g2a, k.g2b, k.b_g2
        omk = f32t("rw_omk", [128, 2]); b_omk = B()
        pprev = f32t("rw_pprev", [128, 9]); b_pprev = B()
        praw = [f32t(f"rw_praw{i}", [128, NS + 1]) for i in range(2)]; b_praw = [B(), B()]
        R = [f32t(f"rw_R{i}", [128, NS]) for i in range(2)]; b_R = [B(), B()]
        KX = [f32t(f"rw_KX{i}", [128, NS]) for i in range(2)]; b_KX = [B(), B()]
        V = [f32t(f"rw_V{i}", [128, NS]) for i in range(2)]; b_V = [B(), B()]
        XWA = f32t("rw_XWA", [128, NS]); b_XWA = B()
        XG0 = f32t("rw_XG0", [128, NS]); b_XG0 = B()
        XG1 = f32t("rw_XG1", [32, NS]); b_XG1 = B()
        sgx0 = bft("rw_sgx0", [128, NS]); sgx1 = bft("rw_sgx1", [32, NS]); b_sgx = B()
        EW = f32t("rw_EW", [128, NS]); b_EW = B()
        AL = f32t("rw_AL", [128, NS]); b_AL = B()
        GT = [bft(f"rw_GT{i}", [128, NS]) for i in range(2)]; b_GT = [B(), B()]
        KKN = f32t("rw_KKN", [128, NS]); b_KKN = B()
        TMP = f32t("rw_TMP", [128, NS]); b_TMP = B()
        CS = f32t("rw_CS", [128, NS]); b_CS = B()
        E1 = f32t("rw_E1", [128, NS]); b_E1 = B()
        E2 = f32t("rw_E2", [128, NS]); b_E2 = B()
        E3 = f32t("rw_E3", [128, NS]); b_E3 = B()
        WC = [f32t(f"rw_WC{i}", [128, NCH]) for i in range(2)]; b_WC = [B(), B()]
        BON = [f32t(f"rw_BON{i}", [128, NS]) for i in range(2)]; b_BON = [B(), B()]
        ones32 = f32t("rw_ones", [128, 32]); b_ones = B()
        rh = [bft(f"rw_rh{i}", [128, NS]) for i in range(2)]; b_rh = [B(), B()]
        kh = [bft(f"rw_kh{i}", [128, NS]) for i in range(2)]; b_kh = [B(), B()]
        bh = [bft(f"rw_bh{i}", [128, NS]) for i in range(2)]; b_bh = [B(), B()]
        ah = [bft(f"rw_ah{i}", [128, NS]) for i in range(2)]; b_ah = [B(), B()]
        vb = [bft(f"rw_vb{i}", [128, NS]) for i in range(2)]; b_vb = [B(), B()]
        M4 = [bft(f"rw_M4{i}", [128, 4, 128]) for i in range(2)]; b_M4 = [B(), B()]
        RKT = [bft(f"rw_RKT{i}", [128, 128]) for i in range(2)]; b_RKT = [B(), B()]
        Am = [bft(f"rw_A{i}", [128, 128]) for i in range(2)]; b_A = [B(), B()]
        ATm = [bft(f"rw_AT{i}", [128, 128]) for i in range(2)]; b_AT = [B(), B()]
        TT = [bft(f"rw_TT{i}", [128, 128]) for i in range(2)]; b_TT = [B(), B()]
        BK = [bft(f"rw_BK{i}", [128, 4, 128]) for i in range(2)]; b_BK = [B(), B()]
        V4 = [bft(f"rw_V4{i}", [128, 64]) for i in range(2)]; b_V4 = [B(), B()]
        Xb = [bft(f"rw_Xb{i}", [128, 64]) for i in range(2)]; b_Xb = [B(), B()]
        Ub = [bft(f"rw_Ub{i}", [128, 64]) for i in range(2)]; b_Ub = [B(), B()]
        Hf = [f32t(f"rw_Hf{i}", [128, 64]) for i in range(2)]; b_Hf = [B(), B()]
        Hb = [bft(f"rw_Hb{i}", [128, 64]) for i in range(2)]; b_Hb = [B(), B()]
        YALL = f32t("rw_YALL", [128, NCH, 64]); b_YALL = B()
        ysq = f32t("rw_ysq", [128, NCH, 64]); b_ysq = B()
        s1 = f32t("rw_s1", [128, NCH]); b_s1 = B()
        s2_ = f32t("rw_s2", [128, NCH]); b_s2 = B()
        YN = bft("rw_YN", [128, NCH, 64]); b_YN = B()
        y1, b_y1 = KKN, b_KKN
        masks = cs(k, "rwmask4")
        bd_iu = cs(k, "bd32_iu")

        slA, bwA = load_w(k, k.w_in_d[l][:, 0:512], 512)
        slB, bwB = load_w(k, k.w_in_d[l][:, 512:1024], 512)
        slC, bwC = load_w(k, k.w_in_d[l][:, 1024:1056], 32)
        P.dma("sp", lambda e: e.dma_start(out=wa[:], in_=k.rw_wa_d[l]), writes=[b_wa])
        P.dma("pool", lambda e: e.dma_start(out=g2a[:], in_=k.rw_g2_d[l][0:128, :]), writes=[b_g2])
        P.dma("pool", lambda e: e.dma_start(out=g2b[:], in_=k.rw_g2_d[l][128:160, :]), writes=[b_g2])
        oka, _ = PV["rw_ka"]
        P.op("dve", lambda e: e.tensor_scalar(out=omk[:], in0=k.pv[:, oka:oka + 2], scalar1=-1.0, scalar2=1.0, op0=ALU.mult, op1=ALU.add),
             reads=[k.b_pv], writes=[b_omk])
        P.op("dve", lambda e: e.memset(pprev[:], 0.0), writes=[b_pprev])
        P.op("dve", lambda e: e.memset(ones32[:], 1.0), writes=[b_ones])
        for hp in range(2):
            P.op("dve", lambda e, hp=hp: e.memset(Hf[hp][:], 0.0), writes=[b_Hf[hp]])
            P.op("dve", lambda e, hp=hp: e.memset(Hb[hp][:], 0.0), writes=[b_Hb[hp]])
        dests = [(R[0], b_R[0], 128), (R[1], b_R[1], 128), (KX[0], b_KX[0], 128), (KX[1], b_KX[1], 128),
                 (V[0], b_V[0], 128), (V[1], b_V[1], 128), (XWA, b_XWA, 128), (XG0, b_XG0, 128), (XG1, b_XG1, 32)]
        ip = 0
        it = 0

        def proj_n(slot, bw, col0, ncols, s0):
            tb = s0 // 512
            pt, bp = nbank(k)
            for c in range(KC):
                P.op("pe", lambda e, c=c: e.matmul(pt[0:ncols, 0:NS], lhsT=slot[:, c, col0:col0 + ncols], rhs=k.xT[:, c, s0:s0 + NS],
                                                   start=(c == 0), stop=(c == KC - 1)),
                     reads=[bw, k.b_xT[c][tb]], writes=[bp])
            return pt, bp

        for sb in range(NSB):
            s0 = sb * NS
            tb = s0 // 512
            for f, (dst, bd, rows) in enumerate(dests):
                if f < 4:
                    pt, bp = proj_n(slA, bwA, f * 128, 128, s0)
                elif f < 8:
                    pt, bp = proj_n(slB, bwB, (f - 4) * 128, 128, s0)
                else:
                    pt, bp = proj_n(slC, bwC, 0, 32, s0)
                j = ip % 2
                ip += 1
                P.op("dve", lambda e, j=j, f=f, rows=rows: e.tensor_copy(out=praw[j][0:rows, 0:1], in_=pprev[0:rows, f:f + 1]),
                     reads=[b_pprev], writes=[b_praw[j]])
                P.op("act", lambda e, j=j, pt=pt, rows=rows: e.activation(out=praw[j][0:rows, 1:NS + 1], in_=pt[0:rows, 0:NS], func=AF.Copy),
                     reads=[bp], writes=[b_praw[j]])
                P.op("dve", lambda e, j=j, f=f, rows=rows: e.tensor_copy(out=pprev[0:rows, f:f + 1], in_=praw[j][0:rows, NS:NS + 1]),
                     reads=[b_praw[j]], writes=[b_pprev])
                P.op("dve", lambda e, j=j, rows=rows, dst=dst: e.tensor_tensor(out=dst[0:rows, :], in0=praw[j][0:rows, 0:NS], in1=praw[j][0:rows, 1:NS + 1], op=ALU.subtract),
                     reads=[b_praw[j]], writes=[bd])
                P.op("dve", lambda e, j=j, rows=rows, f=f, dst=dst: e.scalar_tensor_tensor(
                    out=dst[0:rows, :], in0=dst[0:rows, :], scalar=pcol(k, "rw_mu", f, rows=rows), in1=praw[j][0:rows, 1:NS + 1],
                    op0=ALU.mult, op1=ALU.add),
                    reads=[bd, b_praw[j], k.b_pv], writes=[bd])
            P.op("act", lambda e: e.activation(out=sgx0[:], in_=XG0[:], func=AF.Sigmoid), reads=[b_XG0], writes=[b_sgx])
            P.op("act", lambda e: e.activation(out=sgx1[:], in_=XG1[:], func=AF.Sigmoid), reads=[b_XG1], writes=[b_sgx])
            for hp in range(2):
                pg, bpg = nbank(k)
                P.op("pe", lambda e, pg=pg, hp=hp: e.matmul(pg[:, 0:NS], lhsT=g2a[:, hp * 128:(hp + 1) * 128], rhs=sgx0[:], start=True, stop=False),
                     reads=[b_g2, b_sgx], writes=[bpg])
                P.op("pe", lambda e, pg=pg, hp=hp: e.matmul(pg[:, 0:NS], lhsT=g2b[:, hp * 128:(hp + 1) * 128], rhs=sgx1[:], start=False, stop=True),
                     reads=[b_g2, b_sgx], writes=[bpg])
                P.op("act", lambda e, pg=pg, hp=hp: e.activation(out=GT[hp][:], in_=pg[:, 0:NS], func=AF.Copy), reads=[bpg], writes=[b_GT[hp]])
            P.op("act", lambda e: e.activation(out=XWA[0:64, :], in_=XWA[0:64, :], func=AF.Tanh), reads=[b_XWA], writes=[b_XWA])
            for hp in range(2):
                hs = slice(hp * 128, (hp + 1) * 128)
                pw, bpw = nbank(k)
                P.op("pe", lambda e, pw=pw, hs=hs: mm(e, pw[:, 0:NS], wa[0:64, hs], XWA[0:64, :], (0, 0)), reads=[b_wa, b_XWA], writes=[bpw], rt=0)
                P.op("act", lambda e, pw=pw, hp=hp: e.activation(out=EW[:], in_=pw[:, 0:NS], func=AF.Exp, scale=-1.0, bias=pcol(k, "rw_w0", hp, neg=True)),
                     reads=[bpw, k.b_npv], writes=[b_EW])
                P.op("act", lambda e: e.activation(out=EW[:], in_=EW[:], func=AF.Ln, bias=1.0), reads=[b_EW], writes=[b_EW])
                P.op("act", lambda e: e.activation(out=EW[:], in_=EW[:], func=AF.Exp, scale=-1.0, bias=-0.5), reads=[b_EW], writes=[b_EW])
                pa, bpa = nbank(k)
                P.op("pe", lambda e, pa=pa, hs=hs: mm(e, pa[:, 0:NS], wa[64:128, hs], XWA[64:128, :], (64, 0)), reads=[b_wa, b_XWA], writes=[bpa], rt=64)
                P.op("act", lambda e, pa=pa, hp=hp: e.activation(out=AL[:], in_=pa[:, 0:NS], func=AF.Sigmoid, bias=pcol(k, "rw_a0", hp)),
                     reads=[bpa, k.b_pv], writes=[b_AL])
                P.op("dve", lambda e, hp=hp: e.tensor_scalar(out=KKN[:], in0=KX[hp][:], scalar1=pcol(k, "rw_kk", hp), scalar2=None, op0=ALU.mult),
                     reads=[b_KX[hp], k.b_pv], writes=[b_KKN])
                P.op("dve", lambda e: e.tensor_tensor(out=TMP[:], in0=KKN[:], in1=KKN[:], op=ALU.mult), reads=[b_KKN], writes=[b_TMP])
                pss, bpss = nbank(k)
                P.op("pe", lambda e, pss=pss: e.matmul(pss[:, 0:NS], lhsT=bones, rhs=TMP[:], start=True, stop=True), reads=[k.b_cst, b_TMP], writes=[bpss])
                P.op("act", lambda e, pss=pss: e.activation(out=TMP[:], in_=pss[:, 0:NS], func=AF.Ln), reads=[bpss], writes=[b_TMP])
                P.op("act", lambda e: e.activation(out=TMP[:], in_=TMP[:], func=AF.Exp, scale=-0.5), reads=[b_TMP], writes=[b_TMP])
                P.op("dve", lambda e: e.tensor_tensor(out=KKN[:], in0=KKN[:], in1=TMP[:], op=ALU.mult), reads=[b_KKN, b_TMP], writes=[b_KKN])
                P.op("dve", lambda e, hp=hp: e.tensor_scalar(out=TMP[:], in0=AL[:], scalar1=pcol(k, "rw_ka", hp), scalar2=omk[:, hp:hp + 1], op0=ALU.mult, op1=ALU.add),
                     reads=[b_AL, k.b_pv, b_omk], writes=[b_TMP])
                P.op("dve", lambda e, hp=hp: e.tensor_tensor(out=KX[hp][:], in0=KX[hp][:], in1=TMP[:], op=ALU.mult), reads=[b_KX[hp], b_TMP], writes=[b_KX[hp]])
                P.op("dve", lambda e, hp=hp: e.scalar_tensor_tensor(out=TMP[:], in0=R[hp][:], scalar=pcol(k, "rw_rk", hp), in1=KX[hp][:], op0=ALU.mult, op1=ALU.mult),
                     reads=[b_R[hp], b_KX[hp], k.b_pv], writes=[b_TMP])
                pbo, bpbo = nbank(k)
                P.op("pe", lambda e, pbo=pbo: e.matmul(pbo[:, 0:NS], lhsT=bones, rhs=TMP[:], start=True, stop=True), reads=[k.b_cst, b_TMP], writes=[bpbo])
                P.op("dve", lambda e, pbo=pbo, hp=hp: e.tensor_tensor(out=BON[hp][:], in0=pbo[:, 0:NS], in1=V[hp][:], op=ALU.mult),
                     reads=[bpbo, b_V[hp]], writes=[b_BON[hp]])
                P.op("act", lambda e, hp=hp: e.activation(out=vb[hp][:], in_=V[hp][:], func=AF.Copy), reads=[b_V[hp]], writes=[b_vb[hp]])
                for c in range(NCH):
                    cc = slice(c * 32, (c + 1) * 32)
                    P.op("dve", lambda e, cc=cc: e.tensor_tensor_scan(out=CS[:, cc], data0=ones32[:], data1=EW[:, cc], initial=0.0, op0=ALU.mult, op1=ALU.add),
                         reads=[b_ones, b_EW], writes=[b_CS])
                P.op("act", lambda e: e.activation(out=E1[:], in_=CS[:], func=AF.Exp), reads=[b_CS], writes=[b_E1])
                P.op("act", lambda e: e.activation(out=E2[:], in_=CS[:], func=AF.Exp, scale=-1.0), reads=[b_CS], writes=[b_E2])
                P.op("dve", lambda e: e.tensor_tensor(out=TMP[:], in0=EW[:], in1=CS[:], op=ALU.subtract), reads=[b_EW, b_CS, b_TMP], writes=[b_TMP])
                P.op("act", lambda e: e.activation(out=E3[:], in_=TMP[:], func=AF.Exp), reads=[b_TMP], writes=[b_E3])
                E2v = E2[:].rearrange("p (c t) -> p c t", t=32)
                P.op("dve", lambda e, hp=hp, E2v=E2v: e.tensor_copy(out=WC[hp][:], in_=E2v[:, :, 31]), reads=[b_E2], writes=[b_WC[hp]])
                P.op("dve", lambda e, hp=hp: e.tensor_tensor(out=rh[hp][:], in0=R[hp][:], in1=E2[:], op=ALU.mult), reads=[b_R[hp], b_E2], writes=[b_rh[hp]])
                P.op("dve", lambda e, hp=hp: e.tensor_tensor(out=kh[hp][:], in0=KX[hp][:], in1=E1[:], op=ALU.mult), reads=[b_KX[hp], b_E1], writes=[b_kh[hp]])
                P.op("dve", lambda e: e.tensor_tensor(out=TMP[:], in0=KKN[:], in1=AL[:], op=ALU.mult), reads=[b_KKN, b_AL], writes=[b_TMP])
                P.op("dve", lambda e, hp=hp: e.tensor_tensor(out=bh[hp][:], in0=TMP[:], in1=E1[:], op=ALU.mult), reads=[b_TMP, b_E1], writes=[b_bh[hp]])
                P.op("dve", lambda e, hp=hp: e.scalar_tensor_tensor(out=ah[hp][:], in0=KKN[:], scalar=-1.0, in1=E3[:], op0=ALU.mult, op1=ALU.mult),
                     reads=[b_KKN, b_E3], writes=[b_ah[hp]])
            for c in range(NCH):
                cc = slice(c * 32, (c + 1) * 32)
                j = it % 2
                it += 1
                pX_, bpX_ = nbank(k)
                pY_, bpY_ = nbank(k)
                P.op("dve", lambda e, pX_=pX_: e.memset(pX_[:], 0.0), writes=[bpX_])
                P.op("dve", lambda e, pY_=pY_: e.memset(pY_[:, 0:128], 0.0), writes=[bpY_])
                for h in range(4):
                    hp = h // 2
                    ks = slice((h % 2) * 64, (h % 2) * 64 + 64)
                    hs = slice(h * 32, (h + 1) * 32)
                    pos = (ks.start, hs.start)
                    for mi, (lt, blt, rt, brt) in enumerate(((bh, b_bh, ah, b_ah), (ah, b_ah, bh, b_bh), (kh, b_kh, ah, b_ah), (bh, b_bh, rh, b_rh))):
                        P.op("pe", lambda e, pX_=pX_, hs=hs, ks=ks, hp=hp, cc=cc, mi=mi, lt=lt, rt=rt, pos=pos, h=h: mm(
                            e, pX_[hs, mi * 128 + h * 32: mi * 128 + (h + 1) * 32], lt[hp][ks, cc], rt[hp][ks, cc], pos),
                            reads=[blt[hp], brt[hp]], writes=[bpX_], rt=pos[0])
                    P.op("pe", lambda e, pY_=pY_, hs=hs, ks=ks, hp=hp, cc=cc, pos=pos, h=h: mm(
                        e, pY_[hs, h * 32:(h + 1) * 32], kh[hp][ks, cc], rh[hp][ks, cc], pos),
                        reads=[b_kh[hp], b_rh[hp]], writes=[bpY_], rt=pos[0])
                P.op("dve", lambda e, pX_=pX_, j=j: e.tensor_tensor(out=M4[j][:].rearrange("p a b -> p (a b)"), in0=pX_[:], in1=masks, op=ALU.mult),
                     reads=[bpX_, k.b_cst], writes=[b_M4[j]])
                P.op("dve", lambda e, pY_=pY_, j=j: e.tensor_tensor(out=RKT[j][:], in0=pY_[:, 0:128], in1=bd_iu, op=ALU.mult),
                     reads=[bpY_, k.b_cst], writes=[b_RKT[j]])
                LT = M4[j][:, 0, :]; Lm = M4[j][:, 1, :]; AKT = M4[j][:, 2, :]; RBT = M4[j][:, 3, :]
                P.op("dve", lambda e, LT=LT: e.tensor_tensor(out=TT[0][:], in0=LT, in1=identb, op=ALU.add), reads=[b_M4[j], k.b_cstb], writes=[b_TT[0]])
                A_prev, bA_prev, AT_prev, bAT_prev = Lm, b_M4[j], LT, b_M4[j]
                ti = 0
                for kq in range(1, 5):
                    an = kq % 2
                    pA, bpA = nbank(k)
                    P.op("pe", lambda e, pA=pA, AT_prev=AT_prev, A_prev=A_prev: e.matmul(pA[:, 0:128], lhsT=AT_prev, rhs=A_prev, start=True, stop=True),
                         reads=[bA_prev, bAT_prev], writes=[bpA])
                    P.op("act", lambda e, pA=pA, an=an: e.activation(out=Am[an][:], in_=pA[:, 0:128], func=AF.Copy), reads=[bpA], writes=[b_A[an]])
                    if kq < 4:
                        pAT, bpAT = nbank(k)
                        P.op("pe", lambda e, pAT=pAT, AT_prev=AT_prev, A_prev=A_prev: e.matmul(pAT[:, 0:128], lhsT=A_prev, rhs=AT_prev, start=True, stop=True),
                             reads=[bA_prev, bAT_prev], writes=[bpAT])
                        P.op("act", lambda e, pAT=pAT, an=an: e.activation(out=ATm[an][:], in_=pAT[:, 0:128], func=AF.Copy), reads=[bpAT], writes=[b_AT[an]])
                    pT, bpT = nbank(k)
                    P.op("pe", lambda e, pT=pT, an=an, ti=ti: e.matmul(pT[:, 0:128], lhsT=Am[an][:], rhs=TT[ti][:], start=True, stop=True),
                         reads=[b_A[an], b_TT[ti]], writes=[bpT])
                    P.op("dve", lambda e, pT=pT, ti=ti: e.tensor_tensor(out=TT[1 - ti][:], in0=pT[:, 0:128], in1=TT[ti][:], op=ALU.add),
                         reads=[bpT, b_TT[ti]], writes=[b_TT[1 - ti]])
                    ti = 1 - ti
                    A_prev, bA_prev, AT_prev, bAT_prev = Am[an][:], b_A[an], ATm[an][:], b_AT[an]
                TTf, bTTf = TT[ti], b_TT[ti]
                pTr, bpTr = nbank(k)
                pTb = k.psb[k.ps.index(pTr)]
                P.op("dve", lambda e, pTr=pTr: e.memset(pTr[:, 0:256], 0.0), writes=[bpTr])
                for h in range(4):
                    hp = h // 2
                    ks = slice((h % 2) * 64, (h % 2) * 64 + 64)
                    hs = slice(h * 32, (h + 1) * 32)
                    pos = (ks.start, hs.start)
                    P.op("pe", lambda e, pTb=pTb, hs=hs, ks=ks, hp=hp, cc=cc, pos=pos: tr(e, pTb[hs, hp * 128 + ks.start: hp * 128 + ks.start + 64], bh[hp][ks, cc], identb[ks, ks], pos),
                         reads=[b_bh[hp], k.b_cstb], writes=[bpTr], rt=pos[0])
                    P.op("pe", lambda e, pTb=pTb, hs=hs, ks=ks, hp=hp, cc=cc, pos=pos: tr(e, pTb[hs, 256 + hp * 128 + ks.start: 256 + hp * 128 + ks.start + 64], kh[hp][ks, cc], identb[ks, ks], pos),
                         reads=[b_kh[hp], k.b_cstb], writes=[bpTr], rt=pos[0])
                    P.op("pe", lambda e, pTb=pTb, hs=hs, ks=ks, hp=hp, cc=cc, pos=pos: tr(e, pTb[hs, 512:576], vb[hp][ks, cc], identb[ks, ks], pos),
                         reads=[b_vb[hp], k.b_cstb], writes=[bpTr], rt=pos[0])
                P.op("dve", lambda e, pTb=pTb, j=j: e.tensor_copy(out=BK[j][:].rearrange("p a b -> p (a b)"), in_=pTb[:, 0:512]),
                     reads=[bpTr], writes=[b_BK[j]])
                P.op("act", lambda e, pTb=pTb, j=j: e.activation(out=V4[j][:], in_=pTb[:, 512:576], func=AF.Copy), reads=[bpTr], writes=[b_V4[j]])
                pX, bpX = nbank(k)
                P.op("pe", lambda e, pX=pX, AKT=AKT, j=j: e.matmul(pX[:, 0:64], lhsT=AKT, rhs=V4[j][:], start=True, stop=False),
                     reads=[b_M4[j], b_V4[j]], writes=[bpX])
                for h in range(4):
                    hp = h // 2
                    ks = slice((h % 2) * 64, (h % 2) * 64 + 64)
                    hs = slice(h * 32, (h + 1) * 32)
                    P.op("pe", lambda e, pX=pX, hs=hs, ks=ks, hp=hp, cc=cc: mm(e, pX[hs, 0:64], ah[hp][ks, cc], Hb[hp][ks, :], (ks.start, hs.start), start=False, stop=True),
                         reads=[b_ah[hp], b_Hb[hp]], writes=[bpX], rt=ks.start)
                P.op("act", lambda e, pX=pX, j=j: e.activation(out=Xb[j][:], in_=pX[:, 0:64], func=AF.Copy), reads=[bpX], writes=[b_Xb[j]])
                pU, bpU = nbank(k)
                P.op("pe", lambda e, pU=pU, TTf=TTf, j=j: e.matmul(pU[:, 0:64], lhsT=TTf[:], rhs=Xb[j][:], start=True, stop=True),
                     reads=[bTTf, b_Xb[j]], writes=[bpU])
                P.op("act", lambda e, pU=pU, j=j: e.activation(out=Ub[j][:], in_=pU[:, 0:64], func=AF.Copy), reads=[bpU], writes=[b_Ub[j]])
                pY, bpY = nbank(k)
                P.op("pe", lambda e, pY=pY, RBT=RBT, j=j: e.matmul(pY[:, 0:64], lhsT=RBT, rhs=Ub[j][:], start=True, stop=False),
                     reads=[b_M4[j], b_Ub[j]], writes=[bpY])
                P.op("pe", lambda e, pY=pY, j=j: e.matmul(pY[:, 0:64], lhsT=RKT[j][:], rhs=V4[j][:], start=False, stop=False),
                     reads=[b_RKT[j], b_V4[j]], writes=[bpY])
                for h in range(4):
                    hp = h // 2
                    ks = slice((h % 2) * 64, (h % 2) * 64 + 64)
                    hs = slice(h * 32, (h + 1) * 32)
                    P.op("pe", lambda e, pY=pY, hs=hs, ks=ks, hp=hp, cc=cc: mm(e, pY[hs, 0:64], rh[hp][ks, cc], Hb[hp][ks, :], (ks.start, hs.start), start=False, stop=True),
                         reads=[b_rh[hp], b_Hb[hp]], writes=[bpY], rt=ks.start)
                P.op("act", lambda e, pY=pY, c=c: e.activation(out=YALL[:, c, :], in_=pY[:, 0:64], func=AF.Copy), reads=[bpY], writes=[b_YALL])
                for hp in range(2):
                    pH, bpH = nbank(k)
                    P.op("pe", lambda e, pH=pH, hp=hp, j=j: e.matmul(pH[:, 0:64], lhsT=BK[j][:, hp, :], rhs=Ub[j][:], start=True, stop=False),
                         reads=[b_BK[j], b_Ub[j]], writes=[bpH])
                    P.op("pe", lambda e, pH=pH, hp=hp, j=j: e.matmul(pH[:, 0:64], lhsT=BK[j][:, 2 + hp, :], rhs=V4[j][:], start=False, stop=True),
                         reads=[b_BK[j], b_V4[j]], writes=[bpH])
                    P.op("dve", lambda e, pH=pH, hp=hp: e.tensor_tensor(out=Hf[hp][:], in0=pH[:, 0:64], in1=Hf[hp][:], op=ALU.add),
                         reads=[bpH, b_Hf[hp]], writes=[b_Hf[hp]])
                    P.op("dve", lambda e, hp=hp, c=c: e.tensor_scalar(out=Hf[hp][:], in0=Hf[hp][:], scalar1=WC[hp][:, c:c + 1], scalar2=None, op0=ALU.mult),
                         reads=[b_Hf[hp], b_WC[hp]], writes=[b_Hf[hp]])
                    P.op("dve", lambda e, hp=hp: e.tensor_copy(out=Hb[hp][:], in_=Hf[hp][:]), reads=[b_Hf[hp]], writes=[b_Hb[hp]])
            P.op("dve", lambda e: e.tensor_reduce(out=s1[:], in_=YALL[:], axis=mybir.AxisListType.X, op=ALU.add), reads=[b_YALL], writes=[b_s1])
            P.op("dve", lambda e: e.tensor_tensor(out=ysq[:], in0=YALL[:], in1=YALL[:], op=ALU.mult), reads=[b_YALL], writes=[b_ysq])
            P.op("dve", lambda e: e.tensor_reduce(out=s2_[:], in_=ysq[:], axis=mybir.AxisListType.X, op=ALU.add), reads=[b_ysq], writes=[b_s2])
            P.op("dve", lambda e: e.tensor_scalar(out=s1[:], in0=s1[:], scalar1=1.0 / 64.0, scalar2=None, op0=ALU.mult), reads=[b_s1], writes=[b_s1])
            P.op("dve", lambda e: e.scalar_tensor_tensor(out=s2_[:], in0=s2_[:], scalar=1.0 / 64.0, in1=s2_[:], op0=ALU.mult, op1=ALU.bypass) if False else
                 e.tensor_scalar(out=s2_[:], in0=s2_[:], scalar1=1.0 / 64.0, scalar2=64e-5, op0=ALU.mult, op1=ALU.add), reads=[b_s2], writes=[b_s2])
            P.op("dve", lambda e: e.tensor_tensor(out=ysq[:, :, 0], in0=s1[:], in1=s1[:], op=ALU.mult), reads=[b_s1, b_ysq], writes=[b_ysq])
            P.op("dve", lambda e: e.tensor_tensor(out=s2_[:], in0=s2_[:], in1=ysq[:, :, 0], op=ALU.subtract), reads=[b_s2, b_ysq], writes=[b_s2])
            P.op("act", lambda e: e.activation(out=s2_[:], in_=s2_[:], func=AF.Ln), reads=[b_s2], writes=[b_s2])
            P.op("act", lambda e: e.activation(out=s2_[:], in_=s2_[:], func=AF.Exp, scale=-0.5), reads=[b_s2], writes=[b_s2])
            P.op("dve", lambda e: e.scalar_tensor_tensor(out=s1[:], in0=s1[:], scalar=-1.0, in1=s2_[:], op0=ALU.mult, op1=ALU.mult), reads=[b_s1, b_s2], writes=[b_s1])
            for c in range(NCH):
                P.op("act", lambda e, c=c: e.activation(out=YN[:, c, :], in_=YALL[:, c, :], func=AF.Identity, scale=s2_[:, c:c + 1], bias=s1[:, c:c + 1]),
                     reads=[b_YALL, b_s1, b_s2], writes=[b_YN])
            pF, bpF = nbank(k)
            pFb = k.psb[k.ps.index(pF)]
            for c in range(NCH):
                for h in range(4):
                    hs = slice(h * 32, (h + 1) * 32)
                    vs = slice((h % 2) * 64, (h % 2) * 64 + 64)
                    o0 = (h // 2) * 512 + c * 32
                    P.op("pe", lambda e, pFb=pFb, hs=hs, vs=vs, o0=o0, c=c: tr(e, pFb[vs, o0:o0 + 32], YN[hs, c, :], identb[hs, hs], (hs.start, vs.start)),
                         reads=[b_YN, k.b_cstb], writes=[bpF], rt=hs.start)
            for hp in range(2):
                P.op("act", lambda e, pFb=pFb, hp=hp: e.activation(out=y1[:], in_=pFb[:, hp * 512: hp * 512 + NS], func=AF.Identity,
                                                               scale=pcol(k, "rw_ln_g", hp), bias=pcol(k, "rw_ln_b", hp)),
                     reads=[bpF, k.b_pv], writes=[b_y1])
                P.op("dve", lambda e, hp=hp: e.tensor_tensor(out=y1[:], in0=y1[:], in1=BON[hp][:], op=ALU.add), reads=[b_y1, b_BON[hp]], writes=[b_y1])
                P.op("dve", lambda e, hp=hp, s0=s0: e.tensor_tensor(out=brT[:, hp, s0:s0 + NS], in0=y1[:], in1=GT[hp][:], op=ALU.mult),
                     reads=[b_y1, b_GT[hp]], writes=[b_brT[hp][tb]])
        P.barrier()

def stage_gate(k, l, brT, b_brT):
    P = k.P; T = k.T; NTB = k.NTB
    with ExitStack() as s2:
        mg = sbt(k, s2, "mg", [128, 4, T], BF16)
        b_mg = [[Buf() for _ in range(NTB)] for _ in range(4)]
        acc = sbt(k, s2, "mg_acc", [128, 512], F32); b_acc = Buf()
        sg = [sbt(k, s2, f"mg_sg{i}", [128, 512], F32) for i in range(2)]; b_sg = [Buf(), Buf()]
        pr = [sbt(k, s2, f"mg_pr{i}", [128, 512], F32) for i in range(2)]; b_pr = [Buf(), Buf()]
        ups, b_ups = k.ups, k.b_ups
        lnb = alloc_ln(k, s2)
        og, _ = PV["gate_b"]
        i = 0
        for half in range(2):
            for fq in range(4):
                fc = half * 4 + fq
                u = fc % 2
                for b in range(4):
                    v = k.ups_d[b][l][:, fc * 128:(fc + 1) * 128].rearrange("(c p) n -> p c n", p=128)
                    P.dma("pool", lambda e, b=b, v=v, u=u: e.dma_start(out=ups[u][:, :, b, :], in_=v), writes=[b_ups[u]])
                i0 = k.wr_i
                k.wr_i = (i0 + 1) % k.NW
                slot, bw = k.wr[i0], k.b_wr[i0]
                for b in range(4):
                    c0 = COL_GATE + b * D + fc * 128
                    v = k.w_in_d[l][:, c0:c0 + 128].rearrange("(c p) n -> p c n", p=128)
                    P.dma("pool", lambda e, b=b, v=v, slot=slot: e.dma_start(out=slot[:, :, b * 128:(b + 1) * 128], in_=v), writes=[bw])
                for tb in range(NTB):
                    sl = slice(tb * 512, (tb + 1) * 512)
                    for b in range(4):
                        pg, bpg = nbank(k)
                        for c in range(KC):
                            P.op("pe", lambda e, pg=pg, c=c, b=b, sl=sl, slot=slot: e.matmul(
                                pg[:], lhsT=slot[:, c, b * 128:(b + 1) * 128], rhs=k.xT[:, c, sl], start=(c == 0), stop=(c == KC - 1)),
                                reads=[bw, k.b_xT[c][tb]], writes=[bpg])
                        pu, bpu = nbank(k)
                        for c in range(2):
                            P.op("pe", lambda e, pu=pu, c=c, b=b, sl=sl, u=u: e.matmul(
                                pu[:], lhsT=ups[u][:, c, b, :], rhs=brT[:, b * 2 + c, sl], start=(c == 0), stop=(c == 1)),
                                reads=[b_ups[u], b_brT[b * 2 + c][tb]], writes=[bpu])
                        j = i % 2
                        i += 1
                        gcol = k.pv[:, og + b * 8 + fc: og + b * 8 + fc + 1]
                        P.op("act", lambda e, pg=pg, j=j, gcol=gcol: e.activation(out=sg[j][:], in_=pg[:], func=AF.Sigmoid, bias=gcol),
                             reads=[bpg, k.b_pv], writes=[b_sg[j]])
                        if b == 0:
                            P.op("dve", lambda e, pu=pu, j=j: e.tensor_tensor(out=acc[:], in0=pu[:], in1=sg[j][:], op=ALU.mult),
                                 reads=[bpu, b_sg[j]], writes=[b_acc])
                        else:
                            P.op("dve", lambda e, pu=pu, j=j: e.tensor_tensor(out=pr[j][:], in0=pu[:], in1=sg[j][:], op=ALU.mult),
                                 reads=[bpu, b_sg[j]], writes=[b_pr[j]])
                            if b < 3:
                                P.op("dve", lambda e, j=j: e.tensor_tensor(out=acc[:], in0=acc[:], in1=pr[j][:], op=ALU.add),
                                     reads=[b_acc, b_pr[j]], writes=[b_acc])
                            else:
                                P.op("dve", lambda e, j=j, fq=fq, sl=sl: e.tensor_tensor(out=mg[:, fq, sl], in0=acc[:], in1=pr[j][:], op=ALU.add),
                                     reads=[b_acc, b_pr[j]], writes=[b_mg[fq][tb]])
            if "mg" in k.dbg_d:
                for c in range(4):
                    P.dma("sp", lambda e, c=c, half=half: e.dma_start(out=k.dbg_d["mg"][half * 4 + c], in_=mg[:, c, :]),
                          reads=[b_mg[c][tb] for tb in range(NTB)], is_output=True)
            out_proj_ln(k, l, 0, mg, b_mg, 4, k.w_out_d[l], first=(half == 0), last=(half == 1), row0=half * 512, lnbufs=lnb)
        P.barrier()


def stage_mix(k, l):
    P = k.P; T = k.T; NTB = k.NTB
    with ExitStack() as s2:
        brT = sbt(k, s2, "brT", [128, 8, T], BF16)
        b_brT = [[Buf() for _ in range(NTB)] for _ in range(8)]
        todo = k.mixers
        if "rw" in todo:
            mixer_rwkv(k, l, brT, b_brT)
        else:
            for c in (0, 1):
                P.op("dve", lambda e, c=c: e.memset(brT[:, c, :], 0.0), writes=[b_brT[c][tb] for tb in range(NTB)])
        if "cv" in todo:
            mixer_conv(k, l, brT, b_brT)
        else:
            for c in (2, 3):
                P.op("dve", lambda e, c=c: e.memset(brT[:, c, :], 0.0), writes=[b_brT[c][tb] for tb in range(NTB)])
        if "gla" in todo:
            mixer_gla(k, l, brT, b_brT)
        else:
            for c in (4, 5):
                P.op("dve", lambda e, c=c: e.memset(brT[:, c, :], 0.0), writes=[b_brT[c][tb] for tb in range(NTB)])
        if "fox" in todo:
            mixer_fox(k, l, brT, b_brT)
        else:
            for c in (6, 7):
                P.op("dve", lambda e, c=c: e.memset(brT[:, c, :], 0.0), writes=[b_brT[c][tb] for tb in range(NTB)])
        if "brT" in k.dbg_d:
            for c in range(8):
                P.dma("sp", lambda e, c=c: e.dma_start(out=k.dbg_d["brT"][c], in_=brT[:, c, :]),
                      reads=[b_brT[c][tb] for tb in range(NTB)], is_output=True)
        stage_gate(k, l, brT, b_brT)


def prep_inputs(inp, L=DEPTH):
    f = lambda a: np.ascontiguousarray(np.asarray(a, dtype=np.float32))
    shared = {
        "consts": CONSTS,
        "pvec": np.stack([pack_pvec(inp, l) for l in range(L)]),
        "w_in": f(inp["w_in"][:L]),
        "rw_wa": f(np.concatenate([np.asarray(inp["rw_w2"][:L]), np.asarray(inp["rw_a2"][:L])], axis=1)),
        "rw_g2": f(inp["rw_g2"][:L]),
        "gla_a2": f(inp["gla_a2"][:L]),
        "w_out": f(inp["w_out"][:L]),
    }
    for n in ("rw_up", "cv_up", "gla_up", "fox_up", "xa_wq", "xa_wk", "xa_wv", "xa_wo", "ffn_w1", "ffn_w3", "ffn_w2"):
        shared[n] = f(inp[n][:L])
    return shared


_CACHE = {}


def kernel(**inputs):
    x = np.asarray(inputs["x"], np.float32)
    mem = np.asarray(inputs["mem"], np.float32)
    B = x.shape[0]
    if "nc" not in _CACHE:
        _CACHE["nc"] = build()[0]
    nc = _CACHE["nc"]
    shared = prep_inputs(inputs)
    in_maps = []
    for b in range(B):
        m = dict(shared)
        m["x"] = np.ascontiguousarray(x[b])
        m["mem"] = np.ascontiguousarray(mem[b])
        in_maps.append(m)
    res = run_bass_kernel_spmd(nc, in_maps, core_ids=list(range(B)))
    return np.stack([r["out"] for r in res.results], axis=0).astype(np.float32)
```

```python
import numpy as np
from contextlib import ExitStack
import concourse.bass as bass
import concourse.mybir as mybir
from concourse.bass_utils import run_bass_kernel_spmd

F32 = mybir.dt.float32
BF16 = mybir.dt.bfloat16
AF = mybir.ActivationFunctionType
ALU = mybir.AluOpType

D = 1024
KC = 8
DEPTH = 4
SEQ = 2048
MEM = 256
DFF = 2816
DIN = 7220
ALPHA = (2.0 * DEPTH) ** 0.25
LN_EPS = 1e-5
COL_GATE = 3124

ENGS = ("pe", "act", "dve", "pool", "sp")
EPOCH = 30000
NDMA_SEM = {"sp": 16, "pool": 12, "act": 4}


class Buf:
    __slots__ = ("name", "w", "rs", "excl")

    def __init__(self, name="", excl=False):
        self.name = name
        self.w = None
        self.rs = []
        self.excl = excl


class Op:
    __slots__ = ("eng", "fn", "pos", "needs_inc", "inc", "isdma", "dsem", "dval", "waits", "vc")


class Prog:
    def __init__(self, nc):
        self.nc = nc
        self.ops = {e: [] for e in ENGS}
        self.clock = {e: {} for e in ENGS}
        self.dma_uses = {}
        self.dma_rr = {e: 0 for e in NDMA_SEM}
        self.dma_last = {}
        self.out_dmas = []
        self.rd_dmas = []

    def op(self, eng, fn, reads=(), writes=(), extra=(), rt=None):
        o = Op()
        o.eng = eng; o.fn = fn
        o.isdma = False; o.needs_inc = False; o.inc = None
        ex = list(extra)
        force = None
        if eng == "pe":
            lr = getattr(self, "last_rt", None)
            cur = rt if rt is not None else "full"
            if lr is not None and lr[1] != cur and (lr[1] != "full" and cur != "full"):
                force = lr[0]
            self._force = force
        self._record(o, reads, writes, ex)
        if eng == "pe":
            self.last_rt = (o, rt if rt is not None else "full")
        return o

    def dma(self, queue, fn, reads=(), writes=(), is_output=False):
        o = Op()
        o.eng = queue; o.fn = fn
        o.isdma = True; o.needs_inc = False; o.inc = None
        k = self.dma_rr[queue]
        self.dma_rr[queue] = (k + 1) % NDMA_SEM[queue]
        key = (queue, k)
        uses = self.dma_uses.get(key, 0)
        o.dsem = key
        o.dval = 16 * (uses + 1)
        self.dma_uses[key] = uses + 1
        prev = self.dma_last.get(key)
        self.dma_last[key] = o
        self._record(o, reads, writes, [prev] if prev is not None else [])
        if is_output:
            self.out_dmas.append(o)
        if len(reads) > 0:
            self.rd_dmas.append(o)
        return o

    def barrier(self):
        last = {e: (self.ops[e][-1] if self.ops[e] else None) for e in ("pe", "act", "dve")}
        for e in ("pe", "act", "dve", "sp"):
            ex = []
            for f, o in last.items():
                if f == e or o is None:
                    continue
                j = len(self.ops[f]) - 1
                while j >= 0 and (self.ops[f][j].isdma or self.ops[f][j].fn is None):
                    j -= 1
                if j >= 0:
                    ex.append(self.ops[f][j])
            self.op(e, None, extra=ex + list(self.rd_dmas))
        self.rd_dmas = []

    def _record(self, o, reads, writes, extra=()):
        if any(b.excl for b in reads):
            writes = list(writes) + [b for b in reads if b.excl and b not in writes]
            reads = [b for b in reads if not b.excl]
        e = o.eng
        lst = self.ops[e]
        o.pos = len(lst) + 1
        deps = []
        for b in reads:
            if b.w is not None:
                deps.append((b.w, "raw"))
        for b in writes:
            if b.w is not None:
                deps.append((b.w, "waw"))
            for r in b.rs:
                deps.append((r, "war"))
        for d in extra:
            deps.append((d, "raw"))
        clk = self.clock[e]
        waits = []
        force = getattr(self, "_force", None)
        self._force = None
        if force is not None and e == "pe" and clk.get(("self", e), 0) < force.pos:
            waits.append(force)
            force.needs_inc = True
            clk[("self", e)] = force.pos
        best = {}
        d2 = []
        for (y, kind) in deps:
            if y is o:
                continue
            if (not y.isdma) and y.eng != e:
                if y.eng not in best or best[y.eng].pos < y.pos:
                    best[y.eng] = y
            else:
                d2.append((y, kind))
        deps = d2 + [(y, "raw") for y in best.values()]
        for (y, kind) in deps:
            if y is o:
                continue
            if y.isdma:
                if clk.get(y.dsem, 0) >= y.dval:
                    continue
                waits.append(y)
                self._merge(clk, y.vc)
            elif y.eng == e:
                if e != "pe" and clk.get(("self", e), 0) < y.pos and y.fn is not None:
                    waits.append(y)
                    y.needs_inc = True
                    clk[("self", e)] = y.pos
            else:
                if clk.get(y.eng, 0) >= y.pos:
                    continue
                waits.append(y)
                y.needs_inc = True
                self._merge(clk, y.vc)
        o.waits = waits
        if o.isdma:
            vc = dict(clk)
            vc[o.dsem] = o.dval
            o.vc = vc
            clk[e] = o.pos
        else:
            clk[e] = o.pos
            o.vc = dict(clk)
        lst.append(o)
        for b in reads:
            b.rs.append(o)
        for b in writes:
            b.w = o
            b.rs = []

    @staticmethod
    def _merge(clk, vc):
        for k, v in vc.items():
            if isinstance(k, tuple) and k and k[0] == "self":
                continue
            if clk.get(k, 0) < v:
                clk[k] = v

    def finalize(self, block, stack):
        nc = self.nc
        fin = self.op("sp", None)
        clk = self.clock["sp"]
        for d in self.out_dmas:
            if clk.get(d.dsem, 0) < d.dval:
                fin.waits.append(d)
                clk[d.dsem] = d.dval
        esems = {}
        for e in ENGS:
            c = 0
            for o in self.ops[e]:
                if o.needs_inc and not o.isdma:
                    assert o.fn is not None
                    c += 1
                    o.inc = c
            nep = c // EPOCH + 1
            esems[e] = [stack.enter_context(nc.semaphore(f"s_{e}_{i}")) for i in range(nep)]
        dsems = {}
        for (q, k) in self.dma_uses:
            dsems[(q, k)] = stack.enter_context(nc.semaphore(f"d_{q}_{k}"))
        stats = {e: [len(self.ops[e]), 0] for e in ENGS}

        def emit(e, eng):
            for o in self.ops[e]:
                for y in o.waits:
                    if y.isdma:
                        eng.wait_ge(dsems[y.dsem], y.dval)
                    else:
                        ep = (y.inc - 1) // EPOCH
                        eng.wait_ge(esems[y.eng][ep], y.inc - ep * EPOCH)
                    stats[e][1] += 1
                if o.fn is None:
                    continue
                ins = o.fn(eng)
                if o.isdma:
                    ins.then_inc(dsems[o.dsem], 16)
                elif o.needs_inc:
                    ep = (o.inc - 1) // EPOCH
                    ins.then_inc(esems[e][ep], 1)

        @block.tensor
        def _(eng):
            emit("pe", eng)

        @block.scalar
        def _(eng):
            emit("act", eng)

        @block.vector
        def _(eng):
            emit("dve", eng)

        @block.gpsimd
        def _(eng):
            emit("pool", eng)

        @block.sync
        def _(eng):
            emit("sp", eng)
        return stats


def make_consts():
    c = {}
    c["ident"] = np.eye(128, dtype=np.float32)
    c["ones"] = np.ones((128, 128), np.float32)
    b64 = np.zeros((128, 128), np.float32)
    b64[:64, :64] = 1; b64[64:, 64:] = 1
    c["bones64"] = b64
    s = np.arange(128)[:, None]
    t = np.arange(128)[None, :]
    c["iu128"] = (s <= t).astype(np.float32)
    p = np.arange(128)[:, None]
    f = np.arange(128)[None, :]
    same = (p // 32) == (f // 32)
    c["bd32_iu"] = (same & ((p % 32) <= (f % 32))).astype(np.float32)
    su = (same & ((p % 32) < (f % 32))).astype(np.float32)
    sl_ = (same & ((p % 32) > (f % 32))).astype(np.float32)
    c["rwmask4"] = np.concatenate([su, sl_, su, c["bd32_iu"]], axis=1)
    names = list(c.keys())
    offs = {}
    o = 0
    for n in names:
        offs[n] = (o, c[n].shape[1])
        o += c[n].shape[1]
    arr = np.concatenate([c[n] for n in names], axis=1)
    return arr, offs


CONSTS, COFF = make_consts()
NCONST = CONSTS.shape[1]

PV = {}


def _pv_layout():
    o = 0
    for n, k in (("rw_mu", 9), ("rw_w0", 2), ("rw_a0", 2), ("rw_kk", 2), ("rw_ka", 2), ("rw_rk", 2),
                 ("rw_ln_g", 2), ("rw_ln_b", 2), ("cv_w", 62), ("cv_b", 2), ("cv_ln_g", 2),
                 ("cv_ln_b", 2), ("gla_ab", 1), ("gla_ln_g", 2), ("fox_bf", 1), ("gate_b", 32),
                 ("ln_g", 24), ("ln_b", 24)):
        PV[n] = (o, k)
        o += k
    return o


NPV = _pv_layout()


def _cols(v):
    v = np.asarray(v, np.float32).reshape(-1)
    n = v.shape[0]
    k = (n + 127) // 128
    buf = np.zeros((k * 128,), np.float32)
    buf[:n] = v
    return buf.reshape(k, 128).T


def pack_pvec(inp, l):
    out = np.zeros((128, NPV), np.float32)

    def put(name, arr):
        o, k = PV[name]
        assert arr.shape == (128, k), (name, arr.shape, k)
        out[:, o:o + k] = arr

    put("rw_mu", _cols(inp["rw_mu"][l]))
    for n in ("rw_w0", "rw_a0", "rw_kk", "rw_ka", "rw_ln_g", "rw_ln_b", "cv_b", "cv_ln_g", "cv_ln_b",
              "gla_ab", "gla_ln_g"):
        put(n, _cols(inp[n][l]))
    put("rw_rk", _cols(inp["rw_rk"][l].reshape(-1)))
    cw = np.asarray(inp["cv_w"][l], np.float32)
    cwp = cw.T.reshape(2, 128, 31).transpose(1, 0, 2).reshape(128, 62)
    put("cv_w", cwp)
    put("fox_bf", _cols(inp["fox_bf"][l]))
    put("gate_b", _cols(inp["gate_b"][l].reshape(-1)))
    put("ln_g", _cols(inp["ln_g"][l].reshape(-1)))
    put("ln_b", _cols(inp["ln_b"][l].reshape(-1)))
    return out


class K:
    pass


def build(T=SEQ, L=DEPTH, stages=("mix", "xa", "ffn"), dbg=(), mixers=("rw", "cv", "gla", "fox")):
    nc = bass.Bass("TRN2", target_bir_lowering=False)
    NTB = T // 512
    k = K()
    k.nc = nc; k.T = T; k.L = L; k.NTB = NTB; k.mixers = mixers
    dr = lambda n, s, kind="ExternalInput", dt=F32: nc.dram_tensor(n, s, dt, kind=kind).ap()
    k.x_d = dr("x", [T, D])
    if "xa" in stages:
        k.mem_d = dr("mem", [MEM, D])
    k.consts_d = dr("consts", [128, NCONST])
    k.pvec_d = dr("pvec", [L, 128, NPV])
    if "mix" in stages:
        k.w_in_d = dr("w_in", [L, D, DIN])
        k.rw_wa_d = dr("rw_wa", [L, 128, 256])
        k.rw_g2_d = dr("rw_g2", [L, 160, 256])
        k.gla_a2_d = dr("gla_a2", [L, 16, 128])
        k.ups_d = [dr(n, [L, 256, D]) for n in ("rw_up", "cv_up", "gla_up", "fox_up")]
        k.w_out_d = dr("w_out", [L, D, D])
    if "xa" in stages:
        k.xa_wq_d = dr("xa_wq", [L, D, D]); k.xa_wk_d = dr("xa_wk", [L, D, D])
        k.xa_wv_d = dr("xa_wv", [L, D, D]); k.xa_wo_d = dr("xa_wo", [L, D, D])
    if "ffn" in stages:
        k.w1_d = dr("ffn_w1", [L, D, DFF]); k.w3_d = dr("ffn_w3", [L, D, DFF]); k.w2_d = dr("ffn_w2", [L, DFF, D])
    k.out_d = dr("out", [T, D], kind="ExternalOutput")
    k.xscr_d = nc.dram_tensor("xscr", [128, KC * T], F32, kind="Internal").ap()
    k.dbg_d = {}
    for (name, shape) in dbg:
        k.dbg_d[name] = dr("dbg_" + name, list(shape), kind="ExternalOutput", dt=BF16 if name in ("brT", "mg") else F32)

    with ExitStack() as st:
        k.st = st
        sb = lambda n, s, d=F32: st.enter_context(nc.sbuf_tensor(n, s, d))
        k.b_xres = [[Buf() for _ in range(NTB)] for _ in range(KC)]
        k.xres_n = 0
        alloc_xres(k)
        k.xT = sb("xT", [128, KC, T], BF16); k.b_xT = [[Buf() for _ in range(NTB)] for _ in range(KC)]
        k.cst = sb("cst", [128, NCONST]); k.b_cst = Buf()
        k.cstb = sb("cstb", [128, NCONST], BF16); k.b_cstb = Buf()
        k.pv = sb("pv", [128, NPV]); k.b_pv = Buf()
        k.npv = sb("npv", [128, NPV]); k.b_npv = Buf()
        k.g2a = sb("rw_g2a", [128, 256], BF16); k.g2b = sb("rw_g2b", [32, 256], BF16); k.b_g2 = Buf()
        k.ups = [sb(f"mg_ups{i}", [128, 2, 4, 128], BF16) for i in range(2)]; k.b_ups = [Buf(), Buf()]
        k.wsm = sb("w_small", [128, KC, 32], BF16); k.b_wsm = Buf()
        NW = 4
        k.NW = NW
        k.wr = [sb(f"wr{i}", [128, KC, 512], BF16) for i in range(NW)]
        k.b_wr = [Buf() for _ in range(NW)]
        k.wr_i = 0
        k.ps = [st.enter_context(nc.psum_tensor(f"ps{i}", [128, 512], F32)) for i in range(8)]
        k.b_ps = [Buf(excl=True) for _ in range(8)]
        k.ps_i = 0
        k.held = set()
        k.psb = [p.bitcast(BF16) for p in k.ps]
        block = st.enter_context(nc.Block())
        P = Prog(nc)
        k.P = P

        prologue(k)
        for l in range(L):
            layer_params(k, l)
            if "mix" in stages:
                stage_mix(k, l)
            if "xa" in stages:
                stage_xa(k, l)
            if "ffn" in stages:
                stage_ffn(k, l)
        epilogue(k)
        k.stats = P.finalize(block, st)
        k.xres_stack.close()
    return nc, k


_UID = [0]


def sbt(k, stack, name, shape, dt=F32):
    _UID[0] += 1
    return stack.enter_context(k.nc.sbuf_tensor(f"{name}_{_UID[0]}", list(shape), dt))


def alloc_xres(k):
    k.xres_stack = ExitStack()
    k.xres_n += 1
    k.xres = k.xres_stack.enter_context(k.nc.sbuf_tensor(f"xres{k.xres_n}", [128, KC, k.T], F32, side="right"))


def spill_xres(k):
    P = k.P; T = k.T
    for c in range(KC):
        P.dma("sp", lambda e, c=c: e.dma_start(out=k.xscr_d[:, c * T:(c + 1) * T], in_=k.xres[:, c, :]),
              reads=[k.b_xres[c][tb] for tb in range(k.NTB)])
    P.barrier()
    k.xres_stack.close()
    k.xres = None


def reload_xres(k):
    P = k.P; T = k.T
    alloc_xres(k)
    for c in range(KC):
        P.dma("sp", lambda e, c=c: e.dma_start(out=k.xres[:, c, :], in_=k.xscr_d[:, c * T:(c + 1) * T]),
              writes=[k.b_xres[c][tb] for tb in range(k.NTB)])


def cs(k, name, bf=False):
    o, n = COFF[name]
    return (k.cstb if bf else k.cst)[:, o:o + n]


def pcol(k, name, j=0, neg=False, rows=128, r0=0):
    o, n = PV[name]
    t = k.npv if neg else k.pv
    return t[r0:r0 + rows, o + j:o + j + 1]


def mm(e, out, lhsT, rhs, pos, start=True, stop=True):
    return e.matmul(out, lhsT=lhsT, rhs=rhs, start=start, stop=stop, tile_position=pos)


def tr(e, out, in_, identity, pos):
    return e.transpose(out=out, in_=in_, identity=identity, tile_position=pos)


def nbank(k, hold=False):
    i = k.ps_i
    while i in k.held:
        i = (i + 1) % 8
    k.ps_i = (i + 1) % 8
    if hold:
        k.held.add(i)
    return k.ps[i], k.b_ps[i]


def release(k, pt):
    for i in range(8):
        if k.ps[i] is pt:
            k.held.discard(i)
            return
    raise AssertionError


def load_w(k, src, ncols, nk=KC):
    i = k.wr_i
    k.wr_i = (i + 1) % k.NW
    slot, b = k.wr[i], k.b_wr[i]
    v = src.rearrange("(c p) n -> p c n", p=128)
    k.P.dma("pool", lambda e: e.dma_start(out=slot[:, 0:nk, 0:ncols], in_=v), writes=[b])
    return slot, b


def prologue(k):
    P = k.P; T = k.T
    P.dma("sp", lambda e: e.dma_start(out=k.cst[:], in_=k.consts_d), writes=[k.b_cst])
    P.op("dve", lambda e: e.tensor_copy(out=k.cstb[:], in_=k.cst[:]), reads=[k.b_cst], writes=[k.b_cstb])
    for i in range(8):
        P.op("dve", lambda e, i=i: e.memset(k.ps[i][:], 0.0), writes=[k.b_ps[i]])
    with ExitStack() as s2:
        xin = [sbt(k, s2, f"xin{i}", [128, D], F32) for i in range(2)]
        b_xin = [Buf(), Buf()]
        ident = cs(k, "ident")
        for tt in range(T // 128):
            j = tt % 2
            P.dma("sp", lambda e, tt=tt, j=j: e.dma_start(out=xin[j][:], in_=k.x_d[tt * 128:(tt + 1) * 128, :]),
                  writes=[b_xin[j]])
            tb = tt // 4
            for g in range(2):
                pt, bp = nbank(k)
                for q in range(4):
                    c = g * 4 + q
                    P.op("pe", lambda e, pt=pt, j=j, c=c, q=q: e.transpose(
                        out=pt[:, q * 128:(q + 1) * 128], in_=xin[j][:, c * 128:(c + 1) * 128], identity=ident),
                        reads=[b_xin[j], k.b_cst], writes=[bp])
                for q in range(4):
                    c = g * 4 + q
                    dst = slice(tt * 128, (tt + 1) * 128)
                    P.op("act", lambda e, pt=pt, c=c, q=q, dst=dst: e.activation(
                        out=k.xres[:, c, dst], in_=pt[:, q * 128:(q + 1) * 128], func=AF.Copy),
                        reads=[bp], writes=[k.b_xres[c][tb]])
                    P.op("dve", lambda e, pt=pt, c=c, q=q, dst=dst: e.tensor_copy(
                        out=k.xT[:, c, dst], in_=pt[:, q * 128:(q + 1) * 128]),
                        reads=[bp], writes=[k.b_xT[c][tb]])
        P.barrier()


def layer_params(k, l):
    P = k.P
    P.dma("sp", lambda e: e.dma_start(out=k.pv[:], in_=k.pvec_d[l]), writes=[k.b_pv])
    P.op("dve", lambda e: e.tensor_scalar(out=k.npv[:], in0=k.pv[:], scalar1=-1.0, scalar2=None, op0=ALU.mult),
         reads=[k.b_pv], writes=[k.b_npv])


def epilogue(k):
    P = k.P; T = k.T
    with ExitStack() as s2:
        xo = [sbt(k, s2, f"xo{i}", [128, D], F32) for i in range(2)]
        b_xo = [Buf(), Buf()]
        ident = cs(k, "ident")
        for tt in range(T // 128):
            j = tt % 2
            tb = tt // 4
            for g in range(2):
                pt, bp = nbank(k)
                for q in range(4):
                    c = g * 4 + q
                    P.op("pe", lambda e, pt=pt, c=c, q=q, tt=tt: e.transpose(
                        out=pt[:, q * 128:(q + 1) * 128], in_=k.xres[:, c, tt * 128:(tt + 1) * 128], identity=ident),
                        reads=[k.b_xres[c][tb], k.b_cst], writes=[bp])
                eng = "act" if g == 0 else "dve"
                if g == 0:
                    P.op("act", lambda e, pt=pt, j=j: e.activation(out=xo[j][:, 0:512], in_=pt[:], func=AF.Copy),
                         reads=[bp], writes=[b_xo[j]])
                else:
                    P.op("dve", lambda e, pt=pt, j=j: e.tensor_copy(out=xo[j][:, 512:1024], in_=pt[:]),
                         reads=[bp], writes=[b_xo[j]])
            P.dma("sp", lambda e, tt=tt, j=j: e.dma_start(out=k.out_d[tt * 128:(tt + 1) * 128, :], in_=xo[j][:]),
                  reads=[b_xo[j]], is_output=True)
        P.barrier()


def ln_block(k, l, s, tb, zsq, b_zsq, st_t, b_st):
    P = k.P
    sl = slice(tb * 512, (tb + 1) * 512)
    rstd, nmr, b_r, b_n = ln_stats(k, [(k.xres[:, c, sl], k.b_xres[c][tb]) for c in range(KC)], D, LN_EPS, zsq, b_zsq, st_t, b_st)
    og, _ = PV["ln_g"]
    ob, _ = PV["ln_b"]
    for c in range(KC):
        xs = k.xres[:, c, sl]
        P.op("dve", lambda e, xs=xs: e.tensor_tensor(out=xs, in0=xs, in1=rstd, op=ALU.mult),
             reads=[k.b_xres[c][tb], b_r], writes=[k.b_xres[c][tb]])
        P.op("dve", lambda e, xs=xs: e.tensor_tensor(out=xs, in0=xs, in1=nmr, op=ALU.add),
             reads=[k.b_xres[c][tb], b_n], writes=[k.b_xres[c][tb]])
        gcol = k.pv[:, og + s * 8 + c: og + s * 8 + c + 1]
        bcol = k.pv[:, ob + s * 8 + c: ob + s * 8 + c + 1]
        P.op("act", lambda e, xs=xs, c=c, gcol=gcol, bcol=bcol: e.activation(out=k.xT[:, c, sl], in_=xs, func=AF.Identity, scale=gcol, bias=bcol),
             reads=[k.b_xres[c][tb], k.b_pv], writes=[k.b_xT[c][tb]])
        P.op("act", lambda e, xs=xs, gcol=gcol, bcol=bcol: e.activation(out=xs, in_=xs, func=AF.Identity, scale=gcol, bias=bcol),
             reads=[k.b_xres[c][tb], k.b_pv], writes=[k.b_xres[c][tb]])


def out_proj_ln(k, l, s, src, b_src, nkc, w_d, first=True, last=True, alpha_first=True, row0=0,
                lnbufs=None):
    P = k.P
    NTB = k.NTB
    halves = []
    for h in range(2):
        slot, b = load_w(k, w_d[row0:row0 + nkc * 128, h * 512:(h + 1) * 512], 512, nk=nkc)
        halves.append((slot, b))
    for tb in range(NTB):
        sl = slice(tb * 512, (tb + 1) * 512)
        for fc in range(KC):
            slot, bw = halves[fc // 4]
            co = (fc % 4) * 128
            pt, bp = nbank(k)
            for c in range(nkc):
                P.op("pe", lambda e, pt=pt, slot=slot, c=c, co=co, sl=sl: e.matmul(
                    pt[:], lhsT=slot[:, c, co:co + 128], rhs=src[:, c, sl], start=(c == 0), stop=(c == nkc - 1)),
                    reads=[bw, b_src[c][tb]], writes=[bp])
            xs = k.xres[:, fc, sl]
            if first:
                P.op("dve", lambda e, pt=pt, xs=xs: e.scalar_tensor_tensor(
                    out=xs, in0=xs, scalar=ALPHA, in1=pt[:], op0=ALU.mult, op1=ALU.add),
                    reads=[bp, k.b_xres[fc][tb]], writes=[k.b_xres[fc][tb]])
            else:
                P.op("dve", lambda e, pt=pt, xs=xs: e.tensor_tensor(out=xs, in0=xs, in1=pt[:], op=ALU.add),
                     reads=[bp, k.b_xres[fc][tb]], writes=[k.b_xres[fc][tb]])
        if last:
            ln_block(k, l, s, tb, *lnbufs)


def alloc_ln(k, s2):
    zsq = sbt(k, s2, "zsq", [128, 2, 512], F32)
    st_t = sbt(k, s2, "lnst", [128, 3, 512], F32)
    return (zsq, [Buf(), Buf()], st_t, [Buf() for _ in range(3)])


def stage_ffn(k, l):
    P = k.P; T = k.T; NTB = k.NTB
    parts = [(0, 8), (8, 16), (16, 22)]
    with ExitStack() as s2:
        g = sbt(k, s2, "ffg", [128, 8, T], BF16)
        b_g = [[Buf() for _ in range(NTB)] for _ in range(8)]
        sg = [sbt(k, s2, f"ffs{i}", [128, 512], F32) for i in range(2)]
        b_sg = [Buf(), Buf()]
        lnb = alloc_ln(k, s2)
        si = 0
        for pi, (c0, c1) in enumerate(parts):
            n = c1 - c0
            for q0 in range(c0, c1, 4):
                nq = min(4, c1 - q0)
                s1, bw1 = load_w(k, k.w1_d[l][:, q0 * 128:(q0 + nq) * 128], nq * 128)
                s3, bw3 = load_w(k, k.w3_d[l][:, q0 * 128:(q0 + nq) * 128], nq * 128)
                for q in range(nq):
                    cg = q0 + q - c0
                    for tb in range(NTB):
                        sl = slice(tb * 512, (tb + 1) * 512)
                        p1, bp1 = nbank(k)
                        p3, bp3 = nbank(k)
                        for c in range(KC):
                            P.op("pe", lambda e, p1=p1, s1=s1, c=c, q=q, sl=sl: e.matmul(
                                p1[:], lhsT=s1[:, c, q * 128:(q + 1) * 128], rhs=k.xT[:, c, sl], start=(c == 0), stop=(c == KC - 1)),
                                reads=[bw1, k.b_xT[c][tb]], writes=[bp1])
                        for c in range(KC):
                            P.op("pe", lambda e, p3=p3, s3=s3, c=c, q=q, sl=sl: e.matmul(
                                p3[:], lhsT=s3[:, c, q * 128:(q + 1) * 128], rhs=k.xT[:, c, sl], start=(c == 0), stop=(c == KC - 1)),
                                reads=[bw3, k.b_xT[c][tb]], writes=[bp3])
                        j = si % 2
                        si += 1
                        P.op("act", lambda e, p1=p1, j=j: e.activation(out=sg[j][:], in_=p1[:], func=AF.Silu),
                             reads=[bp1], writes=[b_sg[j]])
                        P.op("dve", lambda e, p3=p3, j=j, cg=cg, sl=sl: e.tensor_tensor(
                            out=g[:, cg, sl], in0=sg[j][:], in1=p3[:], op=ALU.mult),
                            reads=[bp3, b_sg[j]], writes=[b_g[cg][tb]])
            out_proj_ln(k, l, 2, g, b_g, n, k.w2_d[l], first=(pi == 0), last=(pi == len(parts) - 1),
                        row0=c0 * 128, lnbufs=lnb)
        P.barrier()


def stage_xa(k, l):
    P = k.P; T = k.T; NTB = k.NTB
    ident = cs(k, "ident")
    with ExitStack() as s2:
        sbt_ = lambda n, s, d=F32: sbt(k, s2, n, s, d)
        k.xaK = sbt_("xaK", [128, KC, MEM], BF16); k.b_xaK = Buf()
        k.xaV = sbt_("xaV", [128, 2, D], BF16); k.b_xaV = Buf()
        smem = ExitStack()
        memT = sbt(k, smem, "memT", [128, KC, MEM], BF16)
        b_memT = Buf()
        with ExitStack() as s3:
            mt = [sbt(k, s3, f"memin{i}", [128, D], F32) for i in range(2)]
            b_mt = [Buf(), Buf()]
            for m in range(2):
                P.dma("sp", lambda e, m=m: e.dma_start(out=mt[m][:], in_=k.mem_d[m * 128:(m + 1) * 128, :]), writes=[b_mt[m]])
                for g in range(2):
                    pt, bp = nbank(k)
                    for q in range(4):
                        c = g * 4 + q
                        P.op("pe", lambda e, pt=pt, m=m, c=c, q=q: e.transpose(
                            out=pt[:, q * 128:(q + 1) * 128], in_=mt[m][:, c * 128:(c + 1) * 128], identity=ident),
                            reads=[b_mt[m], k.b_cst], writes=[bp])
                    for q in range(4):
                        c = g * 4 + q
                        P.op("dve", lambda e, pt=pt, m=m, c=c, q=q: e.tensor_copy(
                            out=memT[:, c, m * 128:(m + 1) * 128], in_=pt[:, q * 128:(q + 1) * 128]),
                            reads=[bp], writes=[b_memT])
            P.barrier()
        for h in range(2):
            slot, bw = load_w(k, k.xa_wk_d[l][:, h * 512:(h + 1) * 512], 512)
            for q in range(4):
                fc = h * 4 + q
                pt, bp = nbank(k)
                for c in range(KC):
                    P.op("pe", lambda e, pt=pt, slot=slot, c=c, q=q: e.matmul(
                        pt[:, 0:MEM], lhsT=slot[:, c, q * 128:(q + 1) * 128], rhs=memT[:, c, :], start=(c == 0), stop=(c == KC - 1)),
                        reads=[bw, b_memT], writes=[bp])
                P.op("act", lambda e, pt=pt, fc=fc: e.activation(out=k.xaK[:, fc, :], in_=pt[:, 0:MEM], func=AF.Copy),
                     reads=[bp], writes=[k.b_xaK])
        for h in range(2):
            slot, bw = load_w(k, k.xa_wv_d[l][:, h * 512:(h + 1) * 512], 512)
            for m in range(2):
                pt, bp = nbank(k)
                for c in range(KC):
                    P.op("pe", lambda e, pt=pt, slot=slot, c=c, m=m: e.matmul(
                        pt[:], lhsT=memT[:, c, m * 128:(m + 1) * 128], rhs=slot[:, c, :], start=(c == 0), stop=(c == KC - 1)),
                        reads=[bw, b_memT], writes=[bp])
                P.op("act", lambda e, pt=pt, m=m, h=h: e.activation(out=k.xaV[:, m, h * 512:(h + 1) * 512], in_=pt[:], func=AF.Copy),
                     reads=[bp], writes=[k.b_xaV])
        P.barrier()
        smem.close()
        oT = sbt_("xa_oT", [128, KC, T], BF16)
        b_oT = [[Buf() for _ in range(NTB)] for _ in range(KC)]
        qT = sbt_("xa_qT", [128, 2, T], BF16)
        b_qT = [[Buf() for _ in range(NTB)] for _ in range(2)]
        PT = [sbt_(f"xa_PT{i}", [128, 2, 512], BF16) for i in range(2)]
        b_PT = [[Buf(), Buf()], [Buf(), Buf()]]
        rd0 = sbt_("xa_rd", [128, 512])
        rden = [rd0, rd0]
        b0 = Buf()
        b_rden = [b0, b0]
        onesb = cs(k, "ones", bf=True)
        scale = 1.0 / 16.0
        it = 0
        for hh in range(2):
            slot, bw = load_w(k, k.xa_wq_d[l][:, hh * 512:(hh + 1) * 512], 512)
            for h2 in range(2):
                h = hh * 2 + h2
                for tb in range(NTB):
                    sl = slice(tb * 512, (tb + 1) * 512)
                    for dc in range(2):
                        pt, bp = nbank(k)
                        co = (h2 * 2 + dc) * 128
                        for c in range(KC):
                            P.op("pe", lambda e, pt=pt, slot=slot, c=c, co=co, sl=sl: e.matmul(
                                pt[:], lhsT=slot[:, c, co:co + 128], rhs=k.xT[:, c, sl], start=(c == 0), stop=(c == KC - 1)),
                                reads=[bw, k.b_xT[c][tb]], writes=[bp])
                        P.op("act", lambda e, pt=pt, dc=dc, sl=sl: e.activation(out=qT[:, dc, sl], in_=pt[:], func=AF.Copy),
                             reads=[bp], writes=[b_qT[dc][tb]])
                    j = it % 2
                    it += 1
                    for m in range(2):
                        pt, bp = nbank(k)
                        for dc in range(2):
                            P.op("pe", lambda e, pt=pt, h=h, dc=dc, m=m, sl=sl: e.matmul(
                                pt[:], lhsT=k.xaK[:, h * 2 + dc, m * 128:(m + 1) * 128], rhs=qT[:, dc, sl],
                                start=(dc == 0), stop=(dc == 1)),
                                reads=[k.b_xaK, b_qT[dc][tb]], writes=[bp])
                        P.op("act", lambda e, pt=pt, j=j, m=m: e.activation(out=PT[j][:, m, :], in_=pt[:], func=AF.Exp, scale=scale),
                             reads=[bp], writes=[b_PT[j][m]])
                    pd, bpd = nbank(k)
                    for m in range(2):
                        P.op("pe", lambda e, pd=pd, j=j, m=m: e.matmul(pd[:], lhsT=onesb, rhs=PT[j][:, m, :], start=(m == 0), stop=(m == 1)),
                             reads=[k.b_cstb, b_PT[j][m]], writes=[bpd])
                    P.op("dve", lambda e, pd=pd, j=j: e.reciprocal(out=rden[j][:], in_=pd[:]), reads=[bpd], writes=[b_rden[j]])
                    for dc in range(2):
                        po, bpo = nbank(k)
                        for m in range(2):
                            P.op("pe", lambda e, po=po, j=j, m=m, h=h, dc=dc: e.matmul(
                                po[:], lhsT=k.xaV[:, m, h * 256 + dc * 128: h * 256 + (dc + 1) * 128], rhs=PT[j][:, m, :],
                                start=(m == 0), stop=(m == 1)),
                                reads=[k.b_xaV, b_PT[j][m]], writes=[bpo])
                        P.op("dve", lambda e, po=po, j=j, h=h, dc=dc, sl=sl: e.tensor_tensor(
                            out=oT[:, h * 2 + dc, sl], in0=po[:], in1=rden[j][:], op=ALU.mult),
                            reads=[bpo, b_rden[j]], writes=[b_oT[h * 2 + dc][tb]])
        lnb = alloc_ln(k, s2)
        out_proj_ln(k, l, 1, oT, b_oT, KC, k.xa_wo_d[l], lnbufs=lnb)
        P.barrier()


def proj_fm(k, slot, bw, col0, ncols, tb, hold=False):
    P = k.P
    sl = slice(tb * 512, (tb + 1) * 512)
    pt, bp = nbank(k, hold=hold)
    for c in range(KC):
        P.op("pe", lambda e, c=c: e.matmul(pt[0:ncols, :], lhsT=slot[:, c, col0:col0 + ncols], rhs=k.xT[:, c, sl],
                                           start=(c == 0), stop=(c == KC - 1)),
             reads=[bw, k.b_xT[c][tb]], writes=[bp])
    return pt, bp


def ln_stats(k, srcs, nfeat, eps, zsq, b_zsq, st_t, b_st):
    P = k.P
    ones = cs(k, "ones")
    S1, b1 = nbank(k)
    S2, b2 = nbank(k)
    n = len(srcs)
    for i, (ap, b) in enumerate(srcs):
        j = i % 2
        P.op("act", lambda e, j=j, ap=ap: e.activation(out=zsq[:, j, :], in_=ap, func=AF.Square), reads=[b], writes=[b_zsq[j]])
        P.op("pe", lambda e, i=i, ap=ap: e.matmul(S1[:], lhsT=ones, rhs=ap, start=(i == 0), stop=(i == n - 1)),
             reads=[b, k.b_cst], writes=[b1])
        P.op("pe", lambda e, i=i, j=j: e.matmul(S2[:], lhsT=ones, rhs=zsq[:, j, :], start=(i == 0), stop=(i == n - 1)),
             reads=[b_zsq[j], k.b_cst], writes=[b2])
    mean, var, rstd = (st_t[:, i, :] for i in range(3))
    P.op("act", lambda e: e.activation(out=mean, in_=S1[:], func=AF.Copy, scale=1.0 / nfeat), reads=[b1], writes=[b_st[0]])
    P.op("dve", lambda e: e.tensor_tensor(out=var, in0=mean, in1=mean, op=ALU.mult), reads=[b_st[0]], writes=[b_st[1]])
    P.op("dve", lambda e: e.scalar_tensor_tensor(out=var, in0=S2[:], scalar=1.0 / nfeat, in1=var, op0=ALU.mult, op1=ALU.subtract),
         reads=[b2, b_st[1]], writes=[b_st[1]])
    P.op("dve", lambda e: e.tensor_scalar(out=var, in0=var, scalar1=eps, scalar2=None, op0=ALU.add),
         reads=[b_st[1]], writes=[b_st[1]])
    P.op("act", lambda e: e.activation(out=rstd, in_=var, func=AF.Ln), reads=[b_st[1]], writes=[b_st[2]])
    P.op("act", lambda e: e.activation(out=rstd, in_=rstd, func=AF.Exp, scale=-0.5), reads=[b_st[2]], writes=[b_st[2]])
    P.op("dve", lambda e: e.scalar_tensor_tensor(out=mean, in0=mean, scalar=-1.0, in1=rstd, op0=ALU.mult, op1=ALU.mult),
         reads=[b_st[0], b_st[2]], writes=[b_st[0]])
    return rstd, mean, b_st[2], b_st[0]


def mixer_conv(k, l, brT, b_brT, s2):
    P = k.P; T = k.T; NTB = k.NTB
    if True:
        ub = sbt(k, s2, "cv_u", [128, 2, 30 + T], F32)
        b_ub = [Buf(), Buf()]
        acc = sbt(k, s2, "cv_acc", [128, 2, T], F32)
        b_acc = [Buf(), Buf()]
        lnb = alloc_ln(k, s2)
        sg = [lnb[0][:, i, :] for i in range(2)]
        b_sg = lnb[1]
        slot, bw = load_w(k, k.w_in_d[l][:, 1056:1568], 512)
        for ch in range(2):
            P.op("dve", lambda e, ch=ch: e.memset(ub[:, ch, 0:30], 0.0), writes=[b_ub[ch]])
        i = 0
        for tb in range(NTB):
            for ch in range(2):
                pa, bpa = proj_fm(k, slot, bw, ch * 128, 128, tb)
                pb, bpb = proj_fm(k, slot, bw, 256 + ch * 128, 128, tb)
                j = i % 2
                i += 1
                P.op("act", lambda e, pb=pb, j=j: e.activation(out=sg[j], in_=pb[:], func=AF.Sigmoid), reads=[bpb], writes=[b_sg[j]])
                P.op("dve", lambda e, pa=pa, j=j, ch=ch, tb=tb: e.tensor_tensor(
                    out=ub[:, ch, 30 + tb * 512: 30 + (tb + 1) * 512], in0=pa[:], in1=sg[j], op=ALU.mult),
                    reads=[bpa, b_sg[j]], writes=[b_ub[ch]])
                yield
        ow, _ = PV["cv_w"]
        for ch in range(2):
            eng = "dve"
            for kk in range(31):
                wcol = k.pv[:, ow + ch * 31 + kk: ow + ch * 31 + kk + 1]
                if kk == 0:
                    bcol = pcol(k, "cv_b", ch)
                    P.op(eng, lambda e, ch=ch, wcol=wcol, bcol=bcol: e.tensor_scalar(
                        out=acc[:, ch, :], in0=ub[:, ch, 0:T], scalar1=wcol, scalar2=bcol, op0=ALU.mult, op1=ALU.add),
                        reads=[b_ub[ch], k.b_pv], writes=[b_acc[ch]])
                else:
                    P.op(eng, lambda e, ch=ch, wcol=wcol, kk=kk: e.scalar_tensor_tensor(
                        out=acc[:, ch, :], in0=ub[:, ch, kk:kk + T], scalar=wcol, in1=acc[:, ch, :], op0=ALU.mult, op1=ALU.add),
                        reads=[b_ub[ch], k.b_pv, b_acc[ch]], writes=[b_acc[ch]])
                yield
        for tb in range(NTB):
            sl = slice(tb * 512, (tb + 1) * 512)
            rstd, nmr, b_r, b_n = ln_stats(k, [(acc[:, ch, sl], b_acc[ch]) for ch in range(2)], 256, LN_EPS, *lnb)
            for ch in range(2):
                a = acc[:, ch, sl]
                P.op("dve", lambda e, a=a, rstd=rstd: e.tensor_tensor(out=a, in0=a, in1=rstd, op=ALU.mult),
                     reads=[b_acc[ch], b_r], writes=[b_acc[ch]])
                P.op("dve", lambda e, a=a, nmr=nmr: e.tensor_tensor(out=a, in0=a, in1=nmr, op=ALU.add),
                     reads=[b_acc[ch], b_n], writes=[b_acc[ch]])
                P.op("act", lambda e, a=a, ch=ch, sl=sl: e.activation(
                    out=brT[:, 2 + ch, sl], in_=a, func=AF.Silu, scale=pcol(k, "cv_ln_g", ch), bias=pcol(k, "cv_ln_b", ch)),
                    reads=[b_acc[ch], k.b_pv], writes=[b_brT[2 + ch][tb]])
            yield
        yield


def mixer_fox(k, l, brT, b_brT, s2):
    P = k.P; T = k.T; NTB = k.NTB
    NT = T // 128
    if True:
        fq = sbt(k, s2, "fx_q", [128, 2, T], BF16); b_fq = [[Buf() for _ in range(NTB)] for _ in range(2)]
        fk = sbt(k, s2, "fx_k", [128, 2, T], BF16); b_fk = [[Buf() for _ in range(NTB)] for _ in range(2)]
        fv = sbt(k, s2, "fx_v", [128, NT, 256], BF16); b_fv = [Buf() for _ in range(NT)]
        spl = sbt(k, s2, "fx_spl", [4, T], F32); b_spl = Buf()
        sig, b_sig = spl, b_spl
        rsel = sbt(k, s2, "fx_rsel", [4, NT * 4], F32); b_rsel = Buf()
        stok = sbt(k, s2, "fx_stok", [128, NT * 4], F32); b_stok = Buf()
        sref = sbt(k, s2, "fx_sref", [128, NT * 4], F32); b_sref = Buf()
        bias = sbt(k, s2, "fx_bias", [128, NT, NT], F32); b_bias = Buf()
        PT = [sbt(k, s2, f"fx_PT{i}", [128, 512], BF16) for i in range(2)]
        b_PT = [Buf(), Buf()]
        rd = sbt(k, s2, "fx_rd", [128, 512], F32); b_rd = Buf()
        slA, bwA = load_w(k, k.w_in_d[l][:, 2352:2864], 512)
        slB, bwB = load_w(k, k.w_in_d[l][:, 2864:3124], 260)
        for tb in range(NTB):
            sl = slice(tb * 512, (tb + 1) * 512)
            for ch in range(2):
                pq, bpq = proj_fm(k, slA, bwA, ch * 128, 128, tb)
                P.op("act", lambda e, pq=pq, ch=ch, sl=sl: e.activation(out=fq[:, ch, sl], in_=pq[:], func=AF.Copy, scale=0.125),
                     reads=[bpq], writes=[b_fq[ch][tb]])
                pk, bpk = proj_fm(k, slA, bwA, 256 + ch * 128, 128, tb)
                P.op("dve", lambda e, pk=pk, ch=ch, sl=sl: e.tensor_copy(out=fk[:, ch, sl], in_=pk[:]),
                     reads=[bpk], writes=[b_fk[ch][tb]])
            pz, bpz = proj_fm(k, slB, bwB, 256, 4, tb)
            P.op("act", lambda e, pz=pz, sl=sl: e.activation(out=spl[:, sl], in_=pz[0:4, :], func=AF.Exp, scale=-1.0,
                                                             bias=pcol(k, "fox_bf", 0, neg=True, rows=4)),
                 reads=[bpz, k.b_npv], writes=[b_spl])
            P.op("act", lambda e, sl=sl: e.activation(out=spl[:, sl], in_=spl[:, sl], func=AF.Ln, bias=1.0),
                 reads=[b_spl], writes=[b_spl])
            yield
        for tt in range(NT):
            tb = tt // 4
            pt, bp = nbank(k)
            for c in range(KC):
                P.op("pe", lambda e, c=c, tt=tt, pt=pt: e.matmul(pt[:, 0:256], lhsT=k.xT[:, c, tt * 128:(tt + 1) * 128], rhs=slB[:, c, 0:256],
                                                             start=(c == 0), stop=(c == KC - 1)),
                     reads=[bwB, k.b_xT[c][tb]], writes=[bp])
            P.op("act", lambda e, pt=pt, tt=tt: e.activation(out=fv[:, tt, :], in_=pt[:, 0:256], func=AF.Copy), reads=[bp], writes=[b_fv[tt]])
            yield
        P.op("dve", lambda e: e.tensor_tensor_scan(out=sig[:], data0=spl[:], data1=spl[:], initial=0.0, op0=ALU.add, op1=ALU.max),
             reads=[b_spl], writes=[b_sig])
        ident = cs(k, "ident")
        pt, bp = nbank(k)
        for tt in range(NT):
            P.op("pe", lambda e, tt=tt: e.transpose(out=pt[:, tt * 4:(tt + 1) * 4], in_=sig[0:4, tt * 128:(tt + 1) * 128], identity=ident[0:4, 0:4]),
                 reads=[b_sig, k.b_cst], writes=[bp])
        P.op("dve", lambda e: e.tensor_copy(out=stok[:], in_=pt[:, 0:NT * 4]), reads=[bp], writes=[b_stok])
        for qs in range(NT):
            P.op("dve", lambda e, qs=qs: e.tensor_scalar(out=rsel[:, qs * 4:(qs + 1) * 4], in0=ident[0:4, 0:4], scalar1=sig[:, qs * 128:qs * 128 + 1],
                                                     scalar2=None, op0=ALU.mult),
                 reads=[b_sig, k.b_cst], writes=[b_rsel])
        pr, bpr = nbank(k)
        ones = cs(k, "ones")
        P.op("pe", lambda e: e.matmul(pr[:, 0:NT * 4], lhsT=ones[0:4, :], rhs=rsel[:], start=True, stop=True),
             reads=[b_rsel, k.b_cst], writes=[bpr])
        P.op("dve", lambda e: e.tensor_copy(out=sref[:], in_=pr[:, 0:NT * 4]), reads=[bpr], writes=[b_sref])
        stok3 = stok[:].rearrange("p (n h) -> p n h", h=4)
        onesb = cs(k, "ones", bf=True)
        iu = cs(k, "iu128", bf=True)
        it = 0
        for h in range(4):
            ch = h // 2
            pb = (h % 2) * 64
            for qs in range(NT):
                P.op("dve", lambda e, h=h, qs=qs: e.tensor_scalar(out=bias[:, qs, :], in0=stok3[:, :, h], scalar1=sref[:, qs * 4 + h:qs * 4 + h + 1],
                                                            scalar2=None, op0=ALU.subtract),
                     reads=[b_stok, b_sref], writes=[b_bias])
            for Q in range(NTB):
                nkt = 4 * (Q + 1)
                po, bpo = nbank(k, hold=True)
                pd, bpd = nbank(k, hold=True)
                for kt in range(nkt):
                    d = kt - 4 * Q
                    q0 = d * 128 if d > 0 else 0
                    j = it % 2
                    it += 1
                    ps_, bps = nbank(k, hold=True)
                    P.op("pe", lambda e, ps_=ps_, pb=pb, ch=ch, kt=kt, Q=Q, q0=q0: e.matmul(
                        ps_[:, q0:512], lhsT=fk[pb:pb + 64, ch, kt * 128:(kt + 1) * 128], rhs=fq[pb:pb + 64, ch, Q * 512 + q0:(Q + 1) * 512],
                        start=True, stop=True),
                        reads=[b_fk[ch][kt // 4], b_fq[ch][Q]], writes=[bps])
                    yield
                    for qi in range(q0 // 128, 4):
                        qs = Q * 4 + qi
                        P.op("act", lambda e, ps_=ps_, j=j, qi=qi, qs=qs, h=h, kt=kt: e.activation(
                            out=PT[j][:, qi * 128:(qi + 1) * 128], in_=ps_[:, qi * 128:(qi + 1) * 128], func=AF.Exp,
                            bias=bias[:, qs, kt:kt + 1]),
                            reads=[bps, b_bias], writes=[b_PT[j]])
                    release(k, ps_)
                    if d >= 0:
                        P.op("dve", lambda e, j=j, q0=q0: e.tensor_tensor(out=PT[j][:, q0:q0 + 128], in0=PT[j][:, q0:q0 + 128], in1=iu, op=ALU.mult),
                             reads=[b_PT[j], k.b_cstb], writes=[b_PT[j]])
                    P.op("pe", lambda e, po=po, pb=pb, j=j, q0=q0, kt=kt, h=h, nkt=nkt: e.matmul(
                        po[pb:pb + 64, q0:512], lhsT=fv[:, kt, h * 64:(h + 1) * 64], rhs=PT[j][:, q0:512], start=(kt == 0), stop=(kt == nkt - 1)),
                        reads=[b_fv[kt], b_PT[j]], writes=[bpo])
                    P.op("pe", lambda e, pd=pd, pb=pb, j=j, q0=q0, kt=kt, nkt=nkt: e.matmul(
                        pd[pb:pb + 64, q0:512], lhsT=onesb[:, 0:64], rhs=PT[j][:, q0:512], start=(kt == 0), stop=(kt == nkt - 1)),
                        reads=[k.b_cstb, b_PT[j]], writes=[bpd])
                    yield
                P.op("dve", lambda e, pd=pd, pb=pb: e.reciprocal(out=rd[pb:pb + 64, :], in_=pd[pb:pb + 64, :]), reads=[bpd], writes=[b_rd])
                P.op("dve", lambda e, po=po, pb=pb, ch=ch, Q=Q: e.tensor_tensor(
                    out=brT[pb:pb + 64, 6 + ch, Q * 512:(Q + 1) * 512], in0=po[pb:pb + 64, :], in1=rd[pb:pb + 64, :], op=ALU.mult),
                    reads=[bpo, b_rd], writes=[b_brT[6 + ch][Q]])
                release(k, po); release(k, pd)
        yield


def mixer_gla(k, l, brT, b_brT, s2):
    P = k.P; T = k.T; NTB = k.NTB
    identb = cs(k, "ident", bf=True)
    if True:
        f32t = lambda n, shp: sbt(k, s2, n, shp, F32)
        bft = lambda n, shp: sbt(k, s2, n, shp, BF16)
        a2 = f32t("gl_a2", [16, 128]); b_a2 = Buf()
        zT = f32t("gl_z", [16, 512]); b_zT = Buf()
        spl = f32t("gl_spl", [128, 512]); b_spl = Buf()
        bcs = f32t("gl_bcs", [128, 512]); b_bcs = Buf()
        Ep = f32t("gl_Ep", [128, 512]); b_Ep = Buf()
        En = f32t("gl_En", [128, 512]); b_En = Buf()
        ones32 = f32t("gl_ones", [128, 32]); b_ones = Buf()
        qd = bft("gl_qd", [128, 512]); b_qd = Buf()
        ki = bft("gl_ki", [128, 512]); b_ki = Buf()
        vb = bft("gl_vb", [128, 2, 512]); b_vb = Buf()
        sr = f32t("gl_sr", [128, 2, 512]); b_sr = Buf()
        S4f = f32t("gl_S4f", [128, 64]); b_S4f = Buf()
        S4b = bft("gl_S4b", [128, 64]); b_S4b = Buf()
        STm = [bft(f"gl_STm{i}", [128, 128]) for i in range(2)]; b_STm = [Buf(), Buf()]
        V4 = [bft(f"gl_V4{i}", [128, 64]) for i in range(2)]; b_V4 = [Buf(), Buf()]
        KT = [bft(f"gl_KT{i}", [128, 128]) for i in range(2)]; b_KT = [Buf(), Buf()]
        OALL = f32t("gl_OALL", [128, 16, 64]); b_OALL = Buf()
        osq = f32t("gl_osq", [128, 16, 64]); b_osq = Buf()
        ss = f32t("gl_ss", [128, 16]); b_ss = Buf()
        ONALL = bft("gl_ON", [128, 16, 64]); b_ON = Buf()
        slA, bwA = load_w(k, k.w_in_d[l][:, 1568:2080], 512)
        slB, bwB = load_w(k, k.w_in_d[l][:, 2080:2352], 272)
        P.dma("sp", lambda e: e.dma_start(out=a2[:], in_=k.gla_a2_d[l]), writes=[b_a2])
        P.op("dve", lambda e: e.memset(ones32[:], 1.0), writes=[b_ones])
        P.op("dve", lambda e: e.memset(S4f[:], 0.0), writes=[b_S4f])
        P.op("dve", lambda e: e.memset(S4b[:], 0.0), writes=[b_S4b])
        bd_iu = cs(k, "bd32_iu")
        it = 0
        for tb in range(NTB):
            sl = slice(tb * 512, (tb + 1) * 512)
            pz, bpz = proj_fm(k, slB, bwB, 256, 16, tb)
            P.op("act", lambda e, pz=pz: e.activation(out=zT[:], in_=pz[0:16, :], func=AF.Copy), reads=[bpz], writes=[b_zT])
            pla, bpla = nbank(k)
            P.op("pe", lambda e, pla=pla: e.matmul(pla[:], lhsT=a2[:], rhs=zT[:], start=True, stop=True), reads=[b_a2, b_zT], writes=[bpla])
            P.op("act", lambda e, pla=pla: e.activation(out=spl[:], in_=pla[:], func=AF.Exp, scale=-1.0, bias=pcol(k, "gla_ab", 0, neg=True)),
                 reads=[bpla, k.b_npv], writes=[b_spl])
            P.op("act", lambda e: e.activation(out=spl[:], in_=spl[:], func=AF.Ln, bias=1.0), reads=[b_spl], writes=[b_spl])
            for c in range(16):
                cc = slice(c * 32, (c + 1) * 32)
                P.op("dve", lambda e, cc=cc: e.tensor_tensor_scan(out=bcs[:, cc], data0=ones32[:], data1=spl[:, cc], initial=0.0,
                                                              op0=ALU.mult, op1=ALU.add),
                     reads=[b_ones, b_spl], writes=[b_bcs])
            P.op("act", lambda e: e.activation(out=Ep[:], in_=bcs[:], func=AF.Exp, scale=1.0 / 16.0), reads=[b_bcs], writes=[b_Ep])
            P.op("act", lambda e: e.activation(out=En[:], in_=bcs[:], func=AF.Exp, scale=-1.0 / 16.0), reads=[b_bcs], writes=[b_En])
            yield
            pq, bpq = proj_fm(k, slA, bwA, 0, 128, tb)
            P.op("dve", lambda e, pq=pq: e.scalar_tensor_tensor(out=qd[:], in0=pq[:], scalar=32.0 ** -0.5, in1=En[:], op0=ALU.mult, op1=ALU.mult),
                 reads=[bpq, b_En], writes=[b_qd])
            yield
            pk, bpk = proj_fm(k, slA, bwA, 128, 128, tb)
            P.op("dve", lambda e, pk=pk: e.tensor_tensor(out=ki[:], in0=pk[:], in1=Ep[:], op=ALU.mult), reads=[bpk, b_Ep], writes=[b_ki])
            yield
            for ch in range(2):
                pv, bpv = proj_fm(k, slA, bwA, 256 + ch * 128, 128, tb)
                P.op("act", lambda e, pv=pv, ch=ch: e.activation(out=vb[:, ch, :], in_=pv[:], func=AF.Copy), reads=[bpv], writes=[b_vb])
                pr, bpr = proj_fm(k, slB, bwB, ch * 128, 128, tb)
                P.op("act", lambda e, pr=pr, ch=ch: e.activation(out=sr[:, ch, :], in_=pr[:], func=AF.Silu), reads=[bpr], writes=[b_sr])
                yield
            for c in range(16):
                cc = slice(c * 32, (c + 1) * 32)
                j = it % 2
                it += 1
                pS, bpS = nbank(k, hold=True)
                P.op("dve", lambda e, pS=pS: e.memset(pS[:, 0:128], 0.0), writes=[bpS])
                yield
                for h in range(4):
                    hs = slice(h * 32, (h + 1) * 32)
                    P.op("pe", lambda e, pS=pS, hs=hs, cc=cc: mm(e, pS[hs, hs], ki[hs, cc], qd[hs, cc], (hs.start, hs.start)),
                         reads=[b_ki, b_qd], writes=[bpS], rt=hs.start)
                yield
                P.op("dve", lambda e, pS=pS, j=j: e.tensor_tensor(out=STm[j][:], in0=pS[:, 0:128], in1=bd_iu, op=ALU.mult),
                     reads=[bpS, k.b_cst], writes=[b_STm[j]])
                release(k, pS)
                pTr, bpTr = nbank(k, hold=True)
                pTb = k.psb[k.ps.index(pTr)]
                P.op("dve", lambda e, pTr=pTr: e.memset(pTr[:, 64:128], 0.0), writes=[bpTr])
                yield
                for h in range(4):
                    hs = slice(h * 32, (h + 1) * 32)
                    vs = slice((h % 2) * 64, (h % 2) * 64 + 64)
                    P.op("pe", lambda e, pTb=pTb, hs=hs, vs=vs, h=h, cc=cc: tr(e, pTb[hs, 0:64], vb[vs, h // 2, cc], identb[vs, vs], (vs.start, hs.start)),
                         reads=[b_vb, k.b_cstb], writes=[bpTr], rt=vs.start)
                for h in range(4):
                    hs = slice(h * 32, (h + 1) * 32)
                    P.op("pe", lambda e, pTb=pTb, hs=hs, cc=cc, h=h: tr(e, pTb[hs, 128 + h * 32:128 + (h + 1) * 32], ki[hs, cc], identb[hs, hs], (hs.start, hs.start)),
                         reads=[b_ki, k.b_cstb], writes=[bpTr], rt=hs.start)
                yield
                P.op("act", lambda e, pTb=pTb, j=j: e.activation(out=V4[j][:], in_=pTb[:, 0:64], func=AF.Copy), reads=[bpTr], writes=[b_V4[j]])
                P.op("dve", lambda e, pTb=pTb, j=j: e.tensor_copy(out=KT[j][:], in_=pTb[:, 128:256]),
                     reads=[bpTr], writes=[b_KT[j]])
                release(k, pTr)
                yield
                pO, bpO = nbank(k, hold=True)
                P.op("pe", lambda e, pO=pO, j=j: e.matmul(pO[:, 0:64], lhsT=STm[j][:], rhs=V4[j][:], start=True, stop=False),
                     reads=[b_STm[j], b_V4[j]], writes=[bpO])
                for h in range(4):
                    hs = slice(h * 32, (h + 1) * 32)
                    P.op("pe", lambda e, pO=pO, hs=hs, cc=cc, h=h: mm(e, pO[hs, 0:64], qd[hs, cc], S4b[hs, :], (hs.start, hs.start), start=False, stop=True),
                         reads=[b_qd, b_S4b], writes=[bpO], rt=hs.start)
                pSt, bpSt = nbank(k, hold=True)
                P.op("pe", lambda e, pSt=pSt, j=j: e.matmul(pSt[:, 0:64], lhsT=KT[j][:], rhs=V4[j][:], start=True, stop=True),
                     reads=[b_KT[j], b_V4[j]], writes=[bpSt])
                yield
                P.op("act", lambda e, pO=pO, c=c: e.activation(out=OALL[:, c, :], in_=pO[:, 0:64], func=AF.Copy), reads=[bpO], writes=[b_OALL])
                release(k, pO)
                P.op("dve", lambda e, pSt=pSt: e.tensor_tensor(out=S4f[:], in0=pSt[:, 0:64], in1=S4f[:], op=ALU.add), reads=[bpSt, b_S4f], writes=[b_S4f])
                release(k, pSt)
                yield
                P.op("dve", lambda e, c=c: e.tensor_scalar(out=S4f[:], in0=S4f[:], scalar1=En[:, c * 32 + 31:c * 32 + 32], scalar2=None, op0=ALU.mult),
                     reads=[b_S4f, b_En], writes=[b_S4f])
                yield
                P.op("act", lambda e: e.activation(out=S4b[:], in_=S4f[:], func=AF.Copy), reads=[b_S4f], writes=[b_S4b])
                yield
            P.op("dve", lambda e: e.tensor_tensor(out=osq[:], in0=OALL[:], in1=OALL[:], op=ALU.mult), reads=[b_OALL], writes=[b_osq])
            P.op("dve", lambda e: e.tensor_reduce(out=ss[:], in_=osq[:], axis=mybir.AxisListType.X, op=ALU.add), reads=[b_osq], writes=[b_ss])
            P.op("dve", lambda e: e.tensor_scalar(out=ss[:], in0=ss[:], scalar1=1.0 / 64.0, scalar2=1e-5, op0=ALU.mult, op1=ALU.add),
                 reads=[b_ss], writes=[b_ss])
            P.op("act", lambda e: e.activation(out=ss[:], in_=ss[:], func=AF.Ln), reads=[b_ss], writes=[b_ss])
            P.op("act", lambda e: e.activation(out=ss[:], in_=ss[:], func=AF.Exp, scale=-0.5), reads=[b_ss], writes=[b_ss])
            for c in range(16):
                P.op("dve", lambda e, c=c: e.tensor_scalar(out=ONALL[:, c, :], in0=OALL[:, c, :], scalar1=ss[:, c:c + 1], scalar2=None, op0=ALU.mult),
                     reads=[b_OALL, b_ss], writes=[b_ON])
            yield
            pF, bpF = nbank(k, hold=True)
            pFb = k.psb[k.ps.index(pF)]
            for c in range(16):
                if c % 4 == 0:
                    yield
                for h in range(4):
                    hs = slice(h * 32, (h + 1) * 32)
                    vs = slice((h % 2) * 64, (h % 2) * 64 + 64)
                    o0 = (h // 2) * 512 + c * 32
                    P.op("pe", lambda e, pFb=pFb, hs=hs, vs=vs, o0=o0, c=c: tr(e, pFb[vs, o0:o0 + 32], ONALL[hs, c, :], identb[hs, hs], (hs.start, vs.start)),
                         reads=[b_ON, k.b_cstb], writes=[bpF], rt=hs.start)
            for fc in range(2):
                P.op("dve", lambda e, pFb=pFb, fc=fc, sl=sl: e.scalar_tensor_tensor(
                    out=brT[:, 4 + fc, sl], in0=pFb[:, fc * 512:(fc + 1) * 512], scalar=pcol(k, "gla_ln_g", fc), in1=sr[:, fc, :],
                    op0=ALU.mult, op1=ALU.mult),
                    reads=[bpF, k.b_pv, b_sr], writes=[b_brT[4 + fc][tb]])
            release(k, pF)
            yield
        yield


def mixer_rwkv(k, l, brT, b_brT, s2):
    P = k.P; T = k.T
    NS = 256
    NSB = T // NS
    NCH = NS // 32
    identb = cs(k, "ident", bf=True)
    bones = cs(k, "bones64")
    if True:
        f32t = lambda n, shp: sbt(k, s2, n, shp, F32)
        bft = lambda n, shp: sbt(k, s2, n, shp, BF16)
        B = lambda: Buf()
        wa = f32t("rw_wa", [128, 256]); b_wa = B()
        g2a, g2b, b_g2 = k.g2a, k.g2b, k.b_g2
        omk = f32t("rw_omk", [128, 2]); b_omk = B()
        pprev = f32t("rw_pprev", [128, 9]); b_pprev = B()
        praw = [f32t(f"rw_praw{i}", [128, NS + 1]) for i in range(2)]; b_praw = [B(), B()]
        R = [f32t(f"rw_R{i}", [128, NS]) for i in range(2)]; b_R = [B(), B()]
        KX = [f32t(f"rw_KX{i}", [128, NS]) for i in range(2)]; b_KX = [B(), B()]
        V = [f32t(f"rw_V{i}", [128, NS]) for i in range(2)]; b_V = [B(), B()]
        XWA = f32t("rw_XWA", [128, NS]); b_XWA = B()
        XG0 = f32t("rw_XG0", [128, NS]); b_XG0 = B()
        XG1 = f32t("rw_XG1", [32, NS]); b_XG1 = B()
        sgx0 = bft("rw_sgx0", [128, NS]); sgx1 = bft("rw_sgx1", [32, NS]); b_sgx = B()
        EW = f32t("rw_EW", [128, NS]); b_EW = B()
        AL = f32t("rw_AL", [128, NS]); b_AL = B()
        GT = [bft(f"rw_GT{i}", [128, NS]) for i in range(2)]; b_GT = [B(), B()]
        KKN = f32t("rw_KKN", [128, NS]); b_KKN = B()
        TMP = f32t("rw_TMP", [128, NS]); b_TMP = B()
        CS = f32t("rw_CS", [128, NS]); b_CS = B()
        E1 = f32t("rw_E1", [128, NS]); b_E1 = B()
        E2 = f32t("rw_E2", [128, NS]); b_E2 = B()
        E3 = f32t("rw_E3", [128, NS]); b_E3 = B()
        WC = [f32t(f"rw_WC{i}", [128, NCH]) for i in range(2)]; b_WC = [B(), B()]
        BON = [f32t(f"rw_BON{i}", [128, NS]) for i in range(2)]; b_BON = [B(), B()]
        ones32 = f32t("rw_ones", [128, 32]); b_ones = B()
        rh = [bft(f"rw_rh{i}", [128, NS]) for i in range(2)]; b_rh = [B(), B()]
        kh = [bft(f"rw_kh{i}", [128, NS]) for i in range(2)]; b_kh = [B(), B()]
        bh = [bft(f"rw_bh{i}", [128, NS]) for i in range(2)]; b_bh = [B(), B()]
        ah = [bft(f"rw_ah{i}", [128, NS]) for i in range(2)]; b_ah = [B(), B()]
        vb = [bft(f"rw_vb{i}", [128, NS]) for i in range(2)]; b_vb = [B(), B()]
        M4 = [bft(f"rw_M4{i}", [128, 4, 128]) for i in range(2)]; b_M4 = [B(), B()]
        RKT = [bft(f"rw_RKT{i}", [128, 128]) for i in range(2)]; b_RKT = [B(), B()]
        Am = [bft(f"rw_A{i}", [128, 128]) for i in range(2)]; b_A = [B(), B()]
        ATm = [bft(f"rw_AT{i}", [128, 128]) for i in range(2)]; b_AT = [B(), B()]
        TT = [[bft(f"rw_TT{i}{q}", [128, 128]) for q in range(2)] for i in range(2)]; b_TT = [[B(), B()], [B(), B()]]
        BK = [bft(f"rw_BK{i}", [128, 4, 128]) for i in range(2)]; b_BK = [B(), B()]
        V4 = [bft(f"rw_V4{i}", [128, 64]) for i in range(2)]; b_V4 = [B(), B()]
        Xb = [bft(f"rw_Xb{i}", [128, 64]) for i in range(2)]; b_Xb = [B(), B()]
        Ub = [bft(f"rw_Ub{i}", [128, 64]) for i in range(2)]; b_Ub = [B(), B()]
        Hf = [f32t(f"rw_Hf{i}", [128, 64]) for i in range(2)]; b_Hf = [B(), B()]
        Hb = [bft(f"rw_Hb{i}", [128, 64]) for i in range(2)]; b_Hb = [B(), B()]
        YALL = f32t("rw_YALL", [128, NCH, 64]); b_YALL = B()
        ysq = f32t("rw_ysq", [128, NCH, 64]); b_ysq = B()
        s1 = f32t("rw_s1", [128, NCH]); b_s1 = B()
        s2_ = f32t("rw_s2", [128, NCH]); b_s2 = B()
        YN = bft("rw_YN", [128, NCH, 64]); b_YN = B()
        y1, b_y1 = KKN, b_KKN
        masks = cs(k, "rwmask4")
        bd_iu = cs(k, "bd32_iu")

        slA, bwA = load_w(k, k.w_in_d[l][:, 0:512], 512)
        slB, bwB = load_w(k, k.w_in_d[l][:, 512:1024], 512)
        slC, bwC = k.wsm, k.b_wsm
        vC = k.w_in_d[l][:, 1024:1056].rearrange("(c p) n -> p c n", p=128)
        P.dma("pool", lambda e: e.dma_start(out=slC[:, :, :], in_=vC), writes=[bwC])
        P.dma("sp", lambda e: e.dma_start(out=wa[:], in_=k.rw_wa_d[l]), writes=[b_wa])
        P.dma("pool", lambda e: e.dma_start(out=g2a[:], in_=k.rw_g2_d[l][0:128, :]), writes=[b_g2])
        P.dma("pool", lambda e: e.dma_start(out=g2b[:], in_=k.rw_g2_d[l][128:160, :]), writes=[b_g2])
        oka, _ = PV["rw_ka"]
        P.op("dve", lambda e: e.tensor_scalar(out=omk[:], in0=k.pv[:, oka:oka + 2], scalar1=-1.0, scalar2=1.0, op0=ALU.mult, op1=ALU.add),
             reads=[k.b_pv], writes=[b_omk])
        P.op("dve", lambda e: e.memset(pprev[:], 0.0), writes=[b_pprev])
        P.op("dve", lambda e: e.memset(ones32[:], 1.0), writes=[b_ones])
        for hp in range(2):
            P.op("dve", lambda e, hp=hp: e.memset(Hf[hp][:], 0.0), writes=[b_Hf[hp]])
            P.op("dve", lambda e, hp=hp: e.memset(Hb[hp][:], 0.0), writes=[b_Hb[hp]])
        dests = [(R[0], b_R[0], 128), (R[1], b_R[1], 128), (KX[0], b_KX[0], 128), (KX[1], b_KX[1], 128),
                 (V[0], b_V[0], 128), (V[1], b_V[1], 128), (XWA, b_XWA, 128), (XG0, b_XG0, 128), (XG1, b_XG1, 32)]
        ip = 0
        it = 0

        def proj_n(slot, bw, col0, ncols, s0):
            tb = s0 // 512
            pt, bp = nbank(k)
            for c in range(KC):
                P.op("pe", lambda e, c=c: e.matmul(pt[0:ncols, 0:NS], lhsT=slot[:, c, col0:col0 + ncols], rhs=k.xT[:, c, s0:s0 + NS],
                                                   start=(c == 0), stop=(c == KC - 1)),
                     reads=[bw, k.b_xT[c][tb]], writes=[bp])
            return pt, bp

        for sb in range(NSB):
            s0 = sb * NS
            tb = s0 // 512
            for f, (dst, bd, rows) in enumerate(dests):
                if f < 4:
                    pt, bp = proj_n(slA, bwA, f * 128, 128, s0)
                elif f < 8:
                    pt, bp = proj_n(slB, bwB, (f - 4) * 128, 128, s0)
                else:
                    pt, bp = proj_n(slC, bwC, 0, 32, s0)
                j = ip % 2
                ip += 1
                P.op("dve", lambda e, j=j, f=f, rows=rows: e.tensor_copy(out=praw[j][0:rows, 0:1], in_=pprev[0:rows, f:f + 1]),
                     reads=[b_pprev], writes=[b_praw[j]])
                P.op("act", lambda e, j=j, pt=pt, rows=rows: e.activation(out=praw[j][0:rows, 1:NS + 1], in_=pt[0:rows, 0:NS], func=AF.Copy),
                     reads=[bp], writes=[b_praw[j]])
                P.op("dve", lambda e, j=j, f=f, rows=rows: e.tensor_copy(out=pprev[0:rows, f:f + 1], in_=praw[j][0:rows, NS:NS + 1]),
                     reads=[b_praw[j]], writes=[b_pprev])
                P.op("dve", lambda e, j=j, rows=rows, dst=dst: e.tensor_tensor(out=dst[0:rows, :], in0=praw[j][0:rows, 0:NS], in1=praw[j][0:rows, 1:NS + 1], op=ALU.subtract),
                     reads=[b_praw[j]], writes=[bd])
                P.op("dve", lambda e, j=j, rows=rows, f=f, dst=dst: e.scalar_tensor_tensor(
                    out=dst[0:rows, :], in0=dst[0:rows, :], scalar=pcol(k, "rw_mu", f, rows=rows), in1=praw[j][0:rows, 1:NS + 1],
                    op0=ALU.mult, op1=ALU.add),
                    reads=[bd, b_praw[j], k.b_pv], writes=[bd])
                yield
            P.op("act", lambda e: e.activation(out=sgx0[:], in_=XG0[:], func=AF.Sigmoid), reads=[b_XG0], writes=[b_sgx])
            P.op("act", lambda e: e.activation(out=sgx1[:], in_=XG1[:], func=AF.Sigmoid), reads=[b_XG1], writes=[b_sgx])
            for hp in range(2):
                pg, bpg = nbank(k)
                P.op("pe", lambda e, pg=pg, hp=hp: e.matmul(pg[:, 0:NS], lhsT=g2a[:, hp * 128:(hp + 1) * 128], rhs=sgx0[:], start=True, stop=False),
                     reads=[b_g2, b_sgx], writes=[bpg])
                P.op("pe", lambda e, pg=pg, hp=hp: e.matmul(pg[:, 0:NS], lhsT=g2b[:, hp * 128:(hp + 1) * 128], rhs=sgx1[:], start=False, stop=True),
                     reads=[b_g2, b_sgx], writes=[bpg])
                P.op("act", lambda e, pg=pg, hp=hp: e.activation(out=GT[hp][:], in_=pg[:, 0:NS], func=AF.Copy), reads=[bpg], writes=[b_GT[hp]])
                yield
            P.op("act", lambda e: e.activation(out=XWA[0:64, :], in_=XWA[0:64, :], func=AF.Tanh), reads=[b_XWA], writes=[b_XWA])
            for hp in range(2):
                hs = slice(hp * 128, (hp + 1) * 128)
                pw, bpw = nbank(k)
                P.op("pe", lambda e, pw=pw, hs=hs: mm(e, pw[:, 0:NS], wa[0:64, hs], XWA[0:64, :], (0, 0)), reads=[b_wa, b_XWA], writes=[bpw], rt=0)
                P.op("act", lambda e, pw=pw, hp=hp: e.activation(out=EW[:], in_=pw[:, 0:NS], func=AF.Exp, scale=-1.0, bias=pcol(k, "rw_w0", hp, neg=True)),
                     reads=[bpw, k.b_npv], writes=[b_EW])
                P.op("act", lambda e: e.activation(out=EW[:], in_=EW[:], func=AF.Ln, bias=1.0), reads=[b_EW], writes=[b_EW])
                P.op("act", lambda e: e.activation(out=EW[:], in_=EW[:], func=AF.Exp, scale=-1.0, bias=-0.5), reads=[b_EW], writes=[b_EW])
                yield
                pa, bpa = nbank(k)
                P.op("pe", lambda e, pa=pa, hs=hs: mm(e, pa[:, 0:NS], wa[64:128, hs], XWA[64:128, :], (64, 0)), reads=[b_wa, b_XWA], writes=[bpa], rt=64)
                P.op("act", lambda e, pa=pa, hp=hp: e.activation(out=AL[:], in_=pa[:, 0:NS], func=AF.Sigmoid, bias=pcol(k, "rw_a0", hp)),
                     reads=[bpa, k.b_pv], writes=[b_AL])
                yield
                P.op("dve", lambda e, hp=hp: e.tensor_scalar(out=KKN[:], in0=KX[hp][:], scalar1=pcol(k, "rw_kk", hp), scalar2=None, op0=ALU.mult),
                     reads=[b_KX[hp], k.b_pv], writes=[b_KKN])
                P.op("dve", lambda e: e.tensor_tensor(out=TMP[:], in0=KKN[:], in1=KKN[:], op=ALU.mult), reads=[b_KKN], writes=[b_TMP])
                pss, bpss = nbank(k)
                P.op("pe", lambda e, pss=pss: e.matmul(pss[:, 0:NS], lhsT=bones, rhs=TMP[:], start=True, stop=True), reads=[k.b_cst, b_TMP], writes=[bpss])
                P.op("act", lambda e, pss=pss: e.activation(out=TMP[:], in_=pss[:, 0:NS], func=AF.Ln), reads=[bpss], writes=[b_TMP])
                P.op("act", lambda e: e.activation(out=TMP[:], in_=TMP[:], func=AF.Exp, scale=-0.5), reads=[b_TMP], writes=[b_TMP])
                P.op("dve", lambda e: e.tensor_tensor(out=KKN[:], in0=KKN[:], in1=TMP[:], op=ALU.mult), reads=[b_KKN, b_TMP], writes=[b_KKN])
                yield
                P.op("dve", lambda e, hp=hp: e.tensor_scalar(out=TMP[:], in0=AL[:], scalar1=pcol(k, "rw_ka", hp), scalar2=omk[:, hp:hp + 1], op0=ALU.mult, op1=ALU.add),
                     reads=[b_AL, k.b_pv, b_omk], writes=[b_TMP])
                P.op("dve", lambda e, hp=hp: e.tensor_tensor(out=KX[hp][:], in0=KX[hp][:], in1=TMP[:], op=ALU.mult), reads=[b_KX[hp], b_TMP], writes=[b_KX[hp]])
                P.op("dve", lambda e, hp=hp: e.scalar_tensor_tensor(out=TMP[:], in0=R[hp][:], scalar=pcol(k, "rw_rk", hp), in1=KX[hp][:], op0=ALU.mult, op1=ALU.mult),
                     reads=[b_R[hp], b_KX[hp], k.b_pv], writes=[b_TMP])
                pbo, bpbo = nbank(k)
                P.op("pe", lambda e, pbo=pbo: e.matmul(pbo[:, 0:NS], lhsT=bones, rhs=TMP[:], start=True, stop=True), reads=[k.b_cst, b_TMP], writes=[bpbo])
                P.op("dve", lambda e, pbo=pbo, hp=hp: e.tensor_tensor(out=BON[hp][:], in0=pbo[:, 0:NS], in1=V[hp][:], op=ALU.mult),
                     reads=[bpbo, b_V[hp]], writes=[b_BON[hp]])
                yield
                P.op("act", lambda e, hp=hp: e.activation(out=vb[hp][:], in_=V[hp][:], func=AF.Copy), reads=[b_V[hp]], writes=[b_vb[hp]])
                for c in range(NCH):
                    cc = slice(c * 32, (c + 1) * 32)
                    P.op("dve", lambda e, cc=cc: e.tensor_tensor_scan(out=CS[:, cc], data0=ones32[:], data1=EW[:, cc], initial=0.0, op0=ALU.mult, op1=ALU.add),
                         reads=[b_ones, b_EW], writes=[b_CS])
                P.op("act", lambda e: e.activation(out=E1[:], in_=CS[:], func=AF.Exp), reads=[b_CS], writes=[b_E1])
                P.op("act", lambda e: e.activation(out=E2[:], in_=CS[:], func=AF.Exp, scale=-1.0), reads=[b_CS], writes=[b_E2])
                P.op("dve", lambda e: e.tensor_tensor(out=TMP[:], in0=EW[:], in1=CS[:], op=ALU.subtract), reads=[b_EW, b_CS, b_TMP], writes=[b_TMP])
                P.op("act", lambda e: e.activation(out=E3[:], in_=TMP[:], func=AF.Exp), reads=[b_TMP], writes=[b_E3])
                yield
                E2v = E2[:].rearrange("p (c t) -> p c t", t=32)
                P.op("dve", lambda e, hp=hp, E2v=E2v: e.tensor_copy(out=WC[hp][:], in_=E2v[:, :, 31]), reads=[b_E2], writes=[b_WC[hp]])
                P.op("dve", lambda e, hp=hp: e.tensor_tensor(out=rh[hp][:], in0=R[hp][:], in1=E2[:], op=ALU.mult), reads=[b_R[hp], b_E2], writes=[b_rh[hp]])
                P.op("dve", lambda e, hp=hp: e.tensor_tensor(out=kh[hp][:], in0=KX[hp][:], in1=E1[:], op=ALU.mult), reads=[b_KX[hp], b_E1], writes=[b_kh[hp]])
                P.op("dve", lambda e: e.tensor_tensor(out=TMP[:], in0=KKN[:], in1=AL[:], op=ALU.mult), reads=[b_KKN, b_AL], writes=[b_TMP])
                P.op("dve", lambda e, hp=hp: e.tensor_tensor(out=bh[hp][:], in0=TMP[:], in1=E1[:], op=ALU.mult), reads=[b_TMP, b_E1], writes=[b_bh[hp]])
                P.op("dve", lambda e, hp=hp: e.scalar_tensor_tensor(out=ah[hp][:], in0=KKN[:], scalar=-1.0, in1=E3[:], op0=ALU.mult, op1=ALU.mult),
                     reads=[b_KKN, b_E3], writes=[b_ah[hp]])
                yield
            def A_gen(c):
                cc = slice(c * 32, (c + 1) * 32)
                j = c % 2
                pX_, bpX_ = nbank(k, hold=True)
                pY_, bpY_ = nbank(k, hold=True)
                P.op("dve", lambda e: e.memset(pX_[:], 0.0), writes=[bpX_])
                P.op("dve", lambda e: e.memset(pY_[:, 0:128], 0.0), writes=[bpY_])
                yield
                for h in range(4):
                    hp = h // 2
                    ks = slice((h % 2) * 64, (h % 2) * 64 + 64)
                    hs = slice(h * 32, (h + 1) * 32)
                    pos = (ks.start, hs.start)
                    for mi, (lt, blt, rt_, brt) in enumerate(((bh, b_bh, ah, b_ah), (ah, b_ah, bh, b_bh), (kh, b_kh, ah, b_ah), (bh, b_bh, rh, b_rh))):
                        P.op("pe", lambda e, hs=hs, ks=ks, hp=hp, mi=mi, lt=lt, rt_=rt_, pos=pos, h=h: mm(
                            e, pX_[hs, mi * 128 + h * 32: mi * 128 + (h + 1) * 32], lt[hp][ks, cc], rt_[hp][ks, cc], pos),
                            reads=[blt[hp], brt[hp]], writes=[bpX_], rt=pos[0])
                    P.op("pe", lambda e, hs=hs, ks=ks, hp=hp, pos=pos, h=h: mm(
                        e, pY_[hs, h * 32:(h + 1) * 32], kh[hp][ks, cc], rh[hp][ks, cc], pos),
                        reads=[b_kh[hp], b_rh[hp]], writes=[bpY_], rt=pos[0])
                yield
                P.op("dve", lambda e: e.tensor_tensor(out=M4[j][:].rearrange("p a b -> p (a b)"), in0=pX_[:], in1=masks, op=ALU.mult),
                     reads=[bpX_, k.b_cst], writes=[b_M4[j]])
                P.op("dve", lambda e: e.tensor_tensor(out=RKT[j][:], in0=pY_[:, 0:128], in1=bd_iu, op=ALU.mult),
                     reads=[bpY_, k.b_cst], writes=[b_RKT[j]])
                release(k, pX_); release(k, pY_)
                LT = M4[j][:, 0, :]; Lm = M4[j][:, 1, :]
                TTj = TT[j]
                bTTj = b_TT[j]
                P.op("dve", lambda e: e.tensor_tensor(out=TTj[0][:], in0=LT, in1=identb, op=ALU.add), reads=[b_M4[j], k.b_cstb], writes=[bTTj[0]])
                yield
                A_prev, bA_prev, AT_prev, bAT_prev = Lm, b_M4[j], LT, b_M4[j]
                ti = 0
                for kq in range(1, 5):
                    an = kq % 2
                    pA, bpA = nbank(k, hold=True)
                    P.op("pe", lambda e, pA=pA, AT_prev=AT_prev, A_prev=A_prev: e.matmul(pA[:, 0:128], lhsT=AT_prev, rhs=A_prev, start=True, stop=True),
                         reads=[bA_prev, bAT_prev], writes=[bpA])
                    if kq < 4:
                        P.op("pe", lambda e, pA=pA, AT_prev=AT_prev, A_prev=A_prev: e.matmul(pA[:, 128:256], lhsT=A_prev, rhs=AT_prev, start=True, stop=True),
                             reads=[bA_prev, bAT_prev], writes=[bpA])
                    yield
                    P.op("act", lambda e, pA=pA, an=an: e.activation(out=Am[an][:], in_=pA[:, 0:128], func=AF.Copy), reads=[bpA], writes=[b_A[an]])
                    if kq < 4:
                        P.op("act", lambda e, pA=pA, an=an: e.activation(out=ATm[an][:], in_=pA[:, 128:256], func=AF.Copy), reads=[bpA], writes=[b_AT[an]])
                    release(k, pA)
                    yield
                    pT, bpT = nbank(k, hold=True)
                    P.op("pe", lambda e, pT=pT, an=an, ti=ti: e.matmul(pT[:, 0:128], lhsT=Am[an][:], rhs=TTj[ti][:], start=True, stop=True),
                         reads=[b_A[an], bTTj[ti]], writes=[bpT])
                    yield
                    P.op("dve", lambda e, pT=pT, ti=ti: e.tensor_tensor(out=TTj[1 - ti][:], in0=pT[:, 0:128], in1=TTj[ti][:], op=ALU.add),
                         reads=[bpT, bTTj[ti]], writes=[bTTj[1 - ti]])
                    release(k, pT)
                    ti = 1 - ti
                    A_prev, bA_prev, AT_prev, bAT_prev = Am[an][:], b_A[an], ATm[an][:], b_AT[an]
                    yield
                assert ti == 0
                pTr, bpTr = nbank(k, hold=True)
                pTb = k.psb[k.ps.index(pTr)]
                P.op("dve", lambda e: e.memset(pTr[:, 0:256], 0.0), writes=[bpTr])
                yield
                for h in range(4):
                    hp = h // 2
                    ks = slice((h % 2) * 64, (h % 2) * 64 + 64)
                    hs = slice(h * 32, (h + 1) * 32)
                    pos = (ks.start, hs.start)
                    P.op("pe", lambda e, hs=hs, ks=ks, hp=hp, pos=pos: tr(e, pTb[hs, hp * 128 + ks.start: hp * 128 + ks.start + 64], bh[hp][ks, cc], identb[ks, ks], pos),
                         reads=[b_bh[hp], k.b_cstb], writes=[bpTr], rt=pos[0])
                    P.op("pe", lambda e, hs=hs, ks=ks, hp=hp, pos=pos: tr(e, pTb[hs, 256 + hp * 128 + ks.start: 256 + hp * 128 + ks.start + 64], kh[hp][ks, cc], identb[ks, ks], pos),
                         reads=[b_kh[hp], k.b_cstb], writes=[bpTr], rt=pos[0])
                    P.op("pe", lambda e, hs=hs, ks=ks, hp=hp, pos=pos: tr(e, pTb[hs, 512:576], vb[hp][ks, cc], identb[ks, ks], pos),
                         reads=[b_vb[hp], k.b_cstb], writes=[bpTr], rt=pos[0])
                yield
                P.op("dve", lambda e: e.tensor_copy(out=BK[j][:].rearrange("p a b -> p (a b)"), in_=pTb[:, 0:512]),
                     reads=[bpTr], writes=[b_BK[j]])
                P.op("act", lambda e: e.activation(out=V4[j][:], in_=pTb[:, 512:576], func=AF.Copy), reads=[bpTr], writes=[b_V4[j]])
                release(k, pTr)
                yield

            def B_gen(c):
                cc = slice(c * 32, (c + 1) * 32)
                j = c % 2
                AKT = M4[j][:, 2, :]; RBT = M4[j][:, 3, :]
                TTf, bTTf = TT[j][0], b_TT[j][0]
                pX, bpX = nbank(k, hold=True)
                P.op("pe", lambda e: e.matmul(pX[:, 0:64], lhsT=AKT, rhs=V4[j][:], start=True, stop=False),
                     reads=[b_M4[j], b_V4[j]], writes=[bpX])
                for h in range(4):
                    hp = h // 2
                    ks = slice((h % 2) * 64, (h % 2) * 64 + 64)
                    hs = slice(h * 32, (h + 1) * 32)
                    P.op("pe", lambda e, hs=hs, ks=ks, hp=hp: mm(e, pX[hs, 0:64], ah[hp][ks, cc], Hb[hp][ks, :], (ks.start, hs.start), start=False, stop=True),
                         reads=[b_ah[hp], b_Hb[hp]], writes=[bpX], rt=ks.start)
                yield
                P.op("act", lambda e: e.activation(out=Xb[j][:], in_=pX[:, 0:64], func=AF.Copy), reads=[bpX], writes=[b_Xb[j]])
                release(k, pX)
                yield
                pU, bpU = nbank(k, hold=True)
                P.op("pe", lambda e: e.matmul(pU[:, 0:64], lhsT=TTf[:], rhs=Xb[j][:], start=True, stop=True),
                     reads=[bTTf, b_Xb[j]], writes=[bpU])
                yield
                P.op("act", lambda e: e.activation(out=Ub[j][:], in_=pU[:, 0:64], func=AF.Copy), reads=[bpU], writes=[b_Ub[j]])
                release(k, pU)
                yield
                pHs = []
                for hp in range(2):
                    pH, bpH = nbank(k, hold=True)
                    pHs.append((pH, bpH))
                    P.op("pe", lambda e, pH=pH, hp=hp: e.matmul(pH[:, 0:64], lhsT=BK[j][:, hp, :], rhs=Ub[j][:], start=True, stop=False),
                         reads=[b_BK[j], b_Ub[j]], writes=[bpH])
                    P.op("pe", lambda e, pH=pH, hp=hp: e.matmul(pH[:, 0:64], lhsT=BK[j][:, 2 + hp, :], rhs=V4[j][:], start=False, stop=True),
                         reads=[b_BK[j], b_V4[j]], writes=[bpH])
                pY, bpY = nbank(k, hold=True)
                P.op("pe", lambda e: e.matmul(pY[:, 0:64], lhsT=RBT, rhs=Ub[j][:], start=True, stop=False),
                     reads=[b_M4[j], b_Ub[j]], writes=[bpY])
                P.op("pe", lambda e: e.matmul(pY[:, 0:64], lhsT=RKT[j][:], rhs=V4[j][:], start=False, stop=False),
                     reads=[b_RKT[j], b_V4[j]], writes=[bpY])
                for h in range(4):
                    hp = h // 2
                    ks = slice((h % 2) * 64, (h % 2) * 64 + 64)
                    hs = slice(h * 32, (h + 1) * 32)
                    P.op("pe", lambda e, hs=hs, ks=ks, hp=hp: mm(e, pY[hs, 0:64], rh[hp][ks, cc], Hb[hp][ks, :], (ks.start, hs.start), start=False, stop=True),
                         reads=[b_rh[hp], b_Hb[hp]], writes=[bpY], rt=ks.start)
                yield
                for hp in range(2):
                    pH, bpH = pHs[hp]
                    P.op("dve", lambda e, pH=pH, hp=hp: e.tensor_tensor(out=Hf[hp][:], in0=pH[:, 0:64], in1=Hf[hp][:], op=ALU.add),
                         reads=[bpH, b_Hf[hp]], writes=[b_Hf[hp]])
                    release(k, pH)
                P.op("act", lambda e: e.activation(out=YALL[:, c, :], in_=pY[:, 0:64], func=AF.Copy), reads=[bpY], writes=[b_YALL])
                release(k, pY)
                yield
                for hp in range(2):
                    P.op("dve", lambda e, hp=hp: e.tensor_scalar(out=Hf[hp][:], in0=Hf[hp][:], scalar1=WC[hp][:, c:c + 1], scalar2=None, op0=ALU.mult),
                         reads=[b_Hf[hp], b_WC[hp]], writes=[b_Hf[hp]])
                yield
                for hp in range(2):
                    P.op("act", lambda e, hp=hp: e.activation(out=Hb[hp][:], in_=Hf[hp][:], func=AF.Copy), reads=[b_Hf[hp]], writes=[b_Hb[hp]])
                yield

            for _ in A_gen(0):
                yield
            for c in range(NCH):
                gens = [B_gen(c)]
                if c + 1 < NCH:
                    gens.append(A_gen(c + 1))
                while gens:
                    for g_ in list(gens):
                        try:
                            next(g_)
                        except StopIteration:
                            gens.remove(g_)
                    yield
            P.op("dve", lambda e: e.tensor_reduce(out=s1[:], in_=YALL[:], axis=mybir.AxisListType.X, op=ALU.add), reads=[b_YALL], writes=[b_s1])
            P.op("dve", lambda e: e.tensor_tensor(out=ysq[:], in0=YALL[:], in1=YALL[:], op=ALU.mult), reads=[b_YALL], writes=[b_ysq])
            P.op("dve", lambda e: e.tensor_reduce(out=s2_[:], in_=ysq[:], axis=mybir.AxisListType.X, op=ALU.add), reads=[b_ysq], writes=[b_s2])
            P.op("dve", lambda e: e.tensor_scalar(out=s1[:], in0=s1[:], scalar1=1.0 / 64.0, scalar2=None, op0=ALU.mult), reads=[b_s1], writes=[b_s1])
            P.op("dve", lambda e: e.scalar_tensor_tensor(out=s2_[:], in0=s2_[:], scalar=1.0 / 64.0, in1=s2_[:], op0=ALU.mult, op1=ALU.bypass) if False else
                 e.tensor_scalar(out=s2_[:], in0=s2_[:], scalar1=1.0 / 64.0, scalar2=64e-5, op0=ALU.mult, op1=ALU.add), reads=[b_s2], writes=[b_s2])
            P.op("dve", lambda e: e.tensor_tensor(out=ysq[:, :, 0], in0=s1[:], in1=s1[:], op=ALU.mult), reads=[b_s1, b_ysq], writes=[b_ysq])
            P.op("dve", lambda e: e.tensor_tensor(out=s2_[:], in0=s2_[:], in1=ysq[:, :, 0], op=ALU.subtract), reads=[b_s2, b_ysq], writes=[b_s2])
            P.op("act", lambda e: e.activation(out=s2_[:], in_=s2_[:], func=AF.Ln), reads=[b_s2], writes=[b_s2])
            P.op("act", lambda e: e.activation(out=s2_[:], in_=s2_[:], func=AF.Exp, scale=-0.5), reads=[b_s2], writes=[b_s2])
            P.op("dve", lambda e: e.scalar_tensor_tensor(out=s1[:], in0=s1[:], scalar=-1.0, in1=s2_[:], op0=ALU.mult, op1=ALU.mult), reads=[b_s1, b_s2], writes=[b_s1])
            for c in range(NCH):
                P.op("act", lambda e, c=c: e.activation(out=YN[:, c, :], in_=YALL[:, c, :], func=AF.Identity, scale=s2_[:, c:c + 1], bias=s1[:, c:c + 1]),
                     reads=[b_YALL, b_s1, b_s2], writes=[b_YN])
            yield
            pF, bpF = nbank(k, hold=True)
            pFb = k.psb[k.ps.index(pF)]
            for c in range(NCH):
                if c % 4 == 0:
                    yield
                for h in range(4):
                    hs = slice(h * 32, (h + 1) * 32)
                    vs = slice((h % 2) * 64, (h % 2) * 64 + 64)
                    o0 = (h // 2) * 512 + c * 32
                    P.op("pe", lambda e, pFb=pFb, hs=hs, vs=vs, o0=o0, c=c: tr(e, pFb[vs, o0:o0 + 32], YN[hs, c, :], identb[hs, hs], (hs.start, vs.start)),
                         reads=[b_YN, k.b_cstb], writes=[bpF], rt=hs.start)
            for hp in range(2):
                P.op("act", lambda e, pFb=pFb, hp=hp: e.activation(out=y1[:], in_=pFb[:, hp * 512: hp * 512 + NS], func=AF.Identity,
                                                               scale=pcol(k, "rw_ln_g", hp), bias=pcol(k, "rw_ln_b", hp)),
                     reads=[bpF, k.b_pv], writes=[b_y1])
                P.op("dve", lambda e, hp=hp: e.tensor_tensor(out=y1[:], in0=y1[:], in1=BON[hp][:], op=ALU.add), reads=[b_y1, b_BON[hp]], writes=[b_y1])
                P.op("dve", lambda e, hp=hp, s0=s0: e.tensor_tensor(out=brT[:, hp, s0:s0 + NS], in0=y1[:], in1=GT[hp][:], op=ALU.mult),
                     reads=[b_y1, b_GT[hp]], writes=[b_brT[hp][tb]])
            release(k, pF)
            yield
        yield

def stage_gate(k, l, brT, b_brT):
    P = k.P; T = k.T; NTB = k.NTB
    with ExitStack() as s2:
        mg = sbt(k, s2, "mg", [128, 4, T], BF16)
        b_mg = [[Buf() for _ in range(NTB)] for _ in range(4)]
        acc = sbt(k, s2, "mg_acc", [128, 512], F32); b_acc = Buf()
        sg = [sbt(k, s2, f"mg_sg{i}", [128, 512], F32) for i in range(2)]; b_sg = [Buf(), Buf()]
        pr0 = sbt(k, s2, "mg_pr", [128, 512], F32); pr = [pr0, pr0]; bpr0 = Buf(); b_pr = [bpr0, bpr0]
        ups, b_ups = k.ups, k.b_ups
        lnb = alloc_ln(k, s2)
        og, _ = PV["gate_b"]
        i = 0
        for half in range(2):
            for fq in range(4):
                fc = half * 4 + fq
                u = fc % 2
                for b in range(4):
                    v = k.ups_d[b][l][:, fc * 128:(fc + 1) * 128].rearrange("(c p) n -> p c n", p=128)
                    P.dma("pool", lambda e, b=b, v=v, u=u: e.dma_start(out=ups[u][:, :, b, :], in_=v), writes=[b_ups[u]])
                i0 = k.wr_i
                k.wr_i = (i0 + 1) % k.NW
                slot, bw = k.wr[i0], k.b_wr[i0]
                for b in range(4):
                    c0 = COL_GATE + b * D + fc * 128
                    v = k.w_in_d[l][:, c0:c0 + 128].rearrange("(c p) n -> p c n", p=128)
                    P.dma("pool", lambda e, b=b, v=v, slot=slot: e.dma_start(out=slot[:, :, b * 128:(b + 1) * 128], in_=v), writes=[bw])
                for tb in range(NTB):
                    sl = slice(tb * 512, (tb + 1) * 512)
                    for b in range(4):
                        pg, bpg = nbank(k)
                        for c in range(KC):
                            P.op("pe", lambda e, pg=pg, c=c, b=b, sl=sl, slot=slot: e.matmul(
                                pg[:], lhsT=slot[:, c, b * 128:(b + 1) * 128], rhs=k.xT[:, c, sl], start=(c == 0), stop=(c == KC - 1)),
                                reads=[bw, k.b_xT[c][tb]], writes=[bpg])
                        pu, bpu = nbank(k)
                        for c in range(2):
                            P.op("pe", lambda e, pu=pu, c=c, b=b, sl=sl, u=u: e.matmul(
                                pu[:], lhsT=ups[u][:, c, b, :], rhs=brT[:, b * 2 + c, sl], start=(c == 0), stop=(c == 1)),
                                reads=[b_ups[u], b_brT[b * 2 + c][tb]], writes=[bpu])
                        j = i % 2
                        i += 1
                        gcol = k.pv[:, og + b * 8 + fc: og + b * 8 + fc + 1]
                        P.op("act", lambda e, pg=pg, j=j, gcol=gcol: e.activation(out=sg[j][:], in_=pg[:], func=AF.Sigmoid, bias=gcol),
                             reads=[bpg, k.b_pv], writes=[b_sg[j]])
                        if b == 0:
                            P.op("dve", lambda e, pu=pu, j=j: e.tensor_tensor(out=acc[:], in0=pu[:], in1=sg[j][:], op=ALU.mult),
                                 reads=[bpu, b_sg[j]], writes=[b_acc])
                        else:
                            P.op("dve", lambda e, pu=pu, j=j: e.tensor_tensor(out=pr[j][:], in0=pu[:], in1=sg[j][:], op=ALU.mult),
                                 reads=[bpu, b_sg[j]], writes=[b_pr[j]])
                            if b < 3:
                                P.op("dve", lambda e, j=j: e.tensor_tensor(out=acc[:], in0=acc[:], in1=pr[j][:], op=ALU.add),
                                     reads=[b_acc, b_pr[j]], writes=[b_acc])
                            else:
                                P.op("dve", lambda e, j=j, fq=fq, sl=sl: e.tensor_tensor(out=mg[:, fq, sl], in0=acc[:], in1=pr[j][:], op=ALU.add),
                                     reads=[b_acc, b_pr[j]], writes=[b_mg[fq][tb]])
            if "mg" in k.dbg_d:
                for c in range(4):
                    P.dma("sp", lambda e, c=c, half=half: e.dma_start(out=k.dbg_d["mg"][half * 4 + c], in_=mg[:, c, :]),
                          reads=[b_mg[c][tb] for tb in range(NTB)], is_output=True)
            out_proj_ln(k, l, 0, mg, b_mg, 4, k.w_out_d[l], first=(half == 0), last=(half == 1), row0=half * 512, lnbufs=lnb)
        P.barrier()


def run_concurrent(gens):
    gens = list(gens)
    while gens:
        for g in list(gens):
            try:
                next(g)
            except StopIteration:
                gens.remove(g)


def stage_mix(k, l):
    P = k.P; T = k.T; NTB = k.NTB
    spill_xres(k)
    with ExitStack() as s2:
        brT = sbt(k, s2, "brT", [128, 8, T], BF16)
        b_brT = [[Buf() for _ in range(NTB)] for _ in range(8)]
        todo = k.mixers
        for nm, cs_ in (("rw", (0, 1)), ("cv", (2, 3)), ("gla", (4, 5)), ("fox", (6, 7))):
            if nm not in todo:
                for c in cs_:
                    P.op("dve", lambda e, c=c: e.memset(brT[:, c, :], 0.0), writes=[b_brT[c][tb] for tb in range(NTB)])
        fns = {"rw": mixer_rwkv, "gla": mixer_gla, "fox": mixer_fox, "cv": mixer_conv}
        for group in (("rw", "gla"), ("fox", "cv")):
            act = [n for n in group if n in todo]
            if not act:
                continue
            with ExitStack() as s3:
                run_concurrent([fns[n](k, l, brT, b_brT, s3) for n in act])
                P.barrier()
        if "brT" in k.dbg_d:
            for c in range(8):
                P.dma("sp", lambda e, c=c: e.dma_start(out=k.dbg_d["brT"][c], in_=brT[:, c, :]),
                      reads=[b_brT[c][tb] for tb in range(NTB)], is_output=True)
        reload_xres(k)
        stage_gate(k, l, brT, b_brT)


def prep_inputs(inp, L=DEPTH):
    f = lambda a: np.ascontiguousarray(np.asarray(a, dtype=np.float32))
    shared = {
        "consts": CONSTS,
        "pvec": np.stack([pack_pvec(inp, l) for l in range(L)]),
        "w_in": f(inp["w_in"][:L]),
        "rw_wa": f(np.concatenate([np.asarray(inp["rw_w2"][:L]), np.asarray(inp["rw_a2"][:L])], axis=1)),
        "rw_g2": f(inp["rw_g2"][:L]),
        "gla_a2": f(inp["gla_a2"][:L]),
        "w_out": f(inp["w_out"][:L]),
    }
    for n in ("rw_up", "cv_up", "gla_up", "fox_up", "xa_wq", "xa_wk", "xa_wv", "xa_wo", "ffn_w1", "ffn_w3", "ffn_w2"):
        shared[n] = f(inp[n][:L])
    return shared


_CACHE = {}


def kernel(**inputs):
    x = np.asarray(inputs["x"], np.float32)
    mem = np.asarray(inputs["mem"], np.float32)
    B = x.shape[0]
    if "nc" not in _CACHE:
        _CACHE["nc"] = build()[0]
    nc = _CACHE["nc"]
    shared = prep_inputs(inputs)
    in_maps = []
    for b in range(B):
        m = dict(shared)
        m["x"] = np.ascontiguousarray(x[b])
        m["mem"] = np.ascontiguousarray(mem[b])
        in_maps.append(m)
    res = run_bass_kernel_spmd(nc, in_maps, core_ids=list(range(B)))
    return np.stack([r["out"] for r in res.results], axis=0).astype(np.float32)
```

```python
import numpy as np
from contextlib import ExitStack
import concourse.bass as bass
import concourse.mybir as mybir
from concourse.bass_utils import run_bass_kernel_spmd

F32 = mybir.dt.float32
BF16 = mybir.dt.bfloat16
AF = mybir.ActivationFunctionType
ALU = mybir.AluOpType

D = 1024
KC = 8
DEPTH = 4
SEQ = 2048
MEM = 256
DFF = 2816
DIN = 7220
ALPHA = (2.0 * DEPTH) ** 0.25
LN_EPS = 1e-5
COL_GATE = 3124

ENGS = ("pe", "act", "dve", "pool", "sp")
EPOCH = 30000
NDMA_SEM = {"sp": 16, "pool": 12, "act": 4}


class Buf:
    __slots__ = ("name", "w", "rs", "excl")

    def __init__(self, name="", excl=False):
        self.name = name
        self.w = None
        self.rs = []
        self.excl = excl


class Op:
    __slots__ = ("eng", "fn", "pos", "needs_inc", "inc", "isdma", "dsem", "dval", "waits", "vc")


class Prog:
    def __init__(self, nc):
        self.nc = nc
        self.ops = {e: [] for e in ENGS}
        self.clock = {e: {} for e in ENGS}
        self.dma_uses = {}
        self.dma_rr = {e: 0 for e in NDMA_SEM}
        self.dma_last = {}
        self.out_dmas = []
        self.rd_dmas = []

    def op(self, eng, fn, reads=(), writes=(), extra=(), rt=None):
        o = Op()
        o.eng = eng; o.fn = fn
        o.isdma = False; o.needs_inc = False; o.inc = None
        ex = list(extra)
        force = None
        if eng == "pe":
            lr = getattr(self, "last_rt", None)
            cur = rt if rt is not None else "full"
            if lr is not None and lr[1] != cur and (lr[1] != "full" and cur != "full"):
                force = lr[0]
            self._force = force
        self._record(o, reads, writes, ex)
        if eng == "pe":
            self.last_rt = (o, rt if rt is not None else "full")
        return o

    def dma(self, queue, fn, reads=(), writes=(), is_output=False):
        o = Op()
        o.eng = queue; o.fn = fn
        o.isdma = True; o.needs_inc = False; o.inc = None
        k = self.dma_rr[queue]
        self.dma_rr[queue] = (k + 1) % NDMA_SEM[queue]
        key = (queue, k)
        uses = self.dma_uses.get(key, 0)
        o.dsem = key
        o.dval = 16 * (uses + 1)
        self.dma_uses[key] = uses + 1
        prev = self.dma_last.get(key)
        self.dma_last[key] = o
        self._record(o, reads, writes, [prev] if prev is not None else [])
        if is_output:
            self.out_dmas.append(o)
        if len(reads) > 0:
            self.rd_dmas.append(o)
        return o

    def barrier(self):
        last = {e: (self.ops[e][-1] if self.ops[e] else None) for e in ("pe", "act", "dve")}
        for e in ("pe", "act", "dve", "sp"):
            ex = []
            for f, o in last.items():
                if f == e or o is None:
                    continue
                j = len(self.ops[f]) - 1
                while j >= 0 and (self.ops[f][j].isdma or self.ops[f][j].fn is None):
                    j -= 1
                if j >= 0:
                    ex.append(self.ops[f][j])
            self.op(e, None, extra=ex + list(self.rd_dmas))
        self.rd_dmas = []

    def _record(self, o, reads, writes, extra=()):
        if any(b.excl for b in reads):
            writes = list(writes) + [b for b in reads if b.excl and b not in writes]
            reads = [b for b in reads if not b.excl]
        e = o.eng
        lst = self.ops[e]
        o.pos = len(lst) + 1
        deps = []
        for b in reads:
            if b.w is not None:
                deps.append((b.w, "raw"))
        for b in writes:
            if b.w is not None:
                deps.append((b.w, "waw"))
            for r in b.rs:
                deps.append((r, "war"))
        for d in extra:
            deps.append((d, "raw"))
        clk = self.clock[e]
        waits = []
        force = getattr(self, "_force", None)
        self._force = None
        if force is not None and e == "pe" and clk.get(("self", e), 0) < force.pos:
            waits.append(force)
            force.needs_inc = True
            clk[("self", e)] = force.pos
        best = {}
        d2 = []
        for (y, kind) in deps:
            if y is o:
                continue
            if (not y.isdma) and y.eng != e:
                if y.eng not in best or best[y.eng].pos < y.pos:
                    best[y.eng] = y
            else:
                d2.append((y, kind))
        deps = d2 + [(y, "raw") for y in best.values()]
        for (y, kind) in deps:
            if y is o:
                continue
            if y.isdma:
                if clk.get(y.dsem, 0) >= y.dval:
                    continue
                waits.append(y)
                self._merge(clk, y.vc)
            elif y.eng == e:
                if e != "pe" and clk.get(("self", e), 0) < y.pos and y.fn is not None:
                    waits.append(y)
                    y.needs_inc = True
                    clk[("self", e)] = y.pos
            else:
                if clk.get(y.eng, 0) >= y.pos:
                    continue
                waits.append(y)
                y.needs_inc = True
                self._merge(clk, y.vc)
        o.waits = waits
        if o.isdma:
            vc = dict(clk)
            vc[o.dsem] = o.dval
            o.vc = vc
            clk[e] = o.pos
        else:
            clk[e] = o.pos
            o.vc = dict(clk)
        lst.append(o)
        for b in reads:
            b.rs.append(o)
        for b in writes:
            b.w = o
            b.rs = []

    @staticmethod
    def _merge(clk, vc):
        for k, v in vc.items():
            if isinstance(k, tuple) and k and k[0] == "self":
                continue
            if clk.get(k, 0) < v:
                clk[k] = v

    def finalize(self, block, stack):
        nc = self.nc
        fin = self.op("sp", None)
        clk = self.clock["sp"]
        for d in self.out_dmas:
            if clk.get(d.dsem, 0) < d.dval:
                fin.waits.append(d)
                clk[d.dsem] = d.dval
        esems = {}
        for e in ENGS:
            c = 0
            for o in self.ops[e]:
                if o.needs_inc and not o.isdma:
                    assert o.fn is not None
                    c += 1
                    o.inc = c
            nep = c // EPOCH + 1
            esems[e] = [stack.enter_context(nc.semaphore(f"s_{e}_{i}")) for i in range(nep)]
        dsems = {}
        for (q, k) in self.dma_uses:
            dsems[(q, k)] = stack.enter_context(nc.semaphore(f"d_{q}_{k}"))
        stats = {e: [len(self.ops[e]), 0] for e in ENGS}

        def emit(e, eng):
            for o in self.ops[e]:
                for y in o.waits:
                    if y.isdma:
                        eng.wait_ge(dsems[y.dsem], y.dval)
                    else:
                        ep = (y.inc - 1) // EPOCH
                        eng.wait_ge(esems[y.eng][ep], y.inc - ep * EPOCH)
                    stats[e][1] += 1
                if o.fn is None:
                    continue
                ins = o.fn(eng)
                if o.isdma:
                    ins.then_inc(dsems[o.dsem], 16)
                elif o.needs_inc:
                    ep = (o.inc - 1) // EPOCH
                    ins.then_inc(esems[e][ep], 1)

        @block.tensor
        def _(eng):
            emit("pe", eng)

        @block.scalar
        def _(eng):
            emit("act", eng)

        @block.vector
        def _(eng):
            emit("dve", eng)

        @block.gpsimd
        def _(eng):
            emit("pool", eng)

        @block.sync
        def _(eng):
            emit("sp", eng)
        return stats


def make_consts():
    c = {}
    c["ident"] = np.eye(128, dtype=np.float32)
    c["ones"] = np.ones((128, 128), np.float32)
    b64 = np.zeros((128, 128), np.float32)
    b64[:64, :64] = 1; b64[64:, 64:] = 1
    c["bones64"] = b64
    s = np.arange(128)[:, None]
    t = np.arange(128)[None, :]
    c["iu128"] = (s <= t).astype(np.float32)
    p = np.arange(128)[:, None]
    f = np.arange(128)[None, :]
    same = (p // 32) == (f // 32)
    c["bd32_iu"] = (same & ((p % 32) <= (f % 32))).astype(np.float32)
    su = (same & ((p % 32) < (f % 32))).astype(np.float32)
    sl_ = (same & ((p % 32) > (f % 32))).astype(np.float32)
    c["rwmask4"] = np.concatenate([su, sl_, su, c["bd32_iu"]], axis=1)
    names = list(c.keys())
    offs = {}
    o = 0
    for n in names:
        offs[n] = (o, c[n].shape[1])
        o += c[n].shape[1]
    arr = np.concatenate([c[n] for n in names], axis=1)
    return arr, offs


CONSTS, COFF = make_consts()
NCONST = CONSTS.shape[1]

PV = {}


def _pv_layout():
    o = 0
    for n, k in (("rw_mu", 9), ("rw_w0", 2), ("rw_a0", 2), ("rw_kk", 2), ("rw_ka", 2), ("rw_rk", 2),
                 ("rw_ln_g", 2), ("rw_ln_b", 2), ("cv_w", 62), ("cv_b", 2), ("cv_ln_g", 2),
                 ("cv_ln_b", 2), ("gla_ab", 1), ("gla_ln_g", 2), ("fox_bf", 1), ("gate_b", 32),
                 ("ln_g", 24), ("ln_b", 24)):
        PV[n] = (o, k)
        o += k
    return o


NPV = _pv_layout()


def _cols(v):
    v = np.asarray(v, np.float32).reshape(-1)
    n = v.shape[0]
    k = (n + 127) // 128
    buf = np.zeros((k * 128,), np.float32)
    buf[:n] = v
    return buf.reshape(k, 128).T


def pack_pvec(inp, l):
    out = np.zeros((128, NPV), np.float32)

    def put(name, arr):
        o, k = PV[name]
        assert arr.shape == (128, k), (name, arr.shape, k)
        out[:, o:o + k] = arr

    put("rw_mu", _cols(inp["rw_mu"][l]))
    for n in ("rw_w0", "rw_a0", "rw_kk", "rw_ka", "rw_ln_g", "rw_ln_b", "cv_b", "cv_ln_g", "cv_ln_b",
              "gla_ab", "gla_ln_g"):
        put(n, _cols(inp[n][l]))
    put("rw_rk", _cols(inp["rw_rk"][l].reshape(-1)))
    cw = np.asarray(inp["cv_w"][l], np.float32)
    cwp = cw.T.reshape(2, 128, 31).transpose(1, 0, 2).reshape(128, 62)
    put("cv_w", cwp)
    put("fox_bf", _cols(inp["fox_bf"][l]))
    put("gate_b", _cols(inp["gate_b"][l].reshape(-1)))
    put("ln_g", _cols(inp["ln_g"][l].reshape(-1)))
    put("ln_b", _cols(inp["ln_b"][l].reshape(-1)))
    return out


class K:
    pass


def build(T=SEQ, L=DEPTH, stages=("mix", "xa", "ffn"), dbg=(), mixers=("rw", "cv", "gla", "fox")):
    nc = bass.Bass("TRN2", target_bir_lowering=False)
    NTB = T // 512
    k = K()
    k.nc = nc; k.T = T; k.L = L; k.NTB = NTB; k.mixers = mixers
    dr = lambda n, s, kind="ExternalInput", dt=F32: nc.dram_tensor(n, s, dt, kind=kind).ap()
    k.x_d = dr("x", [T, D])
    if "xa" in stages:
        k.mem_d = dr("mem", [MEM, D])
    k.consts_d = dr("consts", [128, NCONST])
    k.pvec_d = dr("pvec", [L, 128, NPV])
    if "mix" in stages:
        k.w_in_d = dr("w_in", [L, D, DIN])
        k.rw_wa_d = dr("rw_wa", [L, 128, 256])
        k.rw_g2_d = dr("rw_g2", [L, 160, 256])
        k.gla_a2_d = dr("gla_a2", [L, 16, 128])
        k.ups_d = [dr(n, [L, 256, D]) for n in ("rw_up", "cv_up", "gla_up", "fox_up")]
        k.w_out_d = dr("w_out", [L, D, D])
    if "xa" in stages:
        k.xa_wq_d = dr("xa_wq", [L, D, D]); k.xa_wk_d = dr("xa_wk", [L, D, D])
        k.xa_wv_d = dr("xa_wv", [L, D, D]); k.xa_wo_d = dr("xa_wo", [L, D, D])
    if "ffn" in stages:
        k.w1_d = dr("ffn_w1", [L, D, DFF]); k.w3_d = dr("ffn_w3", [L, D, DFF]); k.w2_d = dr("ffn_w2", [L, DFF, D])
    k.out_d = dr("out", [T, D], kind="ExternalOutput")
    k.xscr_d = nc.dram_tensor("xscr", [128, KC * T], F32, kind="Internal").ap()
    k.dbg_d = {}
    for (name, shape) in dbg:
        k.dbg_d[name] = dr("dbg_" + name, list(shape), kind="ExternalOutput", dt=BF16 if name in ("brT", "mg") else F32)

    with ExitStack() as st:
        k.st = st
        sb = lambda n, s, d=F32: st.enter_context(nc.sbuf_tensor(n, s, d))
        k.b_xres = [[Buf() for _ in range(NTB)] for _ in range(KC)]
        k.xres_n = 0
        alloc_xres(k)
        k.xT = sb("xT", [128, KC, T], BF16); k.b_xT = [[Buf() for _ in range(NTB)] for _ in range(KC)]
        k.cst = sb("cst", [128, NCONST]); k.b_cst = Buf()
        k.cstb = sb("cstb", [128, NCONST], BF16); k.b_cstb = Buf()
        k.pv = sb("pv", [128, NPV]); k.b_pv = Buf()
        k.npv = sb("npv", [128, NPV]); k.b_npv = Buf()
        k.g2a = sb("rw_g2a", [128, 256], BF16); k.g2b = sb("rw_g2b", [32, 256], BF16); k.b_g2 = Buf()
        k.ups = [sb(f"mg_ups{i}", [128, 2, 4, 128], BF16) for i in range(2)]; k.b_ups = [Buf(), Buf()]
        k.wsm = sb("w_small", [128, KC, 32], BF16); k.b_wsm = Buf()
        NW = 4
        k.NW = NW
        k.wr = [sb(f"wr{i}", [128, KC, 512], BF16) for i in range(NW)]
        k.b_wr = [Buf() for _ in range(NW)]
        k.wr_i = 0
        k.ps = [st.enter_context(nc.psum_tensor(f"ps{i}", [128, 512], F32)) for i in range(8)]
        k.b_ps = [Buf(excl=True) for _ in range(8)]
        k.ps_i = 0
        k.held = set()
        k.psb = [p.bitcast(BF16) for p in k.ps]
        block = st.enter_context(nc.Block())
        P = Prog(nc)
        k.P = P

        prologue(k)
        for l in range(L):
            layer_params(k, l)
            if "mix" in stages:
                stage_mix(k, l)
            if "xa" in stages:
                stage_xa(k, l)
            if "ffn" in stages:
                stage_ffn(k, l)
        epilogue(k)
        k.stats = P.finalize(block, st)
        k.xres_stack.close()
    return nc, k


_UID = [0]


def sbt(k, stack, name, shape, dt=F32):
    _UID[0] += 1
    return stack.enter_context(k.nc.sbuf_tensor(f"{name}_{_UID[0]}", list(shape), dt))


def alloc_xres(k):
    k.xres_stack = ExitStack()
    k.xres_n += 1
    k.xres = k.xres_stack.enter_context(k.nc.sbuf_tensor(f"xres{k.xres_n}", [128, KC, k.T], F32, side="right"))


def spill_xres(k):
    P = k.P; T = k.T
    for c in range(KC):
        P.dma("sp", lambda e, c=c: e.dma_start(out=k.xscr_d[:, c * T:(c + 1) * T], in_=k.xres[:, c, :]),
              reads=[k.b_xres[c][tb] for tb in range(k.NTB)])
    P.barrier()
    k.xres_stack.close()
    k.xres = None


def reload_xres(k):
    P = k.P; T = k.T
    alloc_xres(k)
    for c in range(KC):
        P.dma("sp", lambda e, c=c: e.dma_start(out=k.xres[:, c, :], in_=k.xscr_d[:, c * T:(c + 1) * T]),
              writes=[k.b_xres[c][tb] for tb in range(k.NTB)])


def cs(k, name, bf=False):
    o, n = COFF[name]
    return (k.cstb if bf else k.cst)[:, o:o + n]


def pcol(k, name, j=0, neg=False, rows=128, r0=0):
    o, n = PV[name]
    t = k.npv if neg else k.pv
    return t[r0:r0 + rows, o + j:o + j + 1]


def mm(e, out, lhsT, rhs, pos, start=True, stop=True):
    return e.matmul(out, lhsT=lhsT, rhs=rhs, start=start, stop=stop, tile_position=pos)


def tr(e, out, in_, identity, pos):
    return e.transpose(out=out, in_=in_, identity=identity, tile_position=pos)


def nbank(k, hold=False):
    i = k.ps_i
    while i in k.held:
        i = (i + 1) % 8
    k.ps_i = (i + 1) % 8
    if hold:
        k.held.add(i)
    return k.ps[i], k.b_ps[i]


def release(k, pt):
    for i in range(8):
        if k.ps[i] is pt:
            k.held.discard(i)
            return
    raise AssertionError


def load_w(k, src, ncols, nk=KC):
    i = k.wr_i
    k.wr_i = (i + 1) % k.NW
    slot, b = k.wr[i], k.b_wr[i]
    v = src.rearrange("(c p) n -> p c n", p=128)
    k.P.dma("pool", lambda e: e.dma_start(out=slot[:, 0:nk, 0:ncols], in_=v), writes=[b])
    return slot, b


def prologue(k):
    P = k.P; T = k.T
    P.dma("sp", lambda e: e.dma_start(out=k.cst[:], in_=k.consts_d), writes=[k.b_cst])
    P.op("dve", lambda e: e.tensor_copy(out=k.cstb[:], in_=k.cst[:]), reads=[k.b_cst], writes=[k.b_cstb])
    for i in range(8):
        P.op("dve", lambda e, i=i: e.memset(k.ps[i][:], 0.0), writes=[k.b_ps[i]])
    with ExitStack() as s2:
        xin = [sbt(k, s2, f"xin{i}", [128, D], F32) for i in range(2)]
        b_xin = [Buf(), Buf()]
        ident = cs(k, "ident")
        for tt in range(T // 128):
            j = tt % 2
            P.dma("sp", lambda e, tt=tt, j=j: e.dma_start(out=xin[j][:], in_=k.x_d[tt * 128:(tt + 1) * 128, :]),
                  writes=[b_xin[j]])
            tb = tt // 4
            for g in range(2):
                pt, bp = nbank(k)
                for q in range(4):
                    c = g * 4 + q
                    P.op("pe", lambda e, pt=pt, j=j, c=c, q=q: e.transpose(
                        out=pt[:, q * 128:(q + 1) * 128], in_=xin[j][:, c * 128:(c + 1) * 128], identity=ident),
                        reads=[b_xin[j], k.b_cst], writes=[bp])
                for q in range(4):
                    c = g * 4 + q
                    dst = slice(tt * 128, (tt + 1) * 128)
                    P.op("act", lambda e, pt=pt, c=c, q=q, dst=dst: e.activation(
                        out=k.xres[:, c, dst], in_=pt[:, q * 128:(q + 1) * 128], func=AF.Copy),
                        reads=[bp], writes=[k.b_xres[c][tb]])
                    P.op("dve", lambda e, pt=pt, c=c, q=q, dst=dst: e.tensor_copy(
                        out=k.xT[:, c, dst], in_=pt[:, q * 128:(q + 1) * 128]),
                        reads=[bp], writes=[k.b_xT[c][tb]])
        P.barrier()


def layer_params(k, l):
    P = k.P
    P.dma("sp", lambda e: e.dma_start(out=k.pv[:], in_=k.pvec_d[l]), writes=[k.b_pv])
    P.op("dve", lambda e: e.tensor_scalar(out=k.npv[:], in0=k.pv[:], scalar1=-1.0, scalar2=None, op0=ALU.mult),
         reads=[k.b_pv], writes=[k.b_npv])


def epilogue(k):
    P = k.P; T = k.T
    with ExitStack() as s2:
        xo = [sbt(k, s2, f"xo{i}", [128, D], F32) for i in range(2)]
        b_xo = [Buf(), Buf()]
        ident = cs(k, "ident")
        for tt in range(T // 128):
            j = tt % 2
            tb = tt // 4
            for g in range(2):
                pt, bp = nbank(k)
                for q in range(4):
                    c = g * 4 + q
                    P.op("pe", lambda e, pt=pt, c=c, q=q, tt=tt: e.transpose(
                        out=pt[:, q * 128:(q + 1) * 128], in_=k.xres[:, c, tt * 128:(tt + 1) * 128], identity=ident),
                        reads=[k.b_xres[c][tb], k.b_cst], writes=[bp])
                eng = "act" if g == 0 else "dve"
                if g == 0:
                    P.op("act", lambda e, pt=pt, j=j: e.activation(out=xo[j][:, 0:512], in_=pt[:], func=AF.Copy),
                         reads=[bp], writes=[b_xo[j]])
                else:
                    P.op("dve", lambda e, pt=pt, j=j: e.tensor_copy(out=xo[j][:, 512:1024], in_=pt[:]),
                         reads=[bp], writes=[b_xo[j]])
            P.dma("sp", lambda e, tt=tt, j=j: e.dma_start(out=k.out_d[tt * 128:(tt + 1) * 128, :], in_=xo[j][:]),
                  reads=[b_xo[j]], is_output=True)
        P.barrier()


def ln_block(k, l, s, tb, zsq, b_zsq, st_t, b_st):
    P = k.P
    sl = slice(tb * 512, (tb + 1) * 512)
    rstd, nmr, b_r, b_n = ln_stats(k, [(k.xres[:, c, sl], k.b_xres[c][tb]) for c in range(KC)], D, LN_EPS, zsq, b_zsq, st_t, b_st)
    og, _ = PV["ln_g"]
    ob, _ = PV["ln_b"]
    for c in range(KC):
        xs = k.xres[:, c, sl]
        P.op("dve", lambda e, xs=xs: e.tensor_tensor(out=xs, in0=xs, in1=rstd, op=ALU.mult),
             reads=[k.b_xres[c][tb], b_r], writes=[k.b_xres[c][tb]])
        P.op("dve", lambda e, xs=xs: e.tensor_tensor(out=xs, in0=xs, in1=nmr, op=ALU.add),
             reads=[k.b_xres[c][tb], b_n], writes=[k.b_xres[c][tb]])
        gcol = k.pv[:, og + s * 8 + c: og + s * 8 + c + 1]
        bcol = k.pv[:, ob + s * 8 + c: ob + s * 8 + c + 1]
        P.op("act", lambda e, xs=xs, c=c, gcol=gcol, bcol=bcol: e.activation(out=k.xT[:, c, sl], in_=xs, func=AF.Identity, scale=gcol, bias=bcol),
             reads=[k.b_xres[c][tb], k.b_pv], writes=[k.b_xT[c][tb]])
        P.op("act", lambda e, xs=xs, gcol=gcol, bcol=bcol: e.activation(out=xs, in_=xs, func=AF.Identity, scale=gcol, bias=bcol),
             reads=[k.b_xres[c][tb], k.b_pv], writes=[k.b_xres[c][tb]])


def out_proj_ln(k, l, s, src, b_src, nkc, w_d, first=True, last=True, alpha_first=True, row0=0,
                lnbufs=None):
    P = k.P
    NTB = k.NTB
    halves = []
    for h in range(2):
        slot, b = load_w(k, w_d[row0:row0 + nkc * 128, h * 512:(h + 1) * 512], 512, nk=nkc)
        halves.append((slot, b))
    for tb in range(NTB):
        sl = slice(tb * 512, (tb + 1) * 512)
        for fc in range(KC):
            slot, bw = halves[fc // 4]
            co = (fc % 4) * 128
            pt, bp = nbank(k)
            for c in range(nkc):
                P.op("pe", lambda e, pt=pt, slot=slot, c=c, co=co, sl=sl: e.matmul(
                    pt[:], lhsT=slot[:, c, co:co + 128], rhs=src[:, c, sl], start=(c == 0), stop=(c == nkc - 1)),
                    reads=[bw, b_src[c][tb]], writes=[bp])
            xs = k.xres[:, fc, sl]
            if first:
                P.op("dve", lambda e, pt=pt, xs=xs: e.scalar_tensor_tensor(
                    out=xs, in0=xs, scalar=ALPHA, in1=pt[:], op0=ALU.mult, op1=ALU.add),
                    reads=[bp, k.b_xres[fc][tb]], writes=[k.b_xres[fc][tb]])
            else:
                P.op("dve", lambda e, pt=pt, xs=xs: e.tensor_tensor(out=xs, in0=xs, in1=pt[:], op=ALU.add),
                     reads=[bp, k.b_xres[fc][tb]], writes=[k.b_xres[fc][tb]])
        if last:
            ln_block(k, l, s, tb, *lnbufs)


def alloc_ln(k, s2):
    zsq = sbt(k, s2, "zsq", [128, 2, 512], F32)
    st_t = sbt(k, s2, "lnst", [128, 3, 512], F32)
    return (zsq, [Buf(), Buf()], st_t, [Buf() for _ in range(3)])


def stage_ffn(k, l):
    P = k.P; T = k.T; NTB = k.NTB
    parts = [(0, 8), (8, 16), (16, 22)]
    with ExitStack() as s2:
        g = sbt(k, s2, "ffg", [128, 8, T], BF16)
        b_g = [[Buf() for _ in range(NTB)] for _ in range(8)]
        sg = [sbt(k, s2, f"ffs{i}", [128, 512], F32) for i in range(2)]
        b_sg = [Buf(), Buf()]
        lnb = alloc_ln(k, s2)
        si = 0
        for pi, (c0, c1) in enumerate(parts):
            n = c1 - c0
            for q0 in range(c0, c1, 4):
                nq = min(4, c1 - q0)
                s1, bw1 = load_w(k, k.w1_d[l][:, q0 * 128:(q0 + nq) * 128], nq * 128)
                s3, bw3 = load_w(k, k.w3_d[l][:, q0 * 128:(q0 + nq) * 128], nq * 128)
                for q in range(nq):
                    cg = q0 + q - c0
                    for tb in range(NTB):
                        sl = slice(tb * 512, (tb + 1) * 512)
                        p1, bp1 = nbank(k)
                        p3, bp3 = nbank(k)
                        for c in range(KC):
                            P.op("pe", lambda e, p1=p1, s1=s1, c=c, q=q, sl=sl: e.matmul(
                                p1[:], lhsT=s1[:, c, q * 128:(q + 1) * 128], rhs=k.xT[:, c, sl], start=(c == 0), stop=(c == KC - 1)),
                                reads=[bw1, k.b_xT[c][tb]], writes=[bp1])
                        for c in range(KC):
                            P.op("pe", lambda e, p3=p3, s3=s3, c=c, q=q, sl=sl: e.matmul(
                                p3[:], lhsT=s3[:, c, q * 128:(q + 1) * 128], rhs=k.xT[:, c, sl], start=(c == 0), stop=(c == KC - 1)),
                                reads=[bw3, k.b_xT[c][tb]], writes=[bp3])
                        j = si % 2
                        si += 1
                        P.op("act", lambda e, p1=p1, j=j: e.activation(out=sg[j][:], in_=p1[:], func=AF.Silu),
                             reads=[bp1], writes=[b_sg[j]])
                        P.op("dve", lambda e, p3=p3, j=j, cg=cg, sl=sl: e.tensor_tensor(
                            out=g[:, cg, sl], in0=sg[j][:], in1=p3[:], op=ALU.mult),
                            reads=[bp3, b_sg[j]], writes=[b_g[cg][tb]])
            out_proj_ln(k, l, 2, g, b_g, n, k.w2_d[l], first=(pi == 0), last=(pi == len(parts) - 1),
                        row0=c0 * 128, lnbufs=lnb)
        P.barrier()


def stage_xa(k, l):
    P = k.P; T = k.T; NTB = k.NTB
    ident = cs(k, "ident")
    with ExitStack() as s2:
        sbt_ = lambda n, s, d=F32: sbt(k, s2, n, s, d)
        k.xaK = sbt_("xaK", [128, KC, MEM], BF16); k.b_xaK = Buf()
        k.xaV = sbt_("xaV", [128, 2, D], BF16); k.b_xaV = Buf()
        smem = ExitStack()
        memT = sbt(k, smem, "memT", [128, KC, MEM], BF16)
        b_memT = Buf()
        with ExitStack() as s3:
            mt = [sbt(k, s3, f"memin{i}", [128, D], F32) for i in range(2)]
            b_mt = [Buf(), Buf()]
            for m in range(2):
                P.dma("sp", lambda e, m=m: e.dma_start(out=mt[m][:], in_=k.mem_d[m * 128:(m + 1) * 128, :]), writes=[b_mt[m]])
                for g in range(2):
                    pt, bp = nbank(k)
                    for q in range(4):
                        c = g * 4 + q
                        P.op("pe", lambda e, pt=pt, m=m, c=c, q=q: e.transpose(
                            out=pt[:, q * 128:(q + 1) * 128], in_=mt[m][:, c * 128:(c + 1) * 128], identity=ident),
                            reads=[b_mt[m], k.b_cst], writes=[bp])
                    for q in range(4):
                        c = g * 4 + q
                        P.op("dve", lambda e, pt=pt, m=m, c=c, q=q: e.tensor_copy(
                            out=memT[:, c, m * 128:(m + 1) * 128], in_=pt[:, q * 128:(q + 1) * 128]),
                            reads=[bp], writes=[b_memT])
            P.barrier()
        for h in range(2):
            slot, bw = load_w(k, k.xa_wk_d[l][:, h * 512:(h + 1) * 512], 512)
            for q in range(4):
                fc = h * 4 + q
                pt, bp = nbank(k)
                for c in range(KC):
                    P.op("pe", lambda e, pt=pt, slot=slot, c=c, q=q: e.matmul(
                        pt[:, 0:MEM], lhsT=slot[:, c, q * 128:(q + 1) * 128], rhs=memT[:, c, :], start=(c == 0), stop=(c == KC - 1)),
                        reads=[bw, b_memT], writes=[bp])
                P.op("act", lambda e, pt=pt, fc=fc: e.activation(out=k.xaK[:, fc, :], in_=pt[:, 0:MEM], func=AF.Copy),
                     reads=[bp], writes=[k.b_xaK])
        for h in range(2):
            slot, bw = load_w(k, k.xa_wv_d[l][:, h * 512:(h + 1) * 512], 512)
            for m in range(2):
                pt, bp = nbank(k)
                for c in range(KC):
                    P.op("pe", lambda e, pt=pt, slot=slot, c=c, m=m: e.matmul(
                        pt[:], lhsT=memT[:, c, m * 128:(m + 1) * 128], rhs=slot[:, c, :], start=(c == 0), stop=(c == KC - 1)),
                        reads=[bw, b_memT], writes=[bp])
                P.op("act", lambda e, pt=pt, m=m, h=h: e.activation(out=k.xaV[:, m, h * 512:(h + 1) * 512], in_=pt[:], func=AF.Copy),
                     reads=[bp], writes=[k.b_xaV])
        P.barrier()
        smem.close()
        oT = sbt_("xa_oT", [128, KC, T], BF16)
        b_oT = [[Buf() for _ in range(NTB)] for _ in range(KC)]
        qT = sbt_("xa_qT", [128, 2, T], BF16)
        b_qT = [[Buf() for _ in range(NTB)] for _ in range(2)]
        PT = [sbt_(f"xa_PT{i}", [128, 2, 512], BF16) for i in range(2)]
        b_PT = [[Buf(), Buf()], [Buf(), Buf()]]
        rd0 = sbt_("xa_rd", [128, 512])
        rden = [rd0, rd0]
        b0 = Buf()
        b_rden = [b0, b0]
        onesb = cs(k, "ones", bf=True)
        scale = 1.0 / 16.0
        it = 0
        for hh in range(2):
            slot, bw = load_w(k, k.xa_wq_d[l][:, hh * 512:(hh + 1) * 512], 512)
            for h2 in range(2):
                h = hh * 2 + h2
                for tb in range(NTB):
                    sl = slice(tb * 512, (tb + 1) * 512)
                    for dc in range(2):
                        pt, bp = nbank(k)
                        co = (h2 * 2 + dc) * 128
                        for c in range(KC):
                            P.op("pe", lambda e, pt=pt, slot=slot, c=c, co=co, sl=sl: e.matmul(
                                pt[:], lhsT=slot[:, c, co:co + 128], rhs=k.xT[:, c, sl], start=(c == 0), stop=(c == KC - 1)),
                                reads=[bw, k.b_xT[c][tb]], writes=[bp])
                        P.op("act", lambda e, pt=pt, dc=dc, sl=sl: e.activation(out=qT[:, dc, sl], in_=pt[:], func=AF.Copy),
                             reads=[bp], writes=[b_qT[dc][tb]])
                    j = it % 2
                    it += 1
                    for m in range(2):
                        pt, bp = nbank(k)
                        for dc in range(2):
                            P.op("pe", lambda e, pt=pt, h=h, dc=dc, m=m, sl=sl: e.matmul(
                                pt[:], lhsT=k.xaK[:, h * 2 + dc, m * 128:(m + 1) * 128], rhs=qT[:, dc, sl],
                                start=(dc == 0), stop=(dc == 1)),
                                reads=[k.b_xaK, b_qT[dc][tb]], writes=[bp])
                        P.op("act", lambda e, pt=pt, j=j, m=m: e.activation(out=PT[j][:, m, :], in_=pt[:], func=AF.Exp, scale=scale),
                             reads=[bp], writes=[b_PT[j][m]])
                    pd, bpd = nbank(k)
                    for m in range(2):
                        P.op("pe", lambda e, pd=pd, j=j, m=m: e.matmul(pd[:], lhsT=onesb, rhs=PT[j][:, m, :], start=(m == 0), stop=(m == 1)),
                             reads=[k.b_cstb, b_PT[j][m]], writes=[bpd])
                    P.op("act", lambda e, pd=pd, j=j: e.activation(out=rden[j][:], in_=pd[:], func=AF.Ln), reads=[bpd], writes=[b_rden[j]])
                    P.op("act", lambda e, j=j: e.activation(out=rden[j][:], in_=rden[j][:], func=AF.Exp, scale=-1.0), reads=[b_rden[j]], writes=[b_rden[j]])
                    for dc in range(2):
                        po, bpo = nbank(k)
                        for m in range(2):
                            P.op("pe", lambda e, po=po, j=j, m=m, h=h, dc=dc: e.matmul(
                                po[:], lhsT=k.xaV[:, m, h * 256 + dc * 128: h * 256 + (dc + 1) * 128], rhs=PT[j][:, m, :],
                                start=(m == 0), stop=(m == 1)),
                                reads=[k.b_xaV, b_PT[j][m]], writes=[bpo])
                        P.op("dve", lambda e, po=po, j=j, h=h, dc=dc, sl=sl: e.tensor_tensor(
                            out=oT[:, h * 2 + dc, sl], in0=po[:], in1=rden[j][:], op=ALU.mult),
                            reads=[bpo, b_rden[j]], writes=[b_oT[h * 2 + dc][tb]])
        lnb = alloc_ln(k, s2)
        out_proj_ln(k, l, 1, oT, b_oT, KC, k.xa_wo_d[l], lnbufs=lnb)
        P.barrier()


def proj_fm(k, slot, bw, col0, ncols, tb, hold=False):
    P = k.P
    sl = slice(tb * 512, (tb + 1) * 512)
    pt, bp = nbank(k, hold=hold)
    for c in range(KC):
        P.op("pe", lambda e, c=c: e.matmul(pt[0:ncols, :], lhsT=slot[:, c, col0:col0 + ncols], rhs=k.xT[:, c, sl],
                                           start=(c == 0), stop=(c == KC - 1)),
             reads=[bw, k.b_xT[c][tb]], writes=[bp])
    return pt, bp


def ln_stats(k, srcs, nfeat, eps, zsq, b_zsq, st_t, b_st):
    P = k.P
    ones = cs(k, "ones")
    S1, b1 = nbank(k)
    S2, b2 = nbank(k)
    n = len(srcs)
    for i, (ap, b) in enumerate(srcs):
        j = i % 2
        P.op("act", lambda e, j=j, ap=ap: e.activation(out=zsq[:, j, :], in_=ap, func=AF.Square), reads=[b], writes=[b_zsq[j]])
        P.op("pe", lambda e, i=i, ap=ap: e.matmul(S1[:], lhsT=ones, rhs=ap, start=(i == 0), stop=(i == n - 1)),
             reads=[b, k.b_cst], writes=[b1])
        P.op("pe", lambda e, i=i, j=j: e.matmul(S2[:], lhsT=ones, rhs=zsq[:, j, :], start=(i == 0), stop=(i == n - 1)),
             reads=[b_zsq[j], k.b_cst], writes=[b2])
    mean, var, rstd = (st_t[:, i, :] for i in range(3))
    P.op("act", lambda e: e.activation(out=mean, in_=S1[:], func=AF.Copy, scale=1.0 / nfeat), reads=[b1], writes=[b_st[0]])
    P.op("dve", lambda e: e.tensor_tensor(out=var, in0=mean, in1=mean, op=ALU.mult), reads=[b_st[0]], writes=[b_st[1]])
    P.op("dve", lambda e: e.scalar_tensor_tensor(out=var, in0=S2[:], scalar=1.0 / nfeat, in1=var, op0=ALU.mult, op1=ALU.subtract),
         reads=[b2, b_st[1]], writes=[b_st[1]])
    P.op("dve", lambda e: e.tensor_scalar(out=var, in0=var, scalar1=eps, scalar2=None, op0=ALU.add),
         reads=[b_st[1]], writes=[b_st[1]])
    P.op("act", lambda e: e.activation(out=rstd, in_=var, func=AF.Ln), reads=[b_st[1]], writes=[b_st[2]])
    P.op("act", lambda e: e.activation(out=rstd, in_=rstd, func=AF.Exp, scale=-0.5), reads=[b_st[2]], writes=[b_st[2]])
    P.op("dve", lambda e: e.scalar_tensor_tensor(out=mean, in0=mean, scalar=-1.0, in1=rstd, op0=ALU.mult, op1=ALU.mult),
         reads=[b_st[0], b_st[2]], writes=[b_st[0]])
    return rstd, mean, b_st[2], b_st[0]


def mixer_conv(k, l, brT, b_brT, s2):
    P = k.P; T = k.T; NTB = k.NTB
    if True:
        ub = sbt(k, s2, "cv_u", [128, 2, 30 + T], F32)
        b_ub = [Buf(), Buf()]
        acc = sbt(k, s2, "cv_acc", [128, 2, T], F32)
        b_acc = [Buf(), Buf()]
        lnb = alloc_ln(k, s2)
        sg = [lnb[0][:, i, :] for i in range(2)]
        b_sg = lnb[1]
        slot, bw = load_w(k, k.w_in_d[l][:, 1056:1568], 512)
        for ch in range(2):
            P.op("dve", lambda e, ch=ch: e.memset(ub[:, ch, 0:30], 0.0), writes=[b_ub[ch]])
        i = 0
        for tb in range(NTB):
            for ch in range(2):
                pa, bpa = proj_fm(k, slot, bw, ch * 128, 128, tb)
                pb, bpb = proj_fm(k, slot, bw, 256 + ch * 128, 128, tb)
                j = i % 2
                i += 1
                P.op("act", lambda e, pb=pb, j=j: e.activation(out=sg[j], in_=pb[:], func=AF.Sigmoid), reads=[bpb], writes=[b_sg[j]])
                P.op("dve", lambda e, pa=pa, j=j, ch=ch, tb=tb: e.tensor_tensor(
                    out=ub[:, ch, 30 + tb * 512: 30 + (tb + 1) * 512], in0=pa[:], in1=sg[j], op=ALU.mult),
                    reads=[bpa, b_sg[j]], writes=[b_ub[ch]])
                yield
        ow, _ = PV["cv_w"]
        for ch in range(2):
            eng = "dve"
            for kk in range(31):
                wcol = k.pv[:, ow + ch * 31 + kk: ow + ch * 31 + kk + 1]
                if kk == 0:
                    bcol = pcol(k, "cv_b", ch)
                    P.op(eng, lambda e, ch=ch, wcol=wcol, bcol=bcol: e.tensor_scalar(
                        out=acc[:, ch, :], in0=ub[:, ch, 0:T], scalar1=wcol, scalar2=bcol, op0=ALU.mult, op1=ALU.add),
                        reads=[b_ub[ch], k.b_pv], writes=[b_acc[ch]])
                else:
                    P.op(eng, lambda e, ch=ch, wcol=wcol, kk=kk: e.scalar_tensor_tensor(
                        out=acc[:, ch, :], in0=ub[:, ch, kk:kk + T], scalar=wcol, in1=acc[:, ch, :], op0=ALU.mult, op1=ALU.add),
                        reads=[b_ub[ch], k.b_pv, b_acc[ch]], writes=[b_acc[ch]])
                yield
        for tb in range(NTB):
            sl = slice(tb * 512, (tb + 1) * 512)
            rstd, nmr, b_r, b_n = ln_stats(k, [(acc[:, ch, sl], b_acc[ch]) for ch in range(2)], 256, LN_EPS, *lnb)
            for ch in range(2):
                a = acc[:, ch, sl]
                P.op("dve", lambda e, a=a, rstd=rstd: e.tensor_tensor(out=a, in0=a, in1=rstd, op=ALU.mult),
                     reads=[b_acc[ch], b_r], writes=[b_acc[ch]])
                P.op("dve", lambda e, a=a, nmr=nmr: e.tensor_tensor(out=a, in0=a, in1=nmr, op=ALU.add),
                     reads=[b_acc[ch], b_n], writes=[b_acc[ch]])
                P.op("act", lambda e, a=a, ch=ch, sl=sl: e.activation(
                    out=brT[:, 2 + ch, sl], in_=a, func=AF.Silu, scale=pcol(k, "cv_ln_g", ch), bias=pcol(k, "cv_ln_b", ch)),
                    reads=[b_acc[ch], k.b_pv], writes=[b_brT[2 + ch][tb]])
            yield
        yield


def mixer_fox(k, l, brT, b_brT, s2):
    P = k.P; T = k.T; NTB = k.NTB
    NT = T // 128
    if True:
        fq = sbt(k, s2, "fx_q", [128, 2, T], BF16); b_fq = [[Buf() for _ in range(NTB)] for _ in range(2)]
        fk = sbt(k, s2, "fx_k", [128, 2, T], BF16); b_fk = [[Buf() for _ in range(NTB)] for _ in range(2)]
        fv = sbt(k, s2, "fx_v", [128, NT, 256], BF16); b_fv = [Buf() for _ in range(NT)]
        spl = sbt(k, s2, "fx_spl", [4, T], F32); b_spl = Buf()
        sig, b_sig = spl, b_spl
        rsel = sbt(k, s2, "fx_rsel", [4, NT * 4], F32); b_rsel = Buf()
        stok = sbt(k, s2, "fx_stok", [128, NT * 4], F32); b_stok = Buf()
        sref = sbt(k, s2, "fx_sref", [128, NT * 4], F32); b_sref = Buf()
        bias = sbt(k, s2, "fx_bias", [128, NT, NT], F32); b_bias = Buf()
        PT = [sbt(k, s2, f"fx_PT{i}", [128, 512], BF16) for i in range(2)]
        b_PT = [Buf(), Buf()]
        rd = sbt(k, s2, "fx_rd", [128, 512], F32); b_rd = Buf()
        slA, bwA = load_w(k, k.w_in_d[l][:, 2352:2864], 512)
        slB, bwB = load_w(k, k.w_in_d[l][:, 2864:3124], 260)
        for tb in range(NTB):
            sl = slice(tb * 512, (tb + 1) * 512)
            for ch in range(2):
                pq, bpq = proj_fm(k, slA, bwA, ch * 128, 128, tb)
                P.op("act", lambda e, pq=pq, ch=ch, sl=sl: e.activation(out=fq[:, ch, sl], in_=pq[:], func=AF.Copy, scale=0.125),
                     reads=[bpq], writes=[b_fq[ch][tb]])
                pk, bpk = proj_fm(k, slA, bwA, 256 + ch * 128, 128, tb)
                P.op("dve", lambda e, pk=pk, ch=ch, sl=sl: e.tensor_copy(out=fk[:, ch, sl], in_=pk[:]),
                     reads=[bpk], writes=[b_fk[ch][tb]])
            pz, bpz = proj_fm(k, slB, bwB, 256, 4, tb)
            P.op("act", lambda e, pz=pz, sl=sl: e.activation(out=spl[:, sl], in_=pz[0:4, :], func=AF.Exp, scale=-1.0,
                                                             bias=pcol(k, "fox_bf", 0, neg=True, rows=4)),
                 reads=[bpz, k.b_npv], writes=[b_spl])
            P.op("act", lambda e, sl=sl: e.activation(out=spl[:, sl], in_=spl[:, sl], func=AF.Ln, bias=1.0),
                 reads=[b_spl], writes=[b_spl])
            yield
        for tt in range(NT):
            tb = tt // 4
            pt, bp = nbank(k)
            for c in range(KC):
                P.op("pe", lambda e, c=c, tt=tt, pt=pt: e.matmul(pt[:, 0:256], lhsT=k.xT[:, c, tt * 128:(tt + 1) * 128], rhs=slB[:, c, 0:256],
                                                             start=(c == 0), stop=(c == KC - 1)),
                     reads=[bwB, k.b_xT[c][tb]], writes=[bp])
            P.op("act", lambda e, pt=pt, tt=tt: e.activation(out=fv[:, tt, :], in_=pt[:, 0:256], func=AF.Copy), reads=[bp], writes=[b_fv[tt]])
            yield
        P.op("dve", lambda e: e.tensor_tensor_scan(out=sig[:], data0=spl[:], data1=spl[:], initial=0.0, op0=ALU.add, op1=ALU.max),
             reads=[b_spl], writes=[b_sig])
        ident = cs(k, "ident")
        pt, bp = nbank(k)
        for tt in range(NT):
            P.op("pe", lambda e, tt=tt: e.transpose(out=pt[:, tt * 4:(tt + 1) * 4], in_=sig[0:4, tt * 128:(tt + 1) * 128], identity=ident[0:4, 0:4]),
                 reads=[b_sig, k.b_cst], writes=[bp])
        P.op("dve", lambda e: e.tensor_copy(out=stok[:], in_=pt[:, 0:NT * 4]), reads=[bp], writes=[b_stok])
        for qs in range(NT):
            P.op("dve", lambda e, qs=qs: e.tensor_scalar(out=rsel[:, qs * 4:(qs + 1) * 4], in0=ident[0:4, 0:4], scalar1=sig[:, qs * 128:qs * 128 + 1],
                                                     scalar2=None, op0=ALU.mult),
                 reads=[b_sig, k.b_cst], writes=[b_rsel])
        pr, bpr = nbank(k)
        ones = cs(k, "ones")
        P.op("pe", lambda e: e.matmul(pr[:, 0:NT * 4], lhsT=ones[0:4, :], rhs=rsel[:], start=True, stop=True),
             reads=[b_rsel, k.b_cst], writes=[bpr])
        P.op("dve", lambda e: e.tensor_copy(out=sref[:], in_=pr[:, 0:NT * 4]), reads=[bpr], writes=[b_sref])
        stok3 = stok[:].rearrange("p (n h) -> p n h", h=4)
        onesb = cs(k, "ones", bf=True)
        iu = cs(k, "iu128", bf=True)
        it = 0
        for h in range(4):
            ch = h // 2
            pb = (h % 2) * 64
            for qs in range(NT):
                P.op("dve", lambda e, h=h, qs=qs: e.tensor_scalar(out=bias[:, qs, :], in0=stok3[:, :, h], scalar1=sref[:, qs * 4 + h:qs * 4 + h + 1],
                                                            scalar2=None, op0=ALU.subtract),
                     reads=[b_stok, b_sref], writes=[b_bias])
            for Q in range(NTB):
                nkt = 4 * (Q + 1)
                po, bpo = nbank(k, hold=True)
                pd, bpd = nbank(k, hold=True)
                for kt in range(nkt):
                    d = kt - 4 * Q
                    q0 = d * 128 if d > 0 else 0
                    j = it % 2
                    it += 1
                    ps_, bps = nbank(k, hold=True)
                    P.op("pe", lambda e, ps_=ps_, pb=pb, ch=ch, kt=kt, Q=Q, q0=q0: e.matmul(
                        ps_[:, q0:512], lhsT=fk[pb:pb + 64, ch, kt * 128:(kt + 1) * 128], rhs=fq[pb:pb + 64, ch, Q * 512 + q0:(Q + 1) * 512],
                        start=True, stop=True),
                        reads=[b_fk[ch][kt // 4], b_fq[ch][Q]], writes=[bps])
                    yield
                    for qi in range(q0 // 128, 4):
                        qs = Q * 4 + qi
                        P.op("act", lambda e, ps_=ps_, j=j, qi=qi, qs=qs, h=h, kt=kt: e.activation(
                            out=PT[j][:, qi * 128:(qi + 1) * 128], in_=ps_[:, qi * 128:(qi + 1) * 128], func=AF.Exp,
                            bias=bias[:, qs, kt:kt + 1]),
                            reads=[bps, b_bias], writes=[b_PT[j]])
                    release(k, ps_)
                    if d >= 0:
                        P.op("dve", lambda e, j=j, q0=q0: e.tensor_tensor(out=PT[j][:, q0:q0 + 128], in0=PT[j][:, q0:q0 + 128], in1=iu, op=ALU.mult),
                             reads=[b_PT[j], k.b_cstb], writes=[b_PT[j]])
                    P.op("pe", lambda e, po=po, pb=pb, j=j, q0=q0, kt=kt, h=h, nkt=nkt: e.matmul(
                        po[pb:pb + 64, q0:512], lhsT=fv[:, kt, h * 64:(h + 1) * 64], rhs=PT[j][:, q0:512], start=(kt == 0), stop=(kt == nkt - 1)),
                        reads=[b_fv[kt], b_PT[j]], writes=[bpo])
                    P.op("pe", lambda e, pd=pd, pb=pb, j=j, q0=q0, kt=kt, nkt=nkt: e.matmul(
                        pd[pb:pb + 64, q0:512], lhsT=onesb[:, 0:64], rhs=PT[j][:, q0:512], start=(kt == 0), stop=(kt == nkt - 1)),
                        reads=[k.b_cstb, b_PT[j]], writes=[bpd])
                    yield
                P.op("act", lambda e, pd=pd, pb=pb: e.activation(out=rd[pb:pb + 64, :], in_=pd[pb:pb + 64, :], func=AF.Ln), reads=[bpd], writes=[b_rd])
                P.op("act", lambda e, pb=pb: e.activation(out=rd[pb:pb + 64, :], in_=rd[pb:pb + 64, :], func=AF.Exp, scale=-1.0), reads=[b_rd], writes=[b_rd])
                P.op("dve", lambda e, po=po, pb=pb, ch=ch, Q=Q: e.tensor_tensor(
                    out=brT[pb:pb + 64, 6 + ch, Q * 512:(Q + 1) * 512], in0=po[pb:pb + 64, :], in1=rd[pb:pb + 64, :], op=ALU.mult),
                    reads=[bpo, b_rd], writes=[b_brT[6 + ch][Q]])
                release(k, po); release(k, pd)
        yield


def mixer_gla(k, l, brT, b_brT, s2):
    P = k.P; T = k.T; NTB = k.NTB
    identb = cs(k, "ident", bf=True)
    if True:
        f32t = lambda n, shp: sbt(k, s2, n, shp, F32)
        bft = lambda n, shp: sbt(k, s2, n, shp, BF16)
        a2 = f32t("gl_a2", [16, 128]); b_a2 = Buf()
        zT = f32t("gl_z", [16, 512]); b_zT = Buf()
        spl = f32t("gl_spl", [128, 512]); b_spl = Buf()
        bcs = f32t("gl_bcs", [128, 512]); b_bcs = Buf()
        Ep = f32t("gl_Ep", [128, 512]); b_Ep = Buf()
        En = f32t("gl_En", [128, 512]); b_En = Buf()
        ones32 = f32t("gl_ones", [128, 32]); b_ones = Buf()
        qd = bft("gl_qd", [128, 512]); b_qd = Buf()
        ki = bft("gl_ki", [128, 512]); b_ki = Buf()
        vb = bft("gl_vb", [128, 2, 512]); b_vb = Buf()
        sr = f32t("gl_sr", [128, 2, 512]); b_sr = Buf()
        S4f = f32t("gl_S4f", [128, 64]); b_S4f = Buf()
        S4b = bft("gl_S4b", [128, 64]); b_S4b = Buf()
        STm = [bft(f"gl_STm{i}", [128, 128]) for i in range(2)]; b_STm = [Buf(), Buf()]
        V4 = [bft(f"gl_V4{i}", [128, 64]) for i in range(2)]; b_V4 = [Buf(), Buf()]
        KT = [bft(f"gl_KT{i}", [128, 128]) for i in range(2)]; b_KT = [Buf(), Buf()]
        OALL = f32t("gl_OALL", [128, 16, 64]); b_OALL = Buf()
        osq = f32t("gl_osq", [128, 16, 64]); b_osq = Buf()
        ss = f32t("gl_ss", [128, 16]); b_ss = Buf()
        ONALL = bft("gl_ON", [128, 16, 64]); b_ON = Buf()
        slA, bwA = load_w(k, k.w_in_d[l][:, 1568:2080], 512)
        slB, bwB = load_w(k, k.w_in_d[l][:, 2080:2352], 272)
        P.dma("sp", lambda e: e.dma_start(out=a2[:], in_=k.gla_a2_d[l]), writes=[b_a2])
        P.op("dve", lambda e: e.memset(ones32[:], 1.0), writes=[b_ones])
        P.op("dve", lambda e: e.memset(S4f[:], 0.0), writes=[b_S4f])
        P.op("dve", lambda e: e.memset(S4b[:], 0.0), writes=[b_S4b])
        bd_iu = cs(k, "bd32_iu")
        it = 0
        for tb in range(NTB):
            sl = slice(tb * 512, (tb + 1) * 512)
            pz, bpz = proj_fm(k, slB, bwB, 256, 16, tb)
            P.op("act", lambda e, pz=pz: e.activation(out=zT[:], in_=pz[0:16, :], func=AF.Copy), reads=[bpz], writes=[b_zT])
            pla, bpla = nbank(k)
            P.op("pe", lambda e, pla=pla: e.matmul(pla[:], lhsT=a2[:], rhs=zT[:], start=True, stop=True), reads=[b_a2, b_zT], writes=[bpla])
            P.op("act", lambda e, pla=pla: e.activation(out=spl[:], in_=pla[:], func=AF.Exp, scale=-1.0, bias=pcol(k, "gla_ab", 0, neg=True)),
                 reads=[bpla, k.b_npv], writes=[b_spl])
            P.op("act", lambda e: e.activation(out=spl[:], in_=spl[:], func=AF.Ln, bias=1.0), reads=[b_spl], writes=[b_spl])
            for c in range(16):
                cc = slice(c * 32, (c + 1) * 32)
                P.op("dve", lambda e, cc=cc: e.tensor_tensor_scan(out=bcs[:, cc], data0=ones32[:], data1=spl[:, cc], initial=0.0,
                                                              op0=ALU.mult, op1=ALU.add),
                     reads=[b_ones, b_spl], writes=[b_bcs])
            P.op("act", lambda e: e.activation(out=Ep[:], in_=bcs[:], func=AF.Exp, scale=1.0 / 16.0), reads=[b_bcs], writes=[b_Ep])
            P.op("act", lambda e: e.activation(out=En[:], in_=bcs[:], func=AF.Exp, scale=-1.0 / 16.0), reads=[b_bcs], writes=[b_En])
            yield
            pq, bpq = proj_fm(k, slA, bwA, 0, 128, tb)
            P.op("dve", lambda e, pq=pq: e.scalar_tensor_tensor(out=qd[:], in0=pq[:], scalar=32.0 ** -0.5, in1=En[:], op0=ALU.mult, op1=ALU.mult),
                 reads=[bpq, b_En], writes=[b_qd])
            yield
            pk, bpk = proj_fm(k, slA, bwA, 128, 128, tb)
            P.op("dve", lambda e, pk=pk: e.tensor_tensor(out=ki[:], in0=pk[:], in1=Ep[:], op=ALU.mult), reads=[bpk, b_Ep], writes=[b_ki])
            yield
            for ch in range(2):
                pv, bpv = proj_fm(k, slA, bwA, 256 + ch * 128, 128, tb)
                P.op("act", lambda e, pv=pv, ch=ch: e.activation(out=vb[:, ch, :], in_=pv[:], func=AF.Copy), reads=[bpv], writes=[b_vb])
                pr, bpr = proj_fm(k, slB, bwB, ch * 128, 128, tb)
                P.op("act", lambda e, pr=pr, ch=ch: e.activation(out=sr[:, ch, :], in_=pr[:], func=AF.Silu), reads=[bpr], writes=[b_sr])
                yield
            for c in range(16):
                cc = slice(c * 32, (c + 1) * 32)
                j = it % 2
                it += 1
                pS, bpS = nbank(k, hold=True)
                P.op("dve", lambda e, pS=pS: e.memset(pS[:, 0:128], 0.0), writes=[bpS])
                yield
                for h in range(4):
                    hs = slice(h * 32, (h + 1) * 32)
                    P.op("pe", lambda e, pS=pS, hs=hs, cc=cc: mm(e, pS[hs, hs], ki[hs, cc], qd[hs, cc], (hs.start, hs.start)),
                         reads=[b_ki, b_qd], writes=[bpS], rt=hs.start)
                yield
                P.op("dve", lambda e, pS=pS, j=j: e.tensor_tensor(out=STm[j][:], in0=pS[:, 0:128], in1=bd_iu, op=ALU.mult),
                     reads=[bpS, k.b_cst], writes=[b_STm[j]])
                release(k, pS)
                pTr, bpTr = nbank(k, hold=True)
                pTb = k.psb[k.ps.index(pTr)]
                P.op("dve", lambda e, pTr=pTr: e.memset(pTr[:, 64:128], 0.0), writes=[bpTr])
                yield
                for h in range(4):
                    hs = slice(h * 32, (h + 1) * 32)
                    vs = slice((h % 2) * 64, (h % 2) * 64 + 64)
                    P.op("pe", lambda e, pTb=pTb, hs=hs, vs=vs, h=h, cc=cc: tr(e, pTb[hs, 0:64], vb[vs, h // 2, cc], identb[vs, vs], (vs.start, hs.start)),
                         reads=[b_vb, k.b_cstb], writes=[bpTr], rt=vs.start)
                for h in range(4):
                    hs = slice(h * 32, (h + 1) * 32)
                    P.op("pe", lambda e, pTb=pTb, hs=hs, cc=cc, h=h: tr(e, pTb[hs, 128 + h * 32:128 + (h + 1) * 32], ki[hs, cc], identb[hs, hs], (hs.start, hs.start)),
                         reads=[b_ki, k.b_cstb], writes=[bpTr], rt=hs.start)
                yield
                P.op("act", lambda e, pTb=pTb, j=j: e.activation(out=V4[j][:], in_=pTb[:, 0:64], func=AF.Copy), reads=[bpTr], writes=[b_V4[j]])
                P.op("dve", lambda e, pTb=pTb, j=j: e.tensor_copy(out=KT[j][:], in_=pTb[:, 128:256]),
                     reads=[bpTr], writes=[b_KT[j]])
                release(k, pTr)
                yield
                pO, bpO = nbank(k, hold=True)
                P.op("pe", lambda e, pO=pO, j=j: e.matmul(pO[:, 0:64], lhsT=STm[j][:], rhs=V4[j][:], start=True, stop=False),
                     reads=[b_STm[j], b_V4[j]], writes=[bpO])
                for h in range(4):
                    hs = slice(h * 32, (h + 1) * 32)
                    P.op("pe", lambda e, pO=pO, hs=hs, cc=cc, h=h: mm(e, pO[hs, 0:64], qd[hs, cc], S4b[hs, :], (hs.start, hs.start), start=False, stop=True),
                         reads=[b_qd, b_S4b], writes=[bpO], rt=hs.start)
                pSt, bpSt = nbank(k, hold=True)
                P.op("pe", lambda e, pSt=pSt, j=j: e.matmul(pSt[:, 0:64], lhsT=KT[j][:], rhs=V4[j][:], start=True, stop=True),
                     reads=[b_KT[j], b_V4[j]], writes=[bpSt])
                yield
                P.op("act", lambda e, pO=pO, c=c: e.activation(out=OALL[:, c, :], in_=pO[:, 0:64], func=AF.Copy), reads=[bpO], writes=[b_OALL])
                release(k, pO)
                P.op("dve", lambda e, pSt=pSt: e.tensor_tensor(out=S4f[:], in0=pSt[:, 0:64], in1=S4f[:], op=ALU.add), reads=[bpSt, b_S4f], writes=[b_S4f])
                release(k, pSt)
                yield
                P.op("dve", lambda e, c=c: e.tensor_scalar(out=S4f[:], in0=S4f[:], scalar1=En[:, c * 32 + 31:c * 32 + 32], scalar2=None, op0=ALU.mult),
                     reads=[b_S4f, b_En], writes=[b_S4f])
                yield
                P.op("act", lambda e: e.activation(out=S4b[:], in_=S4f[:], func=AF.Copy), reads=[b_S4f], writes=[b_S4b])
                yield
            P.op("dve", lambda e: e.tensor_tensor(out=osq[:], in0=OALL[:], in1=OALL[:], op=ALU.mult), reads=[b_OALL], writes=[b_osq])
            P.op("dve", lambda e: e.tensor_reduce(out=ss[:], in_=osq[:], axis=mybir.AxisListType.X, op=ALU.add), reads=[b_osq], writes=[b_ss])
            P.op("dve", lambda e: e.tensor_scalar(out=ss[:], in0=ss[:], scalar1=1.0 / 64.0, scalar2=1e-5, op0=ALU.mult, op1=ALU.add),
                 reads=[b_ss], writes=[b_ss])
            P.op("act", lambda e: e.activation(out=ss[:], in_=ss[:], func=AF.Ln), reads=[b_ss], writes=[b_ss])
            P.op("act", lambda e: e.activation(out=ss[:], in_=ss[:], func=AF.Exp, scale=-0.5), reads=[b_ss], writes=[b_ss])
            for c in range(16):
                P.op("dve", lambda e, c=c: e.tensor_scalar(out=ONALL[:, c, :], in0=OALL[:, c, :], scalar1=ss[:, c:c + 1], scalar2=None, op0=ALU.mult),
                     reads=[b_OALL, b_ss], writes=[b_ON])
            yield
            pF, bpF = nbank(k, hold=True)
            pFb = k.psb[k.ps.index(pF)]
            for c in range(16):
                if c % 4 == 0:
                    yield
                for h in range(4):
                    hs = slice(h * 32, (h + 1) * 32)
                    vs = slice((h % 2) * 64, (h % 2) * 64 + 64)
                    o0 = (h // 2) * 512 + c * 32
                    P.op("pe", lambda e, pFb=pFb, hs=hs, vs=vs, o0=o0, c=c: tr(e, pFb[vs, o0:o0 + 32], ONALL[hs, c, :], identb[hs, hs], (hs.start, vs.start)),
                         reads=[b_ON, k.b_cstb], writes=[bpF], rt=hs.start)
            for fc in range(2):
                P.op("dve", lambda e, pFb=pFb, fc=fc, sl=sl: e.scalar_tensor_tensor(
                    out=brT[:, 4 + fc, sl], in0=pFb[:, fc * 512:(fc + 1) * 512], scalar=pcol(k, "gla_ln_g", fc), in1=sr[:, fc, :],
                    op0=ALU.mult, op1=ALU.mult),
                    reads=[bpF, k.b_pv, b_sr], writes=[b_brT[4 + fc][tb]])
            release(k, pF)
            yield
        yield


def mixer_rwkv(k, l, brT, b_brT, s2):
    P = k.P; T = k.T
    NS = 256
    NSB = T // NS
    NCH = NS // 32
    identb = cs(k, "ident", bf=True)
    bones = cs(k, "bones64")
    if True:
        f32t = lambda n, shp: sbt(k, s2, n, shp, F32)
        bft = lambda n, shp: sbt(k, s2, n, shp, BF16)
        B = lambda: Buf()
        wa = f32t("rw_wa", [128, 256]); b_wa = B()
        g2a, g2b, b_g2 = k.g2a, k.g2b, k.b_g2
        omk = f32t("rw_omk", [128, 2]); b_omk = B()
        pprev = f32t("rw_pprev", [128, 9]); b_pprev = B()
        praw = [f32t(f"rw_praw{i}", [128, NS + 1]) for i in range(2)]; b_praw = [B(), B()]
        R = [f32t(f"rw_R{i}", [128, NS]) for i in range(2)]; b_R = [B(), B()]
        KX = [f32t(f"rw_KX{i}", [128, NS]) for i in range(2)]; b_KX = [B(), B()]
        V = [f32t(f"rw_V{i}", [128, NS]) for i in range(2)]; b_V = [B(), B()]
        XWA = f32t("rw_XWA", [128, NS]); b_XWA = B()
        XG0 = f32t("rw_XG0", [128, NS]); b_XG0 = B()
        XG1 = f32t("rw_XG1", [32, NS]); b_XG1 = B()
        sgx0 = bft("rw_sgx0", [128, NS]); sgx1 = bft("rw_sgx1", [32, NS]); b_sgx = B()
        EW = f32t("rw_EW", [128, NS]); b_EW = B()
        AL = f32t("rw_AL", [128, NS]); b_AL = B()
        GTs = [[bft(f"rw_GT{q}{i}", [128, NS]) for i in range(2)] for q in range(3)]; b_GTs = [[B(), B()] for q in range(3)]
        KKN = f32t("rw_KKN", [128, NS]); b_KKN = B()
        TMP = f32t("rw_TMP", [128, NS]); b_TMP = B()
        CS = f32t("rw_CS", [128, NS]); b_CS = B()
        E1 = f32t("rw_E1", [128, NS]); b_E1 = B()
        E2 = f32t("rw_E2", [128, NS]); b_E2 = B()
        E3 = f32t("rw_E3", [128, NS]); b_E3 = B()
        WCs = [[f32t(f"rw_WC{q}{i}", [128, NCH]) for i in range(2)] for q in range(2)]; b_WCs = [[B(), B()], [B(), B()]]
        BONs = [[f32t(f"rw_BON{q}{i}", [128, NS]) for i in range(2)] for q in range(3)]; b_BONs = [[B(), B()] for q in range(3)]
        ones32 = f32t("rw_ones", [128, 32]); b_ones = B()
        mk2 = lambda nm: ([[bft(f"rw_{nm}{q}{i}", [128, NS]) for i in range(2)] for q in range(2)], [[B(), B()], [B(), B()]])
        RHs, b_RHs = mk2("rh"); KHs, b_KHs = mk2("kh"); BHs, b_BHs = mk2("bh"); AHs, b_AHs = mk2("ah"); VBs, b_VBs = mk2("vb")
        M4 = [bft(f"rw_M4{i}", [128, 4, 128]) for i in range(2)]; b_M4 = [B(), B()]
        RKT = [bft(f"rw_RKT{i}", [128, 128]) for i in range(2)]; b_RKT = [B(), B()]
        Am = [bft(f"rw_A{i}", [128, 128]) for i in range(2)]; b_A = [B(), B()]
        ATm = [bft(f"rw_AT{i}", [128, 128]) for i in range(2)]; b_AT = [B(), B()]
        TT = [[bft(f"rw_TT{i}{q}", [128, 128]) for q in range(2)] for i in range(2)]; b_TT = [[B(), B()], [B(), B()]]
        BK = [bft(f"rw_BK{i}", [128, 4, 128]) for i in range(2)]; b_BK = [B(), B()]
        V4 = [bft(f"rw_V4{i}", [128, 64]) for i in range(2)]; b_V4 = [B(), B()]
        Xb = [bft(f"rw_Xb{i}", [128, 64]) for i in range(2)]; b_Xb = [B(), B()]
        Ub = [bft(f"rw_Ub{i}", [128, 64]) for i in range(2)]; b_Ub = [B(), B()]
        Hf = [f32t(f"rw_Hf{i}", [128, 64]) for i in range(2)]; b_Hf = [B(), B()]
        Hb = [bft(f"rw_Hb{i}", [128, 64]) for i in range(2)]; b_Hb = [B(), B()]
        YALLs = [f32t(f"rw_YALL{q}", [128, NCH, 64]) for q in range(2)]; b_YALLs = [B(), B()]
        ysq = f32t("rw_ysq", [128, NCH, 64]); b_ysq = B()
        s1 = f32t("rw_s1", [128, NCH]); b_s1 = B()
        s2_ = f32t("rw_s2", [128, NCH]); b_s2 = B()
        YN = bft("rw_YN", [128, NCH, 64]); b_YN = B()
        y1 = f32t("rw_y1", [128, NS]); b_y1 = B()
        masks = cs(k, "rwmask4")
        bd_iu = cs(k, "bd32_iu")

        slA, bwA = load_w(k, k.w_in_d[l][:, 0:512], 512)
        slB, bwB = load_w(k, k.w_in_d[l][:, 512:1024], 512)
        slC, bwC = k.wsm, k.b_wsm
        vC = k.w_in_d[l][:, 1024:1056].rearrange("(c p) n -> p c n", p=128)
        P.dma("pool", lambda e: e.dma_start(out=slC[:, :, :], in_=vC), writes=[bwC])
        P.dma("sp", lambda e: e.dma_start(out=wa[:], in_=k.rw_wa_d[l]), writes=[b_wa])
        P.dma("pool", lambda e: e.dma_start(out=g2a[:], in_=k.rw_g2_d[l][0:128, :]), writes=[b_g2])
        P.dma("pool", lambda e: e.dma_start(out=g2b[:], in_=k.rw_g2_d[l][128:160, :]), writes=[b_g2])
        oka, _ = PV["rw_ka"]
        P.op("dve", lambda e: e.tensor_scalar(out=omk[:], in0=k.pv[:, oka:oka + 2], scalar1=-1.0, scalar2=1.0, op0=ALU.mult, op1=ALU.add),
             reads=[k.b_pv], writes=[b_omk])
        P.op("dve", lambda e: e.memset(pprev[:], 0.0), writes=[b_pprev])
        P.op("dve", lambda e: e.memset(ones32[:], 1.0), writes=[b_ones])
        for hp in range(2):
            P.op("dve", lambda e, hp=hp: e.memset(Hf[hp][:], 0.0), writes=[b_Hf[hp]])
            P.op("dve", lambda e, hp=hp: e.memset(Hb[hp][:], 0.0), writes=[b_Hb[hp]])
        dests = [(R[0], b_R[0], 128), (R[1], b_R[1], 128), (KX[0], b_KX[0], 128), (KX[1], b_KX[1], 128),
                 (V[0], b_V[0], 128), (V[1], b_V[1], 128), (XWA, b_XWA, 128), (XG0, b_XG0, 128), (XG1, b_XG1, 32)]
        ipc = [0]

        def proj_n(slot, bw, col0, ncols, s0):
            tb = s0 // 512
            pt, bp = nbank(k)
            for c in range(KC):
                P.op("pe", lambda e, c=c: e.matmul(pt[0:ncols, 0:NS], lhsT=slot[:, c, col0:col0 + ncols], rhs=k.xT[:, c, s0:s0 + NS],
                                                   start=(c == 0), stop=(c == KC - 1)),
                     reads=[bw, k.b_xT[c][tb]], writes=[bp])
            return pt, bp

        def subblock(sb):
            s0 = sb * NS
            tb = s0 // 512
            p_ = sb % 2
            rh, kh, bh, ah, vb = RHs[p_], KHs[p_], BHs[p_], AHs[p_], VBs[p_]
            b_rh, b_kh, b_bh, b_ah, b_vb = b_RHs[p_], b_KHs[p_], b_BHs[p_], b_AHs[p_], b_VBs[p_]
            WC, b_WC = WCs[p_], b_WCs[p_]
            BON, b_BON, GT, b_GT = BONs[sb % 3], b_BONs[sb % 3], GTs[sb % 3], b_GTs[sb % 3]
            YALL, b_YALL = YALLs[p_], b_YALLs[p_]

            def prep():
                for f, (dst, bd, rows) in enumerate(dests):
                    if f < 4:
                        pt, bp = proj_n(slA, bwA, f * 128, 128, s0)
                    elif f < 8:
                        pt, bp = proj_n(slB, bwB, (f - 4) * 128, 128, s0)
                    else:
                        pt, bp = proj_n(slC, bwC, 0, 32, s0)
                    j = ipc[0] % 2
                    ipc[0] += 1
                    P.op("dve", lambda e, j=j, f=f, rows=rows: e.tensor_copy(out=praw[j][0:rows, 0:1], in_=pprev[0:rows, f:f + 1]),
                         reads=[b_pprev], writes=[b_praw[j]])
                    P.op("act", lambda e, j=j, pt=pt, rows=rows: e.activation(out=praw[j][0:rows, 1:NS + 1], in_=pt[0:rows, 0:NS], func=AF.Copy),
                         reads=[bp], writes=[b_praw[j]])
                    P.op("dve", lambda e, j=j, f=f, rows=rows: e.tensor_copy(out=pprev[0:rows, f:f + 1], in_=praw[j][0:rows, NS:NS + 1]),
                         reads=[b_praw[j]], writes=[b_pprev])
                    P.op("dve", lambda e, j=j, rows=rows, dst=dst: e.tensor_tensor(out=dst[0:rows, :], in0=praw[j][0:rows, 0:NS], in1=praw[j][0:rows, 1:NS + 1], op=ALU.subtract),
                         reads=[b_praw[j]], writes=[bd])
                    P.op("dve", lambda e, j=j, rows=rows, f=f, dst=dst: e.scalar_tensor_tensor(
                        out=dst[0:rows, :], in0=dst[0:rows, :], scalar=pcol(k, "rw_mu", f, rows=rows), in1=praw[j][0:rows, 1:NS + 1],
                        op0=ALU.mult, op1=ALU.add),
                        reads=[bd, b_praw[j], k.b_pv], writes=[bd])
                    yield
                P.op("act", lambda e: e.activation(out=sgx0[:], in_=XG0[:], func=AF.Sigmoid), reads=[b_XG0], writes=[b_sgx])
                P.op("act", lambda e: e.activation(out=sgx1[:], in_=XG1[:], func=AF.Sigmoid), reads=[b_XG1], writes=[b_sgx])
                for hp in range(2):
                    pg, bpg = nbank(k)
                    P.op("pe", lambda e, pg=pg, hp=hp: e.matmul(pg[:, 0:NS], lhsT=g2a[:, hp * 128:(hp + 1) * 128], rhs=sgx0[:], start=True, stop=False),
                         reads=[b_g2, b_sgx], writes=[bpg])
                    P.op("pe", lambda e, pg=pg, hp=hp: e.matmul(pg[:, 0:NS], lhsT=g2b[:, hp * 128:(hp + 1) * 128], rhs=sgx1[:], start=False, stop=True),
                         reads=[b_g2, b_sgx], writes=[bpg])
                    P.op("act", lambda e, pg=pg, hp=hp: e.activation(out=GT[hp][:], in_=pg[:, 0:NS], func=AF.Copy), reads=[bpg], writes=[b_GT[hp]])
                    yield
                P.op("act", lambda e: e.activation(out=XWA[0:64, :], in_=XWA[0:64, :], func=AF.Tanh), reads=[b_XWA], writes=[b_XWA])
                for hp in range(2):
                    hs = slice(hp * 128, (hp + 1) * 128)
                    pw, bpw = nbank(k)
                    P.op("pe", lambda e, pw=pw, hs=hs: mm(e, pw[:, 0:NS], wa[0:64, hs], XWA[0:64, :], (0, 0)), reads=[b_wa, b_XWA], writes=[bpw], rt=0)
                    P.op("act", lambda e, pw=pw, hp=hp: e.activation(out=EW[:], in_=pw[:, 0:NS], func=AF.Exp, scale=-1.0, bias=pcol(k, "rw_w0", hp, neg=True)),
                         reads=[bpw, k.b_npv], writes=[b_EW])
                    P.op("act", lambda e: e.activation(out=EW[:], in_=EW[:], func=AF.Ln, bias=1.0), reads=[b_EW], writes=[b_EW])
                    P.op("act", lambda e: e.activation(out=EW[:], in_=EW[:], func=AF.Exp, scale=-1.0, bias=-0.5), reads=[b_EW], writes=[b_EW])
                    yield
                    pa, bpa = nbank(k)
                    P.op("pe", lambda e, pa=pa, hs=hs: mm(e, pa[:, 0:NS], wa[64:128, hs], XWA[64:128, :], (64, 0)), reads=[b_wa, b_XWA], writes=[bpa], rt=64)
                    P.op("act", lambda e, pa=pa, hp=hp: e.activation(out=AL[:], in_=pa[:, 0:NS], func=AF.Sigmoid, bias=pcol(k, "rw_a0", hp)),
                         reads=[bpa, k.b_pv], writes=[b_AL])
                    yield
                    P.op("dve", lambda e, hp=hp: e.tensor_scalar(out=KKN[:], in0=KX[hp][:], scalar1=pcol(k, "rw_kk", hp), scalar2=None, op0=ALU.mult),
                         reads=[b_KX[hp], k.b_pv], writes=[b_KKN])
                    P.op("dve", lambda e: e.tensor_tensor(out=TMP[:], in0=KKN[:], in1=KKN[:], op=ALU.mult), reads=[b_KKN], writes=[b_TMP])
                    pss, bpss = nbank(k)
                    P.op("pe", lambda e, pss=pss: e.matmul(pss[:, 0:NS], lhsT=bones, rhs=TMP[:], start=True, stop=True), reads=[k.b_cst, b_TMP], writes=[bpss])
                    P.op("act", lambda e, pss=pss: e.activation(out=TMP[:], in_=pss[:, 0:NS], func=AF.Ln), reads=[bpss], writes=[b_TMP])
                    P.op("act", lambda e: e.activation(out=TMP[:], in_=TMP[:], func=AF.Exp, scale=-0.5), reads=[b_TMP], writes=[b_TMP])
                    P.op("dve", lambda e: e.tensor_tensor(out=KKN[:], in0=KKN[:], in1=TMP[:], op=ALU.mult), reads=[b_KKN, b_TMP], writes=[b_KKN])
                    yield
                    P.op("dve", lambda e, hp=hp: e.tensor_scalar(out=TMP[:], in0=AL[:], scalar1=pcol(k, "rw_ka", hp), scalar2=omk[:, hp:hp + 1], op0=ALU.mult, op1=ALU.add),
                         reads=[b_AL, k.b_pv, b_omk], writes=[b_TMP])
                    P.op("dve", lambda e, hp=hp: e.tensor_tensor(out=KX[hp][:], in0=KX[hp][:], in1=TMP[:], op=ALU.mult), reads=[b_KX[hp], b_TMP], writes=[b_KX[hp]])
                    P.op("dve", lambda e, hp=hp: e.scalar_tensor_tensor(out=TMP[:], in0=R[hp][:], scalar=pcol(k, "rw_rk", hp), in1=KX[hp][:], op0=ALU.mult, op1=ALU.mult),
                         reads=[b_R[hp], b_KX[hp], k.b_pv], writes=[b_TMP])
                    pbo, bpbo = nbank(k)
                    P.op("pe", lambda e, pbo=pbo: e.matmul(pbo[:, 0:NS], lhsT=bones, rhs=TMP[:], start=True, stop=True), reads=[k.b_cst, b_TMP], writes=[bpbo])
                    P.op("dve", lambda e, pbo=pbo, hp=hp: e.tensor_tensor(out=BON[hp][:], in0=pbo[:, 0:NS], in1=V[hp][:], op=ALU.mult),
                         reads=[bpbo, b_V[hp]], writes=[b_BON[hp]])
                    yield
                    P.op("act", lambda e, hp=hp: e.activation(out=vb[hp][:], in_=V[hp][:], func=AF.Copy), reads=[b_V[hp]], writes=[b_vb[hp]])
                    for c in range(NCH):
                        cc = slice(c * 32, (c + 1) * 32)
                        P.op("dve", lambda e, cc=cc: e.tensor_tensor_scan(out=CS[:, cc], data0=ones32[:], data1=EW[:, cc], initial=0.0, op0=ALU.mult, op1=ALU.add),
                             reads=[b_ones, b_EW], writes=[b_CS])
                    P.op("act", lambda e: e.activation(out=E1[:], in_=CS[:], func=AF.Exp), reads=[b_CS], writes=[b_E1])
                    P.op("act", lambda e: e.activation(out=E2[:], in_=CS[:], func=AF.Exp, scale=-1.0), reads=[b_CS], writes=[b_E2])
                    P.op("dve", lambda e: e.tensor_tensor(out=TMP[:], in0=EW[:], in1=CS[:], op=ALU.subtract), reads=[b_EW, b_CS, b_TMP], writes=[b_TMP])
                    P.op("act", lambda e: e.activation(out=E3[:], in_=TMP[:], func=AF.Exp), reads=[b_TMP], writes=[b_E3])
                    yield
                    E2v = E2[:].rearrange("p (c t) -> p c t", t=32)
                    P.op("dve", lambda e, hp=hp, E2v=E2v: e.tensor_copy(out=WC[hp][:], in_=E2v[:, :, 31]), reads=[b_E2], writes=[b_WC[hp]])
                    P.op("dve", lambda e, hp=hp: e.tensor_tensor(out=rh[hp][:], in0=R[hp][:], in1=E2[:], op=ALU.mult), reads=[b_R[hp], b_E2], writes=[b_rh[hp]])
                    P.op("dve", lambda e, hp=hp: e.tensor_tensor(out=kh[hp][:], in0=KX[hp][:], in1=E1[:], op=ALU.mult), reads=[b_KX[hp], b_E1], writes=[b_kh[hp]])
                    P.op("dve", lambda e: e.tensor_tensor(out=TMP[:], in0=KKN[:], in1=AL[:], op=ALU.mult), reads=[b_KKN, b_AL], writes=[b_TMP])
                    P.op("dve", lambda e, hp=hp: e.tensor_tensor(out=bh[hp][:], in0=TMP[:], in1=E1[:], op=ALU.mult), reads=[b_TMP, b_E1], writes=[b_bh[hp]])
                    P.op("dve", lambda e, hp=hp: e.scalar_tensor_tensor(out=ah[hp][:], in0=KKN[:], scalar=-1.0, in1=E3[:], op0=ALU.mult, op1=ALU.mult),
                         reads=[b_KKN, b_E3], writes=[b_ah[hp]])
                    yield
                yield

            def chunks():
                def A_gen(c):
                    cc = slice(c * 32, (c + 1) * 32)
                    j = c % 2
                    pX_, bpX_ = nbank(k, hold=True)
                    pY_, bpY_ = nbank(k, hold=True)
                    P.op("dve", lambda e: e.memset(pX_[:], 0.0), writes=[bpX_])
                    P.op("dve", lambda e: e.memset(pY_[:, 0:128], 0.0), writes=[bpY_])
                    yield
                    for h in range(4):
                        hp = h // 2
                        ks = slice((h % 2) * 64, (h % 2) * 64 + 64)
                        hs = slice(h * 32, (h + 1) * 32)
                        pos = (ks.start, hs.start)
                        for mi, (lt, blt, rt_, brt) in enumerate(((bh, b_bh, ah, b_ah), (ah, b_ah, bh, b_bh), (kh, b_kh, ah, b_ah), (bh, b_bh, rh, b_rh))):
                            P.op("pe", lambda e, hs=hs, ks=ks, hp=hp, mi=mi, lt=lt, rt_=rt_, pos=pos, h=h: mm(
                                e, pX_[hs, mi * 128 + h * 32: mi * 128 + (h + 1) * 32], lt[hp][ks, cc], rt_[hp][ks, cc], pos),
                                reads=[blt[hp], brt[hp]], writes=[bpX_], rt=pos[0])
                        P.op("pe", lambda e, hs=hs, ks=ks, hp=hp, pos=pos, h=h: mm(
                            e, pY_[hs, h * 32:(h + 1) * 32], kh[hp][ks, cc], rh[hp][ks, cc], pos),
                            reads=[b_kh[hp], b_rh[hp]], writes=[bpY_], rt=pos[0])
                    yield
                    P.op("dve", lambda e: e.tensor_tensor(out=M4[j][:].rearrange("p a b -> p (a b)"), in0=pX_[:], in1=masks, op=ALU.mult),
                         reads=[bpX_, k.b_cst], writes=[b_M4[j]])
                    P.op("dve", lambda e: e.tensor_tensor(out=RKT[j][:], in0=pY_[:, 0:128], in1=bd_iu, op=ALU.mult),
                         reads=[bpY_, k.b_cst], writes=[b_RKT[j]])
                    release(k, pX_); release(k, pY_)
                    LT = M4[j][:, 0, :]; Lm = M4[j][:, 1, :]
                    TTj = TT[j]
                    bTTj = b_TT[j]
                    P.op("dve", lambda e: e.tensor_tensor(out=TTj[0][:], in0=LT, in1=identb, op=ALU.add), reads=[b_M4[j], k.b_cstb], writes=[bTTj[0]])
                    yield
                    A_prev, bA_prev, AT_prev, bAT_prev = Lm, b_M4[j], LT, b_M4[j]
                    ti = 0
                    for kq in range(1, 5):
                        an = kq % 2
                        pA, bpA = nbank(k, hold=True)
                        P.op("pe", lambda e, pA=pA, AT_prev=AT_prev, A_prev=A_prev: e.matmul(pA[:, 0:128], lhsT=AT_prev, rhs=A_prev, start=True, stop=True),
                             reads=[bA_prev, bAT_prev], writes=[bpA])
                        if kq < 4:
                            P.op("pe", lambda e, pA=pA, AT_prev=AT_prev, A_prev=A_prev: e.matmul(pA[:, 128:256], lhsT=A_prev, rhs=AT_prev, start=True, stop=True),
                                 reads=[bA_prev, bAT_prev], writes=[bpA])
                        yield
                        P.op("act", lambda e, pA=pA, an=an: e.activation(out=Am[an][:], in_=pA[:, 0:128], func=AF.Copy), reads=[bpA], writes=[b_A[an]])
                        if kq < 4:
                            P.op("act", lambda e, pA=pA, an=an: e.activation(out=ATm[an][:], in_=pA[:, 128:256], func=AF.Copy), reads=[bpA], writes=[b_AT[an]])
                        release(k, pA)
                        yield
                        pT, bpT = nbank(k, hold=True)
                        P.op("pe", lambda e, pT=pT, an=an, ti=ti: e.matmul(pT[:, 0:128], lhsT=Am[an][:], rhs=TTj[ti][:], start=True, stop=True),
                             reads=[b_A[an], bTTj[ti]], writes=[bpT])
                        yield
                        P.op("dve", lambda e, pT=pT, ti=ti: e.tensor_tensor(out=TTj[1 - ti][:], in0=pT[:, 0:128], in1=TTj[ti][:], op=ALU.add),
                             reads=[bpT, bTTj[ti]], writes=[bTTj[1 - ti]])
                        release(k, pT)
                        ti = 1 - ti
                        A_prev, bA_prev, AT_prev, bAT_prev = Am[an][:], b_A[an], ATm[an][:], b_AT[an]
                        yield
                    assert ti == 0
                    pTr, bpTr = nbank(k, hold=True)
                    pTb = k.psb[k.ps.index(pTr)]
                    P.op("dve", lambda e: e.memset(pTr[:, 0:256], 0.0), writes=[bpTr])
                    yield
                    for h in range(4):
                        hp = h // 2
                        ks = slice((h % 2) * 64, (h % 2) * 64 + 64)
                        hs = slice(h * 32, (h + 1) * 32)
                        pos = (ks.start, hs.start)
                        P.op("pe", lambda e, hs=hs, ks=ks, hp=hp, pos=pos: tr(e, pTb[hs, hp * 128 + ks.start: hp * 128 + ks.start + 64], bh[hp][ks, cc], identb[ks, ks], pos),
                             reads=[b_bh[hp], k.b_cstb], writes=[bpTr], rt=pos[0])
                        P.op("pe", lambda e, hs=hs, ks=ks, hp=hp, pos=pos: tr(e, pTb[hs, 256 + hp * 128 + ks.start: 256 + hp * 128 + ks.start + 64], kh[hp][ks, cc], identb[ks, ks], pos),
                             reads=[b_kh[hp], k.b_cstb], writes=[bpTr], rt=pos[0])
                        P.op("pe", lambda e, hs=hs, ks=ks, hp=hp, pos=pos: tr(e, pTb[hs, 512:576], vb[hp][ks, cc], identb[ks, ks], pos),
                             reads=[b_vb[hp], k.b_cstb], writes=[bpTr], rt=pos[0])
                    yield
                    P.op("dve", lambda e: e.tensor_copy(out=BK[j][:].rearrange("p a b -> p (a b)"), in_=pTb[:, 0:512]),
                         reads=[bpTr], writes=[b_BK[j]])
                    P.op("act", lambda e: e.activation(out=V4[j][:], in_=pTb[:, 512:576], func=AF.Copy), reads=[bpTr], writes=[b_V4[j]])
                    release(k, pTr)
                    yield

                def B_gen(c):
                    cc = slice(c * 32, (c + 1) * 32)
                    j = c % 2
                    AKT = M4[j][:, 2, :]; RBT = M4[j][:, 3, :]
                    TTf, bTTf = TT[j][0], b_TT[j][0]
                    pX, bpX = nbank(k, hold=True)
                    P.op("pe", lambda e: e.matmul(pX[:, 0:64], lhsT=AKT, rhs=V4[j][:], start=True, stop=False),
                         reads=[b_M4[j], b_V4[j]], writes=[bpX])
                    for h in range(4):
                        hp = h // 2
                        ks = slice((h % 2) * 64, (h % 2) * 64 + 64)
                        hs = slice(h * 32, (h + 1) * 32)
                        P.op("pe", lambda e, hs=hs, ks=ks, hp=hp: mm(e, pX[hs, 0:64], ah[hp][ks, cc], Hb[hp][ks, :], (ks.start, hs.start), start=False, stop=True),
                             reads=[b_ah[hp], b_Hb[hp]], writes=[bpX], rt=ks.start)
                    yield
                    P.op("act", lambda e: e.activation(out=Xb[j][:], in_=pX[:, 0:64], func=AF.Copy), reads=[bpX], writes=[b_Xb[j]])
                    release(k, pX)
                    yield
                    pU, bpU = nbank(k, hold=True)
                    P.op("pe", lambda e: e.matmul(pU[:, 0:64], lhsT=TTf[:], rhs=Xb[j][:], start=True, stop=True),
                         reads=[bTTf, b_Xb[j]], writes=[bpU])
                    yield
                    P.op("act", lambda e: e.activation(out=Ub[j][:], in_=pU[:, 0:64], func=AF.Copy), reads=[bpU], writes=[b_Ub[j]])
                    release(k, pU)
                    yield
                    pHs = []
                    for hp in range(2):
                        pH, bpH = nbank(k, hold=True)
                        pHs.append((pH, bpH))
                        P.op("pe", lambda e, pH=pH, hp=hp: e.matmul(pH[:, 0:64], lhsT=BK[j][:, hp, :], rhs=Ub[j][:], start=True, stop=False),
                             reads=[b_BK[j], b_Ub[j]], writes=[bpH])
                        P.op("pe", lambda e, pH=pH, hp=hp: e.matmul(pH[:, 0:64], lhsT=BK[j][:, 2 + hp, :], rhs=V4[j][:], start=False, stop=True),
                             reads=[b_BK[j], b_V4[j]], writes=[bpH])
                    pY, bpY = nbank(k, hold=True)
                    P.op("pe", lambda e: e.matmul(pY[:, 0:64], lhsT=RBT, rhs=Ub[j][:], start=True, stop=False),
                         reads=[b_M4[j], b_Ub[j]], writes=[bpY])
                    P.op("pe", lambda e: e.matmul(pY[:, 0:64], lhsT=RKT[j][:], rhs=V4[j][:], start=False, stop=False),
                         reads=[b_RKT[j], b_V4[j]], writes=[bpY])
                    for h in range(4):
                        hp = h // 2
                        ks = slice((h % 2) * 64, (h % 2) * 64 + 64)
                        hs = slice(h * 32, (h + 1) * 32)
                        P.op("pe", lambda e, hs=hs, ks=ks, hp=hp: mm(e, pY[hs, 0:64], rh[hp][ks, cc], Hb[hp][ks, :], (ks.start, hs.start), start=False, stop=True),
                             reads=[b_rh[hp], b_Hb[hp]], writes=[bpY], rt=ks.start)
                    yield
                    for hp in range(2):
                        pH, bpH = pHs[hp]
                        P.op("dve", lambda e, pH=pH, hp=hp: e.tensor_tensor(out=Hf[hp][:], in0=pH[:, 0:64], in1=Hf[hp][:], op=ALU.add),
                             reads=[bpH, b_Hf[hp]], writes=[b_Hf[hp]])
                        release(k, pH)
                    P.op("act", lambda e: e.activation(out=YALL[:, c, :], in_=pY[:, 0:64], func=AF.Copy), reads=[bpY], writes=[b_YALL])
                    release(k, pY)
                    yield
                    for hp in range(2):
                        P.op("dve", lambda e, hp=hp: e.tensor_scalar(out=Hf[hp][:], in0=Hf[hp][:], scalar1=WC[hp][:, c:c + 1], scalar2=None, op0=ALU.mult),
                             reads=[b_Hf[hp], b_WC[hp]], writes=[b_Hf[hp]])
                    yield
                    for hp in range(2):
                        P.op("act", lambda e, hp=hp: e.activation(out=Hb[hp][:], in_=Hf[hp][:], func=AF.Copy), reads=[b_Hf[hp]], writes=[b_Hb[hp]])
                    yield

                for _ in A_gen(0):
                    yield
                for c in range(NCH):
                    gens = [B_gen(c)]
                    if c + 1 < NCH:
                        gens.append(A_gen(c + 1))
                    while gens:
                        for g_ in list(gens):
                            try:
                                next(g_)
                            except StopIteration:
                                gens.remove(g_)
                        yield
                yield

            def post():
                P.op("dve", lambda e: e.tensor_reduce(out=s1[:], in_=YALL[:], axis=mybir.AxisListType.X, op=ALU.add), reads=[b_YALL], writes=[b_s1])
                P.op("dve", lambda e: e.tensor_tensor(out=ysq[:], in0=YALL[:], in1=YALL[:], op=ALU.mult), reads=[b_YALL], writes=[b_ysq])
                P.op("dve", lambda e: e.tensor_reduce(out=s2_[:], in_=ysq[:], axis=mybir.AxisListType.X, op=ALU.add), reads=[b_ysq], writes=[b_s2])
                P.op("dve", lambda e: e.tensor_scalar(out=s1[:], in0=s1[:], scalar1=1.0 / 64.0, scalar2=None, op0=ALU.mult), reads=[b_s1], writes=[b_s1])
                P.op("dve", lambda e: e.scalar_tensor_tensor(out=s2_[:], in0=s2_[:], scalar=1.0 / 64.0, in1=s2_[:], op0=ALU.mult, op1=ALU.bypass) if False else
                     e.tensor_scalar(out=s2_[:], in0=s2_[:], scalar1=1.0 / 64.0, scalar2=64e-5, op0=ALU.mult, op1=ALU.add), reads=[b_s2], writes=[b_s2])
                P.op("dve", lambda e: e.tensor_tensor(out=ysq[:, :, 0], in0=s1[:], in1=s1[:], op=ALU.mult), reads=[b_s1, b_ysq], writes=[b_ysq])
                P.op("dve", lambda e: e.tensor_tensor(out=s2_[:], in0=s2_[:], in1=ysq[:, :, 0], op=ALU.subtract), reads=[b_s2, b_ysq], writes=[b_s2])
                P.op("act", lambda e: e.activation(out=s2_[:], in_=s2_[:], func=AF.Ln), reads=[b_s2], writes=[b_s2])
                P.op("act", lambda e: e.activation(out=s2_[:], in_=s2_[:], func=AF.Exp, scale=-0.5), reads=[b_s2], writes=[b_s2])
                P.op("dve", lambda e: e.scalar_tensor_tensor(out=s1[:], in0=s1[:], scalar=-1.0, in1=s2_[:], op0=ALU.mult, op1=ALU.mult), reads=[b_s1, b_s2], writes=[b_s1])
                for c in range(NCH):
                    P.op("act", lambda e, c=c: e.activation(out=YN[:, c, :], in_=YALL[:, c, :], func=AF.Identity, scale=s2_[:, c:c + 1], bias=s1[:, c:c + 1]),
                         reads=[b_YALL, b_s1, b_s2], writes=[b_YN])
                yield
                pF, bpF = nbank(k, hold=True)
                pFb = k.psb[k.ps.index(pF)]
                for c in range(NCH):
                    if c % 4 == 0:
                        yield
                    for h in range(4):
                        hs = slice(h * 32, (h + 1) * 32)
                        vs = slice((h % 2) * 64, (h % 2) * 64 + 64)
                        o0 = (h // 2) * 512 + c * 32
                        P.op("pe", lambda e, pFb=pFb, hs=hs, vs=vs, o0=o0, c=c: tr(e, pFb[vs, o0:o0 + 32], YN[hs, c, :], identb[hs, hs], (hs.start, vs.start)),
                             reads=[b_YN, k.b_cstb], writes=[bpF], rt=hs.start)
                for hp in range(2):
                    P.op("act", lambda e, pFb=pFb, hp=hp: e.activation(out=y1[:], in_=pFb[:, hp * 512: hp * 512 + NS], func=AF.Identity,
                                                                   scale=pcol(k, "rw_ln_g", hp), bias=pcol(k, "rw_ln_b", hp)),
                         reads=[bpF, k.b_pv], writes=[b_y1])
                    P.op("dve", lambda e, hp=hp: e.tensor_tensor(out=y1[:], in0=y1[:], in1=BON[hp][:], op=ALU.add), reads=[b_y1, b_BON[hp]], writes=[b_y1])
                    P.op("dve", lambda e, hp=hp, s0=s0: e.tensor_tensor(out=brT[:, hp, s0:s0 + NS], in0=y1[:], in1=GT[hp][:], op=ALU.mult),
                         reads=[b_y1, b_GT[hp]], writes=[b_brT[hp][tb]])
                release(k, pF)
                yield
                yield

            return prep(), chunks(), post()

        phases = [subblock(sb) for sb in range(NSB)]
        for _ in phases[0][0]:
            yield
        for sb in range(NSB):
            gl = [phases[sb][1]]
            wl = [230.0]
            if sb + 1 < NSB:
                gl.append(phases[sb + 1][0]); wl.append(30.0)
            if sb >= 1:
                gl.append(phases[sb - 1][2]); wl.append(8.0)
            done = [0.0] * len(gl)
            live = list(range(len(gl)))
            while live:
                i_ = min(live, key=lambda q: done[q] / wl[q])
                try:
                    next(gl[i_])
                    done[i_] += 1.0
                except StopIteration:
                    live.remove(i_)
                yield
        for _ in phases[NSB - 1][2]:
            yield
        yield

def stage_gate(k, l, brT, b_brT):
    P = k.P; T = k.T; NTB = k.NTB
    with ExitStack() as s2:
        mg = sbt(k, s2, "mg", [128, 4, T], BF16)
        b_mg = [[Buf() for _ in range(NTB)] for _ in range(4)]
        acc = sbt(k, s2, "mg_acc", [128, 512], F32); b_acc = Buf()
        sg = [sbt(k, s2, f"mg_sg{i}", [128, 512], F32) for i in range(2)]; b_sg = [Buf(), Buf()]
        pr0 = sbt(k, s2, "mg_pr", [128, 512], F32); pr = [pr0, pr0]; bpr0 = Buf(); b_pr = [bpr0, bpr0]
        ups, b_ups = k.ups, k.b_ups
        lnb = alloc_ln(k, s2)
        og, _ = PV["gate_b"]
        i = 0
        for half in range(2):
            for fq in range(4):
                fc = half * 4 + fq
                u = fc % 2
                for b in range(4):
                    v = k.ups_d[b][l][:, fc * 128:(fc + 1) * 128].rearrange("(c p) n -> p c n", p=128)
                    P.dma("pool", lambda e, b=b, v=v, u=u: e.dma_start(out=ups[u][:, :, b, :], in_=v), writes=[b_ups[u]])
                i0 = k.wr_i
                k.wr_i = (i0 + 1) % k.NW
                slot, bw = k.wr[i0], k.b_wr[i0]
                for b in range(4):
                    c0 = COL_GATE + b * D + fc * 128
                    v = k.w_in_d[l][:, c0:c0 + 128].rearrange("(c p) n -> p c n", p=128)
                    P.dma("pool", lambda e, b=b, v=v, slot=slot: e.dma_start(out=slot[:, :, b * 128:(b + 1) * 128], in_=v), writes=[bw])
                for tb in range(NTB):
                    sl = slice(tb * 512, (tb + 1) * 512)
                    for b in range(4):
                        pg, bpg = nbank(k)
                        for c in range(KC):
                            P.op("pe", lambda e, pg=pg, c=c, b=b, sl=sl, slot=slot: e.matmul(
                                pg[:], lhsT=slot[:, c, b * 128:(b + 1) * 128], rhs=k.xT[:, c, sl], start=(c == 0), stop=(c == KC - 1)),
                                reads=[bw, k.b_xT[c][tb]], writes=[bpg])
                        pu, bpu = nbank(k)
                        for c in range(2):
                            P.op("pe", lambda e, pu=pu, c=c, b=b, sl=sl, u=u: e.matmul(
                                pu[:], lhsT=ups[u][:, c, b, :], rhs=brT[:, b * 2 + c, sl], start=(c == 0), stop=(c == 1)),
                                reads=[b_ups[u], b_brT[b * 2 + c][tb]], writes=[bpu])
                        j = i % 2
                        i += 1
                        gcol = k.pv[:, og + b * 8 + fc: og + b * 8 + fc + 1]
                        P.op("act", lambda e, pg=pg, j=j, gcol=gcol: e.activation(out=sg[j][:], in_=pg[:], func=AF.Sigmoid, bias=gcol),
                             reads=[bpg, k.b_pv], writes=[b_sg[j]])
                        if b == 0:
                            P.op("dve", lambda e, pu=pu, j=j: e.tensor_tensor(out=acc[:], in0=pu[:], in1=sg[j][:], op=ALU.mult),
                                 reads=[bpu, b_sg[j]], writes=[b_acc])
                        else:
                            P.op("dve", lambda e, pu=pu, j=j: e.tensor_tensor(out=pr[j][:], in0=pu[:], in1=sg[j][:], op=ALU.mult),
                                 reads=[bpu, b_sg[j]], writes=[b_pr[j]])
                            if b < 3:
                                P.op("dve", lambda e, j=j: e.tensor_tensor(out=acc[:], in0=acc[:], in1=pr[j][:], op=ALU.add),
                                     reads=[b_acc, b_pr[j]], writes=[b_acc])
                            else:
                                P.op("dve", lambda e, j=j, fq=fq, sl=sl: e.tensor_tensor(out=mg[:, fq, sl], in0=acc[:], in1=pr[j][:], op=ALU.add),
                                     reads=[b_acc, b_pr[j]], writes=[b_mg[fq][tb]])
            if "mg" in k.dbg_d:
                for c in range(4):
                    P.dma("sp", lambda e, c=c, half=half: e.dma_start(out=k.dbg_d["mg"][half * 4 + c], in_=mg[:, c, :]),
                          reads=[b_mg[c][tb] for tb in range(NTB)], is_output=True)
            out_proj_ln(k, l, 0, mg, b_mg, 4, k.w_out_d[l], first=(half == 0), last=(half == 1), row0=half * 512, lnbufs=lnb)
        P.barrier()


def run_concurrent(gens, weights):
    gens = list(gens)
    done = [0.0] * len(gens)
    live = list(range(len(gens)))
    while live:
        i = min(live, key=lambda q: done[q] / weights[q])
        try:
            next(gens[i])
            done[i] += 1.0
        except StopIteration:
            live.remove(i)


def stage_mix(k, l):
    P = k.P; T = k.T; NTB = k.NTB
    spill_xres(k)
    with ExitStack() as s2:
        brT = sbt(k, s2, "brT", [128, 8, T], BF16)
        b_brT = [[Buf() for _ in range(NTB)] for _ in range(8)]
        todo = k.mixers
        for nm, cs_ in (("rw", (0, 1)), ("cv", (2, 3)), ("gla", (4, 5)), ("fox", (6, 7))):
            if nm not in todo:
                for c in cs_:
                    P.op("dve", lambda e, c=c: e.memset(brT[:, c, :], 0.0), writes=[b_brT[c][tb] for tb in range(NTB)])
        fns = {"rw": mixer_rwkv, "gla": mixer_gla, "fox": mixer_fox, "cv": mixer_conv}
        for group in (("rw", "gla"), ("fox", "cv")):
            act = [n for n in group if n in todo]
            if not act:
                continue
            with ExitStack() as s3:
                wts = {"rw": 1753.0, "gla": 621.0, "fox": 341.0, "cv": 75.0}
                run_concurrent([fns[n](k, l, brT, b_brT, s3) for n in act], [wts[n] for n in act])
                P.barrier()
        if "brT" in k.dbg_d:
            for c in range(8):
                P.dma("sp", lambda e, c=c: e.dma_start(out=k.dbg_d["brT"][c], in_=brT[:, c, :]),
                      reads=[b_brT[c][tb] for tb in range(NTB)], is_output=True)
        reload_xres(k)
        stage_gate(k, l, brT, b_brT)


def prep_inputs(inp, L=DEPTH):
    f = lambda a: np.ascontiguousarray(np.asarray(a, dtype=np.float32))
    shared = {
        "consts": CONSTS,
        "pvec": np.stack([pack_pvec(inp, l) for l in range(L)]),
        "w_in": f(inp["w_in"][:L]),
        "rw_wa": f(np.concatenate([np.asarray(inp["rw_w2"][:L]), np.asarray(inp["rw_a2"][:L])], axis=1)),
        "rw_g2": f(inp["rw_g2"][:L]),
        "gla_a2": f(inp["gla_a2"][:L]),
        "w_out": f(inp["w_out"][:L]),
    }
    for n in ("rw_up", "cv_up", "gla_up", "fox_up", "xa_wq", "xa_wk", "xa_wv", "xa_wo", "ffn_w1", "ffn_w3", "ffn_w2"):
        shared[n] = f(inp[n][:L])
    return shared


_CACHE = {}


def kernel(**inputs):
    x = np.asarray(inputs["x"], np.float32)
    mem = np.asarray(inputs["mem"], np.float32)
    B = x.shape[0]
    if "nc" not in _CACHE:
        _CACHE["nc"] = build()[0]
    nc = _CACHE["nc"]
    shared = prep_inputs(inputs)
    in_maps = []
    for b in range(B):
        m = dict(shared)
        m["x"] = np.ascontiguousarray(x[b])
        m["mem"] = np.ascontiguousarray(mem[b])
        in_maps.append(m)
    res = run_bass_kernel_spmd(nc, in_maps, core_ids=list(range(B)))
    return np.stack([r["out"] for r in res.results], axis=0).astype(np.float32)
```

```python
import numpy as np
from contextlib import ExitStack
import concourse.bass as bass
import concourse.mybir as mybir
from concourse.bass_utils import run_bass_kernel_spmd

F32 = mybir.dt.float32
BF16 = mybir.dt.bfloat16
AF = mybir.ActivationFunctionType
ALU = mybir.AluOpType

D = 1024
KC = 8
DEPTH = 4
SEQ = 2048
MEM = 256
DFF = 2816
DIN = 7220
ALPHA = (2.0 * DEPTH) ** 0.25
LN_EPS = 1e-5
COL_GATE = 3124

ENGS = ("pe", "act", "dve", "pool", "sp")
EPOCH = 30000
NDMA_SEM = {"sp": 16, "pool": 12, "act": 4}


class Buf:
    __slots__ = ("name", "w", "rs", "excl")

    def __init__(self, name="", excl=False):
        self.name = name
        self.w = None
        self.rs = []
        self.excl = excl


class Op:
    __slots__ = ("eng", "fn", "pos", "needs_inc", "inc", "isdma", "dsem", "dval", "waits", "vc")


class Prog:
    def __init__(self, nc):
        self.nc = nc
        self.ops = {e: [] for e in ENGS}
        self.clock = {e: {} for e in ENGS}
        self.dma_uses = {}
        self.dma_rr = {e: 0 for e in NDMA_SEM}
        self.dma_last = {}
        self.out_dmas = []
        self.rd_dmas = []

    def op(self, eng, fn, reads=(), writes=(), extra=(), rt=None):
        o = Op()
        o.eng = eng; o.fn = fn
        o.isdma = False; o.needs_inc = False; o.inc = None
        ex = list(extra)
        force = None
        if eng == "pe":
            lr = getattr(self, "last_rt", None)
            cur = rt if rt is not None else "full"
            if lr is not None and lr[1] != cur and (lr[1] != "full" and cur != "full"):
                force = lr[0]
            self._force = force
        self._record(o, reads, writes, ex)
        if eng == "pe":
            self.last_rt = (o, rt if rt is not None else "full")
        return o

    def dma(self, queue, fn, reads=(), writes=(), is_output=False):
        o = Op()
        o.eng = queue; o.fn = fn
        o.isdma = True; o.needs_inc = False; o.inc = None
        k = self.dma_rr[queue]
        self.dma_rr[queue] = (k + 1) % NDMA_SEM[queue]
        key = (queue, k)
        uses = self.dma_uses.get(key, 0)
        o.dsem = key
        o.dval = 16 * (uses + 1)
        self.dma_uses[key] = uses + 1
        prev = self.dma_last.get(key)
        self.dma_last[key] = o
        self._record(o, reads, writes, [prev] if prev is not None else [])
        if is_output:
            self.out_dmas.append(o)
        if len(reads) > 0:
            self.rd_dmas.append(o)
        return o

    def barrier(self):
        last = {e: (self.ops[e][-1] if self.ops[e] else None) for e in ("pe", "act", "dve")}
        for e in ("pe", "act", "dve", "sp"):
            ex = []
            for f, o in last.items():
                if f == e or o is None:
                    continue
                j = len(self.ops[f]) - 1
                while j >= 0 and (self.ops[f][j].isdma or self.ops[f][j].fn is None):
                    j -= 1
                if j >= 0:
                    ex.append(self.ops[f][j])
            self.op(e, None, extra=ex + list(self.rd_dmas))
        self.rd_dmas = []

    def _record(self, o, reads, writes, extra=()):
        if any(b.excl for b in reads):
            writes = list(writes) + [b for b in reads if b.excl and b not in writes]
            reads = [b for b in reads if not b.excl]
        e = o.eng
        lst = self.ops[e]
        o.pos = len(lst) + 1
        deps = []
        for b in reads:
            if b.w is not None:
                deps.append((b.w, "raw"))
        for b in writes:
            if b.w is not None:
                deps.append((b.w, "waw"))
            for r in b.rs:
                deps.append((r, "war"))
        for d in extra:
            deps.append((d, "raw"))
        clk = self.clock[e]
        waits = []
        force = getattr(self, "_force", None)
        self._force = None
        if force is not None and e == "pe" and clk.get(("self", e), 0) < force.pos:
            waits.append(force)
            force.needs_inc = True
            clk[("self", e)] = force.pos
        best = {}
        d2 = []
        for (y, kind) in deps:
            if y is o:
                continue
            if (not y.isdma) and y.eng != e:
                if y.eng not in best or best[y.eng].pos < y.pos:
                    best[y.eng] = y
            else:
                d2.append((y, kind))
        deps = d2 + [(y, "raw") for y in best.values()]
        for (y, kind) in deps:
            if y is o:
                continue
            if y.isdma:
                if clk.get(y.dsem, 0) >= y.dval:
                    continue
                waits.append(y)
                self._merge(clk, y.vc)
            elif y.eng == e:
                if e != "pe" and clk.get(("self", e), 0) < y.pos and y.fn is not None:
                    waits.append(y)
                    y.needs_inc = True
                    clk[("self", e)] = y.pos
            else:
                if clk.get(y.eng, 0) >= y.pos:
                    continue
                waits.append(y)
                y.needs_inc = True
                self._merge(clk, y.vc)
        o.waits = waits
        if o.isdma:
            vc = dict(clk)
            vc[o.dsem] = o.dval
            o.vc = vc
            clk[e] = o.pos
        else:
            clk[e] = o.pos
            o.vc = dict(clk)
        lst.append(o)
        for b in reads:
            b.rs.append(o)
        for b in writes:
            b.w = o
            b.rs = []

    @staticmethod
    def _merge(clk, vc):
        for k, v in vc.items():
            if isinstance(k, tuple) and k and k[0] == "self":
                continue
            if clk.get(k, 0) < v:
                clk[k] = v

    def finalize(self, block, stack):
        nc = self.nc
        fin = self.op("sp", None)
        clk = self.clock["sp"]
        for d in self.out_dmas:
            if clk.get(d.dsem, 0) < d.dval:
                fin.waits.append(d)
                clk[d.dsem] = d.dval
        esems = {}
        for e in ENGS:
            c = 0
            for o in self.ops[e]:
                if o.needs_inc and not o.isdma:
                    assert o.fn is not None
                    c += 1
                    o.inc = c
            nep = c // EPOCH + 1
            esems[e] = [stack.enter_context(nc.semaphore(f"s_{e}_{i}")) for i in range(nep)]
        dsems = {}
        for (q, k) in self.dma_uses:
            dsems[(q, k)] = stack.enter_context(nc.semaphore(f"d_{q}_{k}"))
        stats = {e: [len(self.ops[e]), 0] for e in ENGS}

        def emit(e, eng):
            for o in self.ops[e]:
                for y in o.waits:
                    if y.isdma:
                        eng.wait_ge(dsems[y.dsem], y.dval)
                    else:
                        ep = (y.inc - 1) // EPOCH
                        eng.wait_ge(esems[y.eng][ep], y.inc - ep * EPOCH)
                    stats[e][1] += 1
                if o.fn is None:
                    continue
                ins = o.fn(eng)
                if o.isdma:
                    ins.then_inc(dsems[o.dsem], 16)
                elif o.needs_inc:
                    ep = (o.inc - 1) // EPOCH
                    ins.then_inc(esems[e][ep], 1)

        @block.tensor
        def _(eng):
            emit("pe", eng)

        @block.scalar
        def _(eng):
            emit("act", eng)

        @block.vector
        def _(eng):
            emit("dve", eng)

        @block.gpsimd
        def _(eng):
            emit("pool", eng)

        @block.sync
        def _(eng):
            emit("sp", eng)
        return stats


def make_consts():
    c = {}
    c["ident"] = np.eye(128, dtype=np.float32)
    c["ones"] = np.ones((128, 128), np.float32)
    b64 = np.zeros((128, 128), np.float32)
    b64[:64, :64] = 1; b64[64:, 64:] = 1
    c["bones64"] = b64
    s = np.arange(128)[:, None]
    t = np.arange(128)[None, :]
    c["iu128"] = (s <= t).astype(np.float32)
    p = np.arange(128)[:, None]
    f = np.arange(128)[None, :]
    same = (p // 32) == (f // 32)
    c["bd32_iu"] = (same & ((p % 32) <= (f % 32))).astype(np.float32)
    su = (same & ((p % 32) < (f % 32))).astype(np.float32)
    sl_ = (same & ((p % 32) > (f % 32))).astype(np.float32)
    c["rwmask4"] = np.concatenate([su, sl_, su, c["bd32_iu"]], axis=1)
    names = list(c.keys())
    offs = {}
    o = 0
    for n in names:
        offs[n] = (o, c[n].shape[1])
        o += c[n].shape[1]
    arr = np.concatenate([c[n] for n in names], axis=1)
    return arr, offs


CONSTS, COFF = make_consts()
NCONST = CONSTS.shape[1]

PV = {}


def _pv_layout():
    o = 0
    for n, k in (("rw_mu", 9), ("rw_w0", 2), ("rw_a0", 2), ("rw_kk", 2), ("rw_ka", 2), ("rw_rk", 2),
                 ("rw_ln_g", 2), ("rw_ln_b", 2), ("cv_w", 62), ("cv_b", 2), ("cv_ln_g", 2),
                 ("cv_ln_b", 2), ("gla_ab", 1), ("gla_ln_g", 2), ("fox_bf", 1), ("gate_b", 32),
                 ("ln_g", 24), ("ln_b", 24)):
        PV[n] = (o, k)
        o += k
    return o


NPV = _pv_layout()


def _cols(v):
    v = np.asarray(v, np.float32).reshape(-1)
    n = v.shape[0]
    k = (n + 127) // 128
    buf = np.zeros((k * 128,), np.float32)
    buf[:n] = v
    return buf.reshape(k, 128).T


def pack_pvec(inp, l):
    out = np.zeros((128, NPV), np.float32)

    def put(name, arr):
        o, k = PV[name]
        assert arr.shape == (128, k), (name, arr.shape, k)
        out[:, o:o + k] = arr

    put("rw_mu", _cols(inp["rw_mu"][l]))
    for n in ("rw_w0", "rw_a0", "rw_kk", "rw_ka", "rw_ln_g", "rw_ln_b", "cv_b", "cv_ln_g", "cv_ln_b",
              "gla_ab", "gla_ln_g"):
        put(n, _cols(inp[n][l]))
    put("rw_rk", _cols(inp["rw_rk"][l].reshape(-1)))
    cw = np.asarray(inp["cv_w"][l], np.float32)
    cwp = cw.T.reshape(2, 128, 31).transpose(1, 0, 2).reshape(128, 62)
    put("cv_w", cwp)
    put("fox_bf", _cols(inp["fox_bf"][l]))
    put("gate_b", _cols(inp["gate_b"][l].reshape(-1)))
    put("ln_g", _cols(inp["ln_g"][l].reshape(-1)))
    put("ln_b", _cols(inp["ln_b"][l].reshape(-1)))
    return out


class K:
    pass


def build(T=SEQ, L=DEPTH, stages=("mix", "xa", "ffn"), dbg=(), mixers=("rw", "cv", "gla", "fox")):
    nc = bass.Bass("TRN2", target_bir_lowering=False)
    NTB = T // 512
    k = K()
    k.nc = nc; k.T = T; k.L = L; k.NTB = NTB; k.mixers = mixers
    dr = lambda n, s, kind="ExternalInput", dt=F32: nc.dram_tensor(n, s, dt, kind=kind).ap()
    k.x_d = dr("x", [T, D])
    if "xa" in stages:
        k.mem_d = dr("mem", [MEM, D])
    k.consts_d = dr("consts", [128, NCONST])
    k.pvec_d = dr("pvec", [L, 128, NPV])
    if "mix" in stages:
        k.w_in_d = dr("w_in", [L, D, DIN])
        k.rw_wa_d = dr("rw_wa", [L, 128, 256])
        k.rw_g2_d = dr("rw_g2", [L, 160, 256])
        k.gla_a2_d = dr("gla_a2", [L, 16, 128])
        k.ups_d = [dr(n, [L, 256, D]) for n in ("rw_up", "cv_up", "gla_up", "fox_up")]
        k.w_out_d = dr("w_out", [L, D, D])
    if "xa" in stages:
        k.xa_wq_d = dr("xa_wq", [L, D, D]); k.xa_wk_d = dr("xa_wk", [L, D, D])
        k.xa_wv_d = dr("xa_wv", [L, D, D]); k.xa_wo_d = dr("xa_wo", [L, D, D])
    if "ffn" in stages:
        k.w1_d = dr("ffn_w1", [L, D, DFF]); k.w3_d = dr("ffn_w3", [L, D, DFF]); k.w2_d = dr("ffn_w2", [L, DFF, D])
    k.out_d = dr("out", [T, D], kind="ExternalOutput")
    k.xscr_d = nc.dram_tensor("xscr", [128, KC * T], F32, kind="Internal").ap()
    k.dbg_d = {}
    for (name, shape) in dbg:
        k.dbg_d[name] = dr("dbg_" + name, list(shape), kind="ExternalOutput", dt=BF16 if name in ("brT", "mg") else F32)

    with ExitStack() as st:
        k.st = st
        sb = lambda n, s, d=F32: st.enter_context(nc.sbuf_tensor(n, s, d))
        k.b_xres = [[Buf() for _ in range(NTB)] for _ in range(KC)]
        k.xres_n = 0
        alloc_xres(k)
        k.xT = sb("xT", [128, KC, T], BF16); k.b_xT = [[Buf() for _ in range(NTB)] for _ in range(KC)]
        k.cst = sb("cst", [128, NCONST]); k.b_cst = Buf()
        k.cstb = sb("cstb", [128, NCONST], BF16); k.b_cstb = Buf()
        k.pv = sb("pv", [128, NPV]); k.b_pv = Buf()
        k.npv = sb("npv", [128, NPV]); k.b_npv = Buf()
        k.g2a = sb("rw_g2a", [128, 256], BF16); k.g2b = sb("rw_g2b", [32, 256], BF16); k.b_g2 = Buf()
        k.ups = [sb(f"mg_ups{i}", [128, 2, 4, 128], BF16) for i in range(2)]; k.b_ups = [Buf(), Buf()]
        k.wsm = sb("w_small", [128, KC, 32], BF16); k.b_wsm = Buf()
        NW = 4
        k.NW = NW
        k.wr = [sb(f"wr{i}", [128, KC, 512], BF16) for i in range(NW)]
        k.b_wr = [Buf() for _ in range(NW)]
        k.wr_i = 0
        k.ps = [st.enter_context(nc.psum_tensor(f"ps{i}", [128, 512], F32)) for i in range(8)]
        k.b_ps = [Buf(excl=True) for _ in range(8)]
        k.ps_i = 0
        k.held = set()
        k.psb = [p.bitcast(BF16) for p in k.ps]
        block = st.enter_context(nc.Block())
        P = Prog(nc)
        k.P = P

        prologue(k)
        for l in range(L):
            layer_params(k, l)
            if "mix" in stages:
                stage_mix(k, l)
            if "xa" in stages:
                stage_xa(k, l)
            if "ffn" in stages:
                stage_ffn(k, l)
        epilogue(k)
        k.stats = P.finalize(block, st)
        k.xres_stack.close()
    return nc, k


_UID = [0]


def sbt(k, stack, name, shape, dt=F32):
    _UID[0] += 1
    return stack.enter_context(k.nc.sbuf_tensor(f"{name}_{_UID[0]}", list(shape), dt))


def alloc_xres(k):
    k.xres_stack = ExitStack()
    k.xres_n += 1
    k.xres = k.xres_stack.enter_context(k.nc.sbuf_tensor(f"xres{k.xres_n}", [128, KC, k.T], F32, side="right"))


def spill_xres(k):
    P = k.P; T = k.T
    for c in range(KC):
        P.dma("sp", lambda e, c=c, xr=k.xres: e.dma_start(out=k.xscr_d[:, c * T:(c + 1) * T], in_=xr[:, c, :]),
              reads=[k.b_xres[c][tb] for tb in range(k.NTB)])
    P.barrier()
    k.xres_stack.close()
    k.xres = None


def reload_xres(k):
    P = k.P; T = k.T
    alloc_xres(k)
    for c in range(KC):
        P.dma("sp", lambda e, c=c, xr=k.xres: e.dma_start(out=xr[:, c, :], in_=k.xscr_d[:, c * T:(c + 1) * T]),
              writes=[k.b_xres[c][tb] for tb in range(k.NTB)])


def cs(k, name, bf=False):
    o, n = COFF[name]
    return (k.cstb if bf else k.cst)[:, o:o + n]


def pcol(k, name, j=0, neg=False, rows=128, r0=0):
    o, n = PV[name]
    t = k.npv if neg else k.pv
    return t[r0:r0 + rows, o + j:o + j + 1]


def mm(e, out, lhsT, rhs, pos, start=True, stop=True):
    return e.matmul(out, lhsT=lhsT, rhs=rhs, start=start, stop=stop, tile_position=pos)


def tr(e, out, in_, identity, pos):
    return e.transpose(out=out, in_=in_, identity=identity, tile_position=pos)


def nbank(k, hold=False):
    i = k.ps_i
    while i in k.held:
        i = (i + 1) % 8
    k.ps_i = (i + 1) % 8
    if hold:
        k.held.add(i)
    return k.ps[i], k.b_ps[i]


def release(k, pt):
    for i in range(8):
        if k.ps[i] is pt:
            k.held.discard(i)
            return
    raise AssertionError


def load_w(k, src, ncols, nk=KC):
    i = k.wr_i
    k.wr_i = (i + 1) % k.NW
    slot, b = k.wr[i], k.b_wr[i]
    v = src.rearrange("(c p) n -> p c n", p=128)
    k.P.dma("pool", lambda e: e.dma_start(out=slot[:, 0:nk, 0:ncols], in_=v), writes=[b])
    return slot, b


def prologue(k):
    P = k.P; T = k.T
    P.dma("sp", lambda e: e.dma_start(out=k.cst[:], in_=k.consts_d), writes=[k.b_cst])
    P.op("dve", lambda e: e.tensor_copy(out=k.cstb[:], in_=k.cst[:]), reads=[k.b_cst], writes=[k.b_cstb])
    for i in range(8):
        P.op("dve", lambda e, i=i: e.memset(k.ps[i][:], 0.0), writes=[k.b_ps[i]])
    with ExitStack() as s2:
        xin = [sbt(k, s2, f"xin{i}", [128, D], F32) for i in range(2)]
        b_xin = [Buf(), Buf()]
        ident = cs(k, "ident")
        for tt in range(T // 128):
            j = tt % 2
            P.dma("sp", lambda e, tt=tt, j=j: e.dma_start(out=xin[j][:], in_=k.x_d[tt * 128:(tt + 1) * 128, :]),
                  writes=[b_xin[j]])
            tb = tt // 4
            for g in range(2):
                pt, bp = nbank(k)
                for q in range(4):
                    c = g * 4 + q
                    P.op("pe", lambda e, pt=pt, j=j, c=c, q=q: e.transpose(
                        out=pt[:, q * 128:(q + 1) * 128], in_=xin[j][:, c * 128:(c + 1) * 128], identity=ident),
                        reads=[b_xin[j], k.b_cst], writes=[bp])
                for q in range(4):
                    c = g * 4 + q
                    dst = slice(tt * 128, (tt + 1) * 128)
                    P.op("act", lambda e, pt=pt, c=c, q=q, dst=dst, xr=k.xres: e.activation(
                        out=xr[:, c, dst], in_=pt[:, q * 128:(q + 1) * 128], func=AF.Copy),
                        reads=[bp], writes=[k.b_xres[c][tb]])
                    P.op("dve", lambda e, pt=pt, c=c, q=q, dst=dst: e.tensor_copy(
                        out=k.xT[:, c, dst], in_=pt[:, q * 128:(q + 1) * 128]),
                        reads=[bp], writes=[k.b_xT[c][tb]])
        P.barrier()


def layer_params(k, l):
    P = k.P
    P.dma("sp", lambda e: e.dma_start(out=k.pv[:], in_=k.pvec_d[l]), writes=[k.b_pv])
    P.op("dve", lambda e: e.tensor_scalar(out=k.npv[:], in0=k.pv[:], scalar1=-1.0, scalar2=None, op0=ALU.mult),
         reads=[k.b_pv], writes=[k.b_npv])


def epilogue(k):
    P = k.P; T = k.T
    with ExitStack() as s2:
        xo = [sbt(k, s2, f"xo{i}", [128, D], F32) for i in range(2)]
        b_xo = [Buf(), Buf()]
        ident = cs(k, "ident")
        for tt in range(T // 128):
            j = tt % 2
            tb = tt // 4
            for g in range(2):
                pt, bp = nbank(k)
                for q in range(4):
                    c = g * 4 + q
                    P.op("pe", lambda e, pt=pt, c=c, q=q, tt=tt, xr=k.xres: e.transpose(
                        out=pt[:, q * 128:(q + 1) * 128], in_=xr[:, c, tt * 128:(tt + 1) * 128], identity=ident),
                        reads=[k.b_xres[c][tb], k.b_cst], writes=[bp])
                eng = "act" if g == 0 else "dve"
                if g == 0:
                    P.op("act", lambda e, pt=pt, j=j: e.activation(out=xo[j][:, 0:512], in_=pt[:], func=AF.Copy),
                         reads=[bp], writes=[b_xo[j]])
                else:
                    P.op("dve", lambda e, pt=pt, j=j: e.tensor_copy(out=xo[j][:, 512:1024], in_=pt[:]),
                         reads=[bp], writes=[b_xo[j]])
            P.dma("sp", lambda e, tt=tt, j=j: e.dma_start(out=k.out_d[tt * 128:(tt + 1) * 128, :], in_=xo[j][:]),
                  reads=[b_xo[j]], is_output=True)
        P.barrier()


def ln_block(k, l, s, tb, zsq, b_zsq, st_t, b_st):
    P = k.P
    sl = slice(tb * 512, (tb + 1) * 512)
    rstd, nmr, b_r, b_n = ln_stats(k, [(k.xres[:, c, sl], k.b_xres[c][tb]) for c in range(KC)], D, LN_EPS, zsq, b_zsq, st_t, b_st)
    og, _ = PV["ln_g"]
    ob, _ = PV["ln_b"]
    for c in range(KC):
        xs = k.xres[:, c, sl]
        P.op("dve", lambda e, xs=xs: e.tensor_tensor(out=xs, in0=xs, in1=rstd, op=ALU.mult),
             reads=[k.b_xres[c][tb], b_r], writes=[k.b_xres[c][tb]])
        P.op("dve", lambda e, xs=xs: e.tensor_tensor(out=xs, in0=xs, in1=nmr, op=ALU.add),
             reads=[k.b_xres[c][tb], b_n], writes=[k.b_xres[c][tb]])
        gcol = k.pv[:, og + s * 8 + c: og + s * 8 + c + 1]
        bcol = k.pv[:, ob + s * 8 + c: ob + s * 8 + c + 1]
        P.op("act", lambda e, xs=xs, c=c, gcol=gcol, bcol=bcol: e.activation(out=k.xT[:, c, sl], in_=xs, func=AF.Identity, scale=gcol, bias=bcol),
             reads=[k.b_xres[c][tb], k.b_pv], writes=[k.b_xT[c][tb]])
        P.op("act", lambda e, xs=xs, gcol=gcol, bcol=bcol: e.activation(out=xs, in_=xs, func=AF.Identity, scale=gcol, bias=bcol),
             reads=[k.b_xres[c][tb], k.b_pv], writes=[k.b_xres[c][tb]])


def out_proj_ln(k, l, s, src, b_src, nkc, w_d, first=True, last=True, alpha_first=True, row0=0,
                lnbufs=None):
    P = k.P
    NTB = k.NTB
    halves = []
    for h in range(2):
        slot, b = load_w(k, w_d[row0:row0 + nkc * 128, h * 512:(h + 1) * 512], 512, nk=nkc)
        halves.append((slot, b))
    for tb in range(NTB):
        sl = slice(tb * 512, (tb + 1) * 512)
        for fc in range(KC):
            slot, bw = halves[fc // 4]
            co = (fc % 4) * 128
            pt, bp = nbank(k)
            for c in range(nkc):
                P.op("pe", lambda e, pt=pt, slot=slot, c=c, co=co, sl=sl: e.matmul(
                    pt[:], lhsT=slot[:, c, co:co + 128], rhs=src[:, c, sl], start=(c == 0), stop=(c == nkc - 1)),
                    reads=[bw, b_src[c][tb]], writes=[bp])
            xs = k.xres[:, fc, sl]
            if first:
                P.op("dve", lambda e, pt=pt, xs=xs: e.scalar_tensor_tensor(
                    out=xs, in0=xs, scalar=ALPHA, in1=pt[:], op0=ALU.mult, op1=ALU.add),
                    reads=[bp, k.b_xres[fc][tb]], writes=[k.b_xres[fc][tb]])
            else:
                P.op("dve", lambda e, pt=pt, xs=xs: e.tensor_tensor(out=xs, in0=xs, in1=pt[:], op=ALU.add),
                     reads=[bp, k.b_xres[fc][tb]], writes=[k.b_xres[fc][tb]])
        if last:
            ln_block(k, l, s, tb, *lnbufs)


def alloc_ln(k, s2):
    zsq = sbt(k, s2, "zsq", [128, 2, 512], F32)
    st_t = sbt(k, s2, "lnst", [128, 3, 512], F32)
    return (zsq, [Buf(), Buf()], st_t, [Buf() for _ in range(3)])


def stage_ffn(k, l):
    P = k.P; T = k.T; NTB = k.NTB
    parts = [(0, 8), (8, 16), (16, 22)]
    with ExitStack() as s2:
        g = sbt(k, s2, "ffg", [128, 8, T], BF16)
        b_g = [[Buf() for _ in range(NTB)] for _ in range(8)]
        sg = [sbt(k, s2, f"ffs{i}", [128, 512], F32) for i in range(2)]
        b_sg = [Buf(), Buf()]
        lnb = alloc_ln(k, s2)
        si = 0
        for pi, (c0, c1) in enumerate(parts):
            n = c1 - c0
            for q0 in range(c0, c1, 4):
                nq = min(4, c1 - q0)
                s1, bw1 = load_w(k, k.w1_d[l][:, q0 * 128:(q0 + nq) * 128], nq * 128)
                s3, bw3 = load_w(k, k.w3_d[l][:, q0 * 128:(q0 + nq) * 128], nq * 128)
                for q in range(nq):
                    cg = q0 + q - c0
                    for tb in range(NTB):
                        sl = slice(tb * 512, (tb + 1) * 512)
                        p1, bp1 = nbank(k)
                        p3, bp3 = nbank(k)
                        for c in range(KC):
                            P.op("pe", lambda e, p1=p1, s1=s1, c=c, q=q, sl=sl: e.matmul(
                                p1[:], lhsT=s1[:, c, q * 128:(q + 1) * 128], rhs=k.xT[:, c, sl], start=(c == 0), stop=(c == KC - 1)),
                                reads=[bw1, k.b_xT[c][tb]], writes=[bp1])
                        for c in range(KC):
                            P.op("pe", lambda e, p3=p3, s3=s3, c=c, q=q, sl=sl: e.matmul(
                                p3[:], lhsT=s3[:, c, q * 128:(q + 1) * 128], rhs=k.xT[:, c, sl], start=(c == 0), stop=(c == KC - 1)),
                                reads=[bw3, k.b_xT[c][tb]], writes=[bp3])
                        j = si % 2
                        si += 1
                        P.op("act", lambda e, p1=p1, j=j: e.activation(out=sg[j][:], in_=p1[:], func=AF.Silu),
                             reads=[bp1], writes=[b_sg[j]])
                        P.op("dve", lambda e, p3=p3, j=j, cg=cg, sl=sl: e.tensor_tensor(
                            out=g[:, cg, sl], in0=sg[j][:], in1=p3[:], op=ALU.mult),
                            reads=[bp3, b_sg[j]], writes=[b_g[cg][tb]])
            out_proj_ln(k, l, 2, g, b_g, n, k.w2_d[l], first=(pi == 0), last=(pi == len(parts) - 1),
                        row0=c0 * 128, lnbufs=lnb)
        P.barrier()


def stage_xa(k, l):
    P = k.P; T = k.T; NTB = k.NTB
    ident = cs(k, "ident")
    with ExitStack() as s2:
        sbt_ = lambda n, s, d=F32: sbt(k, s2, n, s, d)
        k.xaK = sbt_("xaK", [128, KC, MEM], BF16); k.b_xaK = Buf()
        k.xaV = sbt_("xaV", [128, 2, D], BF16); k.b_xaV = Buf()
        smem = ExitStack()
        memT = sbt(k, smem, "memT", [128, KC, MEM], BF16)
        b_memT = Buf()
        with ExitStack() as s3:
            mt = [sbt(k, s3, f"memin{i}", [128, D], F32) for i in range(2)]
            b_mt = [Buf(), Buf()]
            for m in range(2):
                P.dma("sp", lambda e, m=m: e.dma_start(out=mt[m][:], in_=k.mem_d[m * 128:(m + 1) * 128, :]), writes=[b_mt[m]])
                for g in range(2):
                    pt, bp = nbank(k)
                    for q in range(4):
                        c = g * 4 + q
                        P.op("pe", lambda e, pt=pt, m=m, c=c, q=q: e.transpose(
                            out=pt[:, q * 128:(q + 1) * 128], in_=mt[m][:, c * 128:(c + 1) * 128], identity=ident),
                            reads=[b_mt[m], k.b_cst], writes=[bp])
                    for q in range(4):
                        c = g * 4 + q
                        P.op("dve", lambda e, pt=pt, m=m, c=c, q=q: e.tensor_copy(
                            out=memT[:, c, m * 128:(m + 1) * 128], in_=pt[:, q * 128:(q + 1) * 128]),
                            reads=[bp], writes=[b_memT])
            P.barrier()
        for h in range(2):
            slot, bw = load_w(k, k.xa_wk_d[l][:, h * 512:(h + 1) * 512], 512)
            for q in range(4):
                fc = h * 4 + q
                pt, bp = nbank(k)
                for c in range(KC):
                    P.op("pe", lambda e, pt=pt, slot=slot, c=c, q=q: e.matmul(
                        pt[:, 0:MEM], lhsT=slot[:, c, q * 128:(q + 1) * 128], rhs=memT[:, c, :], start=(c == 0), stop=(c == KC - 1)),
                        reads=[bw, b_memT], writes=[bp])
                P.op("act", lambda e, pt=pt, fc=fc: e.activation(out=k.xaK[:, fc, :], in_=pt[:, 0:MEM], func=AF.Copy),
                     reads=[bp], writes=[k.b_xaK])
        for h in range(2):
            slot, bw = load_w(k, k.xa_wv_d[l][:, h * 512:(h + 1) * 512], 512)
            for m in range(2):
                pt, bp = nbank(k)
                for c in range(KC):
                    P.op("pe", lambda e, pt=pt, slot=slot, c=c, m=m: e.matmul(
                        pt[:], lhsT=memT[:, c, m * 128:(m + 1) * 128], rhs=slot[:, c, :], start=(c == 0), stop=(c == KC - 1)),
                        reads=[bw, b_memT], writes=[bp])
                P.op("act", lambda e, pt=pt, m=m, h=h: e.activation(out=k.xaV[:, m, h * 512:(h + 1) * 512], in_=pt[:], func=AF.Copy),
                     reads=[bp], writes=[k.b_xaV])
        P.barrier()
        smem.close()
        oT = sbt_("xa_oT", [128, KC, T], BF16)
        b_oT = [[Buf() for _ in range(NTB)] for _ in range(KC)]
        qT = sbt_("xa_qT", [128, 2, T], BF16)
        b_qT = [[Buf() for _ in range(NTB)] for _ in range(2)]
        PT = [sbt_(f"xa_PT{i}", [128, 2, 512], BF16) for i in range(2)]
        b_PT = [[Buf(), Buf()], [Buf(), Buf()]]
        rd0 = sbt_("xa_rd", [128, 512])
        rden = [rd0, rd0]
        b0 = Buf()
        b_rden = [b0, b0]
        onesb = cs(k, "ones", bf=True)
        scale = 1.0 / 16.0
        it = 0
        for hh in range(2):
            slot, bw = load_w(k, k.xa_wq_d[l][:, hh * 512:(hh + 1) * 512], 512)
            for h2 in range(2):
                h = hh * 2 + h2
                for tb in range(NTB):
                    sl = slice(tb * 512, (tb + 1) * 512)
                    for dc in range(2):
                        pt, bp = nbank(k)
                        co = (h2 * 2 + dc) * 128
                        for c in range(KC):
                            P.op("pe", lambda e, pt=pt, slot=slot, c=c, co=co, sl=sl: e.matmul(
                                pt[:], lhsT=slot[:, c, co:co + 128], rhs=k.xT[:, c, sl], start=(c == 0), stop=(c == KC - 1)),
                                reads=[bw, k.b_xT[c][tb]], writes=[bp])
                        P.op("act", lambda e, pt=pt, dc=dc, sl=sl: e.activation(out=qT[:, dc, sl], in_=pt[:], func=AF.Copy),
                             reads=[bp], writes=[b_qT[dc][tb]])
                    j = it % 2
                    it += 1
                    for m in range(2):
                        pt, bp = nbank(k)
                        for dc in range(2):
                            P.op("pe", lambda e, pt=pt, h=h, dc=dc, m=m, sl=sl: e.matmul(
                                pt[:], lhsT=k.xaK[:, h * 2 + dc, m * 128:(m + 1) * 128], rhs=qT[:, dc, sl],
                                start=(dc == 0), stop=(dc == 1)),
                                reads=[k.b_xaK, b_qT[dc][tb]], writes=[bp])
                        P.op("act", lambda e, pt=pt, j=j, m=m: e.activation(out=PT[j][:, m, :], in_=pt[:], func=AF.Exp, scale=scale),
                             reads=[bp], writes=[b_PT[j][m]])
                    pd, bpd = nbank(k)
                    for m in range(2):
                        P.op("pe", lambda e, pd=pd, j=j, m=m: e.matmul(pd[:], lhsT=onesb, rhs=PT[j][:, m, :], start=(m == 0), stop=(m == 1)),
                             reads=[k.b_cstb, b_PT[j][m]], writes=[bpd])
                    P.op("act", lambda e, pd=pd, j=j: e.activation(out=rden[j][:], in_=pd[:], func=AF.Ln), reads=[bpd], writes=[b_rden[j]])
                    P.op("act", lambda e, j=j: e.activation(out=rden[j][:], in_=rden[j][:], func=AF.Exp, scale=-1.0), reads=[b_rden[j]], writes=[b_rden[j]])
                    for dc in range(2):
                        po, bpo = nbank(k)
                        for m in range(2):
                            P.op("pe", lambda e, po=po, j=j, m=m, h=h, dc=dc: e.matmul(
                                po[:], lhsT=k.xaV[:, m, h * 256 + dc * 128: h * 256 + (dc + 1) * 128], rhs=PT[j][:, m, :],
                                start=(m == 0), stop=(m == 1)),
                                reads=[k.b_xaV, b_PT[j][m]], writes=[bpo])
                        P.op("dve", lambda e, po=po, j=j, h=h, dc=dc, sl=sl: e.tensor_tensor(
                            out=oT[:, h * 2 + dc, sl], in0=po[:], in1=rden[j][:], op=ALU.mult),
                            reads=[bpo, b_rden[j]], writes=[b_oT[h * 2 + dc][tb]])
        lnb = alloc_ln(k, s2)
        out_proj_ln(k, l, 1, oT, b_oT, KC, k.xa_wo_d[l], lnbufs=lnb)
        P.barrier()


def proj_fm(k, slot, bw, col0, ncols, tb, hold=False):
    P = k.P
    sl = slice(tb * 512, (tb + 1) * 512)
    pt, bp = nbank(k, hold=hold)
    for c in range(KC):
        P.op("pe", lambda e, c=c: e.matmul(pt[0:ncols, :], lhsT=slot[:, c, col0:col0 + ncols], rhs=k.xT[:, c, sl],
                                           start=(c == 0), stop=(c == KC - 1)),
             reads=[bw, k.b_xT[c][tb]], writes=[bp])
    return pt, bp


def ln_stats(k, srcs, nfeat, eps, zsq, b_zsq, st_t, b_st):
    P = k.P
    ones = cs(k, "ones")
    S1, b1 = nbank(k)
    S2, b2 = nbank(k)
    n = len(srcs)
    for i, (ap, b) in enumerate(srcs):
        j = i % 2
        P.op("act", lambda e, j=j, ap=ap: e.activation(out=zsq[:, j, :], in_=ap, func=AF.Square), reads=[b], writes=[b_zsq[j]])
        P.op("pe", lambda e, i=i, ap=ap: e.matmul(S1[:], lhsT=ones, rhs=ap, start=(i == 0), stop=(i == n - 1)),
             reads=[b, k.b_cst], writes=[b1])
        P.op("pe", lambda e, i=i, j=j: e.matmul(S2[:], lhsT=ones, rhs=zsq[:, j, :], start=(i == 0), stop=(i == n - 1)),
             reads=[b_zsq[j], k.b_cst], writes=[b2])
    mean, var, rstd = (st_t[:, i, :] for i in range(3))
    P.op("act", lambda e: e.activation(out=mean, in_=S1[:], func=AF.Copy, scale=1.0 / nfeat), reads=[b1], writes=[b_st[0]])
    P.op("dve", lambda e: e.tensor_tensor(out=var, in0=mean, in1=mean, op=ALU.mult), reads=[b_st[0]], writes=[b_st[1]])
    P.op("dve", lambda e: e.scalar_tensor_tensor(out=var, in0=S2[:], scalar=1.0 / nfeat, in1=var, op0=ALU.mult, op1=ALU.subtract),
         reads=[b2, b_st[1]], writes=[b_st[1]])
    P.op("dve", lambda e: e.tensor_scalar(out=var, in0=var, scalar1=eps, scalar2=None, op0=ALU.add),
         reads=[b_st[1]], writes=[b_st[1]])
    P.op("act", lambda e: e.activation(out=rstd, in_=var, func=AF.Ln), reads=[b_st[1]], writes=[b_st[2]])
    P.op("act", lambda e: e.activation(out=rstd, in_=rstd, func=AF.Exp, scale=-0.5), reads=[b_st[2]], writes=[b_st[2]])
    P.op("dve", lambda e: e.scalar_tensor_tensor(out=mean, in0=mean, scalar=-1.0, in1=rstd, op0=ALU.mult, op1=ALU.mult),
         reads=[b_st[0], b_st[2]], writes=[b_st[0]])
    return rstd, mean, b_st[2], b_st[0]


def mixer_conv(k, l, brT, b_brT, s2):
    P = k.P; T = k.T; NTB = k.NTB
    if True:
        ub = sbt(k, s2, "cv_u", [128, 2, 30 + T], F32)
        b_ub = [Buf(), Buf()]
        acc = sbt(k, s2, "cv_acc", [128, 2, T], F32)
        b_acc = [Buf(), Buf()]
        lnb = alloc_ln(k, s2)
        sg = [lnb[0][:, i, :] for i in range(2)]
        b_sg = lnb[1]
        slot, bw = load_w(k, k.w_in_d[l][:, 1056:1568], 512)
        for ch in range(2):
            P.op("dve", lambda e, ch=ch: e.memset(ub[:, ch, 0:30], 0.0), writes=[b_ub[ch]])
        i = 0
        for tb in range(NTB):
            for ch in range(2):
                pa, bpa = proj_fm(k, slot, bw, ch * 128, 128, tb)
                pb, bpb = proj_fm(k, slot, bw, 256 + ch * 128, 128, tb)
                j = i % 2
                i += 1
                P.op("act", lambda e, pb=pb, j=j: e.activation(out=sg[j], in_=pb[:], func=AF.Sigmoid), reads=[bpb], writes=[b_sg[j]])
                P.op("dve", lambda e, pa=pa, j=j, ch=ch, tb=tb: e.tensor_tensor(
                    out=ub[:, ch, 30 + tb * 512: 30 + (tb + 1) * 512], in0=pa[:], in1=sg[j], op=ALU.mult),
                    reads=[bpa, b_sg[j]], writes=[b_ub[ch]])
                yield
        ow, _ = PV["cv_w"]
        for ch in range(2):
            eng = "dve"
            for kk in range(31):
                wcol = k.pv[:, ow + ch * 31 + kk: ow + ch * 31 + kk + 1]
                if kk == 0:
                    bcol = pcol(k, "cv_b", ch)
                    P.op(eng, lambda e, ch=ch, wcol=wcol, bcol=bcol: e.tensor_scalar(
                        out=acc[:, ch, :], in0=ub[:, ch, 0:T], scalar1=wcol, scalar2=bcol, op0=ALU.mult, op1=ALU.add),
                        reads=[b_ub[ch], k.b_pv], writes=[b_acc[ch]])
                else:
                    P.op(eng, lambda e, ch=ch, wcol=wcol, kk=kk: e.scalar_tensor_tensor(
                        out=acc[:, ch, :], in0=ub[:, ch, kk:kk + T], scalar=wcol, in1=acc[:, ch, :], op0=ALU.mult, op1=ALU.add),
                        reads=[b_ub[ch], k.b_pv, b_acc[ch]], writes=[b_acc[ch]])
                yield
        for tb in range(NTB):
            sl = slice(tb * 512, (tb + 1) * 512)
            rstd, nmr, b_r, b_n = ln_stats(k, [(acc[:, ch, sl], b_acc[ch]) for ch in range(2)], 256, LN_EPS, *lnb)
            for ch in range(2):
                a = acc[:, ch, sl]
                P.op("dve", lambda e, a=a, rstd=rstd: e.tensor_tensor(out=a, in0=a, in1=rstd, op=ALU.mult),
                     reads=[b_acc[ch], b_r], writes=[b_acc[ch]])
                P.op("dve", lambda e, a=a, nmr=nmr: e.tensor_tensor(out=a, in0=a, in1=nmr, op=ALU.add),
                     reads=[b_acc[ch], b_n], writes=[b_acc[ch]])
                P.op("act", lambda e, a=a, ch=ch, sl=sl: e.activation(
                    out=brT[:, 2 + ch, sl], in_=a, func=AF.Silu, scale=pcol(k, "cv_ln_g", ch), bias=pcol(k, "cv_ln_b", ch)),
                    reads=[b_acc[ch], k.b_pv], writes=[b_brT[2 + ch][tb]])
            yield
        yield


def mixer_fox(k, l, brT, b_brT, s2):
    P = k.P; T = k.T; NTB = k.NTB
    NT = T // 128
    if True:
        fq = sbt(k, s2, "fx_q", [128, 2, T], BF16); b_fq = [[Buf() for _ in range(NTB)] for _ in range(2)]
        fk = sbt(k, s2, "fx_k", [128, 2, T], BF16); b_fk = [[Buf() for _ in range(NTB)] for _ in range(2)]
        fv = sbt(k, s2, "fx_v", [128, NT, 256], BF16); b_fv = [Buf() for _ in range(NT)]
        spl = sbt(k, s2, "fx_spl", [4, T], F32); b_spl = Buf()
        sig, b_sig = spl, b_spl
        rsel = sbt(k, s2, "fx_rsel", [4, NT * 4], F32); b_rsel = Buf()
        stok = sbt(k, s2, "fx_stok", [128, NT * 4], F32); b_stok = Buf()
        sref = sbt(k, s2, "fx_sref", [128, NT * 4], F32); b_sref = Buf()
        bias = sbt(k, s2, "fx_bias", [128, NT, NT], F32); b_bias = Buf()
        PT = [sbt(k, s2, f"fx_PT{i}", [128, 512], BF16) for i in range(2)]
        b_PT = [Buf(), Buf()]
        rd = sbt(k, s2, "fx_rd", [128, 512], F32); b_rd = Buf()
        slA, bwA = load_w(k, k.w_in_d[l][:, 2352:2864], 512)
        slB, bwB = load_w(k, k.w_in_d[l][:, 2864:3124], 260)
        for tb in range(NTB):
            sl = slice(tb * 512, (tb + 1) * 512)
            for ch in range(2):
                pq, bpq = proj_fm(k, slA, bwA, ch * 128, 128, tb)
                P.op("act", lambda e, pq=pq, ch=ch, sl=sl: e.activation(out=fq[:, ch, sl], in_=pq[:], func=AF.Copy, scale=0.125),
                     reads=[bpq], writes=[b_fq[ch][tb]])
                pk, bpk = proj_fm(k, slA, bwA, 256 + ch * 128, 128, tb)
                P.op("dve", lambda e, pk=pk, ch=ch, sl=sl: e.tensor_copy(out=fk[:, ch, sl], in_=pk[:]),
                     reads=[bpk], writes=[b_fk[ch][tb]])
            pz, bpz = proj_fm(k, slB, bwB, 256, 4, tb)
            P.op("act", lambda e, pz=pz, sl=sl: e.activation(out=spl[:, sl], in_=pz[0:4, :], func=AF.Exp, scale=-1.0,
                                                             bias=pcol(k, "fox_bf", 0, neg=True, rows=4)),
                 reads=[bpz, k.b_npv], writes=[b_spl])
            P.op("act", lambda e, sl=sl: e.activation(out=spl[:, sl], in_=spl[:, sl], func=AF.Ln, bias=1.0),
                 reads=[b_spl], writes=[b_spl])
            yield
        for tt in range(NT):
            tb = tt // 4
            pt, bp = nbank(k)
            for c in range(KC):
                P.op("pe", lambda e, c=c, tt=tt, pt=pt: e.matmul(pt[:, 0:256], lhsT=k.xT[:, c, tt * 128:(tt + 1) * 128], rhs=slB[:, c, 0:256],
                                                             start=(c == 0), stop=(c == KC - 1)),
                     reads=[bwB, k.b_xT[c][tb]], writes=[bp])
            P.op("act", lambda e, pt=pt, tt=tt: e.activation(out=fv[:, tt, :], in_=pt[:, 0:256], func=AF.Copy), reads=[bp], writes=[b_fv[tt]])
            yield
        P.op("dve", lambda e: e.tensor_tensor_scan(out=sig[:], data0=spl[:], data1=spl[:], initial=0.0, op0=ALU.add, op1=ALU.max),
             reads=[b_spl], writes=[b_sig])
        ident = cs(k, "ident")
        pt, bp = nbank(k)
        for tt in range(NT):
            P.op("pe", lambda e, tt=tt: e.transpose(out=pt[:, tt * 4:(tt + 1) * 4], in_=sig[0:4, tt * 128:(tt + 1) * 128], identity=ident[0:4, 0:4]),
                 reads=[b_sig, k.b_cst], writes=[bp])
        P.op("dve", lambda e: e.tensor_copy(out=stok[:], in_=pt[:, 0:NT * 4]), reads=[bp], writes=[b_stok])
        for qs in range(NT):
            P.op("dve", lambda e, qs=qs: e.tensor_scalar(out=rsel[:, qs * 4:(qs + 1) * 4], in0=ident[0:4, 0:4], scalar1=sig[:, qs * 128:qs * 128 + 1],
                                                     scalar2=None, op0=ALU.mult),
                 reads=[b_sig, k.b_cst], writes=[b_rsel])
        pr, bpr = nbank(k)
        ones = cs(k, "ones")
        P.op("pe", lambda e: e.matmul(pr[:, 0:NT * 4], lhsT=ones[0:4, :], rhs=rsel[:], start=True, stop=True),
             reads=[b_rsel, k.b_cst], writes=[bpr])
        P.op("dve", lambda e: e.tensor_copy(out=sref[:], in_=pr[:, 0:NT * 4]), reads=[bpr], writes=[b_sref])
        stok3 = stok[:].rearrange("p (n h) -> p n h", h=4)
        onesb = cs(k, "ones", bf=True)
        iu = cs(k, "iu128", bf=True)
        it = 0
        for h in range(4):
            ch = h // 2
            pb = (h % 2) * 64
            for qs in range(NT):
                P.op("dve", lambda e, h=h, qs=qs: e.tensor_scalar(out=bias[:, qs, :], in0=stok3[:, :, h], scalar1=sref[:, qs * 4 + h:qs * 4 + h + 1],
                                                            scalar2=None, op0=ALU.subtract),
                     reads=[b_stok, b_sref], writes=[b_bias])
            for Q in range(NTB):
                nkt = 4 * (Q + 1)
                po, bpo = nbank(k, hold=True)
                pd, bpd = nbank(k, hold=True)
                for kt in range(nkt):
                    d = kt - 4 * Q
                    q0 = d * 128 if d > 0 else 0
                    j = it % 2
                    it += 1
                    ps_, bps = nbank(k, hold=True)
                    P.op("pe", lambda e, ps_=ps_, pb=pb, ch=ch, kt=kt, Q=Q, q0=q0: e.matmul(
                        ps_[:, q0:512], lhsT=fk[pb:pb + 64, ch, kt * 128:(kt + 1) * 128], rhs=fq[pb:pb + 64, ch, Q * 512 + q0:(Q + 1) * 512],
                        start=True, stop=True),
                        reads=[b_fk[ch][kt // 4], b_fq[ch][Q]], writes=[bps])
                    yield
                    for qi in range(q0 // 128, 4):
                        qs = Q * 4 + qi
                        P.op("act", lambda e, ps_=ps_, j=j, qi=qi, qs=qs, h=h, kt=kt: e.activation(
                            out=PT[j][:, qi * 128:(qi + 1) * 128], in_=ps_[:, qi * 128:(qi + 1) * 128], func=AF.Exp,
                            bias=bias[:, qs, kt:kt + 1]),
                            reads=[bps, b_bias], writes=[b_PT[j]])
                    release(k, ps_)
                    if d >= 0:
                        P.op("dve", lambda e, j=j, q0=q0: e.tensor_tensor(out=PT[j][:, q0:q0 + 128], in0=PT[j][:, q0:q0 + 128], in1=iu, op=ALU.mult),
                             reads=[b_PT[j], k.b_cstb], writes=[b_PT[j]])
                    P.op("pe", lambda e, po=po, pb=pb, j=j, q0=q0, kt=kt, h=h, nkt=nkt: e.matmul(
                        po[pb:pb + 64, q0:512], lhsT=fv[:, kt, h * 64:(h + 1) * 64], rhs=PT[j][:, q0:512], start=(kt == 0), stop=(kt == nkt - 1)),
                        reads=[b_fv[kt], b_PT[j]], writes=[bpo])
                    P.op("pe", lambda e, pd=pd, pb=pb, j=j, q0=q0, kt=kt, nkt=nkt: e.matmul(
                        pd[pb:pb + 64, q0:512], lhsT=onesb[:, 0:64], rhs=PT[j][:, q0:512], start=(kt == 0), stop=(kt == nkt - 1)),
                        reads=[k.b_cstb, b_PT[j]], writes=[bpd])
                    yield
                P.op("act", lambda e, pd=pd, pb=pb: e.activation(out=rd[pb:pb + 64, :], in_=pd[pb:pb + 64, :], func=AF.Ln), reads=[bpd], writes=[b_rd])
                P.op("act", lambda e, pb=pb: e.activation(out=rd[pb:pb + 64, :], in_=rd[pb:pb + 64, :], func=AF.Exp, scale=-1.0), reads=[b_rd], writes=[b_rd])
                P.op("dve", lambda e, po=po, pb=pb, ch=ch, Q=Q: e.tensor_tensor(
                    out=brT[pb:pb + 64, 6 + ch, Q * 512:(Q + 1) * 512], in0=po[pb:pb + 64, :], in1=rd[pb:pb + 64, :], op=ALU.mult),
                    reads=[bpo, b_rd], writes=[b_brT[6 + ch][Q]])
                release(k, po); release(k, pd)
        yield


def mixer_gla(k, l, brT, b_brT, s2):
    P = k.P; T = k.T; NTB = k.NTB
    identb = cs(k, "ident", bf=True)
    if True:
        f32t = lambda n, shp: sbt(k, s2, n, shp, F32)
        bft = lambda n, shp: sbt(k, s2, n, shp, BF16)
        a2 = f32t("gl_a2", [16, 128]); b_a2 = Buf()
        zT = f32t("gl_z", [16, 512]); b_zT = Buf()
        spl = f32t("gl_spl", [128, 512]); b_spl = Buf()
        bcs = f32t("gl_bcs", [128, 512]); b_bcs = Buf()
        Ep = f32t("gl_Ep", [128, 512]); b_Ep = Buf()
        En = f32t("gl_En", [128, 512]); b_En = Buf()
        ones32 = f32t("gl_ones", [128, 32]); b_ones = Buf()
        qd = bft("gl_qd", [128, 512]); b_qd = Buf()
        ki = bft("gl_ki", [128, 512]); b_ki = Buf()
        vb = bft("gl_vb", [128, 2, 512]); b_vb = Buf()
        sr = f32t("gl_sr", [128, 2, 512]); b_sr = Buf()
        S4f = f32t("gl_S4f", [128, 64]); b_S4f = Buf()
        S4b = bft("gl_S4b", [128, 64]); b_S4b = Buf()
        STm = [bft(f"gl_STm{i}", [128, 128]) for i in range(2)]; b_STm = [Buf(), Buf()]
        V4 = [bft(f"gl_V4{i}", [128, 64]) for i in range(2)]; b_V4 = [Buf(), Buf()]
        KT = [bft(f"gl_KT{i}", [128, 128]) for i in range(2)]; b_KT = [Buf(), Buf()]
        OALL = f32t("gl_OALL", [128, 16, 64]); b_OALL = Buf()
        osq = f32t("gl_osq", [128, 16, 64]); b_osq = Buf()
        ss = f32t("gl_ss", [128, 16]); b_ss = Buf()
        ONALL = bft("gl_ON", [128, 16, 64]); b_ON = Buf()
        slA, bwA = load_w(k, k.w_in_d[l][:, 1568:2080], 512)
        slB, bwB = load_w(k, k.w_in_d[l][:, 2080:2352], 272)
        P.dma("sp", lambda e: e.dma_start(out=a2[:], in_=k.gla_a2_d[l]), writes=[b_a2])
        P.op("dve", lambda e: e.memset(ones32[:], 1.0), writes=[b_ones])
        P.op("dve", lambda e: e.memset(S4f[:], 0.0), writes=[b_S4f])
        P.op("dve", lambda e: e.memset(S4b[:], 0.0), writes=[b_S4b])
        bd_iu = cs(k, "bd32_iu")
        it = 0
        for tb in range(NTB):
            sl = slice(tb * 512, (tb + 1) * 512)
            pz, bpz = proj_fm(k, slB, bwB, 256, 16, tb)
            P.op("act", lambda e, pz=pz: e.activation(out=zT[:], in_=pz[0:16, :], func=AF.Copy), reads=[bpz], writes=[b_zT])
            pla, bpla = nbank(k)
            P.op("pe", lambda e, pla=pla: e.matmul(pla[:], lhsT=a2[:], rhs=zT[:], start=True, stop=True), reads=[b_a2, b_zT], writes=[bpla])
            P.op("act", lambda e, pla=pla: e.activation(out=spl[:], in_=pla[:], func=AF.Exp, scale=-1.0, bias=pcol(k, "gla_ab", 0, neg=True)),
                 reads=[bpla, k.b_npv], writes=[b_spl])
            P.op("act", lambda e: e.activation(out=spl[:], in_=spl[:], func=AF.Ln, bias=1.0), reads=[b_spl], writes=[b_spl])
            for c in range(16):
                cc = slice(c * 32, (c + 1) * 32)
                P.op("dve", lambda e, cc=cc: e.tensor_tensor_scan(out=bcs[:, cc], data0=ones32[:], data1=spl[:, cc], initial=0.0,
                                                              op0=ALU.mult, op1=ALU.add),
                     reads=[b_ones, b_spl], writes=[b_bcs])
            P.op("act", lambda e: e.activation(out=Ep[:], in_=bcs[:], func=AF.Exp, scale=1.0 / 16.0), reads=[b_bcs], writes=[b_Ep])
            P.op("act", lambda e: e.activation(out=En[:], in_=bcs[:], func=AF.Exp, scale=-1.0 / 16.0), reads=[b_bcs], writes=[b_En])
            yield
            pq, bpq = proj_fm(k, slA, bwA, 0, 128, tb)
            P.op("dve", lambda e, pq=pq: e.scalar_tensor_tensor(out=qd[:], in0=pq[:], scalar=32.0 ** -0.5, in1=En[:], op0=ALU.mult, op1=ALU.mult),
                 reads=[bpq, b_En], writes=[b_qd])
            yield
            pk, bpk = proj_fm(k, slA, bwA, 128, 128, tb)
            P.op("dve", lambda e, pk=pk: e.tensor_tensor(out=ki[:], in0=pk[:], in1=Ep[:], op=ALU.mult), reads=[bpk, b_Ep], writes=[b_ki])
            yield
            for ch in range(2):
                pv, bpv = proj_fm(k, slA, bwA, 256 + ch * 128, 128, tb)
                P.op("act", lambda e, pv=pv, ch=ch: e.activation(out=vb[:, ch, :], in_=pv[:], func=AF.Copy), reads=[bpv], writes=[b_vb])
                pr, bpr = proj_fm(k, slB, bwB, ch * 128, 128, tb)
                P.op("act", lambda e, pr=pr, ch=ch: e.activation(out=sr[:, ch, :], in_=pr[:], func=AF.Silu), reads=[bpr], writes=[b_sr])
                yield
            for c in range(16):
                cc = slice(c * 32, (c + 1) * 32)
                j = it % 2
                it += 1
                pS, bpS = nbank(k, hold=True)
                P.op("dve", lambda e, pS=pS: e.memset(pS[:, 0:128], 0.0), writes=[bpS])
                yield
                for h in range(4):
                    hs = slice(h * 32, (h + 1) * 32)
                    P.op("pe", lambda e, pS=pS, hs=hs, cc=cc: mm(e, pS[hs, hs], ki[hs, cc], qd[hs, cc], (hs.start, hs.start)),
                         reads=[b_ki, b_qd], writes=[bpS], rt=hs.start)
                yield
                P.op("dve", lambda e, pS=pS, j=j: e.tensor_tensor(out=STm[j][:], in0=pS[:, 0:128], in1=bd_iu, op=ALU.mult),
                     reads=[bpS, k.b_cst], writes=[b_STm[j]])
                release(k, pS)
                pTr, bpTr = nbank(k, hold=True)
                pTb = k.psb[k.ps.index(pTr)]
                P.op("dve", lambda e, pTr=pTr: e.memset(pTr[:, 64:128], 0.0), writes=[bpTr])
                yield
                for h in (0, 2, 1, 3):
                    hs = slice(h * 32, (h + 1) * 32)
                    vs = slice((h % 2) * 64, (h % 2) * 64 + 64)
                    P.op("pe", lambda e, pTb=pTb, hs=hs, vs=vs, h=h, cc=cc: tr(e, pTb[hs, 0:64], vb[vs, h // 2, cc], identb[vs, vs], (vs.start, hs.start)),
                         reads=[b_vb, k.b_cstb], writes=[bpTr], rt=vs.start)
                for h in range(4):
                    hs = slice(h * 32, (h + 1) * 32)
                    P.op("pe", lambda e, pTb=pTb, hs=hs, cc=cc, h=h: tr(e, pTb[hs, 128 + h * 32:128 + (h + 1) * 32], ki[hs, cc], identb[hs, hs], (hs.start, hs.start)),
                         reads=[b_ki, k.b_cstb], writes=[bpTr], rt=hs.start)
                yield
                P.op("act", lambda e, pTb=pTb, j=j: e.activation(out=V4[j][:], in_=pTb[:, 0:64], func=AF.Copy), reads=[bpTr], writes=[b_V4[j]])
                P.op("dve", lambda e, pTb=pTb, j=j: e.tensor_copy(out=KT[j][:], in_=pTb[:, 128:256]),
                     reads=[bpTr], writes=[b_KT[j]])
                release(k, pTr)
                yield
                pO, bpO = nbank(k, hold=True)
                P.op("pe", lambda e, pO=pO, j=j: e.matmul(pO[:, 0:64], lhsT=STm[j][:], rhs=V4[j][:], start=True, stop=False),
                     reads=[b_STm[j], b_V4[j]], writes=[bpO])
                for h in range(4):
                    hs = slice(h * 32, (h + 1) * 32)
                    P.op("pe", lambda e, pO=pO, hs=hs, cc=cc, h=h: mm(e, pO[hs, 0:64], qd[hs, cc], S4b[hs, :], (hs.start, hs.start), start=False, stop=True),
                         reads=[b_qd, b_S4b], writes=[bpO], rt=hs.start)
                pSt, bpSt = nbank(k, hold=True)
                P.op("pe", lambda e, pSt=pSt, j=j: e.matmul(pSt[:, 0:64], lhsT=KT[j][:], rhs=V4[j][:], start=True, stop=True),
                     reads=[b_KT[j], b_V4[j]], writes=[bpSt])
                yield
                P.op("act", lambda e, pO=pO, c=c: e.activation(out=OALL[:, c, :], in_=pO[:, 0:64], func=AF.Copy), reads=[bpO], writes=[b_OALL])
                release(k, pO)
                P.op("dve", lambda e, pSt=pSt: e.tensor_tensor(out=S4f[:], in0=pSt[:, 0:64], in1=S4f[:], op=ALU.add), reads=[bpSt, b_S4f], writes=[b_S4f])
                release(k, pSt)
                yield
                P.op("dve", lambda e, c=c: e.tensor_scalar(out=S4f[:], in0=S4f[:], scalar1=En[:, c * 32 + 31:c * 32 + 32], scalar2=None, op0=ALU.mult),
                     reads=[b_S4f, b_En], writes=[b_S4f])
                yield
                P.op("act", lambda e: e.activation(out=S4b[:], in_=S4f[:], func=AF.Copy), reads=[b_S4f], writes=[b_S4b])
                yield
            P.op("dve", lambda e: e.tensor_tensor(out=osq[:], in0=OALL[:], in1=OALL[:], op=ALU.mult), reads=[b_OALL], writes=[b_osq])
            P.op("dve", lambda e: e.tensor_reduce(out=ss[:], in_=osq[:], axis=mybir.AxisListType.X, op=ALU.add), reads=[b_osq], writes=[b_ss])
            P.op("dve", lambda e: e.tensor_scalar(out=ss[:], in0=ss[:], scalar1=1.0 / 64.0, scalar2=1e-5, op0=ALU.mult, op1=ALU.add),
                 reads=[b_ss], writes=[b_ss])
            P.op("act", lambda e: e.activation(out=ss[:], in_=ss[:], func=AF.Ln), reads=[b_ss], writes=[b_ss])
            P.op("act", lambda e: e.activation(out=ss[:], in_=ss[:], func=AF.Exp, scale=-0.5), reads=[b_ss], writes=[b_ss])
            for c in range(16):
                P.op("dve", lambda e, c=c: e.tensor_scalar(out=ONALL[:, c, :], in0=OALL[:, c, :], scalar1=ss[:, c:c + 1], scalar2=None, op0=ALU.mult),
                     reads=[b_OALL, b_ss], writes=[b_ON])
            yield
            pF, bpF = nbank(k, hold=True)
            pFb = k.psb[k.ps.index(pF)]
            for c in range(16):
                if c % 4 == 0:
                    yield
                for h in range(4):
                    hs = slice(h * 32, (h + 1) * 32)
                    vs = slice((h % 2) * 64, (h % 2) * 64 + 64)
                    o0 = (h // 2) * 512 + c * 32
                    P.op("pe", lambda e, pFb=pFb, hs=hs, vs=vs, o0=o0, c=c: tr(e, pFb[vs, o0:o0 + 32], ONALL[hs, c, :], identb[hs, hs], (hs.start, vs.start)),
                         reads=[b_ON, k.b_cstb], writes=[bpF], rt=hs.start)
            for fc in range(2):
                P.op("dve", lambda e, pFb=pFb, fc=fc, sl=sl: e.scalar_tensor_tensor(
                    out=brT[:, 4 + fc, sl], in0=pFb[:, fc * 512:(fc + 1) * 512], scalar=pcol(k, "gla_ln_g", fc), in1=sr[:, fc, :],
                    op0=ALU.mult, op1=ALU.mult),
                    reads=[bpF, k.b_pv, b_sr], writes=[b_brT[4 + fc][tb]])
            release(k, pF)
            yield
        yield


def mixer_rwkv(k, l, brT, b_brT, s2):
    P = k.P; T = k.T
    NS = 256
    NSB = T // NS
    NCH = NS // 32
    identb = cs(k, "ident", bf=True)
    bones = cs(k, "bones64")
    if True:
        f32t = lambda n, shp: sbt(k, s2, n, shp, F32)
        bft = lambda n, shp: sbt(k, s2, n, shp, BF16)
        B = lambda: Buf()
        wa = f32t("rw_wa", [128, 256]); b_wa = B()
        g2a, g2b, b_g2 = k.g2a, k.g2b, k.b_g2
        omk = f32t("rw_omk", [128, 2]); b_omk = B()
        pprev = f32t("rw_pprev", [128, 9]); b_pprev = B()
        praw = [f32t(f"rw_praw{i}", [128, NS + 1]) for i in range(2)]; b_praw = [B(), B()]
        R = [f32t(f"rw_R{i}", [128, NS]) for i in range(2)]; b_R = [B(), B()]
        KX = [f32t(f"rw_KX{i}", [128, NS]) for i in range(2)]; b_KX = [B(), B()]
        V = [f32t(f"rw_V{i}", [128, NS]) for i in range(2)]; b_V = [B(), B()]
        XWA = f32t("rw_XWA", [128, NS]); b_XWA = B()
        XG0 = f32t("rw_XG0", [128, NS]); b_XG0 = B()
        XG1 = f32t("rw_XG1", [32, NS]); b_XG1 = B()
        sgx0 = bft("rw_sgx0", [128, NS]); sgx1 = bft("rw_sgx1", [32, NS]); b_sgx = B()
        EW = f32t("rw_EW", [128, NS]); b_EW = B()
        AL = f32t("rw_AL", [128, NS]); b_AL = B()
        GTs = [[bft(f"rw_GT{q}{i}", [128, NS]) for i in range(2)] for q in range(3)]; b_GTs = [[B(), B()] for q in range(3)]
        KKN = f32t("rw_KKN", [128, NS]); b_KKN = B()
        TMP = f32t("rw_TMP", [128, NS]); b_TMP = B()
        CS = f32t("rw_CS", [128, NS]); b_CS = B()
        E1 = f32t("rw_E1", [128, NS]); b_E1 = B()
        E2 = f32t("rw_E2", [128, NS]); b_E2 = B()
        E3 = f32t("rw_E3", [128, NS]); b_E3 = B()
        WCs = [[f32t(f"rw_WC{q}{i}", [128, NCH]) for i in range(2)] for q in range(2)]; b_WCs = [[B(), B()], [B(), B()]]
        BONs = [[f32t(f"rw_BON{q}{i}", [128, NS]) for i in range(2)] for q in range(3)]; b_BONs = [[B(), B()] for q in range(3)]
        ones32 = f32t("rw_ones", [128, 32]); b_ones = B()
        mk2 = lambda nm: ([[bft(f"rw_{nm}{q}{i}", [128, NS]) for i in range(2)] for q in range(2)], [[B(), B()], [B(), B()]])
        RHs, b_RHs = mk2("rh"); KHs, b_KHs = mk2("kh"); BHs, b_BHs = mk2("bh"); AHs, b_AHs = mk2("ah"); VBs, b_VBs = mk2("vb")
        M4 = [bft(f"rw_M4{i}", [128, 4, 128]) for i in range(2)]; b_M4 = [B(), B()]
        RKT = [bft(f"rw_RKT{i}", [128, 128]) for i in range(2)]; b_RKT = [B(), B()]
        Am = [bft(f"rw_A{i}", [128, 128]) for i in range(2)]; b_A = [B(), B()]
        ATm = [bft(f"rw_AT{i}", [128, 128]) for i in range(2)]; b_AT = [B(), B()]
        TT = [[bft(f"rw_TT{i}{q}", [128, 128]) for q in range(2)] for i in range(2)]; b_TT = [[B(), B()], [B(), B()]]
        BK = [bft(f"rw_BK{i}", [128, 4, 128]) for i in range(2)]; b_BK = [B(), B()]
        V4 = [bft(f"rw_V4{i}", [128, 64]) for i in range(2)]; b_V4 = [B(), B()]
        Xb = [bft(f"rw_Xb{i}", [128, 64]) for i in range(2)]; b_Xb = [B(), B()]
        Ub = [bft(f"rw_Ub{i}", [128, 64]) for i in range(2)]; b_Ub = [B(), B()]
        Hf = [f32t(f"rw_Hf{i}", [128, 64]) for i in range(2)]; b_Hf = [B(), B()]
        Hb = [bft(f"rw_Hb{i}", [128, 64]) for i in range(2)]; b_Hb = [B(), B()]
        YALLs = [f32t(f"rw_YALL{q}", [128, NCH, 64]) for q in range(2)]; b_YALLs = [B(), B()]
        ysq = f32t("rw_ysq", [128, NCH, 64]); b_ysq = B()
        s1 = f32t("rw_s1", [128, NCH]); b_s1 = B()
        s2_ = f32t("rw_s2", [128, NCH]); b_s2 = B()
        YN = bft("rw_YN", [128, NCH, 64]); b_YN = B()
        y1 = f32t("rw_y1", [128, NS]); b_y1 = B()
        masks = cs(k, "rwmask4")
        bd_iu = cs(k, "bd32_iu")

        slA, bwA = load_w(k, k.w_in_d[l][:, 0:512], 512)
        slB, bwB = load_w(k, k.w_in_d[l][:, 512:1024], 512)
        slC, bwC = k.wsm, k.b_wsm
        vC = k.w_in_d[l][:, 1024:1056].rearrange("(c p) n -> p c n", p=128)
        P.dma("pool", lambda e: e.dma_start(out=slC[:, :, :], in_=vC), writes=[bwC])
        P.dma("sp", lambda e: e.dma_start(out=wa[:], in_=k.rw_wa_d[l]), writes=[b_wa])
        P.dma("pool", lambda e: e.dma_start(out=g2a[:], in_=k.rw_g2_d[l][0:128, :]), writes=[b_g2])
        P.dma("pool", lambda e: e.dma_start(out=g2b[:], in_=k.rw_g2_d[l][128:160, :]), writes=[b_g2])
        oka, _ = PV["rw_ka"]
        P.op("dve", lambda e: e.tensor_scalar(out=omk[:], in0=k.pv[:, oka:oka + 2], scalar1=-1.0, scalar2=1.0, op0=ALU.mult, op1=ALU.add),
             reads=[k.b_pv], writes=[b_omk])
        P.op("dve", lambda e: e.memset(pprev[:], 0.0), writes=[b_pprev])
        P.op("dve", lambda e: e.memset(ones32[:], 1.0), writes=[b_ones])
        for hp in range(2):
            P.op("dve", lambda e, hp=hp: e.memset(Hf[hp][:], 0.0), writes=[b_Hf[hp]])
            P.op("dve", lambda e, hp=hp: e.memset(Hb[hp][:], 0.0), writes=[b_Hb[hp]])
        dests = [(R[0], b_R[0], 128), (R[1], b_R[1], 128), (KX[0], b_KX[0], 128), (KX[1], b_KX[1], 128),
                 (V[0], b_V[0], 128), (V[1], b_V[1], 128), (XWA, b_XWA, 128), (XG0, b_XG0, 128), (XG1, b_XG1, 32)]
        ipc = [0]

        def proj_n(slot, bw, col0, ncols, s0):
            tb = s0 // 512
            pt, bp = nbank(k)
            for c in range(KC):
                P.op("pe", lambda e, c=c: e.matmul(pt[0:ncols, 0:NS], lhsT=slot[:, c, col0:col0 + ncols], rhs=k.xT[:, c, s0:s0 + NS],
                                                   start=(c == 0), stop=(c == KC - 1)),
                     reads=[bw, k.b_xT[c][tb]], writes=[bp])
            return pt, bp

        def subblock(sb):
            s0 = sb * NS
            tb = s0 // 512
            p_ = sb % 2
            rh, kh, bh, ah, vb = RHs[p_], KHs[p_], BHs[p_], AHs[p_], VBs[p_]
            b_rh, b_kh, b_bh, b_ah, b_vb = b_RHs[p_], b_KHs[p_], b_BHs[p_], b_AHs[p_], b_VBs[p_]
            WC, b_WC = WCs[p_], b_WCs[p_]
            BON, b_BON, GT, b_GT = BONs[sb % 3], b_BONs[sb % 3], GTs[sb % 3], b_GTs[sb % 3]
            YALL, b_YALL = YALLs[p_], b_YALLs[p_]

            def prep():
                for f, (dst, bd, rows) in enumerate(dests):
                    if f < 4:
                        pt, bp = proj_n(slA, bwA, f * 128, 128, s0)
                    elif f < 8:
                        pt, bp = proj_n(slB, bwB, (f - 4) * 128, 128, s0)
                    else:
                        pt, bp = proj_n(slC, bwC, 0, 32, s0)
                    j = ipc[0] % 2
                    ipc[0] += 1
                    P.op("dve", lambda e, j=j, f=f, rows=rows: e.tensor_copy(out=praw[j][0:rows, 0:1], in_=pprev[0:rows, f:f + 1]),
                         reads=[b_pprev], writes=[b_praw[j]])
                    P.op("act", lambda e, j=j, pt=pt, rows=rows: e.activation(out=praw[j][0:rows, 1:NS + 1], in_=pt[0:rows, 0:NS], func=AF.Copy),
                         reads=[bp], writes=[b_praw[j]])
                    P.op("dve", lambda e, j=j, f=f, rows=rows: e.tensor_copy(out=pprev[0:rows, f:f + 1], in_=praw[j][0:rows, NS:NS + 1]),
                         reads=[b_praw[j]], writes=[b_pprev])
                    P.op("dve", lambda e, j=j, rows=rows, dst=dst: e.tensor_tensor(out=dst[0:rows, :], in0=praw[j][0:rows, 0:NS], in1=praw[j][0:rows, 1:NS + 1], op=ALU.subtract),
                         reads=[b_praw[j]], writes=[bd])
                    P.op("dve", lambda e, j=j, rows=rows, f=f, dst=dst: e.scalar_tensor_tensor(
                        out=dst[0:rows, :], in0=dst[0:rows, :], scalar=pcol(k, "rw_mu", f, rows=rows), in1=praw[j][0:rows, 1:NS + 1],
                        op0=ALU.mult, op1=ALU.add),
                        reads=[bd, b_praw[j], k.b_pv], writes=[bd])
                    yield
                P.op("act", lambda e: e.activation(out=sgx0[:], in_=XG0[:], func=AF.Sigmoid), reads=[b_XG0], writes=[b_sgx])
                P.op("act", lambda e: e.activation(out=sgx1[:], in_=XG1[:], func=AF.Sigmoid), reads=[b_XG1], writes=[b_sgx])
                for hp in range(2):
                    pg, bpg = nbank(k)
                    P.op("pe", lambda e, pg=pg, hp=hp: e.matmul(pg[:, 0:NS], lhsT=g2a[:, hp * 128:(hp + 1) * 128], rhs=sgx0[:], start=True, stop=False),
                         reads=[b_g2, b_sgx], writes=[bpg])
                    P.op("pe", lambda e, pg=pg, hp=hp: e.matmul(pg[:, 0:NS], lhsT=g2b[:, hp * 128:(hp + 1) * 128], rhs=sgx1[:], start=False, stop=True),
                         reads=[b_g2, b_sgx], writes=[bpg])
                    P.op("act", lambda e, pg=pg, hp=hp: e.activation(out=GT[hp][:], in_=pg[:, 0:NS], func=AF.Copy), reads=[bpg], writes=[b_GT[hp]])
                    yield
                P.op("act", lambda e: e.activation(out=XWA[0:64, :], in_=XWA[0:64, :], func=AF.Tanh), reads=[b_XWA], writes=[b_XWA])
                for hp in range(2):
                    hs = slice(hp * 128, (hp + 1) * 128)
                    pw, bpw = nbank(k)
                    P.op("pe", lambda e, pw=pw, hs=hs: mm(e, pw[:, 0:NS], wa[0:64, hs], XWA[0:64, :], (0, 0)), reads=[b_wa, b_XWA], writes=[bpw], rt=0)
                    P.op("act", lambda e, pw=pw, hp=hp: e.activation(out=EW[:], in_=pw[:, 0:NS], func=AF.Exp, scale=-1.0, bias=pcol(k, "rw_w0", hp, neg=True)),
                         reads=[bpw, k.b_npv], writes=[b_EW])
                    P.op("act", lambda e: e.activation(out=EW[:], in_=EW[:], func=AF.Ln, bias=1.0), reads=[b_EW], writes=[b_EW])
                    P.op("act", lambda e: e.activation(out=EW[:], in_=EW[:], func=AF.Exp, scale=-1.0, bias=-0.5), reads=[b_EW], writes=[b_EW])
                    yield
                    pa, bpa = nbank(k)
                    P.op("pe", lambda e, pa=pa, hs=hs: mm(e, pa[:, 0:NS], wa[64:128, hs], XWA[64:128, :], (64, 0)), reads=[b_wa, b_XWA], writes=[bpa], rt=64)
                    P.op("act", lambda e, pa=pa, hp=hp: e.activation(out=AL[:], in_=pa[:, 0:NS], func=AF.Sigmoid, bias=pcol(k, "rw_a0", hp)),
                         reads=[bpa, k.b_pv], writes=[b_AL])
                    yield
                    P.op("dve", lambda e, hp=hp: e.tensor_scalar(out=KKN[:], in0=KX[hp][:], scalar1=pcol(k, "rw_kk", hp), scalar2=None, op0=ALU.mult),
                         reads=[b_KX[hp], k.b_pv], writes=[b_KKN])
                    P.op("dve", lambda e: e.tensor_tensor(out=TMP[:], in0=KKN[:], in1=KKN[:], op=ALU.mult), reads=[b_KKN], writes=[b_TMP])
                    pss, bpss = nbank(k)
                    P.op("pe", lambda e, pss=pss: e.matmul(pss[:, 0:NS], lhsT=bones, rhs=TMP[:], start=True, stop=True), reads=[k.b_cst, b_TMP], writes=[bpss])
                    P.op("act", lambda e, pss=pss: e.activation(out=TMP[:], in_=pss[:, 0:NS], func=AF.Ln), reads=[bpss], writes=[b_TMP])
                    P.op("act", lambda e: e.activation(out=TMP[:], in_=TMP[:], func=AF.Exp, scale=-0.5), reads=[b_TMP], writes=[b_TMP])
                    P.op("dve", lambda e: e.tensor_tensor(out=KKN[:], in0=KKN[:], in1=TMP[:], op=ALU.mult), reads=[b_KKN, b_TMP], writes=[b_KKN])
                    yield
                    P.op("dve", lambda e, hp=hp: e.tensor_scalar(out=TMP[:], in0=AL[:], scalar1=pcol(k, "rw_ka", hp), scalar2=omk[:, hp:hp + 1], op0=ALU.mult, op1=ALU.add),
                         reads=[b_AL, k.b_pv, b_omk], writes=[b_TMP])
                    P.op("dve", lambda e, hp=hp: e.tensor_tensor(out=KX[hp][:], in0=KX[hp][:], in1=TMP[:], op=ALU.mult), reads=[b_KX[hp], b_TMP], writes=[b_KX[hp]])
                    P.op("dve", lambda e, hp=hp: e.scalar_tensor_tensor(out=TMP[:], in0=R[hp][:], scalar=pcol(k, "rw_rk", hp), in1=KX[hp][:], op0=ALU.mult, op1=ALU.mult),
                         reads=[b_R[hp], b_KX[hp], k.b_pv], writes=[b_TMP])
                    pbo, bpbo = nbank(k)
                    P.op("pe", lambda e, pbo=pbo: e.matmul(pbo[:, 0:NS], lhsT=bones, rhs=TMP[:], start=True, stop=True), reads=[k.b_cst, b_TMP], writes=[bpbo])
                    P.op("dve", lambda e, pbo=pbo, hp=hp: e.tensor_tensor(out=BON[hp][:], in0=pbo[:, 0:NS], in1=V[hp][:], op=ALU.mult),
                         reads=[bpbo, b_V[hp]], writes=[b_BON[hp]])
                    yield
                    P.op("act", lambda e, hp=hp: e.activation(out=vb[hp][:], in_=V[hp][:], func=AF.Copy), reads=[b_V[hp]], writes=[b_vb[hp]])
                    for c in range(NCH):
                        cc = slice(c * 32, (c + 1) * 32)
                        P.op("dve", lambda e, cc=cc: e.tensor_tensor_scan(out=CS[:, cc], data0=ones32[:], data1=EW[:, cc], initial=0.0, op0=ALU.mult, op1=ALU.add),
                             reads=[b_ones, b_EW], writes=[b_CS])
                    P.op("act", lambda e: e.activation(out=E1[:], in_=CS[:], func=AF.Exp), reads=[b_CS], writes=[b_E1])
                    P.op("act", lambda e: e.activation(out=E2[:], in_=CS[:], func=AF.Exp, scale=-1.0), reads=[b_CS], writes=[b_E2])
                    P.op("dve", lambda e: e.tensor_tensor(out=TMP[:], in0=EW[:], in1=CS[:], op=ALU.subtract), reads=[b_EW, b_CS, b_TMP], writes=[b_TMP])
                    P.op("act", lambda e: e.activation(out=E3[:], in_=TMP[:], func=AF.Exp), reads=[b_TMP], writes=[b_E3])
                    yield
                    E2v = E2[:].rearrange("p (c t) -> p c t", t=32)
                    P.op("dve", lambda e, hp=hp, E2v=E2v: e.tensor_copy(out=WC[hp][:], in_=E2v[:, :, 31]), reads=[b_E2], writes=[b_WC[hp]])
                    P.op("dve", lambda e, hp=hp: e.tensor_tensor(out=rh[hp][:], in0=R[hp][:], in1=E2[:], op=ALU.mult), reads=[b_R[hp], b_E2], writes=[b_rh[hp]])
                    P.op("dve", lambda e, hp=hp: e.tensor_tensor(out=kh[hp][:], in0=KX[hp][:], in1=E1[:], op=ALU.mult), reads=[b_KX[hp], b_E1], writes=[b_kh[hp]])
                    P.op("dve", lambda e: e.tensor_tensor(out=TMP[:], in0=KKN[:], in1=AL[:], op=ALU.mult), reads=[b_KKN, b_AL], writes=[b_TMP])
                    P.op("dve", lambda e, hp=hp: e.tensor_tensor(out=bh[hp][:], in0=TMP[:], in1=E1[:], op=ALU.mult), reads=[b_TMP, b_E1], writes=[b_bh[hp]])
                    P.op("dve", lambda e, hp=hp: e.scalar_tensor_tensor(out=ah[hp][:], in0=KKN[:], scalar=-1.0, in1=E3[:], op0=ALU.mult, op1=ALU.mult),
                         reads=[b_KKN, b_E3], writes=[b_ah[hp]])
                    yield
                yield

            def chunks():
                def A_gen(c):
                    cc = slice(c * 32, (c + 1) * 32)
                    j = c % 2
                    pX_, bpX_ = nbank(k, hold=True)
                    pY_, bpY_ = nbank(k, hold=True)
                    P.op("dve", lambda e: e.memset(pX_[:], 0.0), writes=[bpX_])
                    P.op("dve", lambda e: e.memset(pY_[:, 0:128], 0.0), writes=[bpY_])
                    yield
                    for h in (0, 2, 1, 3):
                        hp = h // 2
                        ks = slice((h % 2) * 64, (h % 2) * 64 + 64)
                        hs = slice(h * 32, (h + 1) * 32)
                        pos = (ks.start, hs.start)
                        for mi, (lt, blt, rt_, brt) in enumerate(((bh, b_bh, ah, b_ah), (ah, b_ah, bh, b_bh), (kh, b_kh, ah, b_ah), (bh, b_bh, rh, b_rh))):
                            P.op("pe", lambda e, hs=hs, ks=ks, hp=hp, mi=mi, lt=lt, rt_=rt_, pos=pos, h=h: mm(
                                e, pX_[hs, mi * 128 + h * 32: mi * 128 + (h + 1) * 32], lt[hp][ks, cc], rt_[hp][ks, cc], pos),
                                reads=[blt[hp], brt[hp]], writes=[bpX_], rt=pos[0])
                        P.op("pe", lambda e, hs=hs, ks=ks, hp=hp, pos=pos, h=h: mm(
                            e, pY_[hs, h * 32:(h + 1) * 32], kh[hp][ks, cc], rh[hp][ks, cc], pos),
                            reads=[b_kh[hp], b_rh[hp]], writes=[bpY_], rt=pos[0])
                    yield
                    P.op("dve", lambda e: e.tensor_tensor(out=M4[j][:].rearrange("p a b -> p (a b)"), in0=pX_[:], in1=masks, op=ALU.mult),
                         reads=[bpX_, k.b_cst], writes=[b_M4[j]])
                    P.op("dve", lambda e: e.tensor_tensor(out=RKT[j][:], in0=pY_[:, 0:128], in1=bd_iu, op=ALU.mult),
                         reads=[bpY_, k.b_cst], writes=[b_RKT[j]])
                    release(k, pX_); release(k, pY_)
                    LT = M4[j][:, 0, :]; Lm = M4[j][:, 1, :]
                    TTj = TT[j]
                    bTTj = b_TT[j]
                    P.op("dve", lambda e: e.tensor_tensor(out=TTj[0][:], in0=LT, in1=identb, op=ALU.add), reads=[b_M4[j], k.b_cstb], writes=[bTTj[0]])
                    yield
                    A_prev, bA_prev, AT_prev, bAT_prev = Lm, b_M4[j], LT, b_M4[j]
                    ti = 0
                    for kq in range(1, 5):
                        an = kq % 2
                        pA, bpA = nbank(k, hold=True)
                        P.op("pe", lambda e, pA=pA, AT_prev=AT_prev, A_prev=A_prev: e.matmul(pA[:, 0:128], lhsT=AT_prev, rhs=A_prev, start=True, stop=True),
                             reads=[bA_prev, bAT_prev], writes=[bpA])
                        if kq < 4:
                            P.op("pe", lambda e, pA=pA, AT_prev=AT_prev, A_prev=A_prev: e.matmul(pA[:, 128:256], lhsT=A_prev, rhs=AT_prev, start=True, stop=True),
                                 reads=[bA_prev, bAT_prev], writes=[bpA])
                        yield
                        P.op("act", lambda e, pA=pA, an=an: e.activation(out=Am[an][:], in_=pA[:, 0:128], func=AF.Copy), reads=[bpA], writes=[b_A[an]])
                        if kq < 4:
                            P.op("act", lambda e, pA=pA, an=an: e.activation(out=ATm[an][:], in_=pA[:, 128:256], func=AF.Copy), reads=[bpA], writes=[b_AT[an]])
                        release(k, pA)
                        yield
                        pT, bpT = nbank(k, hold=True)
                        P.op("pe", lambda e, pT=pT, an=an, ti=ti: e.matmul(pT[:, 0:128], lhsT=Am[an][:], rhs=TTj[ti][:], start=True, stop=True),
                             reads=[b_A[an], bTTj[ti]], writes=[bpT])
                        yield
                        P.op("dve", lambda e, pT=pT, ti=ti: e.tensor_tensor(out=TTj[1 - ti][:], in0=pT[:, 0:128], in1=TTj[ti][:], op=ALU.add),
                             reads=[bpT, bTTj[ti]], writes=[bTTj[1 - ti]])
                        release(k, pT)
                        ti = 1 - ti
                        A_prev, bA_prev, AT_prev, bAT_prev = Am[an][:], b_A[an], ATm[an][:], b_AT[an]
                        yield
                    assert ti == 0
                    pTr, bpTr = nbank(k, hold=True)
                    pTb = k.psb[k.ps.index(pTr)]
                    P.op("dve", lambda e: e.memset(pTr[:, 0:256], 0.0), writes=[bpTr])
                    yield
                    for h in (0, 2, 1, 3):
                        hp = h // 2
                        ks = slice((h % 2) * 64, (h % 2) * 64 + 64)
                        hs = slice(h * 32, (h + 1) * 32)
                        pos = (ks.start, hs.start)
                        P.op("pe", lambda e, hs=hs, ks=ks, hp=hp, pos=pos: tr(e, pTb[hs, hp * 128 + ks.start: hp * 128 + ks.start + 64], bh[hp][ks, cc], identb[ks, ks], pos),
                             reads=[b_bh[hp], k.b_cstb], writes=[bpTr], rt=pos[0])
                        P.op("pe", lambda e, hs=hs, ks=ks, hp=hp, pos=pos: tr(e, pTb[hs, 256 + hp * 128 + ks.start: 256 + hp * 128 + ks.start + 64], kh[hp][ks, cc], identb[ks, ks], pos),
                             reads=[b_kh[hp], k.b_cstb], writes=[bpTr], rt=pos[0])
                        P.op("pe", lambda e, hs=hs, ks=ks, hp=hp, pos=pos: tr(e, pTb[hs, 512:576], vb[hp][ks, cc], identb[ks, ks], pos),
                             reads=[b_vb[hp], k.b_cstb], writes=[bpTr], rt=pos[0])
                    yield
                    P.op("dve", lambda e: e.tensor_copy(out=BK[j][:].rearrange("p a b -> p (a b)"), in_=pTb[:, 0:512]),
                         reads=[bpTr], writes=[b_BK[j]])
                    P.op("act", lambda e: e.activation(out=V4[j][:], in_=pTb[:, 512:576], func=AF.Copy), reads=[bpTr], writes=[b_V4[j]])
                    release(k, pTr)
                    yield

                def B_gen(c):
                    cc = slice(c * 32, (c + 1) * 32)
                    j = c % 2
                    AKT = M4[j][:, 2, :]; RBT = M4[j][:, 3, :]
                    TTf, bTTf = TT[j][0], b_TT[j][0]
                    pX, bpX = nbank(k, hold=True)
                    P.op("pe", lambda e: e.matmul(pX[:, 0:64], lhsT=AKT, rhs=V4[j][:], start=True, stop=False),
                         reads=[b_M4[j], b_V4[j]], writes=[bpX])
                    for h in (0, 2, 1, 3):
                        hp = h // 2
                        ks = slice((h % 2) * 64, (h % 2) * 64 + 64)
                        hs = slice(h * 32, (h + 1) * 32)
                        P.op("pe", lambda e, hs=hs, ks=ks, hp=hp: mm(e, pX[hs, 0:64], ah[hp][ks, cc], Hb[hp][ks, :], (ks.start, hs.start), start=False, stop=True),
                             reads=[b_ah[hp], b_Hb[hp]], writes=[bpX], rt=ks.start)
                    yield
                    P.op("act", lambda e: e.activation(out=Xb[j][:], in_=pX[:, 0:64], func=AF.Copy), reads=[bpX], writes=[b_Xb[j]])
                    release(k, pX)
                    yield
                    pU, bpU = nbank(k, hold=True)
                    P.op("pe", lambda e: e.matmul(pU[:, 0:64], lhsT=TTf[:], rhs=Xb[j][:], start=True, stop=True),
                         reads=[bTTf, b_Xb[j]], writes=[bpU])
                    yield
                    P.op("act", lambda e: e.activation(out=Ub[j][:], in_=pU[:, 0:64], func=AF.Copy), reads=[bpU], writes=[b_Ub[j]])
                    release(k, pU)
                    yield
                    pHs = []
                    for hp in range(2):
                        pH, bpH = nbank(k, hold=True)
                        pHs.append((pH, bpH))
                        P.op("pe", lambda e, pH=pH, hp=hp: e.matmul(pH[:, 0:64], lhsT=BK[j][:, hp, :], rhs=Ub[j][:], start=True, stop=False),
                             reads=[b_BK[j], b_Ub[j]], writes=[bpH])
                        P.op("pe", lambda e, pH=pH, hp=hp: e.matmul(pH[:, 0:64], lhsT=BK[j][:, 2 + hp, :], rhs=V4[j][:], start=False, stop=True),
                             reads=[b_BK[j], b_V4[j]], writes=[bpH])
                    pY, bpY = nbank(k, hold=True)
                    P.op("pe", lambda e: e.matmul(pY[:, 0:64], lhsT=RBT, rhs=Ub[j][:], start=True, stop=False),
                         reads=[b_M4[j], b_Ub[j]], writes=[bpY])
                    P.op("pe", lambda e: e.matmul(pY[:, 0:64], lhsT=RKT[j][:], rhs=V4[j][:], start=False, stop=False),
                         reads=[b_RKT[j], b_V4[j]], writes=[bpY])
                    for h in (0, 2, 1, 3):
                        hp = h // 2
                        ks = slice((h % 2) * 64, (h % 2) * 64 + 64)
                        hs = slice(h * 32, (h + 1) * 32)
                        P.op("pe", lambda e, hs=hs, ks=ks, hp=hp: mm(e, pY[hs, 0:64], rh[hp][ks, cc], Hb[hp][ks, :], (ks.start, hs.start), start=False, stop=True),
                             reads=[b_rh[hp], b_Hb[hp]], writes=[bpY], rt=ks.start)
                    yield
                    for hp in range(2):
                        pH, bpH = pHs[hp]
                        P.op("dve", lambda e, pH=pH, hp=hp: e.tensor_tensor(out=Hf[hp][:], in0=pH[:, 0:64], in1=Hf[hp][:], op=ALU.add),
                             reads=[bpH, b_Hf[hp]], writes=[b_Hf[hp]])
                        release(k, pH)
                    P.op("act", lambda e: e.activation(out=YALL[:, c, :], in_=pY[:, 0:64], func=AF.Copy), reads=[bpY], writes=[b_YALL])
                    release(k, pY)
                    yield
                    for hp in range(2):
                        P.op("dve", lambda e, hp=hp: e.tensor_scalar(out=Hf[hp][:], in0=Hf[hp][:], scalar1=WC[hp][:, c:c + 1], scalar2=None, op0=ALU.mult),
                             reads=[b_Hf[hp], b_WC[hp]], writes=[b_Hf[hp]])
                    yield
                    for hp in range(2):
                        P.op("act", lambda e, hp=hp: e.activation(out=Hb[hp][:], in_=Hf[hp][:], func=AF.Copy), reads=[b_Hf[hp]], writes=[b_Hb[hp]])
                    yield

                for _ in A_gen(0):
                    yield
                for c in range(NCH):
                    gens = [B_gen(c)]
                    if c + 1 < NCH:
                        gens.append(A_gen(c + 1))
                    while gens:
                        for g_ in list(gens):
                            try:
                                next(g_)
                            except StopIteration:
                                gens.remove(g_)
                        yield
                yield

            def post():
                P.op("dve", lambda e: e.tensor_reduce(out=s1[:], in_=YALL[:], axis=mybir.AxisListType.X, op=ALU.add), reads=[b_YALL], writes=[b_s1])
                P.op("dve", lambda e: e.tensor_tensor(out=ysq[:], in0=YALL[:], in1=YALL[:], op=ALU.mult), reads=[b_YALL], writes=[b_ysq])
                P.op("dve", lambda e: e.tensor_reduce(out=s2_[:], in_=ysq[:], axis=mybir.AxisListType.X, op=ALU.add), reads=[b_ysq], writes=[b_s2])
                P.op("dve", lambda e: e.tensor_scalar(out=s1[:], in0=s1[:], scalar1=1.0 / 64.0, scalar2=None, op0=ALU.mult), reads=[b_s1], writes=[b_s1])
                P.op("dve", lambda e: e.scalar_tensor_tensor(out=s2_[:], in0=s2_[:], scalar=1.0 / 64.0, in1=s2_[:], op0=ALU.mult, op1=ALU.bypass) if False else
                     e.tensor_scalar(out=s2_[:], in0=s2_[:], scalar1=1.0 / 64.0, scalar2=64e-5, op0=ALU.mult, op1=ALU.add), reads=[b_s2], writes=[b_s2])
                P.op("dve", lambda e: e.tensor_tensor(out=ysq[:, :, 0], in0=s1[:], in1=s1[:], op=ALU.mult), reads=[b_s1, b_ysq], writes=[b_ysq])
                P.op("dve", lambda e: e.tensor_tensor(out=s2_[:], in0=s2_[:], in1=ysq[:, :, 0], op=ALU.subtract), reads=[b_s2, b_ysq], writes=[b_s2])
                P.op("act", lambda e: e.activation(out=s2_[:], in_=s2_[:], func=AF.Ln), reads=[b_s2], writes=[b_s2])
                P.op("act", lambda e: e.activation(out=s2_[:], in_=s2_[:], func=AF.Exp, scale=-0.5), reads=[b_s2], writes=[b_s2])
                P.op("dve", lambda e: e.scalar_tensor_tensor(out=s1[:], in0=s1[:], scalar=-1.0, in1=s2_[:], op0=ALU.mult, op1=ALU.mult), reads=[b_s1, b_s2], writes=[b_s1])
                for c in range(NCH):
                    P.op("act", lambda e, c=c: e.activation(out=YN[:, c, :], in_=YALL[:, c, :], func=AF.Identity, scale=s2_[:, c:c + 1], bias=s1[:, c:c + 1]),
                         reads=[b_YALL, b_s1, b_s2], writes=[b_YN])
                yield
                pF, bpF = nbank(k, hold=True)
                pFb = k.psb[k.ps.index(pF)]
                for c in range(NCH):
                    if c % 4 == 0:
                        yield
                    for h in range(4):
                        hs = slice(h * 32, (h + 1) * 32)
                        vs = slice((h % 2) * 64, (h % 2) * 64 + 64)
                        o0 = (h // 2) * 512 + c * 32
                        P.op("pe", lambda e, pFb=pFb, hs=hs, vs=vs, o0=o0, c=c: tr(e, pFb[vs, o0:o0 + 32], YN[hs, c, :], identb[hs, hs], (hs.start, vs.start)),
                             reads=[b_YN, k.b_cstb], writes=[bpF], rt=hs.start)
                for hp in range(2):
                    P.op("act", lambda e, pFb=pFb, hp=hp: e.activation(out=y1[:], in_=pFb[:, hp * 512: hp * 512 + NS], func=AF.Identity,
                                                                   scale=pcol(k, "rw_ln_g", hp), bias=pcol(k, "rw_ln_b", hp)),
                         reads=[bpF, k.b_pv], writes=[b_y1])
                    P.op("dve", lambda e, hp=hp: e.tensor_tensor(out=y1[:], in0=y1[:], in1=BON[hp][:], op=ALU.add), reads=[b_y1, b_BON[hp]], writes=[b_y1])
                    P.op("dve", lambda e, hp=hp, s0=s0: e.tensor_tensor(out=brT[:, hp, s0:s0 + NS], in0=y1[:], in1=GT[hp][:], op=ALU.mult),
                         reads=[b_y1, b_GT[hp]], writes=[b_brT[hp][tb]])
                release(k, pF)
                yield
                yield

            return prep(), chunks(), post()

        phases = [subblock(sb) for sb in range(NSB)]
        for _ in phases[0][0]:
            yield
        for sb in range(NSB):
            gl = [phases[sb][1]]
            wl = [230.0]
            if sb + 1 < NSB:
                gl.append(phases[sb + 1][0]); wl.append(30.0)
            if sb >= 1:
                gl.append(phases[sb - 1][2]); wl.append(8.0)
            done = [0.0] * len(gl)
            live = list(range(len(gl)))
            while live:
                i_ = min(live, key=lambda q: done[q] / wl[q])
                try:
                    next(gl[i_])
                    done[i_] += 1.0
                except StopIteration:
                    live.remove(i_)
                yield
        for _ in phases[NSB - 1][2]:
            yield
        yield

def stage_gate(k, l, brT, b_brT):
    P = k.P; T = k.T; NTB = k.NTB
    with ExitStack() as s2:
        mg = sbt(k, s2, "mg", [128, 4, T], BF16)
        b_mg = [[Buf() for _ in range(NTB)] for _ in range(4)]
        acc = sbt(k, s2, "mg_acc", [128, 512], F32); b_acc = Buf()
        sg = [sbt(k, s2, f"mg_sg{i}", [128, 512], F32) for i in range(2)]; b_sg = [Buf(), Buf()]
        pr0 = sbt(k, s2, "mg_pr", [128, 512], F32); pr = [pr0, pr0]; bpr0 = Buf(); b_pr = [bpr0, bpr0]
        ups, b_ups = k.ups, k.b_ups
        lnb = alloc_ln(k, s2)
        og, _ = PV["gate_b"]
        i = 0
        for half in range(2):
            for fq in range(4):
                fc = half * 4 + fq
                u = fc % 2
                for b in range(4):
                    v = k.ups_d[b][l][:, fc * 128:(fc + 1) * 128].rearrange("(c p) n -> p c n", p=128)
                    P.dma("pool", lambda e, b=b, v=v, u=u: e.dma_start(out=ups[u][:, :, b, :], in_=v), writes=[b_ups[u]])
                i0 = k.wr_i
                k.wr_i = (i0 + 1) % k.NW
                slot, bw = k.wr[i0], k.b_wr[i0]
                for b in range(4):
                    c0 = COL_GATE + b * D + fc * 128
                    v = k.w_in_d[l][:, c0:c0 + 128].rearrange("(c p) n -> p c n", p=128)
                    P.dma("pool", lambda e, b=b, v=v, slot=slot: e.dma_start(out=slot[:, :, b * 128:(b + 1) * 128], in_=v), writes=[bw])
                for tb in range(NTB):
                    sl = slice(tb * 512, (tb + 1) * 512)
                    for b in range(4):
                        pg, bpg = nbank(k)
                        for c in range(KC):
                            P.op("pe", lambda e, pg=pg, c=c, b=b, sl=sl, slot=slot: e.matmul(
                                pg[:], lhsT=slot[:, c, b * 128:(b + 1) * 128], rhs=k.xT[:, c, sl], start=(c == 0), stop=(c == KC - 1)),
                                reads=[bw, k.b_xT[c][tb]], writes=[bpg])
                        pu, bpu = nbank(k)
                        for c in range(2):
                            P.op("pe", lambda e, pu=pu, c=c, b=b, sl=sl, u=u: e.matmul(
                                pu[:], lhsT=ups[u][:, c, b, :], rhs=brT[:, b * 2 + c, sl], start=(c == 0), stop=(c == 1)),
                                reads=[b_ups[u], b_brT[b * 2 + c][tb]], writes=[bpu])
                        j = i % 2
                        i += 1
                        gcol = k.pv[:, og + b * 8 + fc: og + b * 8 + fc + 1]
                        P.op("act", lambda e, pg=pg, j=j, gcol=gcol: e.activation(out=sg[j][:], in_=pg[:], func=AF.Sigmoid, bias=gcol),
                             reads=[bpg, k.b_pv], writes=[b_sg[j]])
                        if b == 0:
                            P.op("dve", lambda e, pu=pu, j=j: e.tensor_tensor(out=acc[:], in0=pu[:], in1=sg[j][:], op=ALU.mult),
                                 reads=[bpu, b_sg[j]], writes=[b_acc])
                        else:
                            P.op("dve", lambda e, pu=pu, j=j: e.tensor_tensor(out=pr[j][:], in0=pu[:], in1=sg[j][:], op=ALU.mult),
                                 reads=[bpu, b_sg[j]], writes=[b_pr[j]])
                            if b < 3:
                                P.op("dve", lambda e, j=j: e.tensor_tensor(out=acc[:], in0=acc[:], in1=pr[j][:], op=ALU.add),
                                     reads=[b_acc, b_pr[j]], writes=[b_acc])
                            else:
                                P.op("dve", lambda e, j=j, fq=fq, sl=sl: e.tensor_tensor(out=mg[:, fq, sl], in0=acc[:], in1=pr[j][:], op=ALU.add),
                                     reads=[b_acc, b_pr[j]], writes=[b_mg[fq][tb]])
            if "mg" in k.dbg_d:
                for c in range(4):
                    P.dma("sp", lambda e, c=c, half=half: e.dma_start(out=k.dbg_d["mg"][half * 4 + c], in_=mg[:, c, :]),
                          reads=[b_mg[c][tb] for tb in range(NTB)], is_output=True)
            out_proj_ln(k, l, 0, mg, b_mg, 4, k.w_out_d[l], first=(half == 0), last=(half == 1), row0=half * 512, lnbufs=lnb)
        P.barrier()


def run_concurrent(gens, weights):
    gens = list(gens)
    done = [0.0] * len(gens)
    live = list(range(len(gens)))
    while live:
        i = min(live, key=lambda q: done[q] / weights[q])
        try:
            next(gens[i])
            done[i] += 1.0
        except StopIteration:
            live.remove(i)


def stage_mix(k, l):
    P = k.P; T = k.T; NTB = k.NTB
    spill_xres(k)
    with ExitStack() as s2:
        brT = sbt(k, s2, "brT", [128, 8, T], BF16)
        b_brT = [[Buf() for _ in range(NTB)] for _ in range(8)]
        todo = k.mixers
        for nm, cs_ in (("rw", (0, 1)), ("cv", (2, 3)), ("gla", (4, 5)), ("fox", (6, 7))):
            if nm not in todo:
                for c in cs_:
                    P.op("dve", lambda e, c=c: e.memset(brT[:, c, :], 0.0), writes=[b_brT[c][tb] for tb in range(NTB)])
        fns = {"rw": mixer_rwkv, "gla": mixer_gla, "fox": mixer_fox, "cv": mixer_conv}
        for group in (("rw", "gla"), ("fox", "cv")):
            act = [n for n in group if n in todo]
            if not act:
                continue
            with ExitStack() as s3:
                wts = {"rw": 1753.0, "gla": 621.0, "fox": 341.0, "cv": 75.0}
                run_concurrent([fns[n](k, l, brT, b_brT, s3) for n in act], [wts[n] for n in act])
                P.barrier()
        if "brT" in k.dbg_d:
            for c in range(8):
                P.dma("sp", lambda e, c=c: e.dma_start(out=k.dbg_d["brT"][c], in_=brT[:, c, :]),
                      reads=[b_brT[c][tb] for tb in range(NTB)], is_output=True)
        reload_xres(k)
        stage_gate(k, l, brT, b_brT)


def prep_inputs(inp, L=DEPTH):
    f = lambda a: np.ascontiguousarray(np.asarray(a, dtype=np.float32))
    shared = {
        "consts": CONSTS,
        "pvec": np.stack([pack_pvec(inp, l) for l in range(L)]),
        "w_in": f(inp["w_in"][:L]),
        "rw_wa": f(np.concatenate([np.asarray(inp["rw_w2"][:L]), np.asarray(inp["rw_a2"][:L])], axis=1)),
        "rw_g2": f(inp["rw_g2"][:L]),
        "gla_a2": f(inp["gla_a2"][:L]),
        "w_out": f(inp["w_out"][:L]),
    }
    for n in ("rw_up", "cv_up", "gla_up", "fox_up", "xa_wq", "xa_wk", "xa_wv", "xa_wo", "ffn_w1", "ffn_w3", "ffn_w2"):
        shared[n] = f(inp[n][:L])
    return shared


_CACHE = {}


def kernel(**inputs):
    x = np.asarray(inputs["x"], np.float32)
    mem = np.asarray(inputs["mem"], np.float32)
    B = x.shape[0]
    if "nc" not in _CACHE:
        _CACHE["nc"] = build()[0]
    nc = _CACHE["nc"]
    shared = prep_inputs(inputs)
    in_maps = []
    for b in range(B):
        m = dict(shared)
        m["x"] = np.ascontiguousarray(x[b])
        m["mem"] = np.ascontiguousarray(mem[b])
        in_maps.append(m)
    res = run_bass_kernel_spmd(nc, in_maps, core_ids=list(range(B)))
    return np.stack([r["out"] for r in res.results], axis=0).astype(np.float32)
```

```python
import numpy as np
from contextlib import ExitStack
import concourse.bass as bass
import concourse.mybir as mybir
from concourse.bass_utils import run_bass_kernel_spmd

F32 = mybir.dt.float32
BF16 = mybir.dt.bfloat16
AF = mybir.ActivationFunctionType
ALU = mybir.AluOpType

D = 1024
KC = 8
DEPTH = 4
SEQ = 2048
MEM = 256
DFF = 2816
DIN = 7220
ALPHA = (2.0 * DEPTH) ** 0.25
LN_EPS = 1e-5
COL_GATE = 3124

ENGS = ("pe", "act", "dve", "pool", "sp")
EPOCH = 30000
NDMA_SEM = {"sp": 16, "pool": 12, "act": 4}


class Buf:
    __slots__ = ("name", "w", "rs", "excl")

    def __init__(self, name="", excl=False):
        self.name = name
        self.w = None
        self.rs = []
        self.excl = excl


class Op:
    __slots__ = ("eng", "fn", "pos", "needs_inc", "inc", "isdma", "dsem", "dval", "waits", "vc")


class Prog:
    def __init__(self, nc):
        self.nc = nc
        self.ops = {e: [] for e in ENGS}
        self.clock = {e: {} for e in ENGS}
        self.dma_uses = {}
        self.dma_rr = {e: 0 for e in NDMA_SEM}
        self.dma_last = {}
        self.out_dmas = []
        self.rd_dmas = []

    def op(self, eng, fn, reads=(), writes=(), extra=(), rt=None):
        o = Op()
        o.eng = eng; o.fn = fn
        o.isdma = False; o.needs_inc = False; o.inc = None
        ex = list(extra)
        force = None
        if eng == "pe":
            lr = getattr(self, "last_rt", None)
            cur = rt if rt is not None else "full"
            if lr is not None and lr[1] != cur and (lr[1] != "full" and cur != "full"):
                force = lr[0]
            self._force = force
        self._record(o, reads, writes, ex)
        if eng == "pe":
            self.last_rt = (o, rt if rt is not None else "full")
        return o

    def dma(self, queue, fn, reads=(), writes=(), is_output=False):
        o = Op()
        o.eng = queue; o.fn = fn
        o.isdma = True; o.needs_inc = False; o.inc = None
        k = self.dma_rr[queue]
        self.dma_rr[queue] = (k + 1) % NDMA_SEM[queue]
        key = (queue, k)
        uses = self.dma_uses.get(key, 0)
        o.dsem = key
        o.dval = 16 * (uses + 1)
        self.dma_uses[key] = uses + 1
        prev = self.dma_last.get(key)
        self.dma_last[key] = o
        self._record(o, reads, writes, [prev] if prev is not None else [])
        if is_output:
            self.out_dmas.append(o)
        if len(reads) > 0:
            self.rd_dmas.append(o)
        return o

    def barrier(self):
        last = {e: (self.ops[e][-1] if self.ops[e] else None) for e in ("pe", "act", "dve")}
        for e in ("pe", "act", "dve", "sp"):
            ex = []
            for f, o in last.items():
                if f == e or o is None:
                    continue
                j = len(self.ops[f]) - 1
                while j >= 0 and (self.ops[f][j].isdma or self.ops[f][j].fn is None):
                    j -= 1
                if j >= 0:
                    ex.append(self.ops[f][j])
            self.op(e, None, extra=ex + list(self.rd_dmas))
        self.rd_dmas = []

    def _record(self, o, reads, writes, extra=()):
        if any(b.excl for b in reads):
            writes = list(writes) + [b for b in reads if b.excl and b not in writes]
            reads = [b for b in reads if not b.excl]
        e = o.eng
        lst = self.ops[e]
        o.pos = len(lst) + 1
        deps = []
        for b in reads:
            if b.w is not None:
                deps.append((b.w, "raw"))
        for b in writes:
            if b.w is not None:
                deps.append((b.w, "waw"))
            for r in b.rs:
                deps.append((r, "war"))
        for d in extra:
            deps.append((d, "raw"))
        clk = self.clock[e]
        waits = []
        force = getattr(self, "_force", None)
        self._force = None
        if force is not None and e == "pe" and clk.get(("self", e), 0) < force.pos:
            waits.append(force)
            force.needs_inc = True
            clk[("self", e)] = force.pos
        best = {}
        d2 = []
        for (y, kind) in deps:
            if y is o:
                continue
            if (not y.isdma) and y.eng != e:
                if y.eng not in best or best[y.eng].pos < y.pos:
                    best[y.eng] = y
            else:
                d2.append((y, kind))
        deps = d2 + [(y, "raw") for y in best.values()]
        for (y, kind) in deps:
            if y is o:
                continue
            if y.isdma:
                if clk.get(y.dsem, 0) >= y.dval:
                    continue
                waits.append(y)
                self._merge(clk, y.vc)
            elif y.eng == e:
                if e != "pe" and clk.get(("self", e), 0) < y.pos and y.fn is not None:
                    waits.append(y)
                    y.needs_inc = True
                    clk[("self", e)] = y.pos
            else:
                if clk.get(y.eng, 0) >= y.pos:
                    continue
                waits.append(y)
                y.needs_inc = True
                self._merge(clk, y.vc)
        o.waits = waits
        if o.isdma:
            vc = dict(clk)
            vc[o.dsem] = o.dval
            o.vc = vc
            clk[e] = o.pos
        else:
            clk[e] = o.pos
            o.vc = dict(clk)
        lst.append(o)
        for b in reads:
            b.rs.append(o)
        for b in writes:
            b.w = o
            b.rs = []

    @staticmethod
    def _merge(clk, vc):
        for k, v in vc.items():
            if isinstance(k, tuple) and k and k[0] == "self":
                continue
            if clk.get(k, 0) < v:
                clk[k] = v

    def finalize(self, block, stack):
        nc = self.nc
        fin = self.op("sp", None)
        clk = self.clock["sp"]
        for d in self.out_dmas:
            if clk.get(d.dsem, 0) < d.dval:
                fin.waits.append(d)
                clk[d.dsem] = d.dval
        esems = {}
        for e in ENGS:
            c = 0
            for o in self.ops[e]:
                if o.needs_inc and not o.isdma:
                    assert o.fn is not None
                    c += 1
                    o.inc = c
            nep = c // EPOCH + 1
            esems[e] = [stack.enter_context(nc.semaphore(f"s_{e}_{i}")) for i in range(nep)]
        dsems = {}
        for (q, k) in self.dma_uses:
            dsems[(q, k)] = stack.enter_context(nc.semaphore(f"d_{q}_{k}"))
        stats = {e: [len(self.ops[e]), 0] for e in ENGS}

        def emit(e, eng):
            for o in self.ops[e]:
                for y in o.waits:
                    if y.isdma:
                        eng.wait_ge(dsems[y.dsem], y.dval)
                    else:
                        ep = (y.inc - 1) // EPOCH
                        eng.wait_ge(esems[y.eng][ep], y.inc - ep * EPOCH)
                    stats[e][1] += 1
                if o.fn is None:
                    continue
                ins = o.fn(eng)
                if o.isdma:
                    ins.then_inc(dsems[o.dsem], 16)
                elif o.needs_inc:
                    ep = (o.inc - 1) // EPOCH
                    ins.then_inc(esems[e][ep], 1)

        @block.tensor
        def _(eng):
            emit("pe", eng)

        @block.scalar
        def _(eng):
            emit("act", eng)

        @block.vector
        def _(eng):
            emit("dve", eng)

        @block.gpsimd
        def _(eng):
            emit("pool", eng)

        @block.sync
        def _(eng):
            emit("sp", eng)
        return stats


def make_consts():
    c = {}
    c["ident"] = np.eye(128, dtype=np.float32)
    c["ones"] = np.ones((128, 128), np.float32)
    b64 = np.zeros((128, 128), np.float32)
    b64[:64, :64] = 1; b64[64:, 64:] = 1
    c["bones64"] = b64
    s = np.arange(128)[:, None]
    t = np.arange(128)[None, :]
    c["iu128"] = (s <= t).astype(np.float32)
    p = np.arange(128)[:, None]
    f = np.arange(128)[None, :]
    same = (p // 32) == (f // 32)
    c["bd32_iu"] = (same & ((p % 32) <= (f % 32))).astype(np.float32)
    su = (same & ((p % 32) < (f % 32))).astype(np.float32)
    sl_ = (same & ((p % 32) > (f % 32))).astype(np.float32)
    c["rwmask4"] = np.concatenate([su, sl_, su, c["bd32_iu"]], axis=1)
    names = list(c.keys())
    offs = {}
    o = 0
    for n in names:
        offs[n] = (o, c[n].shape[1])
        o += c[n].shape[1]
    arr = np.concatenate([c[n] for n in names], axis=1)
    return arr, offs


CONSTS, COFF = make_consts()
NCONST = CONSTS.shape[1]

PV = {}


def _pv_layout():
    o = 0
    for n, k in (("rw_mu", 9), ("rw_w0", 2), ("rw_a0", 2), ("rw_kk", 2), ("rw_ka", 2), ("rw_rk", 2),
                 ("rw_ln_g", 2), ("rw_ln_b", 2), ("cv_w", 62), ("cv_b", 2), ("cv_ln_g", 2),
                 ("cv_ln_b", 2), ("gla_ab", 1), ("gla_ln_g", 2), ("fox_bf", 1), ("gate_b", 32),
                 ("ln_g", 24), ("ln_b", 24)):
        PV[n] = (o, k)
        o += k
    return o


NPV = _pv_layout()


def _cols(v):
    v = np.asarray(v, np.float32).reshape(-1)
    n = v.shape[0]
    k = (n + 127) // 128
    buf = np.zeros((k * 128,), np.float32)
    buf[:n] = v
    return buf.reshape(k, 128).T


def pack_pvec(inp, l):
    out = np.zeros((128, NPV), np.float32)

    def put(name, arr):
        o, k = PV[name]
        assert arr.shape == (128, k), (name, arr.shape, k)
        out[:, o:o + k] = arr

    put("rw_mu", _cols(inp["rw_mu"][l]))
    for n in ("rw_w0", "rw_a0", "rw_kk", "rw_ka", "rw_ln_g", "rw_ln_b", "cv_b", "cv_ln_g", "cv_ln_b",
              "gla_ab", "gla_ln_g"):
        put(n, _cols(inp[n][l]))
    put("rw_rk", _cols(inp["rw_rk"][l].reshape(-1)))
    cw = np.asarray(inp["cv_w"][l], np.float32)
    cwp = cw.T.reshape(2, 128, 31).transpose(1, 0, 2).reshape(128, 62)
    put("cv_w", cwp)
    put("fox_bf", _cols(inp["fox_bf"][l]))
    put("gate_b", _cols(inp["gate_b"][l].reshape(-1)))
    put("ln_g", _cols(inp["ln_g"][l].reshape(-1)))
    put("ln_b", _cols(inp["ln_b"][l].reshape(-1)))
    return out


class K:
    pass


def build(T=SEQ, L=DEPTH, stages=("mix", "xa", "ffn"), dbg=(), mixers=("rw", "cv", "gla", "fox")):
    nc = bass.Bass("TRN2", target_bir_lowering=False)
    NTB = T // 512
    k = K()
    k.nc = nc; k.T = T; k.L = L; k.NTB = NTB; k.mixers = mixers
    dr = lambda n, s, kind="ExternalInput", dt=F32: nc.dram_tensor(n, s, dt, kind=kind).ap()
    k.x_d = dr("x", [T, D])
    if "xa" in stages:
        k.mem_d = dr("mem", [MEM, D])
    k.consts_d = dr("consts", [128, NCONST])
    k.pvec_d = dr("pvec", [L, 128, NPV])
    if "mix" in stages:
        k.w_in_d = dr("w_in", [L, D, DIN])
        k.rw_wa_d = dr("rw_wa", [L, 128, 256])
        k.rw_g2_d = dr("rw_g2", [L, 160, 256])
        k.gla_a2_d = dr("gla_a2", [L, 16, 128])
        k.ups_d = [dr(n, [L, 256, D]) for n in ("rw_up", "cv_up", "gla_up", "fox_up")]
        k.w_out_d = dr("w_out", [L, D, D])
    if "xa" in stages:
        k.xa_wq_d = dr("xa_wq", [L, D, D]); k.xa_wk_d = dr("xa_wk", [L, D, D])
        k.xa_wv_d = dr("xa_wv", [L, D, D]); k.xa_wo_d = dr("xa_wo", [L, D, D])
    if "ffn" in stages:
        k.w1_d = dr("ffn_w1", [L, D, DFF]); k.w3_d = dr("ffn_w3", [L, D, DFF]); k.w2_d = dr("ffn_w2", [L, DFF, D])
    k.out_d = dr("out", [T, D], kind="ExternalOutput")
    k.xscr_d = nc.dram_tensor("xscr", [128, KC * T], F32, kind="Internal").ap()
    k.dbg_d = {}
    for (name, shape) in dbg:
        k.dbg_d[name] = dr("dbg_" + name, list(shape), kind="ExternalOutput", dt=BF16 if name in ("brT", "mg") else F32)

    with ExitStack() as st:
        k.st = st
        sb = lambda n, s, d=F32: st.enter_context(nc.sbuf_tensor(n, s, d))
        k.b_xres = [[Buf() for _ in range(NTB)] for _ in range(KC)]
        k.xres_n = 0
        alloc_xres(k)
        k.xT = sb("xT", [128, KC, T], BF16); k.b_xT = [[Buf() for _ in range(NTB)] for _ in range(KC)]
        k.cst = sb("cst", [128, NCONST]); k.b_cst = Buf()
        k.cstb = sb("cstb", [128, NCONST], BF16); k.b_cstb = Buf()
        k.pv = sb("pv", [128, NPV]); k.b_pv = Buf()
        k.npv = sb("npv", [128, NPV]); k.b_npv = Buf()
        k.g2a = sb("rw_g2a", [128, 256], BF16); k.g2b = sb("rw_g2b", [32, 256], BF16); k.b_g2 = Buf()
        k.ups = [sb(f"mg_ups{i}", [128, 2, 4, 128], BF16) for i in range(2)]; k.b_ups = [Buf(), Buf()]
        k.wsm = sb("w_small", [128, KC, 32], BF16); k.b_wsm = Buf()
        NW = 4
        k.NW = NW
        k.wr = [sb(f"wr{i}", [128, KC, 512], BF16) for i in range(NW)]
        k.b_wr = [Buf() for _ in range(NW)]
        k.wr_i = 0
        k.ps = [st.enter_context(nc.psum_tensor(f"ps{i}", [128, 512], F32)) for i in range(8)]
        k.b_ps = [Buf(excl=True) for _ in range(8)]
        k.ps_i = 0
        k.held = set()
        k.psb = [p.bitcast(BF16) for p in k.ps]
        block = st.enter_context(nc.Block())
        P = Prog(nc)
        k.P = P

        prologue(k)
        for l in range(L):
            layer_params(k, l)
            if "mix" in stages:
                stage_mix(k, l)
            if "xa" in stages:
                stage_xa(k, l)
            if "ffn" in stages:
                stage_ffn(k, l)
        epilogue(k)
        k.stats = P.finalize(block, st)
        k.xres_stack.close()
    return nc, k


_UID = [0]


def sbt(k, stack, name, shape, dt=F32):
    _UID[0] += 1
    return stack.enter_context(k.nc.sbuf_tensor(f"{name}_{_UID[0]}", list(shape), dt))


def alloc_xres(k):
    k.xres_stack = ExitStack()
    k.xres_n += 1
    k.xres = k.xres_stack.enter_context(k.nc.sbuf_tensor(f"xres{k.xres_n}", [128, KC, k.T], F32, side="right"))


def spill_xres(k):
    P = k.P; T = k.T
    for c in range(KC):
        P.dma("sp", lambda e, c=c, xr=k.xres: e.dma_start(out=k.xscr_d[:, c * T:(c + 1) * T], in_=xr[:, c, :]),
              reads=[k.b_xres[c][tb] for tb in range(k.NTB)])
    P.barrier()
    k.xres_stack.close()
    k.xres = None


def reload_xres(k):
    P = k.P; T = k.T
    alloc_xres(k)
    for c in range(KC):
        P.dma("sp", lambda e, c=c, xr=k.xres: e.dma_start(out=xr[:, c, :], in_=k.xscr_d[:, c * T:(c + 1) * T]),
              writes=[k.b_xres[c][tb] for tb in range(k.NTB)])


def cs(k, name, bf=False):
    o, n = COFF[name]
    return (k.cstb if bf else k.cst)[:, o:o + n]


def pcol(k, name, j=0, neg=False, rows=128, r0=0):
    o, n = PV[name]
    t = k.npv if neg else k.pv
    return t[r0:r0 + rows, o + j:o + j + 1]


def mm(e, out, lhsT, rhs, pos, start=True, stop=True):
    return e.matmul(out, lhsT=lhsT, rhs=rhs, start=start, stop=stop, tile_position=pos)


def tr(e, out, in_, identity, pos):
    return e.transpose(out=out, in_=in_, identity=identity, tile_position=pos)


def nbank(k, hold=False):
    i = k.ps_i
    while i in k.held:
        i = (i + 1) % 8
    k.ps_i = (i + 1) % 8
    if hold:
        k.held.add(i)
    return k.ps[i], k.b_ps[i]


def release(k, pt):
    for i in range(8):
        if k.ps[i] is pt:
            k.held.discard(i)
            return
    raise AssertionError


def load_w(k, src, ncols, nk=KC):
    i = k.wr_i
    k.wr_i = (i + 1) % k.NW
    slot, b = k.wr[i], k.b_wr[i]
    v = src.rearrange("(c p) n -> p c n", p=128)
    k.P.dma("pool", lambda e: e.dma_start(out=slot[:, 0:nk, 0:ncols], in_=v), writes=[b])
    return slot, b


def prologue(k):
    P = k.P; T = k.T
    P.dma("sp", lambda e: e.dma_start(out=k.cst[:], in_=k.consts_d), writes=[k.b_cst])
    P.op("dve", lambda e: e.tensor_copy(out=k.cstb[:], in_=k.cst[:]), reads=[k.b_cst], writes=[k.b_cstb])
    for i in range(8):
        P.op("dve", lambda e, i=i: e.memset(k.ps[i][:], 0.0), writes=[k.b_ps[i]])
    with ExitStack() as s2:
        xin = [sbt(k, s2, f"xin{i}", [128, D], F32) for i in range(2)]
        b_xin = [Buf(), Buf()]
        ident = cs(k, "ident")
        for tt in range(T // 128):
            j = tt % 2
            P.dma("sp", lambda e, tt=tt, j=j: e.dma_start(out=xin[j][:], in_=k.x_d[tt * 128:(tt + 1) * 128, :]),
                  writes=[b_xin[j]])
            tb = tt // 4
            for g in range(2):
                pt, bp = nbank(k)
                for q in range(4):
                    c = g * 4 + q
                    P.op("pe", lambda e, pt=pt, j=j, c=c, q=q: e.transpose(
                        out=pt[:, q * 128:(q + 1) * 128], in_=xin[j][:, c * 128:(c + 1) * 128], identity=ident),
                        reads=[b_xin[j], k.b_cst], writes=[bp])
                for q in range(4):
                    c = g * 4 + q
                    dst = slice(tt * 128, (tt + 1) * 128)
                    P.op("act", lambda e, pt=pt, c=c, q=q, dst=dst, xr=k.xres: e.activation(
                        out=xr[:, c, dst], in_=pt[:, q * 128:(q + 1) * 128], func=AF.Copy),
                        reads=[bp], writes=[k.b_xres[c][tb]])
                    P.op("dve", lambda e, pt=pt, c=c, q=q, dst=dst: e.tensor_copy(
                        out=k.xT[:, c, dst], in_=pt[:, q * 128:(q + 1) * 128]),
                        reads=[bp], writes=[k.b_xT[c][tb]])
        P.barrier()


def layer_params(k, l):
    P = k.P
    P.dma("sp", lambda e: e.dma_start(out=k.pv[:], in_=k.pvec_d[l]), writes=[k.b_pv])
    P.op("dve", lambda e: e.tensor_scalar(out=k.npv[:], in0=k.pv[:], scalar1=-1.0, scalar2=None, op0=ALU.mult),
         reads=[k.b_pv], writes=[k.b_npv])


def epilogue(k):
    P = k.P; T = k.T
    with ExitStack() as s2:
        xo = [sbt(k, s2, f"xo{i}", [128, D], F32) for i in range(2)]
        b_xo = [Buf(), Buf()]
        ident = cs(k, "ident")
        for tt in range(T // 128):
            j = tt % 2
            tb = tt // 4
            for g in range(2):
                pt, bp = nbank(k)
                for q in range(4):
                    c = g * 4 + q
                    P.op("pe", lambda e, pt=pt, c=c, q=q, tt=tt, xr=k.xres: e.transpose(
                        out=pt[:, q * 128:(q + 1) * 128], in_=xr[:, c, tt * 128:(tt + 1) * 128], identity=ident),
                        reads=[k.b_xres[c][tb], k.b_cst], writes=[bp])
                eng = "act" if g == 0 else "dve"
                if g == 0:
                    P.op("act", lambda e, pt=pt, j=j: e.activation(out=xo[j][:, 0:512], in_=pt[:], func=AF.Copy),
                         reads=[bp], writes=[b_xo[j]])
                else:
                    P.op("dve", lambda e, pt=pt, j=j: e.tensor_copy(out=xo[j][:, 512:1024], in_=pt[:]),
                         reads=[bp], writes=[b_xo[j]])
            P.dma("sp", lambda e, tt=tt, j=j: e.dma_start(out=k.out_d[tt * 128:(tt + 1) * 128, :], in_=xo[j][:]),
                  reads=[b_xo[j]], is_output=True)
        P.barrier()


def ln_block(k, l, s, tb, zsq, b_zsq, st_t, b_st):
    P = k.P
    sl = slice(tb * 512, (tb + 1) * 512)
    rstd, nmr, b_r, b_n = ln_stats(k, [(k.xres[:, c, sl], k.b_xres[c][tb]) for c in range(KC)], D, LN_EPS, zsq, b_zsq, st_t, b_st)
    og, _ = PV["ln_g"]
    ob, _ = PV["ln_b"]
    for c in range(KC):
        xs = k.xres[:, c, sl]
        P.op("dve", lambda e, xs=xs: e.tensor_tensor(out=xs, in0=xs, in1=rstd, op=ALU.mult),
             reads=[k.b_xres[c][tb], b_r], writes=[k.b_xres[c][tb]])
        P.op("dve", lambda e, xs=xs: e.tensor_tensor(out=xs, in0=xs, in1=nmr, op=ALU.add),
             reads=[k.b_xres[c][tb], b_n], writes=[k.b_xres[c][tb]])
        gcol = k.pv[:, og + s * 8 + c: og + s * 8 + c + 1]
        bcol = k.pv[:, ob + s * 8 + c: ob + s * 8 + c + 1]
        P.op("act", lambda e, xs=xs, c=c, gcol=gcol, bcol=bcol: e.activation(out=k.xT[:, c, sl], in_=xs, func=AF.Identity, scale=gcol, bias=bcol),
             reads=[k.b_xres[c][tb], k.b_pv], writes=[k.b_xT[c][tb]])
        P.op("act", lambda e, xs=xs, gcol=gcol, bcol=bcol: e.activation(out=xs, in_=xs, func=AF.Identity, scale=gcol, bias=bcol),
             reads=[k.b_xres[c][tb], k.b_pv], writes=[k.b_xres[c][tb]])


def out_proj_ln(k, l, s, src, b_src, nkc, w_d, first=True, last=True, alpha_first=True, row0=0,
                lnbufs=None):
    P = k.P
    NTB = k.NTB
    halves = []
    for h in range(2):
        slot, b = load_w(k, w_d[row0:row0 + nkc * 128, h * 512:(h + 1) * 512], 512, nk=nkc)
        halves.append((slot, b))
    for tb in range(NTB):
        sl = slice(tb * 512, (tb + 1) * 512)
        for fc in range(KC):
            slot, bw = halves[fc // 4]
            co = (fc % 4) * 128
            pt, bp = nbank(k)
            for c in range(nkc):
                P.op("pe", lambda e, pt=pt, slot=slot, c=c, co=co, sl=sl: e.matmul(
                    pt[:], lhsT=slot[:, c, co:co + 128], rhs=src[:, c, sl], start=(c == 0), stop=(c == nkc - 1)),
                    reads=[bw, b_src[c][tb]], writes=[bp])
            xs = k.xres[:, fc, sl]
            if first:
                P.op("dve", lambda e, pt=pt, xs=xs: e.scalar_tensor_tensor(
                    out=xs, in0=xs, scalar=ALPHA, in1=pt[:], op0=ALU.mult, op1=ALU.add),
                    reads=[bp, k.b_xres[fc][tb]], writes=[k.b_xres[fc][tb]])
            else:
                P.op("dve", lambda e, pt=pt, xs=xs: e.tensor_tensor(out=xs, in0=xs, in1=pt[:], op=ALU.add),
                     reads=[bp, k.b_xres[fc][tb]], writes=[k.b_xres[fc][tb]])
        if last:
            ln_block(k, l, s, tb, *lnbufs)


def alloc_ln(k, s2):
    zsq = sbt(k, s2, "zsq", [128, 2, 512], F32)
    st_t = sbt(k, s2, "lnst", [128, 3, 512], F32)
    return (zsq, [Buf(), Buf()], st_t, [Buf() for _ in range(3)])


def stage_ffn(k, l):
    P = k.P; T = k.T; NTB = k.NTB
    parts = [(0, 8), (8, 16), (16, 22)]
    with ExitStack() as s2:
        g = sbt(k, s2, "ffg", [128, 8, T], BF16)
        b_g = [[Buf() for _ in range(NTB)] for _ in range(8)]
        sg = [sbt(k, s2, f"ffs{i}", [128, 512], F32) for i in range(2)]
        b_sg = [Buf(), Buf()]
        lnb = alloc_ln(k, s2)
        si = 0
        for pi, (c0, c1) in enumerate(parts):
            n = c1 - c0
            for q0 in range(c0, c1, 4):
                nq = min(4, c1 - q0)
                s1, bw1 = load_w(k, k.w1_d[l][:, q0 * 128:(q0 + nq) * 128], nq * 128)
                s3, bw3 = load_w(k, k.w3_d[l][:, q0 * 128:(q0 + nq) * 128], nq * 128)
                for q in range(nq):
                    cg = q0 + q - c0
                    for tb in range(NTB):
                        sl = slice(tb * 512, (tb + 1) * 512)
                        p1, bp1 = nbank(k)
                        p3, bp3 = nbank(k)
                        for c in range(KC):
                            P.op("pe", lambda e, p1=p1, s1=s1, c=c, q=q, sl=sl: e.matmul(
                                p1[:], lhsT=s1[:, c, q * 128:(q + 1) * 128], rhs=k.xT[:, c, sl], start=(c == 0), stop=(c == KC - 1)),
                                reads=[bw1, k.b_xT[c][tb]], writes=[bp1])
                        for c in range(KC):
                            P.op("pe", lambda e, p3=p3, s3=s3, c=c, q=q, sl=sl: e.matmul(
                                p3[:], lhsT=s3[:, c, q * 128:(q + 1) * 128], rhs=k.xT[:, c, sl], start=(c == 0), stop=(c == KC - 1)),
                                reads=[bw3, k.b_xT[c][tb]], writes=[bp3])
                        j = si % 2
                        si += 1
                        P.op("act", lambda e, p1=p1, j=j: e.activation(out=sg[j][:], in_=p1[:], func=AF.Silu),
                             reads=[bp1], writes=[b_sg[j]])
                        P.op("dve", lambda e, p3=p3, j=j, cg=cg, sl=sl: e.tensor_tensor(
                            out=g[:, cg, sl], in0=sg[j][:], in1=p3[:], op=ALU.mult),
                            reads=[bp3, b_sg[j]], writes=[b_g[cg][tb]])
            out_proj_ln(k, l, 2, g, b_g, n, k.w2_d[l], first=(pi == 0), last=(pi == len(parts) - 1),
                        row0=c0 * 128, lnbufs=lnb)
        P.barrier()


def stage_xa(k, l):
    P = k.P; T = k.T; NTB = k.NTB
    ident = cs(k, "ident")
    with ExitStack() as s2:
        sbt_ = lambda n, s, d=F32: sbt(k, s2, n, s, d)
        k.xaK = sbt_("xaK", [128, KC, MEM], BF16); k.b_xaK = Buf()
        k.xaV = sbt_("xaV", [128, 2, D], BF16); k.b_xaV = Buf()
        smem = ExitStack()
        memT = sbt(k, smem, "memT", [128, KC, MEM], BF16)
        b_memT = Buf()
        with ExitStack() as s3:
            mt = [sbt(k, s3, f"memin{i}", [128, D], F32) for i in range(2)]
            b_mt = [Buf(), Buf()]
            for m in range(2):
                P.dma("sp", lambda e, m=m: e.dma_start(out=mt[m][:], in_=k.mem_d[m * 128:(m + 1) * 128, :]), writes=[b_mt[m]])
                for g in range(2):
                    pt, bp = nbank(k)
                    for q in range(4):
                        c = g * 4 + q
                        P.op("pe", lambda e, pt=pt, m=m, c=c, q=q: e.transpose(
                            out=pt[:, q * 128:(q + 1) * 128], in_=mt[m][:, c * 128:(c + 1) * 128], identity=ident),
                            reads=[b_mt[m], k.b_cst], writes=[bp])
                    for q in range(4):
                        c = g * 4 + q
                        P.op("dve", lambda e, pt=pt, m=m, c=c, q=q: e.tensor_copy(
                            out=memT[:, c, m * 128:(m + 1) * 128], in_=pt[:, q * 128:(q + 1) * 128]),
                            reads=[bp], writes=[b_memT])
            P.barrier()
        for h in range(2):
            slot, bw = load_w(k, k.xa_wk_d[l][:, h * 512:(h + 1) * 512], 512)
            for q in range(4):
                fc = h * 4 + q
                pt, bp = nbank(k)
                for c in range(KC):
                    P.op("pe", lambda e, pt=pt, slot=slot, c=c, q=q: e.matmul(
                        pt[:, 0:MEM], lhsT=slot[:, c, q * 128:(q + 1) * 128], rhs=memT[:, c, :], start=(c == 0), stop=(c == KC - 1)),
                        reads=[bw, b_memT], writes=[bp])
                P.op("act", lambda e, pt=pt, fc=fc: e.activation(out=k.xaK[:, fc, :], in_=pt[:, 0:MEM], func=AF.Copy),
                     reads=[bp], writes=[k.b_xaK])
        for h in range(2):
            slot, bw = load_w(k, k.xa_wv_d[l][:, h * 512:(h + 1) * 512], 512)
            for m in range(2):
                pt, bp = nbank(k)
                for c in range(KC):
                    P.op("pe", lambda e, pt=pt, slot=slot, c=c, m=m: e.matmul(
                        pt[:], lhsT=memT[:, c, m * 128:(m + 1) * 128], rhs=slot[:, c, :], start=(c == 0), stop=(c == KC - 1)),
                        reads=[bw, b_memT], writes=[bp])
                P.op("act", lambda e, pt=pt, m=m, h=h: e.activation(out=k.xaV[:, m, h * 512:(h + 1) * 512], in_=pt[:], func=AF.Copy),
                     reads=[bp], writes=[k.b_xaV])
        P.barrier()
        smem.close()
        oT = sbt_("xa_oT", [128, KC, T], BF16)
        b_oT = [[Buf() for _ in range(NTB)] for _ in range(KC)]
        qT = sbt_("xa_qT", [128, 2, T], BF16)
        b_qT = [[Buf() for _ in range(NTB)] for _ in range(2)]
        PT = [sbt_(f"xa_PT{i}", [128, 2, 512], BF16) for i in range(2)]
        b_PT = [[Buf(), Buf()], [Buf(), Buf()]]
        rd0 = sbt_("xa_rd", [128, 512])
        rden = [rd0, rd0]
        b0 = Buf()
        b_rden = [b0, b0]
        onesb = cs(k, "ones", bf=True)
        scale = 1.0 / 16.0
        it = 0
        for hh in range(2):
            slot, bw = load_w(k, k.xa_wq_d[l][:, hh * 512:(hh + 1) * 512], 512)
            for h2 in range(2):
                h = hh * 2 + h2
                for tb in range(NTB):
                    sl = slice(tb * 512, (tb + 1) * 512)
                    for dc in range(2):
                        pt, bp = nbank(k)
                        co = (h2 * 2 + dc) * 128
                        for c in range(KC):
                            P.op("pe", lambda e, pt=pt, slot=slot, c=c, co=co, sl=sl: e.matmul(
                                pt[:], lhsT=slot[:, c, co:co + 128], rhs=k.xT[:, c, sl], start=(c == 0), stop=(c == KC - 1)),
                                reads=[bw, k.b_xT[c][tb]], writes=[bp])
                        P.op("act", lambda e, pt=pt, dc=dc, sl=sl: e.activation(out=qT[:, dc, sl], in_=pt[:], func=AF.Copy),
                             reads=[bp], writes=[b_qT[dc][tb]])
                    j = it % 2
                    it += 1
                    for m in range(2):
                        pt, bp = nbank(k)
                        for dc in range(2):
                            P.op("pe", lambda e, pt=pt, h=h, dc=dc, m=m, sl=sl: e.matmul(
                                pt[:], lhsT=k.xaK[:, h * 2 + dc, m * 128:(m + 1) * 128], rhs=qT[:, dc, sl],
                                start=(dc == 0), stop=(dc == 1)),
                                reads=[k.b_xaK, b_qT[dc][tb]], writes=[bp])
                        P.op("act", lambda e, pt=pt, j=j, m=m: e.activation(out=PT[j][:, m, :], in_=pt[:], func=AF.Exp, scale=scale),
                             reads=[bp], writes=[b_PT[j][m]])
                    pd, bpd = nbank(k)
                    for m in range(2):
                        P.op("pe", lambda e, pd=pd, j=j, m=m: e.matmul(pd[:], lhsT=onesb, rhs=PT[j][:, m, :], start=(m == 0), stop=(m == 1)),
                             reads=[k.b_cstb, b_PT[j][m]], writes=[bpd])
                    P.op("act", lambda e, pd=pd, j=j: e.activation(out=rden[j][:], in_=pd[:], func=AF.Ln), reads=[bpd], writes=[b_rden[j]])
                    P.op("act", lambda e, j=j: e.activation(out=rden[j][:], in_=rden[j][:], func=AF.Exp, scale=-1.0), reads=[b_rden[j]], writes=[b_rden[j]])
                    for dc in range(2):
                        po, bpo = nbank(k)
                        for m in range(2):
                            P.op("pe", lambda e, po=po, j=j, m=m, h=h, dc=dc: e.matmul(
                                po[:], lhsT=k.xaV[:, m, h * 256 + dc * 128: h * 256 + (dc + 1) * 128], rhs=PT[j][:, m, :],
                                start=(m == 0), stop=(m == 1)),
                                reads=[k.b_xaV, b_PT[j][m]], writes=[bpo])
                        P.op("dve", lambda e, po=po, j=j, h=h, dc=dc, sl=sl: e.tensor_tensor(
                            out=oT[:, h * 2 + dc, sl], in0=po[:], in1=rden[j][:], op=ALU.mult),
                            reads=[bpo, b_rden[j]], writes=[b_oT[h * 2 + dc][tb]])
        lnb = alloc_ln(k, s2)
        out_proj_ln(k, l, 1, oT, b_oT, KC, k.xa_wo_d[l], lnbufs=lnb)
        P.barrier()


def proj_fm(k, slot, bw, col0, ncols, tb, hold=False):
    P = k.P
    sl = slice(tb * 512, (tb + 1) * 512)
    pt, bp = nbank(k, hold=hold)
    for c in range(KC):
        P.op("pe", lambda e, c=c: e.matmul(pt[0:ncols, :], lhsT=slot[:, c, col0:col0 + ncols], rhs=k.xT[:, c, sl],
                                           start=(c == 0), stop=(c == KC - 1)),
             reads=[bw, k.b_xT[c][tb]], writes=[bp])
    return pt, bp


def ln_stats(k, srcs, nfeat, eps, zsq, b_zsq, st_t, b_st):
    P = k.P
    ones = cs(k, "ones")
    S1, b1 = nbank(k)
    S2, b2 = nbank(k)
    n = len(srcs)
    for i, (ap, b) in enumerate(srcs):
        j = i % 2
        P.op("act", lambda e, j=j, ap=ap: e.activation(out=zsq[:, j, :], in_=ap, func=AF.Square), reads=[b], writes=[b_zsq[j]])
        P.op("pe", lambda e, i=i, ap=ap: e.matmul(S1[:], lhsT=ones, rhs=ap, start=(i == 0), stop=(i == n - 1)),
             reads=[b, k.b_cst], writes=[b1])
        P.op("pe", lambda e, i=i, j=j: e.matmul(S2[:], lhsT=ones, rhs=zsq[:, j, :], start=(i == 0), stop=(i == n - 1)),
             reads=[b_zsq[j], k.b_cst], writes=[b2])
    mean, var, rstd = (st_t[:, i, :] for i in range(3))
    P.op("act", lambda e: e.activation(out=mean, in_=S1[:], func=AF.Copy, scale=1.0 / nfeat), reads=[b1], writes=[b_st[0]])
    P.op("dve", lambda e: e.tensor_tensor(out=var, in0=mean, in1=mean, op=ALU.mult), reads=[b_st[0]], writes=[b_st[1]])
    P.op("dve", lambda e: e.scalar_tensor_tensor(out=var, in0=S2[:], scalar=1.0 / nfeat, in1=var, op0=ALU.mult, op1=ALU.subtract),
         reads=[b2, b_st[1]], writes=[b_st[1]])
    P.op("dve", lambda e: e.tensor_scalar(out=var, in0=var, scalar1=eps, scalar2=None, op0=ALU.add),
         reads=[b_st[1]], writes=[b_st[1]])
    P.op("act", lambda e: e.activation(out=rstd, in_=var, func=AF.Ln), reads=[b_st[1]], writes=[b_st[2]])
    P.op("act", lambda e: e.activation(out=rstd, in_=rstd, func=AF.Exp, scale=-0.5), reads=[b_st[2]], writes=[b_st[2]])
    P.op("dve", lambda e: e.scalar_tensor_tensor(out=mean, in0=mean, scalar=-1.0, in1=rstd, op0=ALU.mult, op1=ALU.mult),
         reads=[b_st[0], b_st[2]], writes=[b_st[0]])
    return rstd, mean, b_st[2], b_st[0]


def mixer_conv(k, l, brT, b_brT, s2):
    P = k.P; T = k.T; NTB = k.NTB
    if True:
        ub = sbt(k, s2, "cv_u", [128, 2, 30 + T], F32)
        b_ub = [Buf(), Buf()]
        acc = sbt(k, s2, "cv_acc", [128, 2, T], F32)
        b_acc = [Buf(), Buf()]
        lnb = alloc_ln(k, s2)
        sg = [lnb[0][:, i, :] for i in range(2)]
        b_sg = lnb[1]
        slot, bw = load_w(k, k.w_in_d[l][:, 1056:1568], 512)
        for ch in range(2):
            P.op("dve", lambda e, ch=ch: e.memset(ub[:, ch, 0:30], 0.0), writes=[b_ub[ch]])
        i = 0
        for tb in range(NTB):
            for ch in range(2):
                pa, bpa = proj_fm(k, slot, bw, ch * 128, 128, tb)
                pb, bpb = proj_fm(k, slot, bw, 256 + ch * 128, 128, tb)
                j = i % 2
                i += 1
                P.op("act", lambda e, pb=pb, j=j: e.activation(out=sg[j], in_=pb[:], func=AF.Sigmoid), reads=[bpb], writes=[b_sg[j]])
                P.op("dve", lambda e, pa=pa, j=j, ch=ch, tb=tb: e.tensor_tensor(
                    out=ub[:, ch, 30 + tb * 512: 30 + (tb + 1) * 512], in0=pa[:], in1=sg[j], op=ALU.mult),
                    reads=[bpa, b_sg[j]], writes=[b_ub[ch]])
                yield
        ow, _ = PV["cv_w"]
        for ch in range(2):
            eng = "dve"
            for kk in range(31):
                wcol = k.pv[:, ow + ch * 31 + kk: ow + ch * 31 + kk + 1]
                if kk == 0:
                    bcol = pcol(k, "cv_b", ch)
                    P.op(eng, lambda e, ch=ch, wcol=wcol, bcol=bcol: e.tensor_scalar(
                        out=acc[:, ch, :], in0=ub[:, ch, 0:T], scalar1=wcol, scalar2=bcol, op0=ALU.mult, op1=ALU.add),
                        reads=[b_ub[ch], k.b_pv], writes=[b_acc[ch]])
                else:
                    P.op(eng, lambda e, ch=ch, wcol=wcol, kk=kk: e.scalar_tensor_tensor(
                        out=acc[:, ch, :], in0=ub[:, ch, kk:kk + T], scalar=wcol, in1=acc[:, ch, :], op0=ALU.mult, op1=ALU.add),
                        reads=[b_ub[ch], k.b_pv, b_acc[ch]], writes=[b_acc[ch]])
                yield
        for tb in range(NTB):
            sl = slice(tb * 512, (tb + 1) * 512)
            rstd, nmr, b_r, b_n = ln_stats(k, [(acc[:, ch, sl], b_acc[ch]) for ch in range(2)], 256, LN_EPS, *lnb)
            for ch in range(2):
                a = acc[:, ch, sl]
                P.op("dve", lambda e, a=a, rstd=rstd: e.tensor_tensor(out=a, in0=a, in1=rstd, op=ALU.mult),
                     reads=[b_acc[ch], b_r], writes=[b_acc[ch]])
                P.op("dve", lambda e, a=a, nmr=nmr: e.tensor_tensor(out=a, in0=a, in1=nmr, op=ALU.add),
                     reads=[b_acc[ch], b_n], writes=[b_acc[ch]])
                P.op("act", lambda e, a=a, ch=ch, sl=sl: e.activation(
                    out=brT[:, 2 + ch, sl], in_=a, func=AF.Silu, scale=pcol(k, "cv_ln_g", ch), bias=pcol(k, "cv_ln_b", ch)),
                    reads=[b_acc[ch], k.b_pv], writes=[b_brT[2 + ch][tb]])
            yield
        yield


def mixer_fox(k, l, brT, b_brT, s2):
    P = k.P; T = k.T; NTB = k.NTB
    NT = T // 128
    if True:
        fq = sbt(k, s2, "fx_q", [128, 2, T], BF16); b_fq = [[Buf() for _ in range(NTB)] for _ in range(2)]
        fk = sbt(k, s2, "fx_k", [128, 2, T], BF16); b_fk = [[Buf() for _ in range(NTB)] for _ in range(2)]
        fv = sbt(k, s2, "fx_v", [128, NT, 256], BF16); b_fv = [Buf() for _ in range(NT)]
        spl = sbt(k, s2, "fx_spl", [4, T], F32); b_spl = Buf()
        sig, b_sig = spl, b_spl
        rsel = sbt(k, s2, "fx_rsel", [4, NT * 4], F32); b_rsel = Buf()
        stok = sbt(k, s2, "fx_stok", [128, NT * 4], F32); b_stok = Buf()
        sref = sbt(k, s2, "fx_sref", [128, NT * 4], F32); b_sref = Buf()
        bias = sbt(k, s2, "fx_bias", [128, NT, NT], F32); b_bias = Buf()
        PT = [sbt(k, s2, f"fx_PT{i}", [128, 512], BF16) for i in range(2)]
        b_PT = [Buf(), Buf()]
        rd = sbt(k, s2, "fx_rd", [128, 512], F32); b_rd = Buf()
        slA, bwA = load_w(k, k.w_in_d[l][:, 2352:2864], 512)
        slB, bwB = load_w(k, k.w_in_d[l][:, 2864:3124], 260)
        for tb in range(NTB):
            sl = slice(tb * 512, (tb + 1) * 512)
            for ch in range(2):
                pq, bpq = proj_fm(k, slA, bwA, ch * 128, 128, tb)
                P.op("act", lambda e, pq=pq, ch=ch, sl=sl: e.activation(out=fq[:, ch, sl], in_=pq[:], func=AF.Copy, scale=0.125),
                     reads=[bpq], writes=[b_fq[ch][tb]])
                pk, bpk = proj_fm(k, slA, bwA, 256 + ch * 128, 128, tb)
                P.op("dve", lambda e, pk=pk, ch=ch, sl=sl: e.tensor_copy(out=fk[:, ch, sl], in_=pk[:]),
                     reads=[bpk], writes=[b_fk[ch][tb]])
            pz, bpz = proj_fm(k, slB, bwB, 256, 4, tb)
            P.op("act", lambda e, pz=pz, sl=sl: e.activation(out=spl[:, sl], in_=pz[0:4, :], func=AF.Exp, scale=-1.0,
                                                             bias=pcol(k, "fox_bf", 0, neg=True, rows=4)),
                 reads=[bpz, k.b_npv], writes=[b_spl])
            P.op("act", lambda e, sl=sl: e.activation(out=spl[:, sl], in_=spl[:, sl], func=AF.Ln, bias=1.0),
                 reads=[b_spl], writes=[b_spl])
            yield
        for tt in range(NT):
            tb = tt // 4
            pt, bp = nbank(k)
            for c in range(KC):
                P.op("pe", lambda e, c=c, tt=tt, pt=pt: e.matmul(pt[:, 0:256], lhsT=k.xT[:, c, tt * 128:(tt + 1) * 128], rhs=slB[:, c, 0:256],
                                                             start=(c == 0), stop=(c == KC - 1)),
                     reads=[bwB, k.b_xT[c][tb]], writes=[bp])
            P.op("act", lambda e, pt=pt, tt=tt: e.activation(out=fv[:, tt, :], in_=pt[:, 0:256], func=AF.Copy), reads=[bp], writes=[b_fv[tt]])
            yield
        P.op("dve", lambda e: e.tensor_tensor_scan(out=sig[:], data0=spl[:], data1=spl[:], initial=0.0, op0=ALU.add, op1=ALU.max),
             reads=[b_spl], writes=[b_sig])
        ident = cs(k, "ident")
        pt, bp = nbank(k)
        for tt in range(NT):
            P.op("pe", lambda e, tt=tt: e.transpose(out=pt[:, tt * 4:(tt + 1) * 4], in_=sig[0:4, tt * 128:(tt + 1) * 128], identity=ident[0:4, 0:4]),
                 reads=[b_sig, k.b_cst], writes=[bp])
        P.op("dve", lambda e: e.tensor_copy(out=stok[:], in_=pt[:, 0:NT * 4]), reads=[bp], writes=[b_stok])
        for qs in range(NT):
            P.op("dve", lambda e, qs=qs: e.tensor_scalar(out=rsel[:, qs * 4:(qs + 1) * 4], in0=ident[0:4, 0:4], scalar1=sig[:, qs * 128:qs * 128 + 1],
                                                     scalar2=None, op0=ALU.mult),
                 reads=[b_sig, k.b_cst], writes=[b_rsel])
        pr, bpr = nbank(k)
        ones = cs(k, "ones")
        P.op("pe", lambda e: e.matmul(pr[:, 0:NT * 4], lhsT=ones[0:4, :], rhs=rsel[:], start=True, stop=True),
             reads=[b_rsel, k.b_cst], writes=[bpr])
        P.op("dve", lambda e: e.tensor_copy(out=sref[:], in_=pr[:, 0:NT * 4]), reads=[bpr], writes=[b_sref])
        stok3 = stok[:].rearrange("p (n h) -> p n h", h=4)
        onesb = cs(k, "ones", bf=True)
        iu = cs(k, "iu128", bf=True)
        it = 0
        for h in range(4):
            ch = h // 2
            pb = (h % 2) * 64
            for qs in range(NT):
                P.op("dve", lambda e, h=h, qs=qs: e.tensor_scalar(out=bias[:, qs, :], in0=stok3[:, :, h], scalar1=sref[:, qs * 4 + h:qs * 4 + h + 1],
                                                            scalar2=None, op0=ALU.subtract),
                     reads=[b_stok, b_sref], writes=[b_bias])
            for Q in range(NTB):
                nkt = 4 * (Q + 1)
                po, bpo = nbank(k, hold=True)
                pd, bpd = nbank(k, hold=True)
                for kt in range(nkt):
                    d = kt - 4 * Q
                    q0 = d * 128 if d > 0 else 0
                    j = it % 2
                    it += 1
                    ps_, bps = nbank(k, hold=True)
                    P.op("pe", lambda e, ps_=ps_, pb=pb, ch=ch, kt=kt, Q=Q, q0=q0: e.matmul(
                        ps_[:, q0:512], lhsT=fk[pb:pb + 64, ch, kt * 128:(kt + 1) * 128], rhs=fq[pb:pb + 64, ch, Q * 512 + q0:(Q + 1) * 512],
                        start=True, stop=True),
                        reads=[b_fk[ch][kt // 4], b_fq[ch][Q]], writes=[bps])
                    yield
                    for qi in range(q0 // 128, 4):
                        qs = Q * 4 + qi
                        P.op("act", lambda e, ps_=ps_, j=j, qi=qi, qs=qs, h=h, kt=kt: e.activation(
                            out=PT[j][:, qi * 128:(qi + 1) * 128], in_=ps_[:, qi * 128:(qi + 1) * 128], func=AF.Exp,
                            bias=bias[:, qs, kt:kt + 1]),
                            reads=[bps, b_bias], writes=[b_PT[j]])
                    release(k, ps_)
                    if d >= 0:
                        P.op("dve", lambda e, j=j, q0=q0: e.tensor_tensor(out=PT[j][:, q0:q0 + 128], in0=PT[j][:, q0:q0 + 128], in1=iu, op=ALU.mult),
                             reads=[b_PT[j], k.b_cstb], writes=[b_PT[j]])
                    P.op("pe", lambda e, po=po, pb=pb, j=j, q0=q0, kt=kt, h=h, nkt=nkt: e.matmul(
                        po[pb:pb + 64, q0:512], lhsT=fv[:, kt, h * 64:(h + 1) * 64], rhs=PT[j][:, q0:512], start=(kt == 0), stop=(kt == nkt - 1)),
                        reads=[b_fv[kt], b_PT[j]], writes=[bpo])
                    P.op("pe", lambda e, pd=pd, pb=pb, j=j, q0=q0, kt=kt, nkt=nkt: e.matmul(
                        pd[pb:pb + 64, q0:512], lhsT=onesb[:, 0:64], rhs=PT[j][:, q0:512], start=(kt == 0), stop=(kt == nkt - 1)),
                        reads=[k.b_cstb, b_PT[j]], writes=[bpd])
                    yield
                P.op("act", lambda e, pd=pd, pb=pb: e.activation(out=rd[pb:pb + 64, :], in_=pd[pb:pb + 64, :], func=AF.Ln), reads=[bpd], writes=[b_rd])
                P.op("act", lambda e, pb=pb: e.activation(out=rd[pb:pb + 64, :], in_=rd[pb:pb + 64, :], func=AF.Exp, scale=-1.0), reads=[b_rd], writes=[b_rd])
                P.op("dve", lambda e, po=po, pb=pb, ch=ch, Q=Q: e.tensor_tensor(
                    out=brT[pb:pb + 64, 6 + ch, Q * 512:(Q + 1) * 512], in0=po[pb:pb + 64, :], in1=rd[pb:pb + 64, :], op=ALU.mult),
                    reads=[bpo, b_rd], writes=[b_brT[6 + ch][Q]])
                release(k, po); release(k, pd)
        yield


def mixer_gla(k, l, brT, b_brT, s2):
    P = k.P; T = k.T; NTB = k.NTB
    identb = cs(k, "ident", bf=True)
    if True:
        f32t = lambda n, shp: sbt(k, s2, n, shp, F32)
        bft = lambda n, shp: sbt(k, s2, n, shp, BF16)
        a2 = f32t("gl_a2", [16, 128]); b_a2 = Buf()
        zT = f32t("gl_z", [16, 512]); b_zT = Buf()
        spl = f32t("gl_spl", [128, 512]); b_spl = Buf()
        bcs = f32t("gl_bcs", [128, 512]); b_bcs = Buf()
        Ep = f32t("gl_Ep", [128, 512]); b_Ep = Buf()
        En = f32t("gl_En", [128, 512]); b_En = Buf()
        ones32 = f32t("gl_ones", [128, 32]); b_ones = Buf()
        qd = bft("gl_qd", [128, 512]); b_qd = Buf()
        ki = bft("gl_ki", [128, 512]); b_ki = Buf()
        vb = bft("gl_vb", [128, 2, 512]); b_vb = Buf()
        sr = f32t("gl_sr", [128, 2, 512]); b_sr = Buf()
        S4f = f32t("gl_S4f", [128, 64]); b_S4f = Buf()
        S4b = bft("gl_S4b", [128, 64]); b_S4b = Buf()
        STm = [bft(f"gl_STm{i}", [128, 128]) for i in range(2)]; b_STm = [Buf(), Buf()]
        V4 = [bft(f"gl_V4{i}", [128, 64]) for i in range(2)]; b_V4 = [Buf(), Buf()]
        KT = [bft(f"gl_KT{i}", [128, 128]) for i in range(2)]; b_KT = [Buf(), Buf()]
        OALL = f32t("gl_OALL", [128, 16, 64]); b_OALL = Buf()
        osq = f32t("gl_osq", [128, 16, 64]); b_osq = Buf()
        ss = f32t("gl_ss", [128, 16]); b_ss = Buf()
        ONALL = bft("gl_ON", [128, 16, 64]); b_ON = Buf()
        slA, bwA = load_w(k, k.w_in_d[l][:, 1568:2080], 512)
        slB, bwB = load_w(k, k.w_in_d[l][:, 2080:2352], 272)
        P.dma("sp", lambda e: e.dma_start(out=a2[:], in_=k.gla_a2_d[l]), writes=[b_a2])
        P.op("dve", lambda e: e.memset(ones32[:], 1.0), writes=[b_ones])
        P.op("dve", lambda e: e.memset(S4f[:], 0.0), writes=[b_S4f])
        P.op("dve", lambda e: e.memset(S4b[:], 0.0), writes=[b_S4b])
        bd_iu = cs(k, "bd32_iu")
        it = 0
        for tb in range(NTB):
            sl = slice(tb * 512, (tb + 1) * 512)
            pz, bpz = proj_fm(k, slB, bwB, 256, 16, tb)
            P.op("act", lambda e, pz=pz: e.activation(out=zT[:], in_=pz[0:16, :], func=AF.Copy), reads=[bpz], writes=[b_zT])
            pla, bpla = nbank(k)
            P.op("pe", lambda e, pla=pla: e.matmul(pla[:], lhsT=a2[:], rhs=zT[:], start=True, stop=True), reads=[b_a2, b_zT], writes=[bpla])
            P.op("act", lambda e, pla=pla: e.activation(out=spl[:], in_=pla[:], func=AF.Exp, scale=-1.0, bias=pcol(k, "gla_ab", 0, neg=True)),
                 reads=[bpla, k.b_npv], writes=[b_spl])
            P.op("act", lambda e: e.activation(out=spl[:], in_=spl[:], func=AF.Ln, bias=1.0), reads=[b_spl], writes=[b_spl])
            for c in range(16):
                cc = slice(c * 32, (c + 1) * 32)
                P.op("dve", lambda e, cc=cc: e.tensor_tensor_scan(out=bcs[:, cc], data0=ones32[:], data1=spl[:, cc], initial=0.0,
                                                              op0=ALU.mult, op1=ALU.add),
                     reads=[b_ones, b_spl], writes=[b_bcs])
            P.op("act", lambda e: e.activation(out=Ep[:], in_=bcs[:], func=AF.Exp, scale=1.0 / 16.0), reads=[b_bcs], writes=[b_Ep])
            P.op("act", lambda e: e.activation(out=En[:], in_=bcs[:], func=AF.Exp, scale=-1.0 / 16.0), reads=[b_bcs], writes=[b_En])
            yield
            pq, bpq = proj_fm(k, slA, bwA, 0, 128, tb)
            P.op("dve", lambda e, pq=pq: e.scalar_tensor_tensor(out=qd[:], in0=pq[:], scalar=32.0 ** -0.5, in1=En[:], op0=ALU.mult, op1=ALU.mult),
                 reads=[bpq, b_En], writes=[b_qd])
            yield
            pk, bpk = proj_fm(k, slA, bwA, 128, 128, tb)
            P.op("dve", lambda e, pk=pk: e.tensor_tensor(out=ki[:], in0=pk[:], in1=Ep[:], op=ALU.mult), reads=[bpk, b_Ep], writes=[b_ki])
            yield
            for ch in range(2):
                pv, bpv = proj_fm(k, slA, bwA, 256 + ch * 128, 128, tb)
                P.op("act", lambda e, pv=pv, ch=ch: e.activation(out=vb[:, ch, :], in_=pv[:], func=AF.Copy), reads=[bpv], writes=[b_vb])
                pr, bpr = proj_fm(k, slB, bwB, ch * 128, 128, tb)
                P.op("act", lambda e, pr=pr, ch=ch: e.activation(out=sr[:, ch, :], in_=pr[:], func=AF.Silu), reads=[bpr], writes=[b_sr])
                yield
            for c in range(16):
                cc = slice(c * 32, (c + 1) * 32)
                j = it % 2
                it += 1
                pS, bpS = nbank(k, hold=True)
                P.op("dve", lambda e, pS=pS: e.memset(pS[:, 0:128], 0.0), writes=[bpS])
                pTr, bpTr = nbank(k, hold=True)
                pTb = k.psb[k.ps.index(pTr)]
                P.op("dve", lambda e, pTr=pTr: e.memset(pTr[:, 64:128], 0.0), writes=[bpTr])
                yield
                for h in (0, 2, 1, 3):
                    hs = slice(h * 32, (h + 1) * 32)
                    vs = slice((h % 2) * 64, (h % 2) * 64 + 64)
                    P.op("pe", lambda e, pTb=pTb, hs=hs, vs=vs, h=h, cc=cc: tr(e, pTb[hs, 0:64], vb[vs, h // 2, cc], identb[vs, vs], (vs.start, hs.start)),
                         reads=[b_vb, k.b_cstb], writes=[bpTr], rt=vs.start)
                for h in range(4):
                    hs = slice(h * 32, (h + 1) * 32)
                    P.op("pe", lambda e, pS=pS, hs=hs, cc=cc: mm(e, pS[hs, hs], ki[hs, cc], qd[hs, cc], (hs.start, hs.start)),
                         reads=[b_ki, b_qd], writes=[bpS], rt=hs.start)
                    P.op("pe", lambda e, pTb=pTb, hs=hs, cc=cc, h=h: tr(e, pTb[hs, 128 + h * 32:128 + (h + 1) * 32], ki[hs, cc], identb[hs, hs], (hs.start, hs.start)),
                         reads=[b_ki, k.b_cstb], writes=[bpTr], rt=hs.start)
                yield
                P.op("dve", lambda e, pS=pS, j=j: e.tensor_tensor(out=STm[j][:], in0=pS[:, 0:128], in1=bd_iu, op=ALU.mult),
                     reads=[bpS, k.b_cst], writes=[b_STm[j]])
                release(k, pS)
                P.op("act", lambda e, pTb=pTb, j=j: e.activation(out=V4[j][:], in_=pTb[:, 0:64], func=AF.Copy), reads=[bpTr], writes=[b_V4[j]])
                P.op("dve", lambda e, pTb=pTb, j=j: e.tensor_copy(out=KT[j][:], in_=pTb[:, 128:256]),
                     reads=[bpTr], writes=[b_KT[j]])
                release(k, pTr)
                yield
                pO, bpO = nbank(k, hold=True)
                P.op("pe", lambda e, pO=pO, j=j: e.matmul(pO[:, 0:64], lhsT=STm[j][:], rhs=V4[j][:], start=True, stop=False),
                     reads=[b_STm[j], b_V4[j]], writes=[bpO])
                for h in range(4):
                    hs = slice(h * 32, (h + 1) * 32)
                    P.op("pe", lambda e, pO=pO, hs=hs, cc=cc, h=h: mm(e, pO[hs, 0:64], qd[hs, cc], S4b[hs, :], (hs.start, hs.start), start=False, stop=True),
                         reads=[b_qd, b_S4b], writes=[bpO], rt=hs.start)
                pSt, bpSt = nbank(k, hold=True)
                P.op("pe", lambda e, pSt=pSt, j=j: e.matmul(pSt[:, 0:64], lhsT=KT[j][:], rhs=V4[j][:], start=True, stop=True),
                     reads=[b_KT[j], b_V4[j]], writes=[bpSt])
                yield
                P.op("act", lambda e, pO=pO, c=c: e.activation(out=OALL[:, c, :], in_=pO[:, 0:64], func=AF.Copy), reads=[bpO], writes=[b_OALL])
                release(k, pO)
                P.op("dve", lambda e, pSt=pSt: e.tensor_tensor(out=S4f[:], in0=pSt[:, 0:64], in1=S4f[:], op=ALU.add), reads=[bpSt, b_S4f], writes=[b_S4f])
                release(k, pSt)
                yield
                P.op("dve", lambda e, c=c: e.tensor_scalar(out=S4f[:], in0=S4f[:], scalar1=En[:, c * 32 + 31:c * 32 + 32], scalar2=None, op0=ALU.mult),
                     reads=[b_S4f, b_En], writes=[b_S4f])
                yield
                P.op("act", lambda e: e.activation(out=S4b[:], in_=S4f[:], func=AF.Copy), reads=[b_S4f], writes=[b_S4b])
                yield
            P.op("dve", lambda e: e.tensor_tensor(out=osq[:], in0=OALL[:], in1=OALL[:], op=ALU.mult), reads=[b_OALL], writes=[b_osq])
            P.op("dve", lambda e: e.tensor_reduce(out=ss[:], in_=osq[:], axis=mybir.AxisListType.X, op=ALU.add), reads=[b_osq], writes=[b_ss])
            P.op("dve", lambda e: e.tensor_scalar(out=ss[:], in0=ss[:], scalar1=1.0 / 64.0, scalar2=1e-5, op0=ALU.mult, op1=ALU.add),
                 reads=[b_ss], writes=[b_ss])
            P.op("act", lambda e: e.activation(out=ss[:], in_=ss[:], func=AF.Ln), reads=[b_ss], writes=[b_ss])
            P.op("act", lambda e: e.activation(out=ss[:], in_=ss[:], func=AF.Exp, scale=-0.5), reads=[b_ss], writes=[b_ss])
            for c in range(16):
                P.op("dve", lambda e, c=c: e.tensor_scalar(out=ONALL[:, c, :], in0=OALL[:, c, :], scalar1=ss[:, c:c + 1], scalar2=None, op0=ALU.mult),
                     reads=[b_OALL, b_ss], writes=[b_ON])
            yield
            pF, bpF = nbank(k, hold=True)
            pFb = k.psb[k.ps.index(pF)]
            for h in range(4):
                yield
                for c in range(16):
                    hs = slice(h * 32, (h + 1) * 32)
                    vs = slice((h % 2) * 64, (h % 2) * 64 + 64)
                    o0 = (h // 2) * 512 + c * 32
                    P.op("pe", lambda e, pFb=pFb, hs=hs, vs=vs, o0=o0, c=c: tr(e, pFb[vs, o0:o0 + 32], ONALL[hs, c, :], identb[hs, hs], (hs.start, vs.start)),
                         reads=[b_ON, k.b_cstb], writes=[bpF], rt=hs.start)
            for fc in range(2):
                P.op("dve", lambda e, pFb=pFb, fc=fc, sl=sl: e.scalar_tensor_tensor(
                    out=brT[:, 4 + fc, sl], in0=pFb[:, fc * 512:(fc + 1) * 512], scalar=pcol(k, "gla_ln_g", fc), in1=sr[:, fc, :],
                    op0=ALU.mult, op1=ALU.mult),
                    reads=[bpF, k.b_pv, b_sr], writes=[b_brT[4 + fc][tb]])
            release(k, pF)
            yield
        yield


def mixer_rwkv(k, l, brT, b_brT, s2):
    P = k.P; T = k.T
    NS = 256
    NSB = T // NS
    NCH = NS // 32
    identb = cs(k, "ident", bf=True)
    bones = cs(k, "bones64")
    if True:
        f32t = lambda n, shp: sbt(k, s2, n, shp, F32)
        bft = lambda n, shp: sbt(k, s2, n, shp, BF16)
        B = lambda: Buf()
        wa = f32t("rw_wa", [128, 256]); b_wa = B()
        g2a, g2b, b_g2 = k.g2a, k.g2b, k.b_g2
        omk = f32t("rw_omk", [128, 2]); b_omk = B()
        pprev = f32t("rw_pprev", [128, 9]); b_pprev = B()
        praw = [f32t(f"rw_praw{i}", [128, NS + 1]) for i in range(2)]; b_praw = [B(), B()]
        R = [f32t(f"rw_R{i}", [128, NS]) for i in range(2)]; b_R = [B(), B()]
        KX = [f32t(f"rw_KX{i}", [128, NS]) for i in range(2)]; b_KX = [B(), B()]
        V = [f32t(f"rw_V{i}", [128, NS]) for i in range(2)]; b_V = [B(), B()]
        XWA = f32t("rw_XWA", [128, NS]); b_XWA = B()
        XG0 = f32t("rw_XG0", [128, NS]); b_XG0 = B()
        XG1 = f32t("rw_XG1", [32, NS]); b_XG1 = B()
        sgx0 = bft("rw_sgx0", [128, NS]); sgx1 = bft("rw_sgx1", [32, NS]); b_sgx = B()
        EW = f32t("rw_EW", [128, NS]); b_EW = B()
        AL = f32t("rw_AL", [128, NS]); b_AL = B()
        GTs = [[bft(f"rw_GT{q}{i}", [128, NS]) for i in range(2)] for q in range(3)]; b_GTs = [[B(), B()] for q in range(3)]
        KKN = f32t("rw_KKN", [128, NS]); b_KKN = B()
        TMP = f32t("rw_TMP", [128, NS]); b_TMP = B()
        CS = f32t("rw_CS", [128, NS]); b_CS = B()
        E1 = f32t("rw_E1", [128, NS]); b_E1 = B()
        E2 = f32t("rw_E2", [128, NS]); b_E2 = B()
        E3 = f32t("rw_E3", [128, NS]); b_E3 = B()
        WCs = [[f32t(f"rw_WC{q}{i}", [128, NCH]) for i in range(2)] for q in range(2)]; b_WCs = [[B(), B()], [B(), B()]]
        BONs = [[f32t(f"rw_BON{q}{i}", [128, NS]) for i in range(2)] for q in range(3)]; b_BONs = [[B(), B()] for q in range(3)]
        ones32 = f32t("rw_ones", [128, 32]); b_ones = B()
        mk2 = lambda nm: ([[bft(f"rw_{nm}{q}{i}", [128, NS]) for i in range(2)] for q in range(2)], [[B(), B()], [B(), B()]])
        RHs, b_RHs = mk2("rh"); KHs, b_KHs = mk2("kh"); BHs, b_BHs = mk2("bh"); AHs, b_AHs = mk2("ah"); VBs, b_VBs = mk2("vb")
        M4 = [bft(f"rw_M4{i}", [128, 4, 128]) for i in range(2)]; b_M4 = [B(), B()]
        RKT = [bft(f"rw_RKT{i}", [128, 128]) for i in range(2)]; b_RKT = [B(), B()]
        Am = [bft(f"rw_A{i}", [128, 128]) for i in range(2)]; b_A = [B(), B()]
        ATm = [bft(f"rw_AT{i}", [128, 128]) for i in range(2)]; b_AT = [B(), B()]
        TT = [[bft(f"rw_TT{i}{q}", [128, 128]) for q in range(2)] for i in range(2)]; b_TT = [[B(), B()], [B(), B()]]
        BK = [bft(f"rw_BK{i}", [128, 4, 128]) for i in range(2)]; b_BK = [B(), B()]
        V4 = [bft(f"rw_V4{i}", [128, 64]) for i in range(2)]; b_V4 = [B(), B()]
        Xb = [bft(f"rw_Xb{i}", [128, 64]) for i in range(2)]; b_Xb = [B(), B()]
        Ub = [bft(f"rw_Ub{i}", [128, 64]) for i in range(2)]; b_Ub = [B(), B()]
        Hf = [f32t(f"rw_Hf{i}", [128, 64]) for i in range(2)]; b_Hf = [B(), B()]
        Hb = [bft(f"rw_Hb{i}", [128, 64]) for i in range(2)]; b_Hb = [B(), B()]
        YALLs = [f32t(f"rw_YALL{q}", [128, NCH, 64]) for q in range(2)]; b_YALLs = [B(), B()]
        ysq = f32t("rw_ysq", [128, NCH, 64]); b_ysq = B()
        s1 = f32t("rw_s1", [128, NCH]); b_s1 = B()
        s2_ = f32t("rw_s2", [128, NCH]); b_s2 = B()
        YN = bft("rw_YN", [128, NCH, 64]); b_YN = B()
        y1 = f32t("rw_y1", [128, NS]); b_y1 = B()
        masks = cs(k, "rwmask4")
        bd_iu = cs(k, "bd32_iu")

        slA, bwA = load_w(k, k.w_in_d[l][:, 0:512], 512)
        slB, bwB = load_w(k, k.w_in_d[l][:, 512:1024], 512)
        slC, bwC = k.wsm, k.b_wsm
        vC = k.w_in_d[l][:, 1024:1056].rearrange("(c p) n -> p c n", p=128)
        P.dma("pool", lambda e: e.dma_start(out=slC[:, :, :], in_=vC), writes=[bwC])
        P.dma("sp", lambda e: e.dma_start(out=wa[:], in_=k.rw_wa_d[l]), writes=[b_wa])
        P.dma("pool", lambda e: e.dma_start(out=g2a[:], in_=k.rw_g2_d[l][0:128, :]), writes=[b_g2])
        P.dma("pool", lambda e: e.dma_start(out=g2b[:], in_=k.rw_g2_d[l][128:160, :]), writes=[b_g2])
        oka, _ = PV["rw_ka"]
        P.op("dve", lambda e: e.tensor_scalar(out=omk[:], in0=k.pv[:, oka:oka + 2], scalar1=-1.0, scalar2=1.0, op0=ALU.mult, op1=ALU.add),
             reads=[k.b_pv], writes=[b_omk])
        P.op("dve", lambda e: e.memset(pprev[:], 0.0), writes=[b_pprev])
        P.op("dve", lambda e: e.memset(ones32[:], 1.0), writes=[b_ones])
        for hp in range(2):
            P.op("dve", lambda e, hp=hp: e.memset(Hf[hp][:], 0.0), writes=[b_Hf[hp]])
            P.op("dve", lambda e, hp=hp: e.memset(Hb[hp][:], 0.0), writes=[b_Hb[hp]])
        dests = [(R[0], b_R[0], 128), (R[1], b_R[1], 128), (KX[0], b_KX[0], 128), (KX[1], b_KX[1], 128),
                 (V[0], b_V[0], 128), (V[1], b_V[1], 128), (XWA, b_XWA, 128), (XG0, b_XG0, 128), (XG1, b_XG1, 32)]
        ipc = [0]

        def proj_n(slot, bw, col0, ncols, s0):
            tb = s0 // 512
            pt, bp = nbank(k)
            for c in range(KC):
                P.op("pe", lambda e, c=c: e.matmul(pt[0:ncols, 0:NS], lhsT=slot[:, c, col0:col0 + ncols], rhs=k.xT[:, c, s0:s0 + NS],
                                                   start=(c == 0), stop=(c == KC - 1)),
                     reads=[bw, k.b_xT[c][tb]], writes=[bp])
            return pt, bp

        def subblock(sb):
            s0 = sb * NS
            tb = s0 // 512
            p_ = sb % 2
            rh, kh, bh, ah, vb = RHs[p_], KHs[p_], BHs[p_], AHs[p_], VBs[p_]
            b_rh, b_kh, b_bh, b_ah, b_vb = b_RHs[p_], b_KHs[p_], b_BHs[p_], b_AHs[p_], b_VBs[p_]
            WC, b_WC = WCs[p_], b_WCs[p_]
            BON, b_BON, GT, b_GT = BONs[sb % 3], b_BONs[sb % 3], GTs[sb % 3], b_GTs[sb % 3]
            YALL, b_YALL = YALLs[p_], b_YALLs[p_]

            def prep():
                for f, (dst, bd, rows) in enumerate(dests):
                    if f < 4:
                        pt, bp = proj_n(slA, bwA, f * 128, 128, s0)
                    elif f < 8:
                        pt, bp = proj_n(slB, bwB, (f - 4) * 128, 128, s0)
                    else:
                        pt, bp = proj_n(slC, bwC, 0, 32, s0)
                    j = ipc[0] % 2
                    ipc[0] += 1
                    P.op("dve", lambda e, j=j, f=f, rows=rows: e.tensor_copy(out=praw[j][0:rows, 0:1], in_=pprev[0:rows, f:f + 1]),
                         reads=[b_pprev], writes=[b_praw[j]])
                    P.op("act", lambda e, j=j, pt=pt, rows=rows: e.activation(out=praw[j][0:rows, 1:NS + 1], in_=pt[0:rows, 0:NS], func=AF.Copy),
                         reads=[bp], writes=[b_praw[j]])
                    P.op("dve", lambda e, j=j, f=f, rows=rows: e.tensor_copy(out=pprev[0:rows, f:f + 1], in_=praw[j][0:rows, NS:NS + 1]),
                         reads=[b_praw[j]], writes=[b_pprev])
                    P.op("dve", lambda e, j=j, rows=rows, dst=dst: e.tensor_tensor(out=dst[0:rows, :], in0=praw[j][0:rows, 0:NS], in1=praw[j][0:rows, 1:NS + 1], op=ALU.subtract),
                         reads=[b_praw[j]], writes=[bd])
                    P.op("dve", lambda e, j=j, rows=rows, f=f, dst=dst: e.scalar_tensor_tensor(
                        out=dst[0:rows, :], in0=dst[0:rows, :], scalar=pcol(k, "rw_mu", f, rows=rows), in1=praw[j][0:rows, 1:NS + 1],
                        op0=ALU.mult, op1=ALU.add),
                        reads=[bd, b_praw[j], k.b_pv], writes=[bd])
                    yield
                P.op("act", lambda e: e.activation(out=sgx0[:], in_=XG0[:], func=AF.Sigmoid), reads=[b_XG0], writes=[b_sgx])
                P.op("act", lambda e: e.activation(out=sgx1[:], in_=XG1[:], func=AF.Sigmoid), reads=[b_XG1], writes=[b_sgx])
                for hp in range(2):
                    pg, bpg = nbank(k)
                    P.op("pe", lambda e, pg=pg, hp=hp: e.matmul(pg[:, 0:NS], lhsT=g2a[:, hp * 128:(hp + 1) * 128], rhs=sgx0[:], start=True, stop=False),
                         reads=[b_g2, b_sgx], writes=[bpg])
                    P.op("pe", lambda e, pg=pg, hp=hp: e.matmul(pg[:, 0:NS], lhsT=g2b[:, hp * 128:(hp + 1) * 128], rhs=sgx1[:], start=False, stop=True),
                         reads=[b_g2, b_sgx], writes=[bpg])
                    P.op("act", lambda e, pg=pg, hp=hp: e.activation(out=GT[hp][:], in_=pg[:, 0:NS], func=AF.Copy), reads=[bpg], writes=[b_GT[hp]])
                    yield
                P.op("act", lambda e: e.activation(out=XWA[0:64, :], in_=XWA[0:64, :], func=AF.Tanh), reads=[b_XWA], writes=[b_XWA])
                for hp in range(2):
                    hs = slice(hp * 128, (hp + 1) * 128)
                    pw, bpw = nbank(k)
                    P.op("pe", lambda e, pw=pw, hs=hs: mm(e, pw[:, 0:NS], wa[0:64, hs], XWA[0:64, :], (0, 0)), reads=[b_wa, b_XWA], writes=[bpw], rt=0)
                    P.op("act", lambda e, pw=pw, hp=hp: e.activation(out=EW[:], in_=pw[:, 0:NS], func=AF.Exp, scale=-1.0, bias=pcol(k, "rw_w0", hp, neg=True)),
                         reads=[bpw, k.b_npv], writes=[b_EW])
                    P.op("act", lambda e: e.activation(out=EW[:], in_=EW[:], func=AF.Ln, bias=1.0), reads=[b_EW], writes=[b_EW])
                    P.op("act", lambda e: e.activation(out=EW[:], in_=EW[:], func=AF.Exp, scale=-1.0, bias=-0.5), reads=[b_EW], writes=[b_EW])
                    yield
                    pa, bpa = nbank(k)
                    P.op("pe", lambda e, pa=pa, hs=hs: mm(e, pa[:, 0:NS], wa[64:128, hs], XWA[64:128, :], (64, 0)), reads=[b_wa, b_XWA], writes=[bpa], rt=64)
                    P.op("act", lambda e, pa=pa, hp=hp: e.activation(out=AL[:], in_=pa[:, 0:NS], func=AF.Sigmoid, bias=pcol(k, "rw_a0", hp)),
                         reads=[bpa, k.b_pv], writes=[b_AL])
                    yield
                    P.op("dve", lambda e, hp=hp: e.tensor_scalar(out=KKN[:], in0=KX[hp][:], scalar1=pcol(k, "rw_kk", hp), scalar2=None, op0=ALU.mult),
                         reads=[b_KX[hp], k.b_pv], writes=[b_KKN])
                    P.op("dve", lambda e: e.tensor_tensor(out=TMP[:], in0=KKN[:], in1=KKN[:], op=ALU.mult), reads=[b_KKN], writes=[b_TMP])
                    pss, bpss = nbank(k)
                    P.op("pe", lambda e, pss=pss: e.matmul(pss[:, 0:NS], lhsT=bones, rhs=TMP[:], start=True, stop=True), reads=[k.b_cst, b_TMP], writes=[bpss])
                    P.op("act", lambda e, pss=pss: e.activation(out=TMP[:], in_=pss[:, 0:NS], func=AF.Ln), reads=[bpss], writes=[b_TMP])
                    P.op("act", lambda e: e.activation(out=TMP[:], in_=TMP[:], func=AF.Exp, scale=-0.5), reads=[b_TMP], writes=[b_TMP])
                    P.op("dve", lambda e: e.tensor_tensor(out=KKN[:], in0=KKN[:], in1=TMP[:], op=ALU.mult), reads=[b_KKN, b_TMP], writes=[b_KKN])
                    yield
                    P.op("dve", lambda e, hp=hp: e.tensor_scalar(out=TMP[:], in0=AL[:], scalar1=pcol(k, "rw_ka", hp), scalar2=omk[:, hp:hp + 1], op0=ALU.mult, op1=ALU.add),
                         reads=[b_AL, k.b_pv, b_omk], writes=[b_TMP])
                    P.op("dve", lambda e, hp=hp: e.tensor_tensor(out=KX[hp][:], in0=KX[hp][:], in1=TMP[:], op=ALU.mult), reads=[b_KX[hp], b_TMP], writes=[b_KX[hp]])
                    P.op("dve", lambda e, hp=hp: e.scalar_tensor_tensor(out=TMP[:], in0=R[hp][:], scalar=pcol(k, "rw_rk", hp), in1=KX[hp][:], op0=ALU.mult, op1=ALU.mult),
                         reads=[b_R[hp], b_KX[hp], k.b_pv], writes=[b_TMP])
                    pbo, bpbo = nbank(k)
                    P.op("pe", lambda e, pbo=pbo: e.matmul(pbo[:, 0:NS], lhsT=bones, rhs=TMP[:], start=True, stop=True), reads=[k.b_cst, b_TMP], writes=[bpbo])
                    P.op("dve", lambda e, pbo=pbo, hp=hp: e.tensor_tensor(out=BON[hp][:], in0=pbo[:, 0:NS], in1=V[hp][:], op=ALU.mult),
                         reads=[bpbo, b_V[hp]], writes=[b_BON[hp]])
                    yield
                    P.op("act", lambda e, hp=hp: e.activation(out=vb[hp][:], in_=V[hp][:], func=AF.Copy), reads=[b_V[hp]], writes=[b_vb[hp]])
                    for c in range(NCH):
                        cc = slice(c * 32, (c + 1) * 32)
                        P.op("dve", lambda e, cc=cc: e.tensor_tensor_scan(out=CS[:, cc], data0=ones32[:], data1=EW[:, cc], initial=0.0, op0=ALU.mult, op1=ALU.add),
                             reads=[b_ones, b_EW], writes=[b_CS])
                    P.op("act", lambda e: e.activation(out=E1[:], in_=CS[:], func=AF.Exp), reads=[b_CS], writes=[b_E1])
                    P.op("act", lambda e: e.activation(out=E2[:], in_=CS[:], func=AF.Exp, scale=-1.0), reads=[b_CS], writes=[b_E2])
                    P.op("dve", lambda e: e.tensor_tensor(out=TMP[:], in0=EW[:], in1=CS[:], op=ALU.subtract), reads=[b_EW, b_CS, b_TMP], writes=[b_TMP])
                    P.op("act", lambda e: e.activation(out=E3[:], in_=TMP[:], func=AF.Exp), reads=[b_TMP], writes=[b_E3])
                    yield
                    E2v = E2[:].rearrange("p (c t) -> p c t", t=32)
                    P.op("dve", lambda e, hp=hp, E2v=E2v: e.tensor_copy(out=WC[hp][:], in_=E2v[:, :, 31]), reads=[b_E2], writes=[b_WC[hp]])
                    P.op("dve", lambda e, hp=hp: e.tensor_tensor(out=rh[hp][:], in0=R[hp][:], in1=E2[:], op=ALU.mult), reads=[b_R[hp], b_E2], writes=[b_rh[hp]])
                    P.op("dve", lambda e, hp=hp: e.tensor_tensor(out=kh[hp][:], in0=KX[hp][:], in1=E1[:], op=ALU.mult), reads=[b_KX[hp], b_E1], writes=[b_kh[hp]])
                    P.op("dve", lambda e: e.tensor_tensor(out=TMP[:], in0=KKN[:], in1=AL[:], op=ALU.mult), reads=[b_KKN, b_AL], writes=[b_TMP])
                    P.op("dve", lambda e, hp=hp: e.tensor_tensor(out=bh[hp][:], in0=TMP[:], in1=E1[:], op=ALU.mult), reads=[b_TMP, b_E1], writes=[b_bh[hp]])
                    P.op("dve", lambda e, hp=hp: e.scalar_tensor_tensor(out=ah[hp][:], in0=KKN[:], scalar=-1.0, in1=E3[:], op0=ALU.mult, op1=ALU.mult),
                         reads=[b_KKN, b_E3], writes=[b_ah[hp]])
                    yield
                yield

            def chunks():
                def A_gen(c):
                    cc = slice(c * 32, (c + 1) * 32)
                    j = c % 2
                    pX_, bpX_ = nbank(k, hold=True)
                    pY_, bpY_ = nbank(k, hold=True)
                    P.op("dve", lambda e: e.memset(pX_[:], 0.0), writes=[bpX_])
                    P.op("dve", lambda e: e.memset(pY_[:, 0:128], 0.0), writes=[bpY_])
                    yield
                    for h in (0, 2, 1, 3):
                        hp = h // 2
                        ks = slice((h % 2) * 64, (h % 2) * 64 + 64)
                        hs = slice(h * 32, (h + 1) * 32)
                        pos = (ks.start, hs.start)
                        for mi, (lt, blt, rt_, brt) in enumerate(((bh, b_bh, ah, b_ah), (ah, b_ah, bh, b_bh), (kh, b_kh, ah, b_ah), (bh, b_bh, rh, b_rh))):
                            P.op("pe", lambda e, hs=hs, ks=ks, hp=hp, mi=mi, lt=lt, rt_=rt_, pos=pos, h=h: mm(
                                e, pX_[hs, mi * 128 + h * 32: mi * 128 + (h + 1) * 32], lt[hp][ks, cc], rt_[hp][ks, cc], pos),
                                reads=[blt[hp], brt[hp]], writes=[bpX_], rt=pos[0])
                        P.op("pe", lambda e, hs=hs, ks=ks, hp=hp, pos=pos, h=h: mm(
                            e, pY_[hs, h * 32:(h + 1) * 32], kh[hp][ks, cc], rh[hp][ks, cc], pos),
                            reads=[b_kh[hp], b_rh[hp]], writes=[bpY_], rt=pos[0])
                    yield
                    P.op("dve", lambda e: e.tensor_tensor(out=M4[j][:].rearrange("p a b -> p (a b)"), in0=pX_[:], in1=masks, op=ALU.mult),
                         reads=[bpX_, k.b_cst], writes=[b_M4[j]])
                    P.op("dve", lambda e: e.tensor_tensor(out=RKT[j][:], in0=pY_[:, 0:128], in1=bd_iu, op=ALU.mult),
                         reads=[bpY_, k.b_cst], writes=[b_RKT[j]])
                    release(k, pX_); release(k, pY_)
                    LT = M4[j][:, 0, :]; Lm = M4[j][:, 1, :]
                    TTj = TT[j]
                    bTTj = b_TT[j]
                    P.op("dve", lambda e: e.tensor_tensor(out=TTj[0][:], in0=LT, in1=identb, op=ALU.add), reads=[b_M4[j], k.b_cstb], writes=[bTTj[0]])
                    yield
                    A_prev, bA_prev, AT_prev, bAT_prev = Lm, b_M4[j], LT, b_M4[j]
                    ti = 0
                    for kq in range(1, 5):
                        an = kq % 2
                        pA, bpA = nbank(k, hold=True)
                        P.op("pe", lambda e, pA=pA, AT_prev=AT_prev, A_prev=A_prev: e.matmul(pA[:, 0:128], lhsT=AT_prev, rhs=A_prev, start=True, stop=True),
                             reads=[bA_prev, bAT_prev], writes=[bpA])
                        if kq < 4:
                            P.op("pe", lambda e, pA=pA, AT_prev=AT_prev, A_prev=A_prev: e.matmul(pA[:, 128:256], lhsT=A_prev, rhs=AT_prev, start=True, stop=True),
                                 reads=[bA_prev, bAT_prev], writes=[bpA])
                        yield
                        P.op("act", lambda e, pA=pA, an=an: e.activation(out=Am[an][:], in_=pA[:, 0:128], func=AF.Copy), reads=[bpA], writes=[b_A[an]])
                        if kq < 4:
                            P.op("act", lambda e, pA=pA, an=an: e.activation(out=ATm[an][:], in_=pA[:, 128:256], func=AF.Copy), reads=[bpA], writes=[b_AT[an]])
                        release(k, pA)
                        yield
                        pT, bpT = nbank(k, hold=True)
                        P.op("pe", lambda e, pT=pT, an=an, ti=ti: e.matmul(pT[:, 0:128], lhsT=Am[an][:], rhs=TTj[ti][:], start=True, stop=True),
                             reads=[b_A[an], bTTj[ti]], writes=[bpT])
                        yield
                        P.op("dve", lambda e, pT=pT, ti=ti: e.tensor_tensor(out=TTj[1 - ti][:], in0=pT[:, 0:128], in1=TTj[ti][:], op=ALU.add),
                             reads=[bpT, bTTj[ti]], writes=[bTTj[1 - ti]])
                        release(k, pT)
                        ti = 1 - ti
                        A_prev, bA_prev, AT_prev, bAT_prev = Am[an][:], b_A[an], ATm[an][:], b_AT[an]
                        yield
                    assert ti == 0
                    pTr, bpTr = nbank(k, hold=True)
                    pTb = k.psb[k.ps.index(pTr)]
                    P.op("dve", lambda e: e.memset(pTr[:, 0:256], 0.0), writes=[bpTr])
                    yield
                    for h in (0, 2, 1, 3):
                        hp = h // 2
                        ks = slice((h % 2) * 64, (h % 2) * 64 + 64)
                        hs = slice(h * 32, (h + 1) * 32)
                        pos = (ks.start, hs.start)
                        P.op("pe", lambda e, hs=hs, ks=ks, hp=hp, pos=pos: tr(e, pTb[hs, hp * 128 + ks.start: hp * 128 + ks.start + 64], bh[hp][ks, cc], identb[ks, ks], pos),
                             reads=[b_bh[hp], k.b_cstb], writes=[bpTr], rt=pos[0])
                        P.op("pe", lambda e, hs=hs, ks=ks, hp=hp, pos=pos: tr(e, pTb[hs, 256 + hp * 128 + ks.start: 256 + hp * 128 + ks.start + 64], kh[hp][ks, cc], identb[ks, ks], pos),
                             reads=[b_kh[hp], k.b_cstb], writes=[bpTr], rt=pos[0])
                        P.op("pe", lambda e, hs=hs, ks=ks, hp=hp, pos=pos: tr(e, pTb[hs, 512:576], vb[hp][ks, cc], identb[ks, ks], pos),
                             reads=[b_vb[hp], k.b_cstb], writes=[bpTr], rt=pos[0])
                    yield
                    P.op("dve", lambda e: e.tensor_copy(out=BK[j][:].rearrange("p a b -> p (a b)"), in_=pTb[:, 0:512]),
                         reads=[bpTr], writes=[b_BK[j]])
                    P.op("act", lambda e: e.activation(out=V4[j][:], in_=pTb[:, 512:576], func=AF.Copy), reads=[bpTr], writes=[b_V4[j]])
                    release(k, pTr)
                    yield

                def B_gen(c):
                    cc = slice(c * 32, (c + 1) * 32)
                    j = c % 2
                    AKT = M4[j][:, 2, :]; RBT = M4[j][:, 3, :]
                    TTf, bTTf = TT[j][0], b_TT[j][0]
                    pX, bpX = nbank(k, hold=True)
                    P.op("pe", lambda e: e.matmul(pX[:, 0:64], lhsT=AKT, rhs=V4[j][:], start=True, stop=False),
                         reads=[b_M4[j], b_V4[j]], writes=[bpX])
                    for h in (0, 2, 1, 3):
                        hp = h // 2
                        ks = slice((h % 2) * 64, (h % 2) * 64 + 64)
                        hs = slice(h * 32, (h + 1) * 32)
                        P.op("pe", lambda e, hs=hs, ks=ks, hp=hp: mm(e, pX[hs, 0:64], ah[hp][ks, cc], Hb[hp][ks, :], (ks.start, hs.start), start=False, stop=True),
                             reads=[b_ah[hp], b_Hb[hp]], writes=[bpX], rt=ks.start)
                    yield
                    P.op("act", lambda e: e.activation(out=Xb[j][:], in_=pX[:, 0:64], func=AF.Copy), reads=[bpX], writes=[b_Xb[j]])
                    release(k, pX)
                    yield
                    pU, bpU = nbank(k, hold=True)
                    P.op("pe", lambda e: e.matmul(pU[:, 0:64], lhsT=TTf[:], rhs=Xb[j][:], start=True, stop=True),
                         reads=[bTTf, b_Xb[j]], writes=[bpU])
                    yield
                    P.op("act", lambda e: e.activation(out=Ub[j][:], in_=pU[:, 0:64], func=AF.Copy), reads=[bpU], writes=[b_Ub[j]])
                    release(k, pU)
                    yield
                    pHs = []
                    for hp in range(2):
                        pH, bpH = nbank(k, hold=True)
                        pHs.append((pH, bpH))
                        P.op("pe", lambda e, pH=pH, hp=hp: e.matmul(pH[:, 0:64], lhsT=BK[j][:, hp, :], rhs=Ub[j][:], start=True, stop=False),
                             reads=[b_BK[j], b_Ub[j]], writes=[bpH])
                        P.op("pe", lambda e, pH=pH, hp=hp: e.matmul(pH[:, 0:64], lhsT=BK[j][:, 2 + hp, :], rhs=V4[j][:], start=False, stop=True),
                             reads=[b_BK[j], b_V4[j]], writes=[bpH])
                    pY, bpY = nbank(k, hold=True)
                    P.op("pe", lambda e: e.matmul(pY[:, 0:64], lhsT=RBT, rhs=Ub[j][:], start=True, stop=False),
                         reads=[b_M4[j], b_Ub[j]], writes=[bpY])
                    P.op("pe", lambda e: e.matmul(pY[:, 0:64], lhsT=RKT[j][:], rhs=V4[j][:], start=False, stop=False),
                         reads=[b_RKT[j], b_V4[j]], writes=[bpY])
                    for h in (0, 2, 1, 3):
                        hp = h // 2
                        ks = slice((h % 2) * 64, (h % 2) * 64 + 64)
                        hs = slice(h * 32, (h + 1) * 32)
                        P.op("pe", lambda e, hs=hs, ks=ks, hp=hp: mm(e, pY[hs, 0:64], rh[hp][ks, cc], Hb[hp][ks, :], (ks.start, hs.start), start=False, stop=True),
                             reads=[b_rh[hp], b_Hb[hp]], writes=[bpY], rt=ks.start)
                    yield
                    for hp in range(2):
                        pH, bpH = pHs[hp]
                        P.op("dve", lambda e, pH=pH, hp=hp: e.tensor_tensor(out=Hf[hp][:], in0=pH[:, 0:64], in1=Hf[hp][:], op=ALU.add),
                             reads=[bpH, b_Hf[hp]], writes=[b_Hf[hp]])
                        release(k, pH)
                    P.op("act", lambda e: e.activation(out=YALL[:, c, :], in_=pY[:, 0:64], func=AF.Copy), reads=[bpY], writes=[b_YALL])
                    release(k, pY)
                    yield
                    for hp in range(2):
                        P.op("dve", lambda e, hp=hp: e.tensor_scalar(out=Hf[hp][:], in0=Hf[hp][:], scalar1=WC[hp][:, c:c + 1], scalar2=None, op0=ALU.mult),
                             reads=[b_Hf[hp], b_WC[hp]], writes=[b_Hf[hp]])
                    yield
                    for hp in range(2):
                        P.op("act", lambda e, hp=hp: e.activation(out=Hb[hp][:], in_=Hf[hp][:], func=AF.Copy), reads=[b_Hf[hp]], writes=[b_Hb[hp]])
                    yield

                for _ in A_gen(0):
                    yield
                for c in range(NCH):
                    gens = [B_gen(c)]
                    if c + 1 < NCH:
                        gens.append(A_gen(c + 1))
                    while gens:
                        for g_ in list(gens):
                            try:
                                next(g_)
                            except StopIteration:
                                gens.remove(g_)
                        yield
                yield

            def post():
                P.op("dve", lambda e: e.tensor_reduce(out=s1[:], in_=YALL[:], axis=mybir.AxisListType.X, op=ALU.add), reads=[b_YALL], writes=[b_s1])
                P.op("dve", lambda e: e.tensor_tensor(out=ysq[:], in0=YALL[:], in1=YALL[:], op=ALU.mult), reads=[b_YALL], writes=[b_ysq])
                P.op("dve", lambda e: e.tensor_reduce(out=s2_[:], in_=ysq[:], axis=mybir.AxisListType.X, op=ALU.add), reads=[b_ysq], writes=[b_s2])
                P.op("dve", lambda e: e.tensor_scalar(out=s1[:], in0=s1[:], scalar1=1.0 / 64.0, scalar2=None, op0=ALU.mult), reads=[b_s1], writes=[b_s1])
                P.op("dve", lambda e: e.scalar_tensor_tensor(out=s2_[:], in0=s2_[:], scalar=1.0 / 64.0, in1=s2_[:], op0=ALU.mult, op1=ALU.bypass) if False else
                     e.tensor_scalar(out=s2_[:], in0=s2_[:], scalar1=1.0 / 64.0, scalar2=64e-5, op0=ALU.mult, op1=ALU.add), reads=[b_s2], writes=[b_s2])
                P.op("dve", lambda e: e.tensor_tensor(out=ysq[:, :, 0], in0=s1[:], in1=s1[:], op=ALU.mult), reads=[b_s1, b_ysq], writes=[b_ysq])
                P.op("dve", lambda e: e.tensor_tensor(out=s2_[:], in0=s2_[:], in1=ysq[:, :, 0], op=ALU.subtract), reads=[b_s2, b_ysq], writes=[b_s2])
                P.op("act", lambda e: e.activation(out=s2_[:], in_=s2_[:], func=AF.Ln), reads=[b_s2], writes=[b_s2])
                P.op("act", lambda e: e.activation(out=s2_[:], in_=s2_[:], func=AF.Exp, scale=-0.5), reads=[b_s2], writes=[b_s2])
                P.op("dve", lambda e: e.scalar_tensor_tensor(out=s1[:], in0=s1[:], scalar=-1.0, in1=s2_[:], op0=ALU.mult, op1=ALU.mult), reads=[b_s1, b_s2], writes=[b_s1])
                for c in range(NCH):
                    P.op("act", lambda e, c=c: e.activation(out=YN[:, c, :], in_=YALL[:, c, :], func=AF.Identity, scale=s2_[:, c:c + 1], bias=s1[:, c:c + 1]),
                         reads=[b_YALL, b_s1, b_s2], writes=[b_YN])
                yield
                pF, bpF = nbank(k, hold=True)
                pFb = k.psb[k.ps.index(pF)]
                for h in range(4):
                    yield
                    for c in range(NCH):
                        hs = slice(h * 32, (h + 1) * 32)
                        vs = slice((h % 2) * 64, (h % 2) * 64 + 64)
                        o0 = (h // 2) * 512 + c * 32
                        P.op("pe", lambda e, pFb=pFb, hs=hs, vs=vs, o0=o0, c=c: tr(e, pFb[vs, o0:o0 + 32], YN[hs, c, :], identb[hs, hs], (hs.start, vs.start)),
                             reads=[b_YN, k.b_cstb], writes=[bpF], rt=hs.start)
                for hp in range(2):
                    P.op("act", lambda e, pFb=pFb, hp=hp: e.activation(out=y1[:], in_=pFb[:, hp * 512: hp * 512 + NS], func=AF.Identity,
                                                                   scale=pcol(k, "rw_ln_g", hp), bias=pcol(k, "rw_ln_b", hp)),
                         reads=[bpF, k.b_pv], writes=[b_y1])
                    P.op("dve", lambda e, hp=hp: e.tensor_tensor(out=y1[:], in0=y1[:], in1=BON[hp][:], op=ALU.add), reads=[b_y1, b_BON[hp]], writes=[b_y1])
                    P.op("dve", lambda e, hp=hp, s0=s0: e.tensor_tensor(out=brT[:, hp, s0:s0 + NS], in0=y1[:], in1=GT[hp][:], op=ALU.mult),
                         reads=[b_y1, b_GT[hp]], writes=[b_brT[hp][tb]])
                release(k, pF)
                yield
                yield

            return prep(), chunks(), post()

        phases = [subblock(sb) for sb in range(NSB)]
        for _ in phases[0][0]:
            yield
        for sb in range(NSB):
            gl = [phases[sb][1]]
            wl = [230.0]
            if sb + 1 < NSB:
                gl.append(phases[sb + 1][0]); wl.append(30.0)
            if sb >= 1:
                gl.append(phases[sb - 1][2]); wl.append(8.0)
            done = [0.0] * len(gl)
            live = list(range(len(gl)))
            while live:
                i_ = min(live, key=lambda q: done[q] / wl[q])
                try:
                    next(gl[i_])
                    done[i_] += 1.0
                except StopIteration:
                    live.remove(i_)
                yield
        for _ in phases[NSB - 1][2]:
            yield
        yield

def stage_gate(k, l, brT, b_brT):
    P = k.P; T = k.T; NTB = k.NTB
    with ExitStack() as s2:
        mg = sbt(k, s2, "mg", [128, 4, T], BF16)
        b_mg = [[Buf() for _ in range(NTB)] for _ in range(4)]
        acc = sbt(k, s2, "mg_acc", [128, 512], F32); b_acc = Buf()
        sg = [sbt(k, s2, f"mg_sg{i}", [128, 512], F32) for i in range(2)]; b_sg = [Buf(), Buf()]
        pr0 = sbt(k, s2, "mg_pr", [128, 512], F32); pr = [pr0, pr0]; bpr0 = Buf(); b_pr = [bpr0, bpr0]
        ups, b_ups = k.ups, k.b_ups
        lnb = alloc_ln(k, s2)
        og, _ = PV["gate_b"]
        i = 0
        for half in range(2):
            for fq in range(4):
                fc = half * 4 + fq
                u = fc % 2
                for b in range(4):
                    v = k.ups_d[b][l][:, fc * 128:(fc + 1) * 128].rearrange("(c p) n -> p c n", p=128)
                    P.dma("pool", lambda e, b=b, v=v, u=u: e.dma_start(out=ups[u][:, :, b, :], in_=v), writes=[b_ups[u]])
                i0 = k.wr_i
                k.wr_i = (i0 + 1) % k.NW
                slot, bw = k.wr[i0], k.b_wr[i0]
                for b in range(4):
                    c0 = COL_GATE + b * D + fc * 128
                    v = k.w_in_d[l][:, c0:c0 + 128].rearrange("(c p) n -> p c n", p=128)
                    P.dma("pool", lambda e, b=b, v=v, slot=slot: e.dma_start(out=slot[:, :, b * 128:(b + 1) * 128], in_=v), writes=[bw])
                for tb in range(NTB):
                    sl = slice(tb * 512, (tb + 1) * 512)
                    for b in range(4):
                        pg, bpg = nbank(k)
                        for c in range(KC):
                            P.op("pe", lambda e, pg=pg, c=c, b=b, sl=sl, slot=slot: e.matmul(
                                pg[:], lhsT=slot[:, c, b * 128:(b + 1) * 128], rhs=k.xT[:, c, sl], start=(c == 0), stop=(c == KC - 1)),
                                reads=[bw, k.b_xT[c][tb]], writes=[bpg])
                        pu, bpu = nbank(k)
                        for c in range(2):
                            P.op("pe", lambda e, pu=pu, c=c, b=b, sl=sl, u=u: e.matmul(
                                pu[:], lhsT=ups[u][:, c, b, :], rhs=brT[:, b * 2 + c, sl], start=(c == 0), stop=(c == 1)),
                                reads=[b_ups[u], b_brT[b * 2 + c][tb]], writes=[bpu])
                        j = i % 2
                        i += 1
                        gcol = k.pv[:, og + b * 8 + fc: og + b * 8 + fc + 1]
                        P.op("act", lambda e, pg=pg, j=j, gcol=gcol: e.activation(out=sg[j][:], in_=pg[:], func=AF.Sigmoid, bias=gcol),
                             reads=[bpg, k.b_pv], writes=[b_sg[j]])
                        if b == 0:
                            P.op("dve", lambda e, pu=pu, j=j: e.tensor_tensor(out=acc[:], in0=pu[:], in1=sg[j][:], op=ALU.mult),
                                 reads=[bpu, b_sg[j]], writes=[b_acc])
                        else:
                            P.op("dve", lambda e, pu=pu, j=j: e.tensor_tensor(out=pr[j][:], in0=pu[:], in1=sg[j][:], op=ALU.mult),
                                 reads=[bpu, b_sg[j]], writes=[b_pr[j]])
                            if b < 3:
                                P.op("dve", lambda e, j=j: e.tensor_tensor(out=acc[:], in0=acc[:], in1=pr[j][:], op=ALU.add),
                                     reads=[b_acc, b_pr[j]], writes=[b_acc])
                            else:
                                P.op("dve", lambda e, j=j, fq=fq, sl=sl: e.tensor_tensor(out=mg[:, fq, sl], in0=acc[:], in1=pr[j][:], op=ALU.add),
                                     reads=[b_acc, b_pr[j]], writes=[b_mg[fq][tb]])
            if "mg" in k.dbg_d:
                for c in range(4):
                    P.dma("sp", lambda e, c=c, half=half: e.dma_start(out=k.dbg_d["mg"][half * 4 + c], in_=mg[:, c, :]),
                          reads=[b_mg[c][tb] for tb in range(NTB)], is_output=True)
            out_proj_ln(k, l, 0, mg, b_mg, 4, k.w_out_d[l], first=(half == 0), last=(half == 1), row0=half * 512, lnbufs=lnb)
        P.barrier()


def run_concurrent(gens, weights):
    gens = list(gens)
    done = [0.0] * len(gens)
    live = list(range(len(gens)))
    while live:
        i = min(live, key=lambda q: done[q] / weights[q])
        try:
            next(gens[i])
            done[i] += 1.0
        except StopIteration:
            live.remove(i)


def stage_mix(k, l):
    P = k.P; T = k.T; NTB = k.NTB
    spill_xres(k)
    with ExitStack() as s2:
        brT = sbt(k, s2, "brT", [128, 8, T], BF16)
        b_brT = [[Buf() for _ in range(NTB)] for _ in range(8)]
        todo = k.mixers
        for nm, cs_ in (("rw", (0, 1)), ("cv", (2, 3)), ("gla", (4, 5)), ("fox", (6, 7))):
            if nm not in todo:
                for c in cs_:
                    P.op("dve", lambda e, c=c: e.memset(brT[:, c, :], 0.0), writes=[b_brT[c][tb] for tb in range(NTB)])
        fns = {"rw": mixer_rwkv, "gla": mixer_gla, "fox": mixer_fox, "cv": mixer_conv}
        for group in (("rw", "gla"), ("fox", "cv")):
            act = [n for n in group if n in todo]
            if not act:
                continue
            with ExitStack() as s3:
                wts = {"rw": 1753.0, "gla": 621.0, "fox": 341.0, "cv": 75.0}
                run_concurrent([fns[n](k, l, brT, b_brT, s3) for n in act], [wts[n] for n in act])
                P.barrier()
        if "brT" in k.dbg_d:
            for c in range(8):
                P.dma("sp", lambda e, c=c: e.dma_start(out=k.dbg_d["brT"][c], in_=brT[:, c, :]),
                      reads=[b_brT[c][tb] for tb in range(NTB)], is_output=True)
        reload_xres(k)
        stage_gate(k, l, brT, b_brT)


def prep_inputs(inp, L=DEPTH):
    f = lambda a: np.ascontiguousarray(np.asarray(a, dtype=np.float32))
    shared = {
        "consts": CONSTS,
        "pvec": np.stack([pack_pvec(inp, l) for l in range(L)]),
        "w_in": f(inp["w_in"][:L]),
        "rw_wa": f(np.concatenate([np.asarray(inp["rw_w2"][:L]), np.asarray(inp["rw_a2"][:L])], axis=1)),
        "rw_g2": f(inp["rw_g2"][:L]),
        "gla_a2": f(inp["gla_a2"][:L]),
        "w_out": f(inp["w_out"][:L]),
    }
    for n in ("rw_up", "cv_up", "gla_up", "fox_up", "xa_wq", "xa_wk", "xa_wv", "xa_wo", "ffn_w1", "ffn_w3", "ffn_w2"):
        shared[n] = f(inp[n][:L])
    return shared


_CACHE = {}


def kernel(**inputs):
    x = np.asarray(inputs["x"], np.float32)
    mem = np.asarray(inputs["mem"], np.float32)
    B = x.shape[0]
    if "nc" not in _CACHE:
        _CACHE["nc"] = build()[0]
    nc = _CACHE["nc"]
    shared = prep_inputs(inputs)
    in_maps = []
    for b in range(B):
        m = dict(shared)
        m["x"] = np.ascontiguousarray(x[b])
        m["mem"] = np.ascontiguousarray(mem[b])
        in_maps.append(m)
    res = run_bass_kernel_spmd(nc, in_maps, core_ids=list(range(B)))
    return np.stack([r["out"] for r in res.results], axis=0).astype(np.float32)
```
